# Optimizing a Trainium2 kernel written in Bass

```python
import math
import jax, jax.numpy as jnp
from jax import lax
import numpy as np

D_MODEL = 1024
BATCH = 4
SEQ = 8192
DEPTH = 2

SSM_HEADS = 16
SSM_HEAD_DIM = 64
SSM_D_INNER = SSM_HEADS * SSM_HEAD_DIM
SSM_GROUPS = 2
SSM_HEADS_PER_GROUP = SSM_HEADS // SSM_GROUPS
SSM_STATE = 128
SSM_CONV = 4
SSM_CHUNK = 128
SSM_XBC = SSM_D_INNER + 2 * SSM_GROUPS * SSM_STATE
ATTN_Q_HEADS = 16
ATTN_KV_HEADS = 2
ATTN_GQA = ATTN_Q_HEADS // ATTN_KV_HEADS
ATTN_HEAD_DIM = 64
ATTN_WINDOW = 128
ATTN_BLOCK = 128
REL_BUCKETS = 32
REL_MAX_DIST = 128
IN_A = SSM_D_INNER + SSM_XBC + SSM_HEADS
IN_B = (ATTN_Q_HEADS + 2 * ATTN_KV_HEADS) * ATTN_HEAD_DIM
IN_TOTAL = IN_A + IN_B
MIX_WIDTH = SSM_D_INNER + ATTN_Q_HEADS * ATTN_HEAD_DIM
CONV_CH = D_MODEL
CONV_K = 31
D_FF = 2816
FFN_CONV = 3
N_EVEN = (DEPTH + 1) // 2
N_ODD = DEPTH // 2
RMS_EPS = 1e-6
LN_EPS = 1e-5

kernel_name = "hybrid_ssd_swa_conformer_convffn"


def rms_norm(x, w, eps=RMS_EPS):
    xf = x.astype(jnp.float32)
    y = xf * lax.rsqrt(jnp.mean(xf * xf, axis=-1, keepdims=True) + eps)
    return (y * w.astype(jnp.float32)).astype(x.dtype)


def layer_norm(x, w, b, eps=LN_EPS):
    xf = x.astype(jnp.float32)
    mu = jnp.mean(xf, axis=-1, keepdims=True)
    var = jnp.mean(jnp.square(xf - mu), axis=-1, keepdims=True)
    y = (xf - mu) * lax.rsqrt(var + eps)
    return (y * w.astype(jnp.float32) + b.astype(jnp.float32)).astype(x.dtype)


def causal_depthwise_conv(x, w, b):
    k_width, ch = w.shape
    y = lax.conv_general_dilated(
        x, w[:, None, :].astype(x.dtype), window_strides=(1,),
        padding=[(k_width - 1, 0)], dimension_numbers=("NWC", "WIO", "NWC"),
        feature_group_count=ch)
    return y + b.astype(x.dtype)


def segsum(a):
    t = a.shape[-1]
    cs = jnp.cumsum(a, axis=-1)
    diff = cs[..., :, None] - cs[..., None, :]
    mask = jnp.tril(jnp.ones((t, t), dtype=bool))
    return jnp.where(mask, diff, -jnp.inf)


def ssd_chunked(xh, dt, a_head, bm, cm, d_skip):
    f32 = jnp.float32
    b, seq, g, e, p = xh.shape
    n = bm.shape[-1]
    nc = seq // SSM_CHUNK
    xf = xh.astype(f32).reshape(b, nc, SSM_CHUNK, g, e, p)
    x = xf * dt.reshape(b, nc, SSM_CHUNK, g, e)[..., None]
    a = (dt * a_head).reshape(b, nc, SSM_CHUNK, g, e).transpose(0, 3, 4, 1, 2)
    bc = bm.astype(f32).reshape(b, nc, SSM_CHUNK, g, n)
    cc = cm.astype(f32).reshape(b, nc, SSM_CHUNK, g, n)
    a_cs = jnp.cumsum(a, axis=-1)
    decay_in = jnp.exp(segsum(a))
    cb = jnp.einsum("bclgn,bcsgn->bgcls", cc, bc)
    y_diag = jnp.einsum("bgecls,bcsgep->bclgep", cb[:, :, None] * decay_in, x)
    decay_states = jnp.exp(a_cs[..., -1:] - a_cs)
    states = jnp.einsum("bclgn,bgecl,bclgep->bcgepn", bc, decay_states, x)
    states = jnp.concatenate([jnp.zeros_like(states[:, :1]), states], axis=1)
    last = jnp.pad(a_cs[..., -1], ((0, 0), (0, 0), (0, 0), (1, 0)))
    chunk_decay = jnp.exp(segsum(last))
    states = jnp.einsum("bgezc,bcgepn->bzgepn", chunk_decay, states)[:, :-1]
    y_off = jnp.einsum("bclgn,bcgepn,bgecl->bclgep", cc, states, jnp.exp(a_cs))
    y = y_diag + y_off + xf * d_skip[:, :, None]
    return y.reshape(b, seq, g, e, p)


def gated_group_rms_norm(y, z, w, groups):
    b, seq, ch = y.shape
    yf = y.astype(jnp.float32) * jax.nn.silu(z.astype(jnp.float32))
    yg = yf.reshape(b, seq, groups, ch // groups)
    yg = yg * lax.rsqrt(jnp.mean(yg * yg, axis=-1, keepdims=True) + RMS_EPS)
    return (yg.reshape(b, seq, ch) * w.astype(jnp.float32)).astype(z.dtype)


def ssd_mixer(proj_a, conv_w, conv_b, dt_bias, a_log, d_skip, norm_w):
    b, seq, _ = proj_a.shape
    z = proj_a[..., :SSM_D_INNER]
    xbc = proj_a[..., SSM_D_INNER:SSM_D_INNER + SSM_XBC]
    dt_raw = proj_a[..., SSM_D_INNER + SSM_XBC:]
    xbc = jax.nn.silu(causal_depthwise_conv(xbc, conv_w, conv_b))
    gn = SSM_GROUPS * SSM_STATE
    xs = xbc[..., :SSM_D_INNER].reshape(b, seq, SSM_GROUPS, SSM_HEADS_PER_GROUP, SSM_HEAD_DIM)
    bm = xbc[..., SSM_D_INNER:SSM_D_INNER + gn].reshape(b, seq, SSM_GROUPS, SSM_STATE)
    cm = xbc[..., SSM_D_INNER + gn:].reshape(b, seq, SSM_GROUPS, SSM_STATE)
    dt = jax.nn.softplus(dt_raw.astype(jnp.float32) + dt_bias.astype(jnp.float32))
    dt = dt.reshape(b, seq, SSM_GROUPS, SSM_HEADS_PER_GROUP)
    a_head = -jnp.exp(a_log.astype(jnp.float32)).reshape(SSM_GROUPS, SSM_HEADS_PER_GROUP)
    d_h = d_skip.astype(jnp.float32).reshape(SSM_GROUPS, SSM_HEADS_PER_GROUP)
    y = ssd_chunked(xs, dt, a_head, bm, cm, d_h).reshape(b, seq, SSM_D_INNER)
    return gated_group_rms_norm(y, z, norm_w, SSM_GROUPS)


def t5_causal_bucket(dist):
    max_exact = REL_BUCKETS // 2
    d_f = jnp.maximum(dist, 1).astype(jnp.float32)
    large = max_exact + (jnp.log(d_f / max_exact) / math.log(REL_MAX_DIST / max_exact)
                         * (REL_BUCKETS - max_exact)).astype(jnp.int32)
    large = jnp.minimum(large, REL_BUCKETS - 1)
    return jnp.where(dist < max_exact, dist, large)


def swa_sink_attention(proj_b, q_norm_w, k_norm_w, sinks, rel_bias):
    f32 = jnp.float32
    b, seq, _ = proj_b.shape
    hq, hkv, dh, blk = ATTN_Q_HEADS, ATTN_KV_HEADS, ATTN_HEAD_DIM, ATTN_BLOCK
    nb = seq // blk
    q = proj_b[..., :hq * dh].reshape(b, seq, hq, dh)
    k = proj_b[..., hq * dh:(hq + hkv) * dh].reshape(b, seq, hkv, dh)
    v = proj_b[..., (hq + hkv) * dh:].reshape(b, seq, hkv, dh)
    q = rms_norm(q, q_norm_w)
    k = rms_norm(k, k_norm_w)
    qb = q.reshape(b, nb, blk, hkv, ATTN_GQA, dh)

    def with_prev(t):
        tb = t.reshape(b, nb, blk, hkv, dh)
        prev = jnp.pad(tb, ((0, 0), (1, 0), (0, 0), (0, 0), (0, 0)))[:, :-1]
        return jnp.concatenate([prev, tb], axis=2)

    kb, vb = with_prev(k), with_prev(v)
    logits = jnp.einsum("bnqkgd,bnskd->bnkgqs", qb, kb).astype(f32) * (dh ** -0.5)
    qi = jnp.arange(blk)[:, None]
    sj = jnp.arange(2 * blk)[None, :]
    dist = qi + blk - sj
    in_window = (dist >= 0) & (dist < ATTN_WINDOW)
    bias = rel_bias[t5_causal_bucket(jnp.maximum(dist, 0))]
    bias = bias.astype(f32).transpose(2, 0, 1).reshape(hkv, ATTN_GQA, blk, 2 * blk)
    key_pos = jnp.arange(nb)[:, None] * blk + sj - blk
    valid = in_window[None] & (key_pos >= 0)[:, None, :]
    logits = jnp.where(valid[None, :, None, None], logits + bias, -jnp.inf)
    sink = sinks.astype(f32).reshape(hkv, ATTN_GQA)[None, None, :, :, None, None]
    m = jnp.maximum(jnp.max(logits, axis=-1, keepdims=True), sink)
    p = jnp.exp(logits - m)
    denom = jnp.sum(p, axis=-1, keepdims=True) + jnp.exp(sink - m)
    out = jnp.einsum("bnkgqs,bnskd->bnqkgd", (p / denom).astype(vb.dtype), vb)
    return out.reshape(b, seq, hq * dh)


def conformer_conv_module(h, pw1_w, pw1_b, dw_w, dw_b, ln_w, ln_b, pw2_w, pw2_b):
    u = h @ pw1_w + pw1_b
    u = u[..., :CONV_CH] * jax.nn.sigmoid(u[..., CONV_CH:])
    u = causal_depthwise_conv(u, dw_w, dw_b)
    u = jax.nn.silu(layer_norm(u, ln_w, ln_b))
    return u @ pw2_w + pw2_b


def conv_ffn(h, w_up, conv_w, conv_b, w_down):
    u = causal_depthwise_conv(h @ w_up, conv_w, conv_b)
    return (jax.nn.silu(u[..., :D_FF]) * u[..., D_FF:]) @ w_down


def setup_inputs(seed: int = 0) -> dict:
    key = jax.random.key(seed)
    ks = jax.random.split(key, 32)
    nrm = jax.random.normal
    f32 = jnp.float32
    dt = jnp.exp(jax.random.uniform(ks[6], (N_EVEN, SSM_HEADS), f32, math.log(1e-3), math.log(1e-1)))
    return {
        "x": nrm(ks[0], (BATCH, SEQ, D_MODEL), f32),
        "mix_norm_w": 1.0 + 0.02 * nrm(ks[1], (DEPTH, D_MODEL), f32),
        "ffn_norm_w": 1.0 + 0.02 * nrm(ks[2], (DEPTH, D_MODEL), f32),
        "hyb_w_in": nrm(ks[3], (N_EVEN, D_MODEL, IN_TOTAL), f32) * D_MODEL ** -0.5,
        "ssm_conv_w": nrm(ks[4], (N_EVEN, SSM_CONV, SSM_XBC), f32) * SSM_CONV ** -0.5,
        "ssm_conv_b": 0.02 * nrm(ks[5], (N_EVEN, SSM_XBC), f32),
        "ssm_dt_bias": dt + jnp.log(-jnp.expm1(-dt)),
        "ssm_a_log": jnp.log(jax.random.uniform(ks[7], (N_EVEN, SSM_HEADS), f32, 1.0, 16.0)),
        "ssm_d": 1.0 + 0.1 * nrm(ks[8], (N_EVEN, SSM_HEADS), f32),
        "ssm_norm_w": 1.0 + 0.02 * nrm(ks[9], (N_EVEN, SSM_D_INNER), f32),
        "attn_q_norm_w": 1.0 + 0.02 * nrm(ks[10], (N_EVEN, ATTN_HEAD_DIM), f32),
        "attn_k_norm_w": 1.0 + 0.02 * nrm(ks[11], (N_EVEN, ATTN_HEAD_DIM), f32),
        "attn_sinks": 0.5 * nrm(ks[12], (N_EVEN, ATTN_Q_HEADS), f32),
        "rel_bias": 0.5 * nrm(ks[13], (REL_BUCKETS, ATTN_Q_HEADS), f32),
        "hyb_w_out": nrm(ks[14], (N_EVEN, MIX_WIDTH, D_MODEL), f32) * MIX_WIDTH ** -0.5,
        "conv_pw1_w": nrm(ks[15], (N_ODD, D_MODEL, 2 * CONV_CH), f32) * D_MODEL ** -0.5,
        "conv_pw1_b": 0.02 * nrm(ks[16], (N_ODD, 2 * CONV_CH), f32),
        "conv_dw_w": nrm(ks[17], (N_ODD, CONV_K, CONV_CH), f32) * CONV_K ** -0.5,
        "conv_dw_b": 0.02 * nrm(ks[18], (N_ODD, CONV_CH), f32),
        "conv_ln_w": 1.0 + 0.02 * nrm(ks[19], (N_ODD, CONV_CH), f32),
        "conv_ln_b": 0.02 * nrm(ks[20], (N_ODD, CONV_CH), f32),
        "conv_pw2_w": nrm(ks[21], (N_ODD, CONV_CH, D_MODEL), f32) * CONV_CH ** -0.5,
        "conv_pw2_b": 0.02 * nrm(ks[22], (N_ODD, D_MODEL), f32),
        "ffn_w_up": nrm(ks[23], (DEPTH, D_MODEL, 2 * D_FF), f32) * D_MODEL ** -0.5,
        "ffn_conv_w": nrm(ks[24], (DEPTH, FFN_CONV, 2 * D_FF), f32) * FFN_CONV ** -0.5,
        "ffn_conv_b": 0.02 * nrm(ks[25], (DEPTH, 2 * D_FF), f32),
        "ffn_w_down": nrm(ks[26], (DEPTH, D_FF, D_MODEL), f32) * D_FF ** -0.5,
    }


def reference(x, mix_norm_w, ffn_norm_w, hyb_w_in, ssm_conv_w, ssm_conv_b, ssm_dt_bias,
              ssm_a_log, ssm_d, ssm_norm_w, attn_q_norm_w, attn_k_norm_w, attn_sinks,
              rel_bias, hyb_w_out, conv_pw1_w, conv_pw1_b, conv_dw_w, conv_dw_b,
              conv_ln_w, conv_ln_b, conv_pw2_w, conv_pw2_b, ffn_w_up, ffn_conv_w,
              ffn_conv_b, ffn_w_down):
    h = x
    for i in range(DEPTH):
        hn = rms_norm(h, mix_norm_w[i])
        j = i // 2
        if i % 2 == 0:
            proj = hn @ hyb_w_in[j]
            y_a = ssd_mixer(proj[..., :IN_A], ssm_conv_w[j], ssm_conv_b[j], ssm_dt_bias[j],
                            ssm_a_log[j], ssm_d[j], ssm_norm_w[j])
            y_b = swa_sink_attention(proj[..., IN_A:], attn_q_norm_w[j], attn_k_norm_w[j],
                                     attn_sinks[j], rel_bias)
            mix = jnp.concatenate([y_a, y_b], axis=-1) @ hyb_w_out[j]
        else:
            mix = conformer_conv_module(hn, conv_pw1_w[j], conv_pw1_b[j], conv_dw_w[j],
                                        conv_dw_b[j], conv_ln_w[j], conv_ln_b[j],
                                        conv_pw2_w[j], conv_pw2_b[j])
        h = h + mix
        h = h + conv_ffn(rms_norm(h, ffn_norm_w[i]), ffn_w_up[i], ffn_conv_w[i],
                         ffn_conv_b[i], ffn_w_down[i])
    return h
```

```python
import contextlib
import math
import numpy as np
import concourse.bass as bass
import concourse.mybir as mybir
from concourse.bass_utils import run_bass_kernel_spmd

F32 = mybir.dt.float32
BF16 = mybir.dt.bfloat16
AF = mybir.ActivationFunctionType
ALU = mybir.AluOpType
AX = mybir.AxisListType

ENGS = ("pe", "act", "dve", "pool", "sp")
EPOCH = 8000
DMA_BPNS = 340.0
SCHED_W = 128
SCHED_EPS = 0.0
XLAT = 300.0
SSD_POOL = (0, 4)
ATT_POOL = (4, 4)
PLAN_ITERS = 4
NDV = 10
GR = 64
SB_LO = 16640
SB_HI = 229376


class Res:
    __slots__ = ("name", "last_writer", "readers", "dma_cnt")

    def __init__(self, name):
        self.name = name
        self.last_writer = None
        self.readers = []
        self.dma_cnt = 0


class Op:
    __slots__ = ("idx", "eng", "fn", "deps", "is_dma", "sync", "eidx", "signal", "waits", "snap", "dval", "cost", "occ",
                 "t0", "t1", "label", "ridx")


class Prog:
    def __init__(self, nc):
        self.nc = nc
        self.ops = []
        self.res = {}

    def R(self, name):
        r = self.res.get(name)
        if r is None:
            r = self.res[name] = Res(name)
        return r

    def add(self, eng, fn, reads=(), writes=(), dma=None, cost=500.0, occ=None):
        op = Op()
        op.ridx = len(self.ops)
        op.label = getattr(self, "label", "")
        op.cost = cost
        op.occ = cost if occ is None else occ
        op.idx = len(self.ops)
        op.eng = eng
        op.fn = fn
        op.is_dma = dma is not None
        op.sync = self.R("dmasem_" + dma) if dma is not None else None
        deps = set()
        rs = [self.R(r) for r in set(reads)]
        ws = [self.R(w) for w in set(writes)]
        if op.is_dma:
            ws.append(self.R("dmachain_" + dma))
        for r in rs:
            if r.last_writer is not None:
                deps.add(r.last_writer)
        for w in ws:
            if w.last_writer is not None:
                deps.add(w.last_writer)
            deps.update(w.readers)
        for w in ws:
            w.last_writer = op.idx
            w.readers = []
        for r in rs:
            r.readers.append(op.idx)
        deps.discard(op.idx)
        op.deps = deps
        self.ops.append(op)
        return op

    def schedule(self, W=None):
        ops = self.ops
        if W is None:
            W = SCHED_W
        from collections import deque
        pend = {e: deque(op for op in ops if op.eng == e) for e in ENGS}
        tfree = {e: 0.0 for e in ENGS}
        self._dma_free = 0.0
        rank = [0.0] * len(ops)
        for op in reversed(ops):
            r = rank[op.idx] + op.cost
            rank[op.idx] = r
            for d in op.deps:
                if rank[d] < r:
                    rank[d] = r
        for op in ops:
            op.t0 = None
        left = len(ops)
        EPS = SCHED_EPS
        while left:
            best = None
            for e in ENGS:
                q = pend[e]
                n = 0
                cands = []
                smin = None
                for op in q:
                    if n >= W:
                        break
                    n += 1
                    rdy = 0.0
                    ok = True
                    for d in op.deps:
                        dop = ops[d]
                        if dop.t0 is None:
                            ok = False
                            break
                        t1d = dop.t1 if dop.eng == e else dop.t1 + XLAT
                        if t1d > rdy:
                            rdy = t1d
                    if not ok:
                        continue
                    st = rdy if rdy > tfree[e] else tfree[e]
                    cands.append((st, op))
                    if smin is None or st < smin:
                        smin = st
                if smin is None:
                    continue
                pick = None
                for st, op in cands:
                    if st <= smin + EPS:
                        k2 = (-rank[op.idx], op.idx)
                        if pick is None or k2 < pick[0]:
                            pick = (k2, st, op)
                key = (pick[1], pick[2].idx)
                if best is None or key < best[0]:
                    best = (key, pick[2])
            key, op = best
            op.t0 = key[0]
            if op.is_dma:
                xfer = max(op.cost - op.occ - 2000.0, 0.0)
                st = max(op.t0 + op.occ, self._dma_free)
                self._dma_free = st + xfer
                op.t1 = st + xfer + 2000.0
            else:
                op.t1 = op.t0 + op.cost
            tfree[op.eng] = op.t0 + op.occ
            pend[op.eng].remove(op)
            left -= 1
        new = sorted(ops, key=lambda o: (o.t0, o.idx))
        remap = {o.idx: i for i, o in enumerate(new)}
        for o in new:
            o.deps = {remap[d] for d in o.deps}
        for i, o in enumerate(new):
            o.idx = i
        self.ops = new
        return max(o.t1 for o in new)

    def emit(self, sched=True):
        nc = self.nc
        self.sim_ns = self.schedule() if sched else 0.0
        ops = self.ops
        cnt = {e: 0 for e in ENGS}
        for op in ops:
            op.eidx = cnt[op.eng]
            cnt[op.eng] += 1
            op.signal = False
            op.waits = []
        known = {e: {f: -1 for f in ENGS} for e in ENGS}
        kdma = {e: set() for e in ENGS}
        for op in ops:
            kn = known[op.eng]
            kd = kdma[op.eng]
            for d in sorted(op.deps):
                dop = ops[d]
                if dop.is_dma:
                    if d in kd:
                        continue
                    kd.add(d)
                else:
                    if kn[dop.eng] >= dop.eidx:
                        continue
                    kn[dop.eng] = dop.eidx
                dop.signal = True
                op.waits.append(d)
                sk, sd = dop.snap
                for f in ENGS:
                    if sk[f] > kn[f]:
                        kn[f] = sk[f]
                kd |= sd
            op.snap = (dict(kn), frozenset(kd))
        scount = {e: 0 for e in ENGS}
        keys = []
        for op in ops:
            if op.is_dma:
                op.sync.dma_cnt += 16
                op.dval = (op.sync.name, op.sync.dma_cnt)
            elif op.signal:
                c = scount[op.eng]
                scount[op.eng] += 1
                op.dval = (f"e_{op.eng}_{c // EPOCH}", c % EPOCH + 1)
            else:
                op.dval = None
            if op.dval is not None and op.dval[0] not in keys:
                keys.append(op.dval[0])
        with contextlib.ExitStack() as st:
            semh = {k: st.enter_context(nc.semaphore(k)) for k in keys}
            block = st.enter_context(nc.Block())
            per = {e: [op for op in ops if op.eng == e] for e in ENGS}

            def run(engobj, lst):
                for op in lst:
                    for d in op.waits:
                        k, v = ops[d].dval
                        engobj.wait_ge(semh[k], v)
                    if op.fn is None:
                        continue
                    ins = op.fn(engobj)
                    if op.is_dma:
                        ins.then_inc(semh[op.dval[0]], 16)
                    elif op.signal:
                        ins.then_inc(semh[op.dval[0]], 1)

            block.tensor(lambda e: run(e, per["pe"]))
            block.scalar(lambda e: run(e, per["act"]))
            block.vector(lambda e: run(e, per["dve"]))
            block.gpsimd(lambda e: run(e, per["pool"]))
            block.sync(lambda e: run(e, per["sp"]))
        return len(keys), {e: len(per[e]) for e in ENGS}


class T:
    def __init__(self, nc, name, shape, dtype, off):
        self.esz = 2 if dtype == BF16 else 4
        self.n = int(np.prod(shape[1:]))
        self.off = off
        self.t = nc.alloc_sbuf_tensor_at(name, list(shape), dtype, offset=off)
        self.name = name

    def r(self, lo=0, hi=None):
        if hi is None:
            hi = self.n
        b0 = (self.off + lo * self.esz) // GR
        b1 = (self.off + hi * self.esz - 1) // GR
        return [f"sb{g}" for g in range(b0, b1 + 1)]


class Arena:
    def __init__(self, nc):
        self.nc = nc
        self.cur = SB_LO
        self.peak = SB_LO
        self.k = 0

    def alloc(self, name, shape, dtype):
        esz = 2 if dtype == BF16 else 4
        nbytes = int(np.prod(shape[1:])) * esz
        off = (self.cur + 63) // 64 * 64
        self.cur = off + nbytes
        self.peak = max(self.peak, self.cur)
        assert self.cur <= SB_HI, (name, self.cur)
        self.k += 1
        return T(self.nc, f"{name}_{self.k}", shape, dtype, off)


D = 1024
IN_TOTAL = 3856
C_Z, C_X, C_B, C_C, C_DT, C_Q, C_K, C_V = 0, 1024, 2048, 2304, 2560, 2576, 3600, 3728
DFF = 2816
NEG = -30000.0

_p = {}
_o = 0
for _n, _w in [("mixw", 16), ("ffnw", 16), ("cw", 48), ("cb", 12), ("dch", 8), ("snw", 8), ("qw", 1), ("kw", 1),
               ("pw1b", 16), ("dww", 248), ("dwb", 8), ("lnw", 8), ("lnb", 8), ("pw2b", 8),
               ("fcw", 264), ("fcb", 88), ("dtb", 16), ("alog", 16), ("sink", 16)]:
    _p[_n] = _o
    _o += _w
NPRM = _o
TL_A, TL_F, TL_C = 0, 36, 36 + 176
NTL = 36 + 176 + 240


def record_program(NPRE, NMAIN, dbg=(), plan=None, do_emit=True):
    nc = bass.Bass("TRN2", target_bir_lowering=False)
    NT = 512 * (NPRE + NMAIN)
    NOUT = 512 * (NMAIN - 1)
    dt_ = nc.dram_tensor
    x_d = dt_("x", [NT, D], F32, kind="ExternalInput").ap()
    flag_d = dt_("flag", [128, 1], F32, kind="ExternalInput").ap()
    prm_d = dt_("prm", [128, NPRM], F32, kind="ExternalInput").ap()
    bias_d = dt_("biasT", [128, 16 * 2 * 128], F32, kind="ExternalInput").ap()
    cst_d = dt_("cst", [128, 512], F32, kind="ExternalInput").ap()
    eall_d = dt_("eall", [48, 2048], F32, kind="ExternalInput").ap()
    win_d = dt_("w_in", [D, IN_TOTAL], F32, kind="ExternalInput").ap()
    wout_d = dt_("w_out", [2048, D], F32, kind="ExternalInput").ap()
    pw1_d = dt_("pw1", [D, 2048], F32, kind="ExternalInput").ap()
    pw2_d = dt_("pw2", [D, D], F32, kind="ExternalInput").ap()
    wup_d = [dt_(f"wup{l}", [D, 2 * DFF], F32, kind="ExternalInput").ap() for l in range(2)]
    wdn_d = [dt_(f"wdn{l}", [DFF, D], F32, kind="ExternalInput").ap() for l in range(2)]
    y_d = dt_("y", [NOUT, D], F32, kind="ExternalOutput").ap()
    diag_d = dt_("diag_scr", [8, 128, 31 * 128], BF16).ap()
    dbg_d = {n: dt_("dbg_" + n, [128, 4096], F32, kind="ExternalOutput").ap() for n in dbg}

    P = Prog(nc)
    ar = Arena(nc)
    PSt = nc.alloc_psum_tensor("PS", [128, 8, 512], F32)
    PS = PSt
    pspool = {"lo": 0, "n": 8}
    pscnt = {}
    allocs = []
    cur_alloc = {}

    def psa(n=1):
        k = len(allocs)
        if plan is not None:
            b = plan[k]
        else:
            key = (0, 8)
            c = pscnt.get(key, 0)
            if n == 2 and c % 2 == 1:
                c += 1
            b = c % 8
            pscnt[key] = c + n
        allocs.append([n, b, []])
        for i in range(n):
            cur_alloc[b + i] = k
        return b

    def pr(b, n=1):
        return [f"pb{b + i}" for i in range(n)]

    def A(eng, fn, R=(), W=(), dma=None, c=512, nbytes=1 << 19, recip=False):
        rl = [x for l in R for x in l]
        wl = [x for l in W for x in l]
        occ = None
        if dma is not None:
            occ = 1200.0 if eng == "pool" else 150.0
            cost = occ + 2000.0 + nbytes / DMA_BPNS
        elif eng == "pe":
            cost = getattr(fn, "cost", 300.0)
        elif eng == "act":
            cost = 220.0 + 0.95 * c
        else:
            cost = 80.0 + (3.2 if recip else 1.3) * c
        if eng != "pe":
            wl = wl + ["px" + nm[2:] for nm in set(rl + wl) if nm.startswith("pb")]
        op = P.add(eng, fn, rl, wl, dma, cost, occ)
        for nm in rl + wl:
            if nm.startswith("pb"):
                lst = allocs[cur_alloc[int(nm[2:])]][2]
                if not lst or lst[-1] is not op:
                    lst.append(op)
        return op

    hT = ar.alloc("hT", [128, 8, 512], F32)
    NSLOT = 4
    WS = [ar.alloc(f"ws{i}", [128, 4096], BF16) for i in range(NSLOT)]
    WK = ar.alloc("wk", [128, 8, 4, 128], BF16)
    WDV = ar.alloc("wdv", [128, 8, 144], BF16)
    biasT = ar.alloc("biasT", [128, 16, 2, 128], BF16)
    Eall = ar.alloc("eall", [48, 16, 128], BF16)
    prm = ar.alloc("prm", [128, NPRM], F32)
    cst = ar.alloc("cst", [128, 512], F32)
    cbf = ar.alloc("cbf", [128, 512], BF16)
    ones_f = ar.alloc("ones_f", [128, 128], F32)
    ones_b = ar.alloc("ones_b", [128, 128], BF16)
    diagD = ar.alloc("diagD", [128, 8, 128], BF16)
    sm = ar.alloc("sm", [128, 128], F32)
    st_f = ar.alloc("st_f", [128, 1024], F32)
    st_b = ar.alloc("st_b", [128, 1024], BF16)
    TL = ar.alloc("TL", [128, NTL], F32)
    Kp = ar.alloc("Kp", [128, 4, 640], BF16)
    Vx = ar.alloc("Vx", [128, 5, 2, 65], BF16)
    flg = ar.alloc("flg", [128, 1], F32)
    xin = [ar.alloc(f"xin{i}", [128, 1024], F32) for i in range(2)]
    ident_f = cst.t[:, 0:128]
    U_f = cst.t[:, 128:256]
    ident_b = cbf.t[:, 0:128]
    bd_b = cbf.t[:, 256:384]
    negm_b = cbf.t[:, 384:512]
    SM_A, SM_ES, SM_E6, SM_E5, SM_QW, SM_KW = 0, 16, 32, 33, 34, 35

    def pc(name, j=0, n=1):
        return prm.t[:, _p[name] + j:_p[name] + j + n]

    base_mark = ar.cur

    xsT = ar.alloc("xsT", [128, 8, 512], BF16)
    BCT = ar.alloc("BCT", [128, 4, 512], BF16)
    qnT = ar.alloc("qnT", [128, 8, 512], BF16)
    sz = ar.alloc("sz", [128, 4, 1024], F32)
    mixT = ar.alloc("mixT", [128, 16, 512], BF16)
    dtr = ar.alloc("dtr", [128, 4, 16], F32)
    dtv = ar.alloc("dtv", [128, 4, 16], F32)
    lndt = ar.alloc("lndt", [128, 4, 16], F32)
    m0_mark = ar.cur
    ar.cur = qnT.off
    hnB = ar.alloc("hnB", [128, 8, 512], BF16)
    ar.cur = mixT.off
    xsTB = ar.alloc("xsTB", [128, 8, 512], BF16)
    BCTB = ar.alloc("BCTB", [128, 4, 512], BF16)
    dtrB = ar.alloc("dtrB", [128, 4, 16], F32)
    dtvB = ar.alloc("dtvB", [128, 4, 16], F32)
    lndtB = ar.alloc("lndtB", [128, 4, 16], F32)
    xtokB = ar.alloc("xtokB", [128, 1024], BF16)
    btokB = ar.alloc("btokB", [128, 256], BF16)
    assert ar.cur <= mixT.off + 16 * 512 * 2
    ar.cur = sz.off
    hTB = ar.alloc("hTB", [128, 8, 512], F32)
    ar.cur = m0_mark
    hn = ar.alloc("hn", [128, 8, 512], BF16)
    sq = ar.alloc("sq", [128, 8, 512], BF16)
    xp = [ar.alloc(f"xp{i}", [128, 515], F32) for i in range(2)]
    acc = [ar.alloc(f"acc{i}", [128, 512], F32) for i in range(2)]
    sqq = [ar.alloc(f"sqq{i}", [128, 512], BF16) for i in range(2)]
    rs_s = ar.alloc("rs_s", [128, 512], F32)
    rs_r = ar.alloc("rs_r", [128, 512], F32)
    rs_s2 = ar.alloc("rs_s2", [128, 512], F32)
    rs_r2 = ar.alloc("rs_r2", [128, 512], F32)
    a13_end = ar.cur
    SET_A = (hn, sq, xsT, BCT, dtr, dtv, lndt, hT)
    SET_B = (hnB, sq, xsTB, BCTB, dtrB, dtvB, lndtB, hTB)
    ar.cur = m0_mark
    xtok = ar.alloc("xtok", [128, 1024], BF16)
    btok = ar.alloc("btok", [128, 256], BF16)
    xw = ar.alloc("xw", [128, 1024], BF16)
    dec = [ar.alloc(f"dec{i}", [128, 1024], F32) for i in range(2)]
    MT = ar.alloc("MT", [128, 16, 128], BF16)
    tmpf = ar.alloc("tmpf", [128, 1024], F32)
    yv = ar.alloc("yv", [128, 1024], F32)
    yn = ar.alloc("yn", [128, 1024], BF16)
    pT = [ar.alloc(f"pT{i}", [128, 512], BF16) for i in range(4)]
    attn = ar.alloc("attn", [128, 1024], BF16)
    s16s = [ar.alloc(f"s16_{i}", [128, 16, 16], F32) for i in range(4)]
    a16s = [ar.alloc(f"a16_{i}", [128, 8, 16], F32) for i in range(4)]
    acsTs = [ar.alloc(f"acsT{i}", [48, 128], BF16) for i in range(4)]
    qTs = [ar.alloc(f"qT{i}", [48, 128], BF16) for i in range(4)]
    a48s = [ar.alloc(f"a48_{i}", [128, 48], F32) for i in range(4)]
    q48s = [ar.alloc(f"q48_{i}", [128, 48], F32) for i in range(4)]
    ar.cur = max(ar.cur, a13_end)
    m0_end = ar.cur
    ar.cur = base_mark
    f_hn = ar.alloc("f_hn", [128, 8, 512], BF16)
    f_sq = ar.alloc("f_sq", [128, 8, 512], BF16)
    f_rs = ar.alloc("f_rs", [128, 512], F32)
    f_rr = ar.alloc("f_rr", [128, 512], F32)
    actT = ar.alloc("actT", [128, 22, 512], BF16)
    f_xp = [ar.alloc(f"f_xp{i}", [128, 514], F32) for i in range(8)]
    f_acc = [ar.alloc(f"f_acc{i}", [128, 512], F32) for i in range(8)]
    f_sg = [ar.alloc(f"f_sg{i}", [128, 512], F32) for i in range(4)]
    f_end = ar.cur
    ar.cur = base_mark
    c_hn = ar.alloc("c_hn", [128, 8, 512], BF16)
    c_sq = ar.alloc("c_sq", [128, 8, 512], BF16)
    c_rs = ar.alloc("c_rs", [128, 512], F32)
    c_rr = ar.alloc("c_rr", [128, 512], F32)
    upad = ar.alloc("upad", [128, 8, 544], BF16)
    yc = ar.alloc("yc", [128, 8, 512], F32)
    ybf = ar.alloc("ybf", [128, 8, 512], BF16)
    c_sig = [ar.alloc(f"c_sig{i}", [128, 512], F32) for i in range(2)]
    c_mean = ar.alloc("c_mean", [128, 512], F32)
    c_msq = ar.alloc("c_msq", [128, 512], F32)
    c_var = ar.alloc("c_var", [128, 512], F32)
    c_end = ar.cur
    ar.cur = base_mark
    dgt = ar.alloc("dgt", [128, 31 * 128], BF16)
    ar.cur = base_mark
    xout = [ar.alloc(f"xout{i}", [128, 1024], F32) for i in range(2)]
    ar.cur = ar.peak
    xtokA = ar.alloc("xtokA", [128, 1024], BF16)
    btokA = ar.alloc("btokA", [128, 256], BF16)
    xwA = ar.alloc("xwA", [128, 1024], BF16)
    xwB = ar.alloc("xwB", [128, 1024], BF16)
    PRE_A = (xtokA, btokA, xwA)
    PRE_B = (xtokB, btokB, xwB)
    sbuf_peak = ar.peak

    A("sp", lambda e: e.dma_start(out=prm.t[:], in_=prm_d), W=[prm.r()], dma="prm")
    A("sp", lambda e: e.dma_start(out=cst.t[:], in_=cst_d), W=[cst.r()], dma="cst")
    A("pool", lambda e: e.dma_start(out=biasT.t[:].rearrange("p a b c -> p (a b c)"), in_=bias_d), W=[biasT.r()], dma="biasT")
    A("pool", lambda e: e.dma_start(out=Eall.t[:].rearrange("p a b -> p (a b)"), in_=eall_d), W=[Eall.r()], dma="eall")
    A("sp", lambda e: e.dma_start(out=flg.t[:], in_=flag_d), W=[flg.r()], dma="flg")
    A("dve", lambda e: e.memset(WK.t[:].rearrange("p a b c -> p (a b c)"), 0.0), W=[WK.r()])
    for kv in range(2):
        for pad in range(2):
            def f(e, kv=kv, pad=pad):
                return e.dma_start(out=WK.t[:, :, 2 * kv + pad, 64 * pad:64 * pad + 64],
                                   in_=win_d[:, C_K + 64 * kv:C_K + 64 * kv + 64].rearrange("(k p) c -> p k c", p=128))
            A("pool", f, W=[WK.r()], dma="wk")
    A("pool", lambda e: e.dma_start(out=WDV.t[:, :, 0:16], in_=win_d[:, C_DT:C_DT + 16].rearrange("(k p) c -> p k c", p=128)),
      W=[WDV.r()], dma="wdv")
    A("pool", lambda e: e.dma_start(out=WDV.t[:, :, 16:144], in_=win_d[:, C_V:C_V + 128].rearrange("(k p) c -> p k c", p=128)),
      W=[WDV.r()], dma="wdv")
    A("dve", lambda e: e.tensor_copy(out=cbf.t[:], in_=cst.t[:]), R=[cst.r()], W=[cbf.r()])
    A("dve", lambda e: e.memset(ones_f.t[:], 1.0), W=[ones_f.r()])
    A("dve", lambda e: e.memset(ones_b.t[:], 1.0), W=[ones_b.r()])
    A("dve", lambda e: e.memset(sm.t[:], 0.0), W=[sm.r()])
    A("dve", lambda e: e.memset(sm.t[:, SM_E6:SM_E6 + 1], 1e-6), W=[sm.r()])
    A("dve", lambda e: e.memset(sm.t[:, SM_E5:SM_E5 + 1], 1e-5), W=[sm.r()])
    A("act", lambda e: e.activation(out=sm.t[:, SM_A:SM_A + 16], in_=pc("alog", 0, 16), func=AF.Exp), R=[prm.r(), sm.r()], W=[sm.r()])
    A("dve", lambda e: e.tensor_scalar(out=sm.t[:, SM_A:SM_A + 16], in0=sm.t[:, SM_A:SM_A + 16], scalar1=-1.0, scalar2=None, op0=ALU.mult),
      R=[sm.r()], W=[sm.r()])
    A("act", lambda e: e.activation(out=sm.t[:, SM_ES:SM_ES + 16], in_=pc("sink", 0, 16), func=AF.Exp), R=[prm.r(), sm.r()], W=[sm.r()])
    A("dve", lambda e: e.tensor_scalar(out=sm.t[:, SM_QW:SM_QW + 1], in0=pc("qw"), scalar1=0.125, scalar2=None, op0=ALU.mult),
      R=[prm.r(), sm.r()], W=[sm.r()])
    A("dve", lambda e: e.tensor_copy(out=sm.t[:, SM_KW:SM_KW + 1], in_=pc("kw")), R=[prm.r(), sm.r()], W=[sm.r()])
    for j in range(8):
        A("dve", lambda e, j=j: e.tensor_scalar(out=diagD.t[:, j, :], in0=ident_f, scalar1=pc("dch", j), scalar2=None, op0=ALU.mult),
          R=[cst.r(), prm.r()], W=[diagD.r()])
    for i in range(4):
        A("dve", lambda e, i=i: e.memset(a48s[i].t[:], 0.0), W=[a48s[i].r()], c=48)
        A("dve", lambda e, i=i: e.memset(q48s[i].t[:], 0.0), W=[q48s[i].r()], c=48)
    A("dve", lambda e: e.memset(st_f.t[:], 0.0), W=[st_f.r()])
    A("dve", lambda e: e.memset(st_b.t[:], 0.0), W=[st_b.r()])
    A("dve", lambda e: e.memset(TL.t[:], 0.0), W=[TL.r()])
    A("dve", lambda e: e.memset(Kp.t[:].rearrange("p a b -> p (a b)"), 0.0), W=[Kp.r()])
    A("dve", lambda e: e.memset(Vx.t[:].rearrange("p a b c -> p (a b c)"), 0.0), W=[Vx.r()])
    A("dve", lambda e: e.memset(Vx.t[:, :, :, 64:65], 1.0), W=[Vx.r()])

    for j in range(8):
        A("dve", lambda e, j=j: e.tensor_tensor(out=dgt.t[:].rearrange("p (k c) -> p k c", c=128),
                                                in0=ident_f.unsqueeze(1).to_broadcast([128, 31, 128]),
                                                in1=pc("dww", j * 31, 31).unsqueeze(2).to_broadcast([128, 31, 128]), op=ALU.mult),
          R=[cst.r(), prm.r()], W=[dgt.r()], c=3968)
        A("sp", lambda e, j=j: e.dma_start(out=diag_d[j], in_=dgt.t[:]), R=[dgt.r()], W=[["dram_diag"]], dma="dgt")

    def wsrc(w, k0, nk, c0, ncol):
        return w[k0 * 128:(k0 + nk) * 128, c0:c0 + ncol].rearrange("(k p) c -> p k c", p=128)

    def loads_mixer0(pre):
        L = []
        if not pre:
            L += [("z0", win_d, 0, 8, C_Z, 512), ("z1", win_d, 0, 8, C_Z + 512, 512)]
        L += [("x0", win_d, 0, 8, C_X, 512), ("x1", win_d, 0, 8, C_X + 512, 512)]
        if pre:
            L += [("bc", win_d, 0, 8, C_B, 256)]
        else:
            L += [("bc", win_d, 0, 8, C_B, 512), ("q0", win_d, 0, 8, C_Q, 512), ("q1", win_d, 0, 8, C_Q + 512, 512)]
            L += [(f"o{i}", wout_d, 0, 16, 256 * i, 256) for i in range(4)]
        return L

    def loads_ffn(l):
        L = []
        for g in range(6):
            nc_ = 512 if g < 5 else 256
            L += [(f"g{g}", wup_d[l], 0, 8, 512 * g, nc_), (f"u{g}", wup_d[l], 0, 8, DFF + 512 * g, nc_)]
        L += [(f"d{f}", wdn_d[l], 0, 22, 128 * f, 128) for f in range(8)]
        return L

    def loads_conf():
        L = []
        for h in range(2):
            L += [(f"a{h}", pw1_d, 0, 8, 512 * h, 512), (f"s{h}", pw1_d, 0, 8, 1024 + 512 * h, 512)]
            L += [(f"dg{j}", None, j, 31, 0, 128) for j in range(4 * h, 4 * h + 4)]
        L += [(f"p{h}", pw2_d, 0, 8, 512 * h, 512) for h in range(2)]
        return L

    stream = []
    for _ in range(NPRE + 1):
        stream += loads_mixer0(True)
    for _ in range(NMAIN):
        stream += loads_mixer0(False) + loads_ffn(0) + loads_conf() + loads_ffn(1)
    wstate = {"issued": 0, "next": 0}
    released = set()
    auto_pending = []

    def issue_loads(upto):
        while wstate["issued"] < min(upto, len(stream)):
            i = wstate["issued"]
            if i >= NSLOT and (i - NSLOT) not in released:
                break
            _, w, k0, nk, c0, ncol = stream[i]
            slot = WS[i % NSLOT]

            def f(e, slot=slot, w=w, k0=k0, nk=nk, c0=c0, ncol=ncol):
                if w is None:
                    return e.dma_start(out=slot.t[:, 0:nk * ncol], in_=diag_d[k0])
                return e.dma_start(out=slot.t[:, 0:nk * ncol].rearrange("p (k c) -> p k c", c=ncol), in_=wsrc(w, k0, nk, c0, ncol))
            A("pool", f, R=[["dram_diag"]] if w is None else [], W=[slot.r()], dma=f"ws{i % NSLOT}", nbytes=nk * ncol * 128 * (2 if w is None else 4))
            wstate["issued"] += 1

    def wrel(*idxs):
        released.update(idxs)
        issue_loads(wstate["next"] + NSLOT)

    def wnext(tag, hold=False):
        i = wstate["next"]
        assert stream[i][0] == tag, (stream[i][0], tag)
        released.update(auto_pending)
        del auto_pending[:]
        issue_loads(i + NSLOT)
        assert wstate["issued"] > i, ("weight ring deadlock", tag)
        wstate["next"] += 1
        if not hold:
            auto_pending.append(i)
        _, w, k0, nk, c0, ncol = stream[i]
        slot = WS[i % NSLOT]
        return slot.t[:, 0:nk * ncol].rearrange("p (k c) -> p k c", c=ncol), slot.r(), i

    def mm_group(specs):
        def f(e):
            ins = None
            for (o, l, r, s0, s1) in specs:
                ins = e.matmul(o, lhsT=l, rhs=r, start=s0, stop=s1)
            return ins
        cost = 0.0
        for (o, l, r, s0, s1) in specs:
            n = int(np.prod(o.shape[1:]))
            cost += (max(n, 96) / 1.9) * (4.0 if l.dtype == F32 else 1.0) + 12.0
        f.cost = cost
        return f

    cfg = {"nt": 512}

    def tr_group(specs):
        def f(e):
            ins = None
            for (o, i_) in specs:
                ins = e.transpose(o, i_, ident_f)
            return ins
        f.cost = len(specs) * 110.0
        return f

    def rmsnorm(wname, wl, hn_, sq_, rs_, rr_, hs=None, bank=None):
        hs = hT if hs is None else hs
        nt = cfg["nt"]
        if bank is None:
            A("act", lambda e: e.activation(out=sq_.t[:, :, 0:nt], in_=hs.t[:, :, 0:nt], func=AF.Square), R=[hs.r()], W=[sq_.r()], c=8 * nt)
            b = psa()
            A("pe", mm_group([(PS[:, b, 0:nt], ones_b.t[:], sq_.t[:, kt, 0:nt], kt == 0, kt == 7) for kt in range(8)]),
              R=[sq_.r(), ones_b.r()], W=[pr(b)])
        else:
            b = bank
        A("act", lambda e: e.activation(out=rs_.t[:, 0:nt], in_=PS[:, b, 0:nt], func=AF.Ln, bias=sm.t[:, SM_E6:SM_E6 + 1], scale=1.0 / 1024),
          R=[pr(b), sm.r()], W=[rs_.r()], c=nt)
        A("act", lambda e: e.activation(out=rr_.t[:, 0:nt], in_=rs_.t[:, 0:nt], func=AF.Exp, scale=-0.5), R=[rs_.r()], W=[rr_.r()], c=nt)
        for kt in range(8):
            A("dve", lambda e, kt=kt: e.scalar_tensor_tensor(out=hn_.t[:, kt, 0:nt], in0=hs.t[:, kt, 0:nt], scalar=pc(wname, wl * 8 + kt),
                                                            in1=rr_.t[:, 0:nt], op0=ALU.mult, op1=ALU.mult),
              R=[hs.r(kt * 512, kt * 512 + 512), rr_.r(), prm.r()], W=[hn_.r(kt * 512, kt * 512 + 512)], c=nt)

    nrm = {"bank": None}

    def ssq_hook(f, sq_next):
        nt = cfg["nt"]
        A("act", lambda e: e.activation(out=sq_next.t[:, f, 0:nt], in_=hT.t[:, f, 0:nt], func=AF.Square),
          R=[hT.r(f * 512, f * 512 + 512)], W=[sq_next.r(f * 512, f * 512 + 512)], c=nt)
        if f == 0:
            nrm["bank"] = psa()
        b = nrm["bank"]
        A("pe", mm_group([(PS[:, b, 0:nt], ones_b.t[:], sq_next.t[:, f, 0:nt], f == 0, f == 7)]),
          R=[sq_next.r(f * 512, f * 512 + 512), ones_b.r()], W=[pr(b)])

    def proj_fm(wv, wres, jloc, hn_):
        nt = cfg["nt"]
        b = psa()
        A("pe", mm_group([(PS[:, b, 0:nt], wv[:, kt, jloc * 128:(jloc + 1) * 128], hn_.t[:, kt, 0:nt], kt == 0, kt == 7) for kt in range(8)]),
          R=[wres, hn_.r()], W=[pr(b)])
        return b

    def conv_fm(b, xp_, acc_, K, wcol, bcol, tl_off):
        nt = cfg["nt"]
        A("act", lambda e: e.activation(out=xp_.t[:, 0:K - 1], in_=TL.t[:, tl_off:tl_off + K - 1], func=AF.Copy),
          R=[TL.r(tl_off, tl_off + K - 1)], W=[xp_.r(0, K - 1)], c=4)
        A("act", lambda e: e.activation(out=xp_.t[:, K - 1:K - 1 + nt], in_=PS[:, b, 0:nt], func=AF.Copy), R=[pr(b)], W=[xp_.r(K - 1, K + 511)], c=nt)
        A("act", lambda e: e.activation(out=acc_.t[:, 0:nt], in_=PS[:, b, 0:nt], func=AF.Identity, scale=wcol(K - 1), bias=bcol),
          R=[pr(b), prm.r()], W=[acc_.r()], c=nt)
        for k in range(K - 2, -1, -1):
            A("dve", lambda e, k=k: e.scalar_tensor_tensor(out=acc_.t[:, 0:nt], in0=xp_.t[:, k:k + nt], scalar=wcol(k), in1=acc_.t[:, 0:nt],
                                                          op0=ALU.mult, op1=ALU.add), R=[xp_.r(), acc_.r(), prm.r()], W=[acc_.r()], c=nt)
        A("act", lambda e: e.activation(out=TL.t[:, tl_off:tl_off + K - 1], in_=xp_.t[:, nt:nt + K - 1], func=AF.Copy),
          R=[xp_.r()], W=[TL.r(tl_off, tl_off + K - 1)], c=4)

    def dump(name, tt):
        if name in dbg_d:
            A("sp", lambda e: e.dma_start(out=dbg_d[name][:, 0:tt.n], in_=tt.t[:].rearrange("p a b -> p (a b)")),
              R=[tt.r()], dma="dbg_" + name)

    def hchunk(hd, c):
        return [x for j in range(8) for x in hd.r(j * 512 + c * 128, j * 512 + c * 128 + 128)]

    def stage_in(row0, hd=None):
        hd = hT if hd is None else hd
        for c in range(cfg["nt"] // 128):
            xi = xin[c % 2]
            r0 = row0 + c * 128
            A("sp", lambda e, xi=xi, r0=r0: e.dma_start(out=xi.t[:], in_=x_d[r0:r0 + 128, :]), W=[xi.r()], dma=xi.name)
            b = psa(2)
            A("pe", tr_group([(PS[:, b + j // 4, (j % 4) * 128:(j % 4) * 128 + 128], xi.t[:, j * 128:(j + 1) * 128]) for j in range(8)]),
              R=[xi.r(), cst.r()], W=[pr(b, 2)])
            A("act", lambda e, b=b, c=c: e.activation(out=hd.t[:, :, c * 128:(c + 1) * 128],
                                                      in_=PS[:, b:b + 2, :].rearrange("p a (j t) -> p (a j) t", t=128), func=AF.Copy),
              R=[pr(b, 2)], W=[hchunk(hd, c)], c=1024)

    def stage_out(orow0):
        for c in range(cfg["nt"] // 128):
            xo = xout[c % 2]
            r0 = orow0 + c * 128
            b = psa(2)
            A("pe", tr_group([(PS[:, b + j // 4, (j % 4) * 128:(j % 4) * 128 + 128], hT.t[:, j, c * 128:(c + 1) * 128]) for j in range(8)]),
              R=[hchunk(hT, c), cst.r()], W=[pr(b, 2)])
            A("act", lambda e, b=b, xo=xo: e.activation(out=xo.t[:], in_=PS[:, b:b + 2, :].rearrange("p a t -> p (a t)"), func=AF.Copy),
              R=[pr(b, 2)], W=[xo.r()], c=1024)
            A("sp", lambda e, xo=xo, r0=r0: e.dma_start(out=y_d[r0:r0 + 128, :], in_=xo.t[:]), R=[xo.r()], dma=xo.name)

    def ssd_chunk(c, pre, bs):
        hn, sq, xsT, BCT, dtr, dtv, lndt, _h = bs
        xtok_, btok_, xw_ = (xtok, btok, xw) if not pre else (PRE_A if bs is SET_A else PRE_B)
        s16, acsT, qT, a48, q48 = s16s[c], acsTs[c], qTs[c], a48s[c], q48s[c]
        S = lambda i: s16.t[:, i, :]
        sr = s16.r()
        cs = slice(c * 128, (c + 1) * 128)
        dup = lambda ap: ap.unsqueeze(1).to_broadcast([128, 2, 16])
        v48 = lambda t_: t_.t[:].rearrange("p (r c) -> p r c", c=16)[:, 0:3:2, :]
        A("dve", lambda e: e.tensor_tensor(out=v48(a48), in0=dup(dtv.t[:, c, :]), in1=dup(sm.t[:, SM_A:SM_A + 16]), op=ALU.mult),
          R=[dtv.r(), sm.r()], W=[a48.r()], c=32)
        a_ = a48.t[:, 0:16]
        b = psa()
        A("pe", mm_group([(PS[:, b, 0:16], U_f, a_, True, True),
                          (PS[0:48, b, 16:144], a48.t[:], U_f, True, True),
                          (PS[:, b, 144:160], ones_f.t[:], a_, True, True)]), R=[a48.r(), cst.r(), ones_f.r()], W=[pr(b)])
        A("dve", lambda e: e.tensor_tensor(out=v48(q48), in0=dup(lndt.t[:, c, :]), in1=dup(PS[:, b, 0:16]), op=ALU.subtract),
          R=[lndt.r(), pr(b)], W=[q48.r()], c=32)
        A("dve", lambda e: e.tensor_tensor(out=S(3), in0=q48.t[:, 0:16], in1=PS[:, b, 144:160], op=ALU.add), R=[q48.r(), pr(b)], W=[sr], c=16)
        A("act", lambda e: e.activation(out=S(4), in_=S(3), func=AF.Exp), R=[sr], W=[sr], c=16)
        A("act", lambda e: e.activation(out=S(6), in_=PS[:, b, 144:160], func=AF.Exp), R=[pr(b)], W=[sr], c=16)
        if not pre:
            A("act", lambda e: e.activation(out=S(5), in_=PS[:, b, 0:16], func=AF.Exp), R=[pr(b)], W=[sr], c=16)
            A("act", lambda e: e.activation(out=acsT.t[:], in_=PS[0:48, b, 16:144], func=AF.Copy), R=[pr(b)], W=[acsT.r()], c=128)
            A("dve", lambda e: e.tensor_tensor(out=acsT.t[32:48, :], in0=PS[32:48, b, 16:144], in1=acsT.t[32:48, :], op=ALU.subtract),
              R=[pr(b), acsT.r()], W=[acsT.r()], c=128)
            b2 = psa()
            A("pe", mm_group([(PS[0:48, b2, 0:128], q48.t[:], ident_f, True, True)]), R=[q48.r(), cst.r()], W=[pr(b2)])
            A("act", lambda e: e.activation(out=qT.t[:], in_=PS[0:48, b2, 0:128], func=AF.Copy), R=[pr(b2)], W=[qT.r()], c=128)
            A("dve", lambda e: e.tensor_tensor(out=qT.t[32:48, :], in0=PS[32:48, b2, 0:128], in1=qT.t[32:48, :], op=ALU.subtract),
              R=[pr(b2), qT.r()], W=[qT.r()], c=128)
            bcb = psa()
            A("pe", mm_group([(PS[:, bcb, g * 128:(g + 1) * 128], BCT.t[:, g, cs], BCT.t[:, 2 + g, cs], True, True) for g in range(2)]),
              R=[BCT.r()], W=[pr(bcb)])
            for half in range(2):
                bb = psa(2)
                specs = []
                for hh in range(8):
                    h = half * 8 + hh
                    o = PS[:, bb + hh // 4, (hh % 4) * 128:(hh % 4) * 128 + 128]
                    specs += [(o, Eall.t[:, h, :], acsT.t[:], True, False), (o, qT.t[:], Eall.t[:, h, :], False, False),
                              (o, ident_b, negm_b, False, True)]
                A("pe", mm_group(specs), R=[Eall.r(), acsT.r(), qT.r(), cbf.r()], W=[pr(bb, 2)])
                dc = dec[half]
                A("act", lambda e, bb=bb, dc=dc: e.activation(out=dc.t[:], in_=PS[:, bb:bb + 2, :].rearrange("p a t -> p (a t)"), func=AF.Exp),
                  R=[pr(bb, 2)], W=[dc.r()], c=1024)
                A("dve", lambda e, half=half, dc=dc: e.tensor_tensor(
                    out=MT.t[:, half * 8:half * 8 + 8, :], in0=dc.t[:].rearrange("p (h l) -> p h l", l=128),
                    in1=PS[:, bcb, half * 128:half * 128 + 128].unsqueeze(1).to_broadcast([128, 8, 128]), op=ALU.mult),
                  R=[dc.r(), pr(bcb)], W=[MT.r(half * 1024, half * 1024 + 1024)], c=1024)
        bx = psa(2)
        A("pe", mm_group([(PS[:, bx + j // 4, (j % 4) * 128:(j % 4) * 128 + 128], xsT.t[:, j, cs], ident_b, True, True) for j in range(8)]),
          R=[xsT.r(), cbf.r()], W=[pr(bx, 2)])
        A("act", lambda e: e.activation(out=xtok_.t[:], in_=PS[:, bx:bx + 2, :].rearrange("p a t -> p (a t)"), func=AF.Copy),
          R=[pr(bx, 2)], W=[xtok_.r()], c=1024)
        bB = psa()
        A("pe", mm_group([(PS[:, bB, g * 128:(g + 1) * 128], BCT.t[:, g, cs], ident_b, True, True) for g in range(2)]),
          R=[BCT.r(), cbf.r()], W=[pr(bB)])
        A("act", lambda e: e.activation(out=btok_.t[:], in_=PS[:, bB, 0:256], func=AF.Copy), R=[pr(bB)], W=[btok_.r()], c=256)
        A("dve", lambda e: e.tensor_tensor(out=xw_.t[:].rearrange("p (h d) -> p h d", d=64), in0=xtok_.t[:].rearrange("p (h d) -> p h d", d=64),
                                           in1=S(4).unsqueeze(2).to_broadcast([128, 16, 64]), op=ALU.mult), R=[xtok_.r(), sr], W=[xw_.r()], c=1024)
        if not pre:
            by = psa(2)
            specs = []
            for h in range(16):
                o = PS[:, by + h // 8, (h % 8) * 64:(h % 8) * 64 + 64]
                specs += [(o, MT.t[:, h, :], xtok_.t[:, h * 64:(h + 1) * 64], True, False),
                          (o, xsT.t[:, h // 2, cs], diagD.t[:, h // 2, (h % 2) * 64:(h % 2) * 64 + 64], False, True)]
            A("pe", mm_group(specs), R=[MT.r(), xtok_.r(), xsT.r(), diagD.r()], W=[pr(by, 2)])
            bo = psa(2)
            A("pe", mm_group([(PS[:, bo + g, :], BCT.t[:, 2 + g, cs], st_b.t[:, g * 512:(g + 1) * 512], True, True) for g in range(2)]),
              R=[BCT.r(), st_b.r()], W=[pr(bo, 2)])
            A("dve", lambda e: e.tensor_tensor(out=tmpf.t[:].rearrange("p (h d) -> p h d", d=64),
                                               in0=PS[:, bo:bo + 2, :].rearrange("p a (h d) -> p (a h) d", d=64),
                                               in1=S(5).unsqueeze(2).to_broadcast([128, 16, 64]), op=ALU.mult), R=[pr(bo, 2), sr], W=[tmpf.r()], c=1024)
            A("dve", lambda e: e.tensor_tensor(out=yv.t[:], in0=PS[:, by:by + 2, :].rearrange("p a t -> p (a t)"), in1=tmpf.t[:], op=ALU.add),
              R=[pr(by, 2), tmpf.r()], W=[yv.r()], c=1024)
        bs = psa(2)
        A("pe", mm_group([(PS[:, bs + g, :], btok_.t[:, g * 128:(g + 1) * 128], xw_.t[:, g * 512:(g + 1) * 512], True, True) for g in range(2)]),
          R=[btok_.r(), xw_.r()], W=[pr(bs, 2)])
        A("dve", lambda e: e.tensor_tensor(out=st_f.t[:].rearrange("p (h d) -> p h d", d=64), in0=st_f.t[:].rearrange("p (h d) -> p h d", d=64),
                                           in1=S(6).unsqueeze(2).to_broadcast([128, 16, 64]), op=ALU.mult), R=[st_f.r(), sr], W=[st_f.r()], c=1024)
        A("dve", lambda e: e.tensor_tensor(out=st_f.t[:], in0=st_f.t[:], in1=PS[:, bs:bs + 2, :].rearrange("p a t -> p (a t)"), op=ALU.add),
          R=[st_f.r(), pr(bs, 2)], W=[st_f.r()], c=1024)
        A("act", lambda e: e.activation(out=st_b.t[:], in_=st_f.t[:], func=AF.Copy), R=[st_f.r()], W=[st_b.r()], c=1024)
        if pre:
            return
        A("dve", lambda e: e.tensor_tensor(out=yv.t[:], in0=yv.t[:], in1=sz.t[:, c, :], op=ALU.mult), R=[yv.r(), sz.r(c * 1024, c * 1024 + 1024)], W=[yv.r()], c=1024)
        A("dve", lambda e: e.memset(S(7)[:, 0:2], 0.0), W=[sr], c=2)
        for g in range(2):
            A("act", lambda e, g=g: e.activation(out=tmpf.t[:, g * 512:(g + 1) * 512], in_=yv.t[:, g * 512:(g + 1) * 512], func=AF.Square,
                                                 accum_out=S(7)[:, g:g + 1]), R=[yv.r(), sr], W=[tmpf.r(), sr])
        A("act", lambda e: e.activation(out=S(8)[:, 0:2], in_=S(7)[:, 0:2], func=AF.Ln, bias=sm.t[:, SM_E6:SM_E6 + 1], scale=1.0 / 512),
          R=[sr, sm.r()], W=[sr], c=2)
        A("act", lambda e: e.activation(out=S(9)[:, 0:2], in_=S(8)[:, 0:2], func=AF.Exp, scale=-0.5), R=[sr], W=[sr], c=2)
        for g in range(2):
            A("act", lambda e, g=g: e.activation(out=yn.t[:, g * 512:(g + 1) * 512], in_=yv.t[:, g * 512:(g + 1) * 512], func=AF.Copy,
                                                 scale=S(9)[:, g:g + 1]), R=[yv.r(), sr], W=[yn.r(g * 512, g * 512 + 512)])
        bt = psa(2)
        A("pe", mm_group([(PS[:, bt + j // 4, (j % 4) * 128:(j % 4) * 128 + 128], yn.t[:, j * 128:(j + 1) * 128], ident_b, True, True)
                          for j in range(8)]), R=[yn.r(), cbf.r()], W=[pr(bt, 2)])
        for j in range(8):
            A("act", lambda e, j=j: e.activation(out=mixT.t[:, j, cs], in_=PS[:, bt + j // 4, (j % 4) * 128:(j % 4) * 128 + 128], func=AF.Copy,
                                                 scale=pc("snw", j)), R=[pr(bt + j // 4), prm.r()], W=[mixT.r(j * 512 + c * 128, j * 512 + c * 128 + 128)], c=128)

    def attn_chunk(c):
        cs = slice(c * 128, (c + 1) * 128)
        S = lambda i: a16s[c].t[:, i - 10, :]
        sr = a16s[c].r()
        gi = 0
        for kv in range(2):
            for pad in range(2):
                h0 = 8 * kv + pad
                pts = []
                for kb in range(2):
                    b = psa()
                    ov_ = PS[:, b, :].rearrange("p (j q) -> p j q", q=128)
                    A("pe", mm_group([(ov_, Kp.t[:, 2 * kv + pad, (c + kb) * 128:(c + kb + 1) * 128], qnT.t[:, 4 * kv:4 * kv + 4, cs], True, False),
                                      (ov_, ident_b, biasT.t[:, h0:h0 + 7:2, kb, :], False, True)]),
                      R=[Kp.r(), qnT.r(), biasT.r(), cbf.r()], W=[pr(b)])
                    p_ = pT[(gi % 2) * 2 + kb]
                    A("act", lambda e, b=b, p_=p_: e.activation(out=p_.t[:], in_=PS[:, b, :], func=AF.Exp), R=[pr(b)], W=[p_.r()])
                    pts.append(p_)
                o = psa()
                specs = []
                for j in range(4):
                    for kb in range(2):
                        specs.append((PS[:, o, j * 65:(j + 1) * 65], pts[kb].t[:, j * 128:(j + 1) * 128], Vx.t[:, c + kb, kv, :], kb == 0, kb == 1))
                A("pe", mm_group(specs), R=[pts[0].r(), pts[1].r(), Vx.r()], W=[pr(o)])
                ov = PS[:, o, 0:260].rearrange("p (j d) -> p j d", d=65)
                A("dve", lambda e, ov=ov, h0=h0: e.tensor_tensor(out=S(10)[:, 0:4], in0=ov[:, :, 64], in1=sm.t[:, SM_ES + h0:SM_ES + h0 + 7:2], op=ALU.add),
                  R=[pr(o), sm.r()], W=[sr])
                A("dve", lambda e: e.reciprocal(out=S(11)[:, 0:4], in_=S(10)[:, 0:4]), R=[sr], W=[sr])
                A("dve", lambda e, ov=ov, h0=h0: e.tensor_tensor(out=attn.t[:].rearrange("p (h d) -> p h d", d=64)[:, h0:h0 + 7:2, :], in0=ov[:, :, 0:64],
                                                                in1=S(11)[:, 0:4].unsqueeze(2).to_broadcast([128, 4, 64]), op=ALU.mult),
                  R=[pr(o), sr], W=[attn.r()])
                gi += 1
        bt = psa(2)
        A("pe", mm_group([(PS[:, bt + j // 4, (j % 4) * 128:(j % 4) * 128 + 128], attn.t[:, j * 128:(j + 1) * 128], ident_b, True, True)
                          for j in range(8)]), R=[attn.r(), cbf.r()], W=[pr(bt, 2)])
        A("act", lambda e: e.activation(out=mixT.t[:, 8:16, cs], in_=PS[:, bt:bt + 2, :].rearrange("p a (j t) -> p (a j) t", t=128), func=AF.Copy),
          R=[pr(bt, 2)], W=[mixT.r(8 * 512, 16 * 512)], c=1024)

    def mixer0(pre, bs=None):
        bs = SET_A if bs is None else bs
        hn, sq, xsT, BCT, dtr, dtv, lndt, hsrc = bs
        nt = cfg["nt"]
        nch = nt // 128
        rmsnorm("mixw", 0, hn, sq, rs_s, rs_r, hsrc)
        if not pre:
            wz0, rz0, iz0 = wnext("z0", hold=True)
            wz1, rz1, iz1 = wnext("z1", hold=True)
            for c in range(nch):
                b = psa(2)
                specs = []
                for hf, wv in ((0, wz0), (1, wz1)):
                    specs += [(PS[:, b + hf, :], hn.t[:, kt, c * 128:(c + 1) * 128], wv[:, kt, :], kt == 0, kt == 7) for kt in range(8)]
                A("pe", mm_group(specs), R=[hn.r(), rz0, rz1], W=[pr(b, 2)])
                A("act", lambda e, b=b, c=c: e.activation(out=sz.t[:, c, :], in_=PS[:, b:b + 2, :].rearrange("p a t -> p (a t)"), func=AF.Silu),
                  R=[pr(b, 2)], W=[sz.r(c * 1024, c * 1024 + 1024)], c=1024)
            wrel(iz0, iz1)
        for c in range(nch):
            b = psa()
            A("pe", mm_group([(PS[:, b, 0:144], hn.t[:, kt, c * 128:(c + 1) * 128], WDV.t[:, kt, :], kt == 0, kt == 7) for kt in range(8)]),
              R=[hn.r(), WDV.r()], W=[pr(b)])
            A("dve", lambda e, b=b, c=c: e.tensor_tensor(out=dtr.t[:, c, :], in0=PS[:, b, 0:16], in1=pc("dtb", 0, 16), op=ALU.add),
              R=[pr(b), prm.r()], W=[dtr.r()])
            if not pre:
                A("act", lambda e, b=b, c=c: e.activation(out=Vx.t[:, 1 + c, :, 0:64], in_=PS[:, b, 16:144].rearrange("p (k d) -> p k d", d=64),
                                                          func=AF.Copy), R=[pr(b)], W=[Vx.r()])
        fl = lambda t_: t_.t[:, 0:nch, :].rearrange("p a b -> p (a b)")
        A("act", lambda e: e.activation(out=fl(dtr), in_=fl(dtr), func=AF.Exp), R=[dtr.r()], W=[dtr.r()])
        A("act", lambda e: e.activation(out=fl(dtv), in_=fl(dtr), func=AF.Ln, bias=1.0), R=[dtr.r()], W=[dtv.r()])
        A("act", lambda e: e.activation(out=fl(lndt), in_=fl(dtv), func=AF.Ln), R=[dtv.r()], W=[lndt.r()])
        tiles = list(range(10)) if pre else list(range(12))
        wv = wr = None
        for j in tiles:
            if j == 0:
                wv, wr, _ = wnext("x0")
            elif j == 4:
                wv, wr, _ = wnext("x1")
            elif j == 8:
                wv, wr, _ = wnext("bc")
            b = proj_fm(wv, wr, j % 4, hn)
            xp_, acc_ = xp[j % 2], acc[j % 2]
            conv_fm(b, xp_, acc_, 4, lambda k, j=j: pc("cw", j * 4 + k), pc("cb", j), TL_A + 3 * j)
            dst = xsT.t[:, j, 0:nt] if j < 8 else BCT.t[:, j - 8, 0:nt]
            dres = xsT.r(j * 512, j * 512 + 512) if j < 8 else BCT.r((j - 8) * 512, (j - 8) * 512 + 512)
            A("act", lambda e, acc_=acc_, dst=dst: e.activation(out=dst, in_=acc_.t[:, 0:nt], func=AF.Silu), R=[acc_.r()], W=[dres], c=nt)
        if not pre:
            for j in range(12):
                if j == 0:
                    wv, wr, _ = wnext("q0")
                elif j == 4:
                    wv, wr, _ = wnext("q1")
                if j < 8:
                    b = proj_fm(wv, wr, j % 4, hn)
                else:
                    b = psa()
                    t = j - 8
                    A("pe", mm_group([(PS[:, b, 0:nt], WK.t[:, kt, t, :], hn.t[:, kt, 0:nt], kt == 0, kt == 7) for kt in range(8)]),
                      R=[WK.r(), hn.r()], W=[pr(b)])
                sq_ = sqq[j % 2]
                A("act", lambda e, b=b, sq_=sq_: e.activation(out=sq_.t[:, 0:nt], in_=PS[:, b, 0:nt], func=AF.Square), R=[pr(b)], W=[sq_.r()], c=nt)
                b2 = psa()
                A("pe", mm_group([(PS[:, b2, 0:nt], bd_b, sq_.t[:, 0:nt], True, True)]), R=[sq_.r(), cbf.r()], W=[pr(b2)])
                rs1, rs2 = (rs_s, rs_r) if j % 2 == 0 else (rs_s2, rs_r2)
                A("act", lambda e, b2=b2, rs1=rs1: e.activation(out=rs1.t[:, 0:nt], in_=PS[:, b2, 0:nt], func=AF.Ln, bias=sm.t[:, SM_E6:SM_E6 + 1], scale=1.0 / 64),
                  R=[pr(b2), sm.r()], W=[rs1.r()], c=nt)
                A("act", lambda e, rs1=rs1, rs2=rs2: e.activation(out=rs2.t[:, 0:nt], in_=rs1.t[:, 0:nt], func=AF.Exp, scale=-0.5), R=[rs1.r()], W=[rs2.r()], c=nt)
                if j < 8:
                    dst, dres, wc = qnT.t[:, j, 0:nt], qnT.r(j * 512, j * 512 + 512), sm.t[:, SM_QW:SM_QW + 1]
                else:
                    dst, dres, wc = Kp.t[:, j - 8, 128:128 + nt], Kp.r(), sm.t[:, SM_KW:SM_KW + 1]
                A("dve", lambda e, b=b, dst=dst, wc=wc, rs2=rs2: e.scalar_tensor_tensor(out=dst, in0=PS[:, b, 0:nt], scalar=wc, in1=rs2.t[:, 0:nt], op0=ALU.mult, op1=ALU.mult),
                  R=[pr(b), rs2.r(), sm.r()], W=[dres], c=nt)
        lab0 = P.label
        pool0 = dict(pspool)
        for c in range(nch):
            P.label = lab0 + f".ssd{c}"
            if not pre:
                pspool.update(lo=SSD_POOL[0], n=SSD_POOL[1])
            ssd_chunk(c, pre, bs)
            if not pre:
                P.label = lab0 + f".att{c}"
                pspool.update(lo=ATT_POOL[0], n=ATT_POOL[1])
                attn_chunk(c)
        pspool.update(pool0)
        P.label = lab0 + ".oproj"
        if pre:
            return
        A("act", lambda e: e.activation(out=Kp.t[:, :, 0:128], in_=Kp.t[:, :, nt:nt + 128], func=AF.Copy), R=[Kp.r()], W=[Kp.r()])
        A("act", lambda e: e.activation(out=Vx.t[:, 0, :, :], in_=Vx.t[:, nch, :, :], func=AF.Copy), R=[Vx.r()], W=[Vx.r()])
        for i in range(4):
            wv, wr, _ = wnext(f"o{i}")
            for f2 in range(2):
                f = 2 * i + f2
                b = psa()
                A("pe", mm_group([(PS[:, b, 0:nt], wv[:, kt, f2 * 128:(f2 + 1) * 128], mixT.t[:, kt, 0:nt], kt == 0, kt == 15) for kt in range(16)]),
                  R=[wr, mixT.r()], W=[pr(b)])
                A("dve", lambda e, b=b, f=f: e.tensor_tensor(out=hT.t[:, f, 0:nt], in0=hT.t[:, f, 0:nt], in1=PS[:, b, 0:nt], op=ALU.add),
                  R=[pr(b), hT.r(f * 512, f * 512 + 512)], W=[hT.r(f * 512, f * 512 + 512)], c=nt)
                ssq_hook(f, f_sq)

    def ffn(l, last, skip_down=False):
        nt = cfg["nt"]
        rmsnorm("ffnw", l, f_hn, f_sq, f_rs, f_rr, bank=nrm["bank"])
        for g in range(6):
            ntl = 4 if g < 5 else 2
            wg, rg, ig = wnext(f"g{g}", hold=True)
            wu, ru, iu = wnext(f"u{g}", hold=True)
            for i in range(ntl):
                t = 4 * g + i
                bg = proj_fm(wg, rg, i, f_hn)
                bu = proj_fm(wu, ru, i, f_hn)
                k2 = (t % 4) * 2
                for (b, tt, xi) in ((bg, t, k2), (bu, 22 + t, k2 + 1)):
                    conv_fm(b, f_xp[xi], f_acc[xi], 3, lambda k, tt=tt: pc("fcw", (l * 44 + tt) * 3 + k), pc("fcb", l * 44 + tt),
                            TL_F + l * 88 + 2 * tt)
                sg = f_sg[t % 4]
                ag, au = f_acc[k2], f_acc[k2 + 1]
                A("act", lambda e, sg=sg, ag=ag: e.activation(out=sg.t[:, 0:nt], in_=ag.t[:, 0:nt], func=AF.Silu), R=[ag.r()], W=[sg.r()])
                A("dve", lambda e, sg=sg, au=au, t=t: e.tensor_tensor(out=actT.t[:, t, 0:nt], in0=sg.t[:, 0:nt], in1=au.t[:, 0:nt], op=ALU.mult),
                  R=[sg.r(), au.r()], W=[actT.r(t * 512, t * 512 + 512)])
            wrel(ig, iu)
        for f in range(8):
            wv, wr, _ = wnext(f"d{f}")
            if skip_down:
                continue
            b = psa()
            A("pe", mm_group([(PS[:, b, 0:nt], wv[:, kt, :], actT.t[:, kt, 0:nt], kt == 0, kt == 21) for kt in range(22)]),
              R=[wr, actT.r()], W=[pr(b)])
            A("dve", lambda e, b=b, f=f: e.tensor_tensor(out=hT.t[:, f, 0:nt], in0=hT.t[:, f, 0:nt], in1=PS[:, b, 0:nt], op=ALU.add),
              R=[pr(b), hT.r(f * 512, f * 512 + 512)], W=[hT.r(f * 512, f * 512 + 512)])
            if not last:
                ssq_hook(f, c_sq)

    def conformer():
        nt = cfg["nt"]
        rmsnorm("mixw", 1, c_hn, c_sq, c_rs, c_rr, bank=nrm["bank"])
        for half in range(2):
            wa, ra, ia = wnext(f"a{half}", hold=True)
            wg, rg, ig = wnext(f"s{half}", hold=True)
            for i in range(4):
                j = half * 4 + i
                ba = proj_fm(wa, ra, i, c_hn)
                bg = proj_fm(wg, rg, i, c_hn)
                sg = c_sig[j % 2]
                A("act", lambda e, bg=bg, sg=sg, j=j: e.activation(out=sg.t[:, 0:nt], in_=PS[:, bg, 0:nt], func=AF.Sigmoid, bias=pc("pw1b", 8 + j)),
                  R=[pr(bg), prm.r()], W=[sg.r()])
                ur = upad.r(j * 544, j * 544 + 544)
                A("dve", lambda e, ba=ba, sg=sg, j=j: e.scalar_tensor_tensor(out=upad.t[:, j, 30:30 + nt], in0=PS[:, ba, 0:nt], scalar=pc("pw1b", j),
                                                                            in1=sg.t[:, 0:nt], op0=ALU.add, op1=ALU.mult),
                  R=[pr(ba), sg.r(), prm.r()], W=[ur])
                A("act", lambda e, j=j: e.activation(out=upad.t[:, j, 0:30], in_=TL.t[:, TL_C + 30 * j:TL_C + 30 * j + 30], func=AF.Copy),
                  R=[TL.r(TL_C + 30 * j, TL_C + 30 * j + 30)], W=[ur], c=30)
                A("act", lambda e, j=j: e.activation(out=TL.t[:, TL_C + 30 * j:TL_C + 30 * j + 30], in_=upad.t[:, j, nt:nt + 30], func=AF.Copy),
                  R=[ur], W=[TL.r(TL_C + 30 * j, TL_C + 30 * j + 30)], c=30)
            wrel(ia, ig)
            for i in range(4):
                j = half * 4 + i
                ur = upad.r(j * 544, j * 544 + 544)
                yr = yc.r(j * 512, j * 512 + 512)
                wd, rd, _ = wnext(f"dg{j}")
                bc_ = psa()
                A("pe", mm_group([(PS[:, bc_, 0:nt], wd[:, k, :], upad.t[:, j, k:k + nt], k == NDV, k == 30) for k in range(NDV, 31)]),
                  R=[rd, ur], W=[pr(bc_)])
                A("dve", lambda e, j=j: e.tensor_scalar(out=yc.t[:, j, 0:nt], in0=upad.t[:, j, 0:nt], scalar1=pc("dww", j * 31), scalar2=None, op0=ALU.mult),
                  R=[ur, prm.r()], W=[yr])
                for k in range(1, NDV):
                    A("dve", lambda e, j=j, k=k: e.scalar_tensor_tensor(out=yc.t[:, j, 0:nt], in0=upad.t[:, j, k:k + nt], scalar=pc("dww", j * 31 + k),
                                                                       in1=yc.t[:, j, 0:nt], op0=ALU.mult, op1=ALU.add), R=[ur, yr, prm.r()], W=[yr])
                A("dve", lambda e, j=j, bc_=bc_: e.scalar_tensor_tensor(out=yc.t[:, j, 0:nt], in0=PS[:, bc_, 0:nt], scalar=pc("dwb", j), in1=yc.t[:, j, 0:nt],
                                                                       op0=ALU.add, op1=ALU.add), R=[pr(bc_), yr, prm.r()], W=[yr])
                A("act", lambda e, j=j: e.activation(out=ybf.t[:, j, 0:nt], in_=yc.t[:, j, 0:nt], func=AF.Copy), R=[yr], W=[ybf.r(j * 512, j * 512 + 512)])
                A("act", lambda e, j=j: e.activation(out=c_sq.t[:, j, 0:nt], in_=yc.t[:, j, 0:nt], func=AF.Square), R=[yr], W=[c_sq.r(j * 512, j * 512 + 512)])
        b1 = psa()
        A("pe", mm_group([(PS[:, b1, 0:nt], ones_b.t[:], ybf.t[:, j, 0:nt], j == 0, j == 7) for j in range(8)]), R=[ybf.r(), ones_b.r()], W=[pr(b1)])
        b2 = psa()
        A("pe", mm_group([(PS[:, b2, 0:nt], ones_b.t[:], c_sq.t[:, j, 0:nt], j == 0, j == 7) for j in range(8)]), R=[c_sq.r(), ones_b.r()], W=[pr(b2)])
        A("dve", lambda e: e.tensor_scalar(out=c_mean.t[:, 0:nt], in0=PS[:, b1, 0:nt], scalar1=1.0 / 1024, scalar2=None, op0=ALU.mult), R=[pr(b1)], W=[c_mean.r()])
        A("dve", lambda e: e.tensor_tensor(out=c_msq.t[:, 0:nt], in0=c_mean.t[:, 0:nt], in1=c_mean.t[:, 0:nt], op=ALU.mult), R=[c_mean.r()], W=[c_msq.r()])
        A("dve", lambda e: e.scalar_tensor_tensor(out=c_var.t[:, 0:nt], in0=PS[:, b2, 0:nt], scalar=1.0 / 1024, in1=c_msq.t[:, 0:nt], op0=ALU.mult, op1=ALU.subtract),
          R=[pr(b2), c_msq.r()], W=[c_var.r()])
        A("act", lambda e: e.activation(out=c_rs.t[:, 0:nt], in_=c_var.t[:, 0:nt], func=AF.Ln, bias=sm.t[:, SM_E5:SM_E5 + 1], scale=1.0), R=[c_var.r(), sm.r()], W=[c_rs.r()])
        A("act", lambda e: e.activation(out=c_rr.t[:, 0:nt], in_=c_rs.t[:, 0:nt], func=AF.Exp, scale=-0.5), R=[c_rs.r()], W=[c_rr.r()])
        for j in range(8):
            yr = yc.r(j * 512, j * 512 + 512)
            A("dve", lambda e, j=j: e.tensor_tensor(out=yc.t[:, j, 0:nt], in0=yc.t[:, j, 0:nt], in1=c_mean.t[:, 0:nt], op=ALU.subtract), R=[yr, c_mean.r()], W=[yr])
            A("dve", lambda e, j=j: e.scalar_tensor_tensor(out=yc.t[:, j, 0:nt], in0=yc.t[:, j, 0:nt], scalar=pc("lnw", j), in1=c_rr.t[:, 0:nt], op0=ALU.mult, op1=ALU.mult),
              R=[yr, c_rr.r(), prm.r()], W=[yr])
            A("act", lambda e, j=j: e.activation(out=ybf.t[:, j, 0:nt], in_=yc.t[:, j, 0:nt], func=AF.Silu, bias=pc("lnb", j)), R=[yr, prm.r()],
              W=[ybf.r(j * 512, j * 512 + 512)])
        for half in range(2):
            wv, wr, _ = wnext(f"p{half}")
            for i in range(4):
                f = half * 4 + i
                b = proj_fm(wv, wr, i, ybf)
                A("dve", lambda e, b=b, f=f: e.scalar_tensor_tensor(out=hT.t[:, f, 0:nt], in0=PS[:, b, 0:nt], scalar=pc("pw2b", f), in1=hT.t[:, f, 0:nt],
                                                                   op0=ALU.add, op1=ALU.add),
                  R=[pr(b), hT.r(f * 512, f * 512 + 512), prm.r()], W=[hT.r(f * 512, f * 512 + 512)])
                ssq_hook(f, f_sq)

    blocks = [("pre", 512 * i, 512) for i in range(NPRE)] + [("pre", 512 * NPRE, 256), ("warm", 512 * NPRE + 256, 256)]
    blocks += [("main", 512 * (NPRE + k), 512) for k in range(1, NMAIN)]
    npre_seen = 0
    nout = 0
    for bi, (kind, row0, ntok) in enumerate(blocks):
        cfg["nt"] = ntok
        P.label = f"b{bi}.in"
        if kind == "pre":
            par = npre_seen % 2
            npre_seen += 1
            stage_in(row0, hTB if par == 1 else hT)
            P.label = f"b{bi}.pre"
            mixer0(True, SET_A if par == 0 else SET_B)
            continue
        stage_in(row0, hT)
        P.label = f"b{bi}.m0"
        mixer0(False)
        if kind == "warm" and "h0" in dbg_d:
            dump("h0", hT)
        P.label = f"b{bi}.f0"
        ffn(0, False)
        P.label = f"b{bi}.cf"
        conformer()
        P.label = f"b{bi}.f1"
        ffn(1, True, skip_down=(kind == "warm"))
        P.label = f"b{bi}.out"
        if kind == "warm":
            f1 = flg.t[:, 0:1]
            A("dve", lambda e: e.tensor_scalar(out=st_f.t[:], in0=st_f.t[:], scalar1=f1, scalar2=None, op0=ALU.mult), R=[st_f.r(), flg.r()], W=[st_f.r()])
            A("dve", lambda e: e.tensor_scalar(out=st_b.t[:], in0=st_b.t[:], scalar1=f1, scalar2=None, op0=ALU.mult), R=[st_b.r(), flg.r()], W=[st_b.r()])
            A("dve", lambda e: e.tensor_scalar(out=TL.t[:], in0=TL.t[:], scalar1=f1, scalar2=None, op0=ALU.mult), R=[TL.r(), flg.r()], W=[TL.r()])
            A("dve", lambda e: e.tensor_scalar(out=Vx.t[:, 0, :, :].rearrange("p a b -> p (a b)"), in0=Vx.t[:, 0, :, :].rearrange("p a b -> p (a b)"),
                                               scalar1=f1, scalar2=None, op0=ALU.mult), R=[Vx.r(), flg.r()], W=[Vx.r()])
        else:
            stage_out(nout)
            nout += 512
    fin = [f"dmachain_{xo.name}" for xo in xout] + [f"dmachain_dbg_{n}" for n in dbg_d]
    P.add("sp", None, fin, ())
    if not do_emit:
        P.sim_ns = P.schedule()
        return nc, P, allocs
    nsem, nops = P.emit()
    print(f"[build] sbuf_peak={sbuf_peak} sems={nsem} ops={nops} loads={len(stream)} sim_us={P.sim_ns / 1e3:.0f}", flush=True)
    return nc, P, allocs


def plan_psum(allocs):
    last_r = [-1] * 8
    last_t = [0.0] * 8
    plan = []
    for n, _b, ops in allocs:
        if not ops:
            plan.append(0)
            continue
        r0 = min(o.ridx for o in ops)
        r1 = max(o.ridx for o in ops)
        t0 = min(o.t0 for o in ops)
        t1 = max(o.t1 for o in ops)
        best = None
        for b in (range(8) if n == 1 else range(0, 8, 2)):
            bs = range(b, b + n)
            if any(last_r[i] >= r0 for i in bs):
                continue
            te = max(last_t[i] for i in bs)
            key = (max(te - t0, 0.0), -te)
            if best is None or key < best[0]:
                best = (key, b)
        assert best is not None, "PSUM over-subscribed in program order"
        b = best[1]
        for i in range(b, b + n):
            last_r[i] = r1
            last_t[i] = t1
        plan.append(b)
    return plan


def build_program(NPRE, NMAIN, dbg=(), iters=PLAN_ITERS):
    plan = None
    best = None
    for it in range(iters):
        _nc, P1, allocs = record_program(NPRE, NMAIN, dbg, plan=plan, do_emit=False)
        if best is None or P1.sim_ns < best[0]:
            best = (P1.sim_ns, plan)
        print(f"[build] plan iter {it}: sim_us={P1.sim_ns / 1e3:.0f}", flush=True)
        plan = plan_psum(allocs)
    nc, _P, _a = record_program(NPRE, NMAIN, dbg, plan=best[1], do_emit=True)
    return nc


def _fm(v, nt):
    return np.ascontiguousarray(np.asarray(v, np.float32).reshape(nt, 128).T)


def _t5_bucket(dist):
    max_exact = 16
    d_f = np.maximum(dist, 1).astype(np.float32)
    large = max_exact + (np.log(d_f / max_exact) / math.log(128 / max_exact) * (32 - max_exact)).astype(np.int32)
    large = np.minimum(large, 31)
    return np.where(dist < max_exact, dist, large)


def host_pack(inp):
    f32 = np.float32
    prm = np.zeros((128, NPRM), f32)

    def put(name, arr):
        arr = np.asarray(arr, f32)
        prm[:, _p[name]:_p[name] + arr.shape[1]] = arr
    put("mixw", np.concatenate([_fm(inp["mix_norm_w"][l], 8) for l in range(2)], 1))
    put("ffnw", np.concatenate([_fm(inp["ffn_norm_w"][l], 8) for l in range(2)], 1))
    cw = np.asarray(inp["ssm_conv_w"][0], f32)
    put("cw", cw.T.reshape(12, 128, 4).transpose(1, 0, 2).reshape(128, 48))
    put("cb", _fm(inp["ssm_conv_b"][0], 12))
    put("dch", _fm(np.repeat(np.asarray(inp["ssm_d"][0], f32), 64), 8))
    put("snw", _fm(inp["ssm_norm_w"][0], 8))
    put("qw", np.tile(np.asarray(inp["attn_q_norm_w"][0], f32), 2)[:, None])
    put("kw", np.tile(np.asarray(inp["attn_k_norm_w"][0], f32), 2)[:, None])
    put("pw1b", _fm(inp["conv_pw1_b"][0], 16))
    dw = np.asarray(inp["conv_dw_w"][0], f32)
    put("dww", dw.T.reshape(8, 128, 31).transpose(1, 0, 2).reshape(128, 248))
    put("dwb", _fm(inp["conv_dw_b"][0], 8))
    put("lnw", _fm(inp["conv_ln_w"][0], 8))
    put("lnb", _fm(inp["conv_ln_b"][0], 8))
    put("pw2b", _fm(inp["conv_pw2_b"][0], 8))
    fcw = np.asarray(inp["ffn_conv_w"], f32)
    put("fcw", fcw.transpose(0, 2, 1).reshape(2, 44, 128, 3).transpose(2, 0, 1, 3).reshape(128, 264))
    put("fcb", np.asarray(inp["ffn_conv_b"], f32).reshape(2, 44, 128).transpose(2, 0, 1).reshape(128, 88))
    put("dtb", np.broadcast_to(np.asarray(inp["ssm_dt_bias"][0], f32)[None, :], (128, 16)))
    put("alog", np.broadcast_to(np.asarray(inp["ssm_a_log"][0], f32)[None, :], (128, 16)))
    put("sink", np.broadcast_to(np.asarray(inp["attn_sinks"][0], f32)[None, :], (128, 16)))
    rb = np.asarray(inp["rel_bias"], f32)
    qi = np.arange(128)[:, None]
    sj = np.arange(256)[None, :]
    dist = qi + 128 - sj
    valid = (dist >= 0) & (dist < 128)
    bias = rb[_t5_bucket(np.maximum(dist, 0))]
    bias = np.where(valid[:, :, None], bias, f32(NEG)).astype(f32)
    biasT = bias.reshape(128, 2, 128, 16).transpose(2, 3, 1, 0)
    biasT = np.ascontiguousarray(biasT).reshape(128, 16 * 2 * 128)
    cst = np.zeros((128, 512), f32)
    cst[:, 0:128] = np.eye(128, dtype=f32)
    cst[:, 128:256] = np.triu(np.ones((128, 128), f32))
    cst[:, 256:384] = np.kron(np.eye(2, dtype=f32), np.ones((64, 64), f32))
    cst[:, 384:512] = np.where(np.arange(128)[None, :] >= np.arange(128)[:, None], 0.0, NEG)
    eall = np.zeros((48, 16, 128), f32)
    for h in range(16):
        eall[h, h, :] = 1.0
        eall[32 + h, h, :] = 1.0
    return prm, biasT, cst, eall.reshape(48, 2048)


_NC_CACHE = {}


def kernel(**inp):
    x = np.asarray(inp["x"], np.float32)
    B, L, _ = x.shape
    NPRE, NMAIN = 7, 9
    prm, biasT, cst, eall = host_pack(inp)
    common = {
        "prm": prm, "biasT": biasT, "cst": cst, "eall": eall,
        "w_in": np.ascontiguousarray(inp["hyb_w_in"][0], np.float32),
        "w_out": np.ascontiguousarray(inp["hyb_w_out"][0], np.float32),
        "pw1": np.ascontiguousarray(inp["conv_pw1_w"][0], np.float32),
        "pw2": np.ascontiguousarray(inp["conv_pw2_w"][0], np.float32),
        "wup0": np.ascontiguousarray(inp["ffn_w_up"][0], np.float32),
        "wup1": np.ascontiguousarray(inp["ffn_w_up"][1], np.float32),
        "wdn0": np.ascontiguousarray(inp["ffn_w_down"][0], np.float32),
        "wdn1": np.ascontiguousarray(inp["ffn_w_down"][1], np.float32),
    }
    in_maps = []
    for core in range(8):
        b, half = core // 2, core % 2
        if half == 1:
            xs = x[b]
            flag = np.ones((128, 1), np.float32)
        else:
            xs = np.concatenate([np.zeros((4096, D), np.float32), x[b, :4096]], 0)
            flag = np.zeros((128, 1), np.float32)
        m = dict(common)
        m["x"] = np.ascontiguousarray(xs)
        m["flag"] = flag
        in_maps.append(m)
    if "nc" not in _NC_CACHE:
        _NC_CACHE["nc"] = build_program(NPRE, NMAIN)
    res = run_bass_kernel_spmd(_NC_CACHE["nc"], in_maps, core_ids=list(range(8)))
    out = np.empty((B, L, D), np.float32)
    for core in range(8):
        b, half = core // 2, core % 2
        out[b, half * 4096:(half + 1) * 4096] = res.results[core]["y"]
    return out
```

```python
import contextlib
import math
import numpy as np
import concourse.bass as bass
import concourse.mybir as mybir
from concourse.bass_utils import run_bass_kernel_spmd

F32 = mybir.dt.float32
BF16 = mybir.dt.bfloat16
AF = mybir.ActivationFunctionType
ALU = mybir.AluOpType
AX = mybir.AxisListType

ENGS = ("pe", "act", "dve", "pool", "sp")
EPOCH = 8000
DMA_BPNS = 340.0
SCHED_W = 128
SCHED_EPS = 0.0
XLAT = 0.0
SSD_POOL = (0, 4)
ATT_POOL = (4, 4)
PLAN_ITERS = 4
NDV = 10
GR = 64
SB_LO = 16640
SB_HI = 229376


class Res:
    __slots__ = ("name", "last_writer", "readers", "dma_cnt")

    def __init__(self, name):
        self.name = name
        self.last_writer = None
        self.readers = []
        self.dma_cnt = 0


class Op:
    __slots__ = ("idx", "eng", "fn", "deps", "is_dma", "sync", "eidx", "signal", "waits", "snap", "dval", "cost", "occ",
                 "t0", "t1", "label", "ridx")


class Prog:
    def __init__(self, nc):
        self.nc = nc
        self.ops = []
        self.res = {}

    def R(self, name):
        r = self.res.get(name)
        if r is None:
            r = self.res[name] = Res(name)
        return r

    def add(self, eng, fn, reads=(), writes=(), dma=None, cost=500.0, occ=None):
        op = Op()
        op.ridx = len(self.ops)
        op.label = getattr(self, "label", "")
        op.cost = cost
        op.occ = cost if occ is None else occ
        op.idx = len(self.ops)
        op.eng = eng
        op.fn = fn
        op.is_dma = dma is not None
        op.sync = self.R("dmasem_" + dma) if dma is not None else None
        deps = set()
        rs = [self.R(r) for r in set(reads)]
        ws = [self.R(w) for w in set(writes)]
        if op.is_dma:
            ws.append(self.R("dmachain_" + dma))
        for r in rs:
            if r.last_writer is not None:
                deps.add(r.last_writer)
        for w in ws:
            if w.last_writer is not None:
                deps.add(w.last_writer)
            deps.update(w.readers)
        for w in ws:
            w.last_writer = op.idx
            w.readers = []
        for r in rs:
            r.readers.append(op.idx)
        deps.discard(op.idx)
        op.deps = deps
        self.ops.append(op)
        return op

    def schedule(self, W=None):
        ops = self.ops
        if W is None:
            W = SCHED_W
        from collections import deque
        pend = {e: deque(op for op in ops if op.eng == e) for e in ENGS}
        tfree = {e: 0.0 for e in ENGS}
        self._dma_free = 0.0
        rank = [0.0] * len(ops)
        for op in reversed(ops):
            r = rank[op.idx] + op.cost
            rank[op.idx] = r
            for d in op.deps:
                if rank[d] < r:
                    rank[d] = r
        for op in ops:
            op.t0 = None
        left = len(ops)
        EPS = SCHED_EPS
        while left:
            best = None
            for e in ENGS:
                q = pend[e]
                n = 0
                cands = []
                smin = None
                for op in q:
                    if n >= W:
                        break
                    n += 1
                    rdy = 0.0
                    ok = True
                    for d in op.deps:
                        dop = ops[d]
                        if dop.t0 is None:
                            ok = False
                            break
                        t1d = dop.t1 if dop.eng == e else dop.t1 + XLAT
                        if t1d > rdy:
                            rdy = t1d
                    if not ok:
                        continue
                    st = rdy if rdy > tfree[e] else tfree[e]
                    cands.append((st, op))
                    if smin is None or st < smin:
                        smin = st
                if smin is None:
                    continue
                pick = None
                for st, op in cands:
                    if st <= smin + EPS:
                        k2 = (-rank[op.idx], op.idx)
                        if pick is None or k2 < pick[0]:
                            pick = (k2, st, op)
                key = (pick[1], pick[2].idx)
                if best is None or key < best[0]:
                    best = (key, pick[2])
            key, op = best
            op.t0 = key[0]
            if op.is_dma:
                xfer = max(op.cost - op.occ - 2000.0, 0.0)
                st = max(op.t0 + op.occ, self._dma_free)
                self._dma_free = st + xfer
                op.t1 = st + xfer + 2000.0
            else:
                op.t1 = op.t0 + op.cost
            tfree[op.eng] = op.t0 + op.occ
            pend[op.eng].remove(op)
            left -= 1
        new = sorted(ops, key=lambda o: (o.t0, o.idx))
        remap = {o.idx: i for i, o in enumerate(new)}
        for o in new:
            o.deps = {remap[d] for d in o.deps}
        for i, o in enumerate(new):
            o.idx = i
        self.ops = new
        return max(o.t1 for o in new)

    def emit(self, sched=True):
        nc = self.nc
        self.sim_ns = self.schedule() if sched else 0.0
        ops = self.ops
        cnt = {e: 0 for e in ENGS}
        for op in ops:
            op.eidx = cnt[op.eng]
            cnt[op.eng] += 1
            op.signal = False
            op.waits = []
        known = {e: {f: -1 for f in ENGS} for e in ENGS}
        kdma = {e: set() for e in ENGS}
        for op in ops:
            kn = known[op.eng]
            kd = kdma[op.eng]
            for d in sorted(op.deps):
                dop = ops[d]
                if dop.is_dma:
                    if d in kd:
                        continue
                    kd.add(d)
                else:
                    if kn[dop.eng] >= dop.eidx:
                        continue
                    kn[dop.eng] = dop.eidx
                dop.signal = True
                op.waits.append(d)
                sk, sd = dop.snap
                for f in ENGS:
                    if sk[f] > kn[f]:
                        kn[f] = sk[f]
                kd |= sd
            op.snap = (dict(kn), frozenset(kd))
        scount = {e: 0 for e in ENGS}
        keys = []
        for op in ops:
            if op.is_dma:
                op.sync.dma_cnt += 16
                op.dval = (op.sync.name, op.sync.dma_cnt)
            elif op.signal:
                c = scount[op.eng]
                scount[op.eng] += 1
                op.dval = (f"e_{op.eng}_{c // EPOCH}", c % EPOCH + 1)
            else:
                op.dval = None
            if op.dval is not None and op.dval[0] not in keys:
                keys.append(op.dval[0])
        with contextlib.ExitStack() as st:
            semh = {k: st.enter_context(nc.semaphore(k)) for k in keys}
            block = st.enter_context(nc.Block())
            per = {e: [op for op in ops if op.eng == e] for e in ENGS}

            def run(engobj, lst):
                for op in lst:
                    for d in op.waits:
                        k, v = ops[d].dval
                        engobj.wait_ge(semh[k], v)
                    if op.fn is None:
                        continue
                    ins = op.fn(engobj)
                    if op.is_dma:
                        ins.then_inc(semh[op.dval[0]], 16)
                    elif op.signal:
                        ins.then_inc(semh[op.dval[0]], 1)

            block.tensor(lambda e: run(e, per["pe"]))
            block.scalar(lambda e: run(e, per["act"]))
            block.vector(lambda e: run(e, per["dve"]))
            block.gpsimd(lambda e: run(e, per["pool"]))
            block.sync(lambda e: run(e, per["sp"]))
        return len(keys), {e: len(per[e]) for e in ENGS}


class T:
    def __init__(self, nc, name, shape, dtype, off):
        self.esz = 2 if dtype == BF16 else 4
        self.n = int(np.prod(shape[1:]))
        self.off = off
        self.t = nc.alloc_sbuf_tensor_at(name, list(shape), dtype, offset=off)
        self.name = name

    def r(self, lo=0, hi=None):
        if hi is None:
            hi = self.n
        b0 = (self.off + lo * self.esz) // GR
        b1 = (self.off + hi * self.esz - 1) // GR
        return [f"sb{g}" for g in range(b0, b1 + 1)]


class Arena:
    def __init__(self, nc):
        self.nc = nc
        self.cur = SB_LO
        self.peak = SB_LO
        self.k = 0

    def alloc(self, name, shape, dtype):
        esz = 2 if dtype == BF16 else 4
        nbytes = int(np.prod(shape[1:])) * esz
        off = (self.cur + 63) // 64 * 64
        self.cur = off + nbytes
        self.peak = max(self.peak, self.cur)
        assert self.cur <= SB_HI, (name, self.cur)
        self.k += 1
        return T(self.nc, f"{name}_{self.k}", shape, dtype, off)


D = 1024
IN_TOTAL = 3856
C_Z, C_X, C_B, C_C, C_DT, C_Q, C_K, C_V = 0, 1024, 2048, 2304, 2560, 2576, 3600, 3728
DFF = 2816
NEG = -30000.0

_p = {}
_o = 0
for _n, _w in [("mixw", 16), ("ffnw", 16), ("cw", 48), ("cb", 12), ("dch", 8), ("snw", 8), ("qw", 1), ("kw", 1),
               ("pw1b", 16), ("dww", 248), ("dwb", 8), ("lnw", 8), ("lnb", 8), ("pw2b", 8),
               ("fcw", 264), ("fcb", 88), ("dtb", 16), ("alog", 16), ("sink", 16)]:
    _p[_n] = _o
    _o += _w
NPRM = _o
TL_A, TL_F, TL_C = 0, 36, 36 + 176
NTL = 36 + 176 + 240


def record_program(NPRE, NMAIN, dbg=(), plan=None, do_emit=True):
    nc = bass.Bass("TRN2", target_bir_lowering=False)
    NT = 512 * (NPRE + NMAIN)
    NOUT = 512 * (NMAIN - 1)
    dt_ = nc.dram_tensor
    x_d = dt_("x", [NT, D], F32, kind="ExternalInput").ap()
    flag_d = dt_("flag", [128, 1], F32, kind="ExternalInput").ap()
    prm_d = dt_("prm", [128, NPRM], F32, kind="ExternalInput").ap()
    bias_d = dt_("biasT", [128, 16 * 2 * 128], F32, kind="ExternalInput").ap()
    cst_d = dt_("cst", [128, 512], F32, kind="ExternalInput").ap()
    eall_d = dt_("eall", [48, 2048], F32, kind="ExternalInput").ap()
    win_d = dt_("w_in", [D, IN_TOTAL], F32, kind="ExternalInput").ap()
    wout_d = dt_("w_out", [2048, D], F32, kind="ExternalInput").ap()
    pw1_d = dt_("pw1", [D, 2048], F32, kind="ExternalInput").ap()
    pw2_d = dt_("pw2", [D, D], F32, kind="ExternalInput").ap()
    wup_d = [dt_(f"wup{l}", [D, 2 * DFF], F32, kind="ExternalInput").ap() for l in range(2)]
    wdn_d = [dt_(f"wdn{l}", [DFF, D], F32, kind="ExternalInput").ap() for l in range(2)]
    y_d = dt_("y", [NOUT, D], F32, kind="ExternalOutput").ap()
    diag_d = dt_("diag_scr", [8, 128, 31 * 128], BF16).ap()
    dbg_d = {n: dt_("dbg_" + n, [128, 4096], F32, kind="ExternalOutput").ap() for n in dbg}

    P = Prog(nc)
    ar = Arena(nc)
    PSt = nc.alloc_psum_tensor("PS", [128, 8, 512], F32)
    PS = PSt
    pspool = {"lo": 0, "n": 8}
    pscnt = {}
    allocs = []
    cur_alloc = {}

    def psa(n=1):
        k = len(allocs)
        if plan is not None:
            b = plan[k]
        else:
            key = (0, 8)
            c = pscnt.get(key, 0)
            if n == 2 and c % 2 == 1:
                c += 1
            b = c % 8
            pscnt[key] = c + n
        allocs.append([n, b, []])
        for i in range(n):
            cur_alloc[b + i] = k
        return b

    def pr(b, n=1):
        return [f"pb{b + i}" for i in range(n)]

    def A(eng, fn, R=(), W=(), dma=None, c=512, nbytes=1 << 19, recip=False):
        rl = [x for l in R for x in l]
        wl = [x for l in W for x in l]
        occ = None
        if dma is not None:
            occ = 1200.0 if eng == "pool" else 150.0
            cost = occ + 2000.0 + nbytes / DMA_BPNS
        elif eng == "pe":
            cost = getattr(fn, "cost", 300.0)
        elif eng == "act":
            cost = 220.0 + 0.95 * c
        else:
            cost = 80.0 + (3.2 if recip else 1.3) * c
        if eng != "pe":
            wl = wl + ["px" + nm[2:] for nm in set(rl + wl) if nm.startswith("pb")]
        op = P.add(eng, fn, rl, wl, dma, cost, occ)
        for nm in rl + wl:
            if nm.startswith("pb"):
                lst = allocs[cur_alloc[int(nm[2:])]][2]
                if not lst or lst[-1] is not op:
                    lst.append(op)
        return op

    hT = ar.alloc("hT", [128, 8, 512], F32)
    NSLOT = 4
    WS = [ar.alloc(f"ws{i}", [128, 4096], BF16) for i in range(NSLOT)]
    WK = ar.alloc("wk", [128, 8, 4, 128], BF16)
    WDV = ar.alloc("wdv", [128, 8, 144], BF16)
    biasT = ar.alloc("biasT", [128, 16, 2, 128], BF16)
    Eall = ar.alloc("eall", [48, 16, 128], BF16)
    prm = ar.alloc("prm", [128, NPRM], F32)
    cst = ar.alloc("cst", [128, 512], F32)
    cbf = ar.alloc("cbf", [128, 512], BF16)
    ones_f = ar.alloc("ones_f", [128, 128], F32)
    ones_b = ar.alloc("ones_b", [128, 128], BF16)
    diagD = ar.alloc("diagD", [128, 8, 128], BF16)
    sm = ar.alloc("sm", [128, 128], F32)
    st_f = ar.alloc("st_f", [128, 1024], F32)
    st_b = ar.alloc("st_b", [128, 1024], BF16)
    TL = ar.alloc("TL", [128, NTL], F32)
    Kp = ar.alloc("Kp", [128, 4, 640], BF16)
    Vx = ar.alloc("Vx", [128, 5, 2, 65], BF16)
    flg = ar.alloc("flg", [128, 1], F32)
    xin = [ar.alloc(f"xin{i}", [128, 1024], F32) for i in range(2)]
    ident_f = cst.t[:, 0:128]
    U_f = cst.t[:, 128:256]
    ident_b = cbf.t[:, 0:128]
    bd_b = cbf.t[:, 256:384]
    negm_b = cbf.t[:, 384:512]
    SM_A, SM_ES, SM_E6, SM_E5, SM_QW, SM_KW = 0, 16, 32, 33, 34, 35

    def pc(name, j=0, n=1):
        return prm.t[:, _p[name] + j:_p[name] + j + n]

    base_mark = ar.cur

    xsT = ar.alloc("xsT", [128, 8, 512], BF16)
    BCT = ar.alloc("BCT", [128, 4, 512], BF16)
    qnT = ar.alloc("qnT", [128, 8, 512], BF16)
    sz = ar.alloc("sz", [128, 4, 1024], F32)
    mixT = ar.alloc("mixT", [128, 16, 512], BF16)
    dtr = ar.alloc("dtr", [128, 4, 16], F32)
    dtv = ar.alloc("dtv", [128, 4, 16], F32)
    lndt = ar.alloc("lndt", [128, 4, 16], F32)
    m0_mark = ar.cur
    ar.cur = qnT.off
    hnB = ar.alloc("hnB", [128, 8, 512], BF16)
    ar.cur = mixT.off
    xsTB = ar.alloc("xsTB", [128, 8, 512], BF16)
    BCTB = ar.alloc("BCTB", [128, 4, 512], BF16)
    dtrB = ar.alloc("dtrB", [128, 4, 16], F32)
    dtvB = ar.alloc("dtvB", [128, 4, 16], F32)
    lndtB = ar.alloc("lndtB", [128, 4, 16], F32)
    xtokB = ar.alloc("xtokB", [128, 1024], BF16)
    btokB = ar.alloc("btokB", [128, 256], BF16)
    assert ar.cur <= mixT.off + 16 * 512 * 2
    ar.cur = sz.off
    hTB = ar.alloc("hTB", [128, 8, 512], F32)
    ar.cur = m0_mark
    hn = ar.alloc("hn", [128, 8, 512], BF16)
    sq = ar.alloc("sq", [128, 8, 512], BF16)
    xp = [ar.alloc(f"xp{i}", [128, 515], F32) for i in range(2)]
    acc = [ar.alloc(f"acc{i}", [128, 512], F32) for i in range(2)]
    sqq = [ar.alloc(f"sqq{i}", [128, 512], BF16) for i in range(2)]
    rs_s = ar.alloc("rs_s", [128, 512], F32)
    rs_r = ar.alloc("rs_r", [128, 512], F32)
    rs_s2 = ar.alloc("rs_s2", [128, 512], F32)
    rs_r2 = ar.alloc("rs_r2", [128, 512], F32)
    a13_end = ar.cur
    SET_A = (hn, sq, xsT, BCT, dtr, dtv, lndt, hT)
    SET_B = (hnB, sq, xsTB, BCTB, dtrB, dtvB, lndtB, hTB)
    ar.cur = m0_mark
    xtok = ar.alloc("xtok", [128, 1024], BF16)
    btok = ar.alloc("btok", [128, 256], BF16)
    xw = ar.alloc("xw", [128, 1024], BF16)
    dec = [ar.alloc(f"dec{i}", [128, 1024], F32) for i in range(2)]
    MT = ar.alloc("MT", [128, 16, 128], BF16)
    tmpf = ar.alloc("tmpf", [128, 1024], F32)
    yv = ar.alloc("yv", [128, 1024], F32)
    yn = ar.alloc("yn", [128, 1024], BF16)
    pT = [ar.alloc(f"pT{i}", [128, 512], BF16) for i in range(4)]
    attn = ar.alloc("attn", [128, 1024], BF16)
    s16s = [ar.alloc(f"s16_{i}", [128, 16, 16], F32) for i in range(4)]
    a16s = [ar.alloc(f"a16_{i}", [128, 8, 16], F32) for i in range(4)]
    acsTs = [ar.alloc(f"acsT{i}", [48, 128], BF16) for i in range(4)]
    qTs = [ar.alloc(f"qT{i}", [48, 128], BF16) for i in range(4)]
    a48s = [ar.alloc(f"a48_{i}", [128, 48], F32) for i in range(4)]
    q48s = [ar.alloc(f"q48_{i}", [128, 48], F32) for i in range(4)]
    ar.cur = max(ar.cur, a13_end)
    m0_end = ar.cur
    ar.cur = base_mark
    f_hn = ar.alloc("f_hn", [128, 8, 512], BF16)
    f_sq = ar.alloc("f_sq", [128, 8, 512], BF16)
    f_rs = ar.alloc("f_rs", [128, 512], F32)
    f_rr = ar.alloc("f_rr", [128, 512], F32)
    actT = ar.alloc("actT", [128, 22, 512], BF16)
    f_xp = [ar.alloc(f"f_xp{i}", [128, 514], F32) for i in range(8)]
    f_acc = [ar.alloc(f"f_acc{i}", [128, 512], F32) for i in range(8)]
    f_sg = [ar.alloc(f"f_sg{i}", [128, 512], F32) for i in range(4)]
    f_end = ar.cur
    ar.cur = base_mark
    c_hn = ar.alloc("c_hn", [128, 8, 512], BF16)
    c_sq = ar.alloc("c_sq", [128, 8, 512], BF16)
    c_rs = ar.alloc("c_rs", [128, 512], F32)
    c_rr = ar.alloc("c_rr", [128, 512], F32)
    upad = ar.alloc("upad", [128, 8, 544], BF16)
    yc = ar.alloc("yc", [128, 8, 512], F32)
    ybf = ar.alloc("ybf", [128, 8, 512], BF16)
    c_sig = [ar.alloc(f"c_sig{i}", [128, 512], F32) for i in range(2)]
    c_mean = ar.alloc("c_mean", [128, 512], F32)
    c_msq = ar.alloc("c_msq", [128, 512], F32)
    c_var = ar.alloc("c_var", [128, 512], F32)
    c_end = ar.cur
    ar.cur = base_mark
    dgt = ar.alloc("dgt", [128, 31 * 128], BF16)
    ar.cur = base_mark
    xout = [ar.alloc(f"xout{i}", [128, 1024], F32) for i in range(2)]
    ar.cur = ar.peak
    xtokA = ar.alloc("xtokA", [128, 1024], BF16)
    btokA = ar.alloc("btokA", [128, 256], BF16)
    xwA = ar.alloc("xwA", [128, 1024], BF16)
    xwB = ar.alloc("xwB", [128, 1024], BF16)
    PRE_A = (xtokA, btokA, xwA)
    PRE_B = (xtokB, btokB, xwB)
    sbuf_peak = ar.peak

    A("sp", lambda e: e.dma_start(out=prm.t[:], in_=prm_d), W=[prm.r()], dma="prm")
    A("sp", lambda e: e.dma_start(out=cst.t[:], in_=cst_d), W=[cst.r()], dma="cst")
    A("pool", lambda e: e.dma_start(out=biasT.t[:].rearrange("p a b c -> p (a b c)"), in_=bias_d), W=[biasT.r()], dma="biasT")
    A("pool", lambda e: e.dma_start(out=Eall.t[:].rearrange("p a b -> p (a b)"), in_=eall_d), W=[Eall.r()], dma="eall")
    A("sp", lambda e: e.dma_start(out=flg.t[:], in_=flag_d), W=[flg.r()], dma="flg")
    A("dve", lambda e: e.memset(WK.t[:].rearrange("p a b c -> p (a b c)"), 0.0), W=[WK.r()])
    for kv in range(2):
        for pad in range(2):
            def f(e, kv=kv, pad=pad):
                return e.dma_start(out=WK.t[:, :, 2 * kv + pad, 64 * pad:64 * pad + 64],
                                   in_=win_d[:, C_K + 64 * kv:C_K + 64 * kv + 64].rearrange("(k p) c -> p k c", p=128))
            A("pool", f, W=[WK.r()], dma="wk")
    A("pool", lambda e: e.dma_start(out=WDV.t[:, :, 0:16], in_=win_d[:, C_DT:C_DT + 16].rearrange("(k p) c -> p k c", p=128)),
      W=[WDV.r()], dma="wdv")
    A("pool", lambda e: e.dma_start(out=WDV.t[:, :, 16:144], in_=win_d[:, C_V:C_V + 128].rearrange("(k p) c -> p k c", p=128)),
      W=[WDV.r()], dma="wdv")
    A("dve", lambda e: e.tensor_copy(out=cbf.t[:], in_=cst.t[:]), R=[cst.r()], W=[cbf.r()])
    A("dve", lambda e: e.memset(ones_f.t[:], 1.0), W=[ones_f.r()])
    A("dve", lambda e: e.memset(ones_b.t[:], 1.0), W=[ones_b.r()])
    A("dve", lambda e: e.memset(sm.t[:], 0.0), W=[sm.r()])
    A("dve", lambda e: e.memset(sm.t[:, SM_E6:SM_E6 + 1], 1e-6), W=[sm.r()])
    A("dve", lambda e: e.memset(sm.t[:, SM_E5:SM_E5 + 1], 1e-5), W=[sm.r()])
    A("act", lambda e: e.activation(out=sm.t[:, SM_A:SM_A + 16], in_=pc("alog", 0, 16), func=AF.Exp), R=[prm.r(), sm.r()], W=[sm.r()])
    A("dve", lambda e: e.tensor_scalar(out=sm.t[:, SM_A:SM_A + 16], in0=sm.t[:, SM_A:SM_A + 16], scalar1=-1.0, scalar2=None, op0=ALU.mult),
      R=[sm.r()], W=[sm.r()])
    A("act", lambda e: e.activation(out=sm.t[:, SM_ES:SM_ES + 16], in_=pc("sink", 0, 16), func=AF.Exp), R=[prm.r(), sm.r()], W=[sm.r()])
    A("dve", lambda e: e.tensor_scalar(out=sm.t[:, SM_QW:SM_QW + 1], in0=pc("qw"), scalar1=0.125, scalar2=None, op0=ALU.mult),
      R=[prm.r(), sm.r()], W=[sm.r()])
    A("dve", lambda e: e.tensor_copy(out=sm.t[:, SM_KW:SM_KW + 1], in_=pc("kw")), R=[prm.r(), sm.r()], W=[sm.r()])
    for j in range(8):
        A("dve", lambda e, j=j: e.tensor_scalar(out=diagD.t[:, j, :], in0=ident_f, scalar1=pc("dch", j), scalar2=None, op0=ALU.mult),
          R=[cst.r(), prm.r()], W=[diagD.r()])
    for i in range(4):
        A("dve", lambda e, i=i: e.memset(a48s[i].t[:], 0.0), W=[a48s[i].r()], c=48)
        A("dve", lambda e, i=i: e.memset(q48s[i].t[:], 0.0), W=[q48s[i].r()], c=48)
    A("dve", lambda e: e.memset(st_f.t[:], 0.0), W=[st_f.r()])
    A("dve", lambda e: e.memset(st_b.t[:], 0.0), W=[st_b.r()])
    A("dve", lambda e: e.memset(TL.t[:], 0.0), W=[TL.r()])
    A("dve", lambda e: e.memset(Kp.t[:].rearrange("p a b -> p (a b)"), 0.0), W=[Kp.r()])
    A("dve", lambda e: e.memset(Vx.t[:].rearrange("p a b c -> p (a b c)"), 0.0), W=[Vx.r()])
    A("dve", lambda e: e.memset(Vx.t[:, :, :, 64:65], 1.0), W=[Vx.r()])

    for j in range(8):
        A("dve", lambda e, j=j: e.tensor_tensor(out=dgt.t[:].rearrange("p (k c) -> p k c", c=128),
                                                in0=ident_f.unsqueeze(1).to_broadcast([128, 31, 128]),
                                                in1=pc("dww", j * 31, 31).unsqueeze(2).to_broadcast([128, 31, 128]), op=ALU.mult),
          R=[cst.r(), prm.r()], W=[dgt.r()], c=3968)
        A("sp", lambda e, j=j: e.dma_start(out=diag_d[j], in_=dgt.t[:]), R=[dgt.r()], W=[["dram_diag"]], dma="dgt")

    def wsrc(w, k0, nk, c0, ncol):
        return w[k0 * 128:(k0 + nk) * 128, c0:c0 + ncol].rearrange("(k p) c -> p k c", p=128)

    def loads_mixer0(pre):
        L = []
        if not pre:
            L += [("z0", win_d, 0, 8, C_Z, 512), ("z1", win_d, 0, 8, C_Z + 512, 512)]
        L += [("x0", win_d, 0, 8, C_X, 512), ("x1", win_d, 0, 8, C_X + 512, 512)]
        if pre:
            L += [("bc", win_d, 0, 8, C_B, 256)]
        else:
            L += [("bc", win_d, 0, 8, C_B, 512), ("q0", win_d, 0, 8, C_Q, 512), ("q1", win_d, 0, 8, C_Q + 512, 512)]
            L += [(f"o{i}", wout_d, 0, 16, 256 * i, 256) for i in range(4)]
        return L

    def loads_ffn(l):
        L = []
        for g in range(6):
            nc_ = 512 if g < 5 else 256
            L += [(f"g{g}", wup_d[l], 0, 8, 512 * g, nc_), (f"u{g}", wup_d[l], 0, 8, DFF + 512 * g, nc_)]
        L += [(f"d{f}", wdn_d[l], 0, 22, 128 * f, 128) for f in range(8)]
        return L

    def loads_conf():
        L = []
        for h in range(2):
            L += [(f"a{h}", pw1_d, 0, 8, 512 * h, 512), (f"s{h}", pw1_d, 0, 8, 1024 + 512 * h, 512)]
            L += [(f"dg{j}", None, j, 31, 0, 128) for j in range(4 * h, 4 * h + 4)]
        L += [(f"p{h}", pw2_d, 0, 8, 512 * h, 512) for h in range(2)]
        return L

    stream = []
    for _ in range(NPRE + 1):
        stream += loads_mixer0(True)
    for _ in range(NMAIN):
        stream += loads_mixer0(False) + loads_ffn(0) + loads_conf() + loads_ffn(1)
    wstate = {"issued": 0, "next": 0}
    released = set()
    auto_pending = []

    def issue_loads(upto):
        while wstate["issued"] < min(upto, len(stream)):
            i = wstate["issued"]
            if i >= NSLOT and (i - NSLOT) not in released:
                break
            _, w, k0, nk, c0, ncol = stream[i]
            slot = WS[i % NSLOT]

            def f(e, slot=slot, w=w, k0=k0, nk=nk, c0=c0, ncol=ncol):
                if w is None:
                    return e.dma_start(out=slot.t[:, 0:nk * ncol], in_=diag_d[k0])
                return e.dma_start(out=slot.t[:, 0:nk * ncol].rearrange("p (k c) -> p k c", c=ncol), in_=wsrc(w, k0, nk, c0, ncol))
            A("pool", f, R=[["dram_diag"]] if w is None else [], W=[slot.r()], dma=f"ws{i % NSLOT}", nbytes=nk * ncol * 128 * (2 if w is None else 4))
            wstate["issued"] += 1

    def wrel(*idxs):
        released.update(idxs)
        issue_loads(wstate["next"] + NSLOT)

    def wnext(tag, hold=False):
        i = wstate["next"]
        assert stream[i][0] == tag, (stream[i][0], tag)
        released.update(auto_pending)
        del auto_pending[:]
        issue_loads(i + NSLOT)
        assert wstate["issued"] > i, ("weight ring deadlock", tag)
        wstate["next"] += 1
        if not hold:
            auto_pending.append(i)
        _, w, k0, nk, c0, ncol = stream[i]
        slot = WS[i % NSLOT]
        return slot.t[:, 0:nk * ncol].rearrange("p (k c) -> p k c", c=ncol), slot.r(), i

    def mm_group(specs):
        def f(e):
            ins = None
            for (o, l, r, s0, s1) in specs:
                ins = e.matmul(o, lhsT=l, rhs=r, start=s0, stop=s1)
            return ins
        cost = 0.0
        for (o, l, r, s0, s1) in specs:
            n = int(np.prod(o.shape[1:]))
            cost += (max(n, 96) / 1.9) * (4.0 if l.dtype == F32 else 1.0) + 12.0
        f.cost = cost
        return f

    cfg = {"nt": 512}

    def tr_group(specs):
        def f(e):
            ins = None
            for (o, i_) in specs:
                ins = e.transpose(o, i_, ident_f)
            return ins
        f.cost = len(specs) * 110.0
        return f

    def rmsnorm(wname, wl, hn_, sq_, rs_, rr_, hs=None, bank=None):
        hs = hT if hs is None else hs
        nt = cfg["nt"]
        if bank is None:
            A("act", lambda e: e.activation(out=sq_.t[:, :, 0:nt], in_=hs.t[:, :, 0:nt], func=AF.Square), R=[hs.r()], W=[sq_.r()], c=8 * nt)
            b = psa()
            A("pe", mm_group([(PS[:, b, 0:nt], ones_b.t[:], sq_.t[:, kt, 0:nt], kt == 0, kt == 7) for kt in range(8)]),
              R=[sq_.r(), ones_b.r()], W=[pr(b)])
        else:
            b = bank
        A("act", lambda e: e.activation(out=rs_.t[:, 0:nt], in_=PS[:, b, 0:nt], func=AF.Ln, bias=sm.t[:, SM_E6:SM_E6 + 1], scale=1.0 / 1024),
          R=[pr(b), sm.r()], W=[rs_.r()], c=nt)
        A("act", lambda e: e.activation(out=rr_.t[:, 0:nt], in_=rs_.t[:, 0:nt], func=AF.Exp, scale=-0.5), R=[rs_.r()], W=[rr_.r()], c=nt)
        for kt in range(8):
            A("dve", lambda e, kt=kt: e.scalar_tensor_tensor(out=hn_.t[:, kt, 0:nt], in0=hs.t[:, kt, 0:nt], scalar=pc(wname, wl * 8 + kt),
                                                            in1=rr_.t[:, 0:nt], op0=ALU.mult, op1=ALU.mult),
              R=[hs.r(kt * 512, kt * 512 + 512), rr_.r(), prm.r()], W=[hn_.r(kt * 512, kt * 512 + 512)], c=nt)

    nrm = {"bank": None}

    def ssq_hook(f, sq_next):
        nt = cfg["nt"]
        A("act", lambda e: e.activation(out=sq_next.t[:, f, 0:nt], in_=hT.t[:, f, 0:nt], func=AF.Square),
          R=[hT.r(f * 512, f * 512 + 512)], W=[sq_next.r(f * 512, f * 512 + 512)], c=nt)
        if f == 0:
            nrm["bank"] = psa()
        b = nrm["bank"]
        A("pe", mm_group([(PS[:, b, 0:nt], ones_b.t[:], sq_next.t[:, f, 0:nt], f == 0, f == 7)]),
          R=[sq_next.r(f * 512, f * 512 + 512), ones_b.r()], W=[pr(b)])

    def proj_fm(wv, wres, jloc, hn_):
        nt = cfg["nt"]
        b = psa()
        A("pe", mm_group([(PS[:, b, 0:nt], wv[:, kt, jloc * 128:(jloc + 1) * 128], hn_.t[:, kt, 0:nt], kt == 0, kt == 7) for kt in range(8)]),
          R=[wres, hn_.r()], W=[pr(b)])
        return b

    def conv_fm(b, xp_, acc_, K, wcol, bcol, tl_off):
        nt = cfg["nt"]
        A("act", lambda e: e.activation(out=xp_.t[:, 0:K - 1], in_=TL.t[:, tl_off:tl_off + K - 1], func=AF.Copy),
          R=[TL.r(tl_off, tl_off + K - 1)], W=[xp_.r(0, K - 1)], c=4)
        A("act", lambda e: e.activation(out=xp_.t[:, K - 1:K - 1 + nt], in_=PS[:, b, 0:nt], func=AF.Copy), R=[pr(b)], W=[xp_.r(K - 1, K + 511)], c=nt)
        A("act", lambda e: e.activation(out=acc_.t[:, 0:nt], in_=PS[:, b, 0:nt], func=AF.Identity, scale=wcol(K - 1), bias=bcol),
          R=[pr(b), prm.r()], W=[acc_.r()], c=nt)
        for k in range(K - 2, -1, -1):
            A("dve", lambda e, k=k: e.scalar_tensor_tensor(out=acc_.t[:, 0:nt], in0=xp_.t[:, k:k + nt], scalar=wcol(k), in1=acc_.t[:, 0:nt],
                                                          op0=ALU.mult, op1=ALU.add), R=[xp_.r(), acc_.r(), prm.r()], W=[acc_.r()], c=nt)
        A("act", lambda e: e.activation(out=TL.t[:, tl_off:tl_off + K - 1], in_=xp_.t[:, nt:nt + K - 1], func=AF.Copy),
          R=[xp_.r()], W=[TL.r(tl_off, tl_off + K - 1)], c=4)

    def dump(name, tt):
        if name in dbg_d:
            A("sp", lambda e: e.dma_start(out=dbg_d[name][:, 0:tt.n], in_=tt.t[:].rearrange("p a b -> p (a b)")),
              R=[tt.r()], dma="dbg_" + name)

    def hchunk(hd, c):
        return [x for j in range(8) for x in hd.r(j * 512 + c * 128, j * 512 + c * 128 + 128)]

    def stage_in(row0, hd=None):
        hd = hT if hd is None else hd
        for c in range(cfg["nt"] // 128):
            xi = xin[c % 2]
            r0 = row0 + c * 128
            A("sp", lambda e, xi=xi, r0=r0: e.dma_start(out=xi.t[:], in_=x_d[r0:r0 + 128, :]), W=[xi.r()], dma=xi.name)
            b = psa(2)
            A("pe", tr_group([(PS[:, b + j // 4, (j % 4) * 128:(j % 4) * 128 + 128], xi.t[:, j * 128:(j + 1) * 128]) for j in range(8)]),
              R=[xi.r(), cst.r()], W=[pr(b, 2)])
            A("act", lambda e, b=b, c=c: e.activation(out=hd.t[:, :, c * 128:(c + 1) * 128],
                                                      in_=PS[:, b:b + 2, :].rearrange("p a (j t) -> p (a j) t", t=128), func=AF.Copy),
              R=[pr(b, 2)], W=[hchunk(hd, c)], c=1024)

    def stage_out(orow0):
        for c in range(cfg["nt"] // 128):
            xo = xout[c % 2]
            r0 = orow0 + c * 128
            b = psa(2)
            A("pe", tr_group([(PS[:, b + j // 4, (j % 4) * 128:(j % 4) * 128 + 128], hT.t[:, j, c * 128:(c + 1) * 128]) for j in range(8)]),
              R=[hchunk(hT, c), cst.r()], W=[pr(b, 2)])
            A("act", lambda e, b=b, xo=xo: e.activation(out=xo.t[:], in_=PS[:, b:b + 2, :].rearrange("p a t -> p (a t)"), func=AF.Copy),
              R=[pr(b, 2)], W=[xo.r()], c=1024)
            A("sp", lambda e, xo=xo, r0=r0: e.dma_start(out=y_d[r0:r0 + 128, :], in_=xo.t[:]), R=[xo.r()], dma=xo.name)

    def ssd_chunk(c, pre, bs):
        hn, sq, xsT, BCT, dtr, dtv, lndt, _h = bs
        xtok_, btok_, xw_ = (xtok, btok, xw) if not pre else (PRE_A if bs is SET_A else PRE_B)
        s16, acsT, qT, a48, q48 = s16s[c], acsTs[c], qTs[c], a48s[c], q48s[c]
        S = lambda i: s16.t[:, i, :]
        sr = s16.r()
        cs = slice(c * 128, (c + 1) * 128)
        dup = lambda ap: ap.unsqueeze(1).to_broadcast([128, 2, 16])
        v48 = lambda t_: t_.t[:].rearrange("p (r c) -> p r c", c=16)[:, 0:3:2, :]
        A("dve", lambda e: e.tensor_tensor(out=v48(a48), in0=dup(dtv.t[:, c, :]), in1=dup(sm.t[:, SM_A:SM_A + 16]), op=ALU.mult),
          R=[dtv.r(), sm.r()], W=[a48.r()], c=32)
        a_ = a48.t[:, 0:16]
        b = psa()
        A("pe", mm_group([(PS[:, b, 0:16], U_f, a_, True, True),
                          (PS[0:48, b, 16:144], a48.t[:], U_f, True, True),
                          (PS[:, b, 144:160], ones_f.t[:], a_, True, True)]), R=[a48.r(), cst.r(), ones_f.r()], W=[pr(b)])
        A("dve", lambda e: e.tensor_tensor(out=v48(q48), in0=dup(lndt.t[:, c, :]), in1=dup(PS[:, b, 0:16]), op=ALU.subtract),
          R=[lndt.r(), pr(b)], W=[q48.r()], c=32)
        A("dve", lambda e: e.tensor_tensor(out=S(3), in0=q48.t[:, 0:16], in1=PS[:, b, 144:160], op=ALU.add), R=[q48.r(), pr(b)], W=[sr], c=16)
        A("act", lambda e: e.activation(out=S(4), in_=S(3), func=AF.Exp), R=[sr], W=[sr], c=16)
        A("act", lambda e: e.activation(out=S(6), in_=PS[:, b, 144:160], func=AF.Exp), R=[pr(b)], W=[sr], c=16)
        if not pre:
            A("act", lambda e: e.activation(out=S(5), in_=PS[:, b, 0:16], func=AF.Exp), R=[pr(b)], W=[sr], c=16)
            A("act", lambda e: e.activation(out=acsT.t[:], in_=PS[0:48, b, 16:144], func=AF.Copy), R=[pr(b)], W=[acsT.r()], c=128)
            A("dve", lambda e: e.tensor_tensor(out=acsT.t[32:48, :], in0=PS[32:48, b, 16:144], in1=acsT.t[32:48, :], op=ALU.subtract),
              R=[pr(b), acsT.r()], W=[acsT.r()], c=128)
            b2 = psa()
            A("pe", mm_group([(PS[0:48, b2, 0:128], q48.t[:], ident_f, True, True)]), R=[q48.r(), cst.r()], W=[pr(b2)])
            A("act", lambda e: e.activation(out=qT.t[:], in_=PS[0:48, b2, 0:128], func=AF.Copy), R=[pr(b2)], W=[qT.r()], c=128)
            A("dve", lambda e: e.tensor_tensor(out=qT.t[32:48, :], in0=PS[32:48, b2, 0:128], in1=qT.t[32:48, :], op=ALU.subtract),
              R=[pr(b2), qT.r()], W=[qT.r()], c=128)
            bcb = psa()
            A("pe", mm_group([(PS[:, bcb, g * 128:(g + 1) * 128], BCT.t[:, g, cs], BCT.t[:, 2 + g, cs], True, True) for g in range(2)]),
              R=[BCT.r()], W=[pr(bcb)])
            for half in range(2):
                bb = psa(2)
                specs = []
                for hh in range(8):
                    h = half * 8 + hh
                    o = PS[:, bb + hh // 4, (hh % 4) * 128:(hh % 4) * 128 + 128]
                    specs += [(o, Eall.t[:, h, :], acsT.t[:], True, False), (o, qT.t[:], Eall.t[:, h, :], False, False),
                              (o, ident_b, negm_b, False, True)]
                A("pe", mm_group(specs), R=[Eall.r(), acsT.r(), qT.r(), cbf.r()], W=[pr(bb, 2)])
                dc = dec[half]
                A("act", lambda e, bb=bb, dc=dc: e.activation(out=dc.t[:], in_=PS[:, bb:bb + 2, :].rearrange("p a t -> p (a t)"), func=AF.Exp),
                  R=[pr(bb, 2)], W=[dc.r()], c=1024)
                A("dve", lambda e, half=half, dc=dc: e.tensor_tensor(
                    out=MT.t[:, half * 8:half * 8 + 8, :], in0=dc.t[:].rearrange("p (h l) -> p h l", l=128),
                    in1=PS[:, bcb, half * 128:half * 128 + 128].unsqueeze(1).to_broadcast([128, 8, 128]), op=ALU.mult),
                  R=[dc.r(), pr(bcb)], W=[MT.r(half * 1024, half * 1024 + 1024)], c=1024)
        bx = psa(2)
        A("pe", mm_group([(PS[:, bx + j // 4, (j % 4) * 128:(j % 4) * 128 + 128], xsT.t[:, j, cs], ident_b, True, True) for j in range(8)]),
          R=[xsT.r(), cbf.r()], W=[pr(bx, 2)])
        A("act", lambda e: e.activation(out=xtok_.t[:], in_=PS[:, bx:bx + 2, :].rearrange("p a t -> p (a t)"), func=AF.Copy),
          R=[pr(bx, 2)], W=[xtok_.r()], c=1024)
        bB = psa()
        A("pe", mm_group([(PS[:, bB, g * 128:(g + 1) * 128], BCT.t[:, g, cs], ident_b, True, True) for g in range(2)]),
          R=[BCT.r(), cbf.r()], W=[pr(bB)])
        A("act", lambda e: e.activation(out=btok_.t[:], in_=PS[:, bB, 0:256], func=AF.Copy), R=[pr(bB)], W=[btok_.r()], c=256)
        A("dve", lambda e: e.tensor_tensor(out=xw_.t[:].rearrange("p (h d) -> p h d", d=64), in0=xtok_.t[:].rearrange("p (h d) -> p h d", d=64),
                                           in1=S(4).unsqueeze(2).to_broadcast([128, 16, 64]), op=ALU.mult), R=[xtok_.r(), sr], W=[xw_.r()], c=1024)
        if not pre:
            by = psa(2)
            specs = []
            for h in range(16):
                o = PS[:, by + h // 8, (h % 8) * 64:(h % 8) * 64 + 64]
                specs += [(o, MT.t[:, h, :], xtok_.t[:, h * 64:(h + 1) * 64], True, False),
                          (o, xsT.t[:, h // 2, cs], diagD.t[:, h // 2, (h % 2) * 64:(h % 2) * 64 + 64], False, True)]
            A("pe", mm_group(specs), R=[MT.r(), xtok_.r(), xsT.r(), diagD.r()], W=[pr(by, 2)])
            bo = psa(2)
            A("pe", mm_group([(PS[:, bo + g, :], BCT.t[:, 2 + g, cs], st_b.t[:, g * 512:(g + 1) * 512], True, True) for g in range(2)]),
              R=[BCT.r(), st_b.r()], W=[pr(bo, 2)])
            A("dve", lambda e: e.tensor_tensor(out=tmpf.t[:].rearrange("p (h d) -> p h d", d=64),
                                               in0=PS[:, bo:bo + 2, :].rearrange("p a (h d) -> p (a h) d", d=64),
                                               in1=S(5).unsqueeze(2).to_broadcast([128, 16, 64]), op=ALU.mult), R=[pr(bo, 2), sr], W=[tmpf.r()], c=1024)
            A("dve", lambda e: e.tensor_tensor(out=yv.t[:], in0=PS[:, by:by + 2, :].rearrange("p a t -> p (a t)"), in1=tmpf.t[:], op=ALU.add),
              R=[pr(by, 2), tmpf.r()], W=[yv.r()], c=1024)
        bs = psa(2)
        A("pe", mm_group([(PS[:, bs + g, :], btok_.t[:, g * 128:(g + 1) * 128], xw_.t[:, g * 512:(g + 1) * 512], True, True) for g in range(2)]),
          R=[btok_.r(), xw_.r()], W=[pr(bs, 2)])
        A("dve", lambda e: e.tensor_tensor(out=st_f.t[:].rearrange("p (h d) -> p h d", d=64), in0=st_f.t[:].rearrange("p (h d) -> p h d", d=64),
                                           in1=S(6).unsqueeze(2).to_broadcast([128, 16, 64]), op=ALU.mult), R=[st_f.r(), sr], W=[st_f.r()], c=1024)
        A("dve", lambda e: e.tensor_tensor(out=st_f.t[:], in0=st_f.t[:], in1=PS[:, bs:bs + 2, :].rearrange("p a t -> p (a t)"), op=ALU.add),
          R=[st_f.r(), pr(bs, 2)], W=[st_f.r()], c=1024)
        A("act", lambda e: e.activation(out=st_b.t[:], in_=st_f.t[:], func=AF.Copy), R=[st_f.r()], W=[st_b.r()], c=1024)
        if pre:
            return
        A("dve", lambda e: e.tensor_tensor(out=yv.t[:], in0=yv.t[:], in1=sz.t[:, c, :], op=ALU.mult), R=[yv.r(), sz.r(c * 1024, c * 1024 + 1024)], W=[yv.r()], c=1024)
        A("dve", lambda e: e.memset(S(7)[:, 0:2], 0.0), W=[sr], c=2)
        for g in range(2):
            A("act", lambda e, g=g: e.activation(out=tmpf.t[:, g * 512:(g + 1) * 512], in_=yv.t[:, g * 512:(g + 1) * 512], func=AF.Square,
                                                 accum_out=S(7)[:, g:g + 1]), R=[yv.r(), sr], W=[tmpf.r(), sr])
        A("act", lambda e: e.activation(out=S(8)[:, 0:2], in_=S(7)[:, 0:2], func=AF.Ln, bias=sm.t[:, SM_E6:SM_E6 + 1], scale=1.0 / 512),
          R=[sr, sm.r()], W=[sr], c=2)
        A("act", lambda e: e.activation(out=S(9)[:, 0:2], in_=S(8)[:, 0:2], func=AF.Exp, scale=-0.5), R=[sr], W=[sr], c=2)
        for g in range(2):
            A("act", lambda e, g=g: e.activation(out=yn.t[:, g * 512:(g + 1) * 512], in_=yv.t[:, g * 512:(g + 1) * 512], func=AF.Copy,
                                                 scale=S(9)[:, g:g + 1]), R=[yv.r(), sr], W=[yn.r(g * 512, g * 512 + 512)])
        bt = psa(2)
        A("pe", mm_group([(PS[:, bt + j // 4, (j % 4) * 128:(j % 4) * 128 + 128], yn.t[:, j * 128:(j + 1) * 128], ident_b, True, True)
                          for j in range(8)]), R=[yn.r(), cbf.r()], W=[pr(bt, 2)])
        for j in range(8):
            A("act", lambda e, j=j: e.activation(out=mixT.t[:, j, cs], in_=PS[:, bt + j // 4, (j % 4) * 128:(j % 4) * 128 + 128], func=AF.Copy,
                                                 scale=pc("snw", j)), R=[pr(bt + j // 4), prm.r()], W=[mixT.r(j * 512 + c * 128, j * 512 + c * 128 + 128)], c=128)

    def attn_chunk(c):
        cs = slice(c * 128, (c + 1) * 128)
        S = lambda i: a16s[c].t[:, i - 10, :]
        sr = a16s[c].r()
        gi = 0
        for kv in range(2):
            for pad in range(2):
                h0 = 8 * kv + pad
                pts = []
                for kb in range(2):
                    b = psa()
                    ov_ = PS[:, b, :].rearrange("p (j q) -> p j q", q=128)
                    A("pe", mm_group([(ov_, Kp.t[:, 2 * kv + pad, (c + kb) * 128:(c + kb + 1) * 128], qnT.t[:, 4 * kv:4 * kv + 4, cs], True, False),
                                      (ov_, ident_b, biasT.t[:, h0:h0 + 7:2, kb, :], False, True)]),
                      R=[Kp.r(), qnT.r(), biasT.r(), cbf.r()], W=[pr(b)])
                    p_ = pT[(gi % 2) * 2 + kb]
                    A("act", lambda e, b=b, p_=p_: e.activation(out=p_.t[:], in_=PS[:, b, :], func=AF.Exp), R=[pr(b)], W=[p_.r()])
                    pts.append(p_)
                o = psa()
                specs = []
                for j in range(4):
                    for kb in range(2):
                        specs.append((PS[:, o, j * 65:(j + 1) * 65], pts[kb].t[:, j * 128:(j + 1) * 128], Vx.t[:, c + kb, kv, :], kb == 0, kb == 1))
                A("pe", mm_group(specs), R=[pts[0].r(), pts[1].r(), Vx.r()], W=[pr(o)])
                ov = PS[:, o, 0:260].rearrange("p (j d) -> p j d", d=65)
                A("dve", lambda e, ov=ov, h0=h0: e.tensor_tensor(out=S(10)[:, 0:4], in0=ov[:, :, 64], in1=sm.t[:, SM_ES + h0:SM_ES + h0 + 7:2], op=ALU.add),
                  R=[pr(o), sm.r()], W=[sr])
                A("dve", lambda e: e.reciprocal(out=S(11)[:, 0:4], in_=S(10)[:, 0:4]), R=[sr], W=[sr])
                A("dve", lambda e, ov=ov, h0=h0: e.tensor_tensor(out=attn.t[:].rearrange("p (h d) -> p h d", d=64)[:, h0:h0 + 7:2, :], in0=ov[:, :, 0:64],
                                                                in1=S(11)[:, 0:4].unsqueeze(2).to_broadcast([128, 4, 64]), op=ALU.mult),
                  R=[pr(o), sr], W=[attn.r()])
                gi += 1
        bt = psa(2)
        A("pe", mm_group([(PS[:, bt + j // 4, (j % 4) * 128:(j % 4) * 128 + 128], attn.t[:, j * 128:(j + 1) * 128], ident_b, True, True)
                          for j in range(8)]), R=[attn.r(), cbf.r()], W=[pr(bt, 2)])
        A("act", lambda e: e.activation(out=mixT.t[:, 8:16, cs], in_=PS[:, bt:bt + 2, :].rearrange("p a (j t) -> p (a j) t", t=128), func=AF.Copy),
          R=[pr(bt, 2)], W=[mixT.r(8 * 512, 16 * 512)], c=1024)

    def mixer0(pre, bs=None):
        bs = SET_A if bs is None else bs
        hn, sq, xsT, BCT, dtr, dtv, lndt, hsrc = bs
        nt = cfg["nt"]
        nch = nt // 128
        rmsnorm("mixw", 0, hn, sq, rs_s, rs_r, hsrc)
        if not pre:
            wz0, rz0, iz0 = wnext("z0", hold=True)
            wz1, rz1, iz1 = wnext("z1", hold=True)
            for c in range(nch):
                b = psa(2)
                specs = []
                for hf, wv in ((0, wz0), (1, wz1)):
                    specs += [(PS[:, b + hf, :], hn.t[:, kt, c * 128:(c + 1) * 128], wv[:, kt, :], kt == 0, kt == 7) for kt in range(8)]
                A("pe", mm_group(specs), R=[hn.r(), rz0, rz1], W=[pr(b, 2)])
                A("act", lambda e, b=b, c=c: e.activation(out=sz.t[:, c, :], in_=PS[:, b:b + 2, :].rearrange("p a t -> p (a t)"), func=AF.Silu),
                  R=[pr(b, 2)], W=[sz.r(c * 1024, c * 1024 + 1024)], c=1024)
            wrel(iz0, iz1)
        for c in range(nch):
            b = psa()
            A("pe", mm_group([(PS[:, b, 0:144], hn.t[:, kt, c * 128:(c + 1) * 128], WDV.t[:, kt, :], kt == 0, kt == 7) for kt in range(8)]),
              R=[hn.r(), WDV.r()], W=[pr(b)])
            A("dve", lambda e, b=b, c=c: e.tensor_tensor(out=dtr.t[:, c, :], in0=PS[:, b, 0:16], in1=pc("dtb", 0, 16), op=ALU.add),
              R=[pr(b), prm.r()], W=[dtr.r()])
            if not pre:
                A("act", lambda e, b=b, c=c: e.activation(out=Vx.t[:, 1 + c, :, 0:64], in_=PS[:, b, 16:144].rearrange("p (k d) -> p k d", d=64),
                                                          func=AF.Copy), R=[pr(b)], W=[Vx.r()])
        fl = lambda t_: t_.t[:, 0:nch, :].rearrange("p a b -> p (a b)")
        A("act", lambda e: e.activation(out=fl(dtr), in_=fl(dtr), func=AF.Exp), R=[dtr.r()], W=[dtr.r()])
        A("act", lambda e: e.activation(out=fl(dtv), in_=fl(dtr), func=AF.Ln, bias=1.0), R=[dtr.r()], W=[dtv.r()])
        A("act", lambda e: e.activation(out=fl(lndt), in_=fl(dtv), func=AF.Ln), R=[dtv.r()], W=[lndt.r()])
        tiles = list(range(10)) if pre else list(range(12))
        wv = wr = None
        for j in tiles:
            if j == 0:
                wv, wr, _ = wnext("x0")
            elif j == 4:
                wv, wr, _ = wnext("x1")
            elif j == 8:
                wv, wr, _ = wnext("bc")
            b = proj_fm(wv, wr, j % 4, hn)
            xp_, acc_ = xp[j % 2], acc[j % 2]
            conv_fm(b, xp_, acc_, 4, lambda k, j=j: pc("cw", j * 4 + k), pc("cb", j), TL_A + 3 * j)
            dst = xsT.t[:, j, 0:nt] if j < 8 else BCT.t[:, j - 8, 0:nt]
            dres = xsT.r(j * 512, j * 512 + 512) if j < 8 else BCT.r((j - 8) * 512, (j - 8) * 512 + 512)
            A("act", lambda e, acc_=acc_, dst=dst: e.activation(out=dst, in_=acc_.t[:, 0:nt], func=AF.Silu), R=[acc_.r()], W=[dres], c=nt)
        if not pre:
            for j in range(12):
                if j == 0:
                    wv, wr, _ = wnext("q0")
                elif j == 4:
                    wv, wr, _ = wnext("q1")
                if j < 8:
                    b = proj_fm(wv, wr, j % 4, hn)
                else:
                    b = psa()
                    t = j - 8
                    A("pe", mm_group([(PS[:, b, 0:nt], WK.t[:, kt, t, :], hn.t[:, kt, 0:nt], kt == 0, kt == 7) for kt in range(8)]),
                      R=[WK.r(), hn.r()], W=[pr(b)])
                sq_ = sqq[j % 2]
                A("act", lambda e, b=b, sq_=sq_: e.activation(out=sq_.t[:, 0:nt], in_=PS[:, b, 0:nt], func=AF.Square), R=[pr(b)], W=[sq_.r()], c=nt)
                b2 = psa()
                A("pe", mm_group([(PS[:, b2, 0:nt], bd_b, sq_.t[:, 0:nt], True, True)]), R=[sq_.r(), cbf.r()], W=[pr(b2)])
                rs1, rs2 = (rs_s, rs_r) if j % 2 == 0 else (rs_s2, rs_r2)
                A("act", lambda e, b2=b2, rs1=rs1: e.activation(out=rs1.t[:, 0:nt], in_=PS[:, b2, 0:nt], func=AF.Ln, bias=sm.t[:, SM_E6:SM_E6 + 1], scale=1.0 / 64),
                  R=[pr(b2), sm.r()], W=[rs1.r()], c=nt)
                A("act", lambda e, rs1=rs1, rs2=rs2: e.activation(out=rs2.t[:, 0:nt], in_=rs1.t[:, 0:nt], func=AF.Exp, scale=-0.5), R=[rs1.r()], W=[rs2.r()], c=nt)
                if j < 8:
                    dst, dres, wc = qnT.t[:, j, 0:nt], qnT.r(j * 512, j * 512 + 512), sm.t[:, SM_QW:SM_QW + 1]
                else:
                    dst, dres, wc = Kp.t[:, j - 8, 128:128 + nt], Kp.r(), sm.t[:, SM_KW:SM_KW + 1]
                A("dve", lambda e, b=b, dst=dst, wc=wc, rs2=rs2: e.scalar_tensor_tensor(out=dst, in0=PS[:, b, 0:nt], scalar=wc, in1=rs2.t[:, 0:nt], op0=ALU.mult, op1=ALU.mult),
                  R=[pr(b), rs2.r(), sm.r()], W=[dres], c=nt)
        lab0 = P.label
        pool0 = dict(pspool)
        for c in range(nch):
            P.label = lab0 + f".ssd{c}"
            if not pre:
                pspool.update(lo=SSD_POOL[0], n=SSD_POOL[1])
            ssd_chunk(c, pre, bs)
            if not pre:
                P.label = lab0 + f".att{c}"
                pspool.update(lo=ATT_POOL[0], n=ATT_POOL[1])
                attn_chunk(c)
        pspool.update(pool0)
        P.label = lab0 + ".oproj"
        if pre:
            return
        A("act", lambda e: e.activation(out=Kp.t[:, :, 0:128], in_=Kp.t[:, :, nt:nt + 128], func=AF.Copy), R=[Kp.r()], W=[Kp.r()])
        A("act", lambda e: e.activation(out=Vx.t[:, 0, :, :], in_=Vx.t[:, nch, :, :], func=AF.Copy), R=[Vx.r()], W=[Vx.r()])
        for i in range(4):
            wv, wr, _ = wnext(f"o{i}")
            for f2 in range(2):
                f = 2 * i + f2
                b = psa()
                A("pe", mm_group([(PS[:, b, 0:nt], wv[:, kt, f2 * 128:(f2 + 1) * 128], mixT.t[:, kt, 0:nt], kt == 0, kt == 15) for kt in range(16)]),
                  R=[wr, mixT.r()], W=[pr(b)])
                A("dve", lambda e, b=b, f=f: e.tensor_tensor(out=hT.t[:, f, 0:nt], in0=hT.t[:, f, 0:nt], in1=PS[:, b, 0:nt], op=ALU.add),
                  R=[pr(b), hT.r(f * 512, f * 512 + 512)], W=[hT.r(f * 512, f * 512 + 512)], c=nt)
                ssq_hook(f, f_sq)

    def ffn(l, last, skip_down=False):
        nt = cfg["nt"]
        rmsnorm("ffnw", l, f_hn, f_sq, f_rs, f_rr, bank=nrm["bank"])
        for g in range(6):
            ntl = 4 if g < 5 else 2
            wg, rg, ig = wnext(f"g{g}", hold=True)
            wu, ru, iu = wnext(f"u{g}", hold=True)
            for i in range(ntl):
                t = 4 * g + i
                bg = proj_fm(wg, rg, i, f_hn)
                bu = proj_fm(wu, ru, i, f_hn)
                k2 = (t % 4) * 2
                for (b, tt, xi) in ((bg, t, k2), (bu, 22 + t, k2 + 1)):
                    conv_fm(b, f_xp[xi], f_acc[xi], 3, lambda k, tt=tt: pc("fcw", (l * 44 + tt) * 3 + k), pc("fcb", l * 44 + tt),
                            TL_F + l * 88 + 2 * tt)
                sg = f_sg[t % 4]
                ag, au = f_acc[k2], f_acc[k2 + 1]
                A("act", lambda e, sg=sg, ag=ag: e.activation(out=sg.t[:, 0:nt], in_=ag.t[:, 0:nt], func=AF.Silu), R=[ag.r()], W=[sg.r()])
                A("dve", lambda e, sg=sg, au=au, t=t: e.tensor_tensor(out=actT.t[:, t, 0:nt], in0=sg.t[:, 0:nt], in1=au.t[:, 0:nt], op=ALU.mult),
                  R=[sg.r(), au.r()], W=[actT.r(t * 512, t * 512 + 512)])
            wrel(ig, iu)
        for f in range(8):
            wv, wr, _ = wnext(f"d{f}")
            if skip_down:
                continue
            b = psa()
            A("pe", mm_group([(PS[:, b, 0:nt], wv[:, kt, :], actT.t[:, kt, 0:nt], kt == 0, kt == 21) for kt in range(22)]),
              R=[wr, actT.r()], W=[pr(b)])
            A("dve", lambda e, b=b, f=f: e.tensor_tensor(out=hT.t[:, f, 0:nt], in0=hT.t[:, f, 0:nt], in1=PS[:, b, 0:nt], op=ALU.add),
              R=[pr(b), hT.r(f * 512, f * 512 + 512)], W=[hT.r(f * 512, f * 512 + 512)])
            if not last:
                ssq_hook(f, c_sq)

    def conformer():
        nt = cfg["nt"]
        rmsnorm("mixw", 1, c_hn, c_sq, c_rs, c_rr, bank=nrm["bank"])
        for half in range(2):
            wa, ra, ia = wnext(f"a{half}", hold=True)
            wg, rg, ig = wnext(f"s{half}", hold=True)
            for i in range(4):
                j = half * 4 + i
                ba = proj_fm(wa, ra, i, c_hn)
                bg = proj_fm(wg, rg, i, c_hn)
                sg = c_sig[j % 2]
                A("act", lambda e, bg=bg, sg=sg, j=j: e.activation(out=sg.t[:, 0:nt], in_=PS[:, bg, 0:nt], func=AF.Sigmoid, bias=pc("pw1b", 8 + j)),
                  R=[pr(bg), prm.r()], W=[sg.r()])
                ur = upad.r(j * 544, j * 544 + 544)
                A("dve", lambda e, ba=ba, sg=sg, j=j: e.scalar_tensor_tensor(out=upad.t[:, j, 30:30 + nt], in0=PS[:, ba, 0:nt], scalar=pc("pw1b", j),
                                                                            in1=sg.t[:, 0:nt], op0=ALU.add, op1=ALU.mult),
                  R=[pr(ba), sg.r(), prm.r()], W=[ur])
                A("act", lambda e, j=j: e.activation(out=upad.t[:, j, 0:30], in_=TL.t[:, TL_C + 30 * j:TL_C + 30 * j + 30], func=AF.Copy),
                  R=[TL.r(TL_C + 30 * j, TL_C + 30 * j + 30)], W=[ur], c=30)
                A("act", lambda e, j=j: e.activation(out=TL.t[:, TL_C + 30 * j:TL_C + 30 * j + 30], in_=upad.t[:, j, nt:nt + 30], func=AF.Copy),
                  R=[ur], W=[TL.r(TL_C + 30 * j, TL_C + 30 * j + 30)], c=30)
            wrel(ia, ig)
            for i in range(4):
                j = half * 4 + i
                ur = upad.r(j * 544, j * 544 + 544)
                yr = yc.r(j * 512, j * 512 + 512)
                wd, rd, _ = wnext(f"dg{j}")
                bc_ = psa()
                A("pe", mm_group([(PS[:, bc_, 0:nt], wd[:, k, :], upad.t[:, j, k:k + nt], k == NDV, k == 30) for k in range(NDV, 31)]),
                  R=[rd, ur], W=[pr(bc_)])
                A("dve", lambda e, j=j: e.tensor_scalar(out=yc.t[:, j, 0:nt], in0=upad.t[:, j, 0:nt], scalar1=pc("dww", j * 31), scalar2=None, op0=ALU.mult),
                  R=[ur, prm.r()], W=[yr])
                for k in range(1, NDV):
                    A("dve", lambda e, j=j, k=k: e.scalar_tensor_tensor(out=yc.t[:, j, 0:nt], in0=upad.t[:, j, k:k + nt], scalar=pc("dww", j * 31 + k),
                                                                       in1=yc.t[:, j, 0:nt], op0=ALU.mult, op1=ALU.add), R=[ur, yr, prm.r()], W=[yr])
                A("dve", lambda e, j=j, bc_=bc_: e.scalar_tensor_tensor(out=yc.t[:, j, 0:nt], in0=PS[:, bc_, 0:nt], scalar=pc("dwb", j), in1=yc.t[:, j, 0:nt],
                                                                       op0=ALU.add, op1=ALU.add), R=[pr(bc_), yr, prm.r()], W=[yr])
                A("act", lambda e, j=j: e.activation(out=ybf.t[:, j, 0:nt], in_=yc.t[:, j, 0:nt], func=AF.Copy), R=[yr], W=[ybf.r(j * 512, j * 512 + 512)])
                A("act", lambda e, j=j: e.activation(out=c_sq.t[:, j, 0:nt], in_=yc.t[:, j, 0:nt], func=AF.Square), R=[yr], W=[c_sq.r(j * 512, j * 512 + 512)])
        b1 = psa()
        A("pe", mm_group([(PS[:, b1, 0:nt], ones_b.t[:], ybf.t[:, j, 0:nt], j == 0, j == 7) for j in range(8)]), R=[ybf.r(), ones_b.r()], W=[pr(b1)])
        b2 = psa()
        A("pe", mm_group([(PS[:, b2, 0:nt], ones_b.t[:], c_sq.t[:, j, 0:nt], j == 0, j == 7) for j in range(8)]), R=[c_sq.r(), ones_b.r()], W=[pr(b2)])
        A("dve", lambda e: e.tensor_scalar(out=c_mean.t[:, 0:nt], in0=PS[:, b1, 0:nt], scalar1=1.0 / 1024, scalar2=None, op0=ALU.mult), R=[pr(b1)], W=[c_mean.r()])
        A("dve", lambda e: e.tensor_tensor(out=c_msq.t[:, 0:nt], in0=c_mean.t[:, 0:nt], in1=c_mean.t[:, 0:nt], op=ALU.mult), R=[c_mean.r()], W=[c_msq.r()])
        A("dve", lambda e: e.scalar_tensor_tensor(out=c_var.t[:, 0:nt], in0=PS[:, b2, 0:nt], scalar=1.0 / 1024, in1=c_msq.t[:, 0:nt], op0=ALU.mult, op1=ALU.subtract),
          R=[pr(b2), c_msq.r()], W=[c_var.r()])
        A("act", lambda e: e.activation(out=c_rs.t[:, 0:nt], in_=c_var.t[:, 0:nt], func=AF.Ln, bias=sm.t[:, SM_E5:SM_E5 + 1], scale=1.0), R=[c_var.r(), sm.r()], W=[c_rs.r()])
        A("act", lambda e: e.activation(out=c_rr.t[:, 0:nt], in_=c_rs.t[:, 0:nt], func=AF.Exp, scale=-0.5), R=[c_rs.r()], W=[c_rr.r()])
        for j in range(8):
            yr = yc.r(j * 512, j * 512 + 512)
            A("dve", lambda e, j=j: e.tensor_tensor(out=yc.t[:, j, 0:nt], in0=yc.t[:, j, 0:nt], in1=c_mean.t[:, 0:nt], op=ALU.subtract), R=[yr, c_mean.r()], W=[yr])
            A("dve", lambda e, j=j: e.scalar_tensor_tensor(out=yc.t[:, j, 0:nt], in0=yc.t[:, j, 0:nt], scalar=pc("lnw", j), in1=c_rr.t[:, 0:nt], op0=ALU.mult, op1=ALU.mult),
              R=[yr, c_rr.r(), prm.r()], W=[yr])
            A("act", lambda e, j=j: e.activation(out=ybf.t[:, j, 0:nt], in_=yc.t[:, j, 0:nt], func=AF.Silu, bias=pc("lnb", j)), R=[yr, prm.r()],
              W=[ybf.r(j * 512, j * 512 + 512)])
        for half in range(2):
            wv, wr, _ = wnext(f"p{half}")
            for i in range(4):
                f = half * 4 + i
                b = proj_fm(wv, wr, i, ybf)
                A("dve", lambda e, b=b, f=f: e.scalar_tensor_tensor(out=hT.t[:, f, 0:nt], in0=PS[:, b, 0:nt], scalar=pc("pw2b", f), in1=hT.t[:, f, 0:nt],
                                                                   op0=ALU.add, op1=ALU.add),
                  R=[pr(b), hT.r(f * 512, f * 512 + 512), prm.r()], W=[hT.r(f * 512, f * 512 + 512)])
                ssq_hook(f, f_sq)

    blocks = [("pre", 512 * i, 512) for i in range(NPRE)] + [("pre", 512 * NPRE, 256), ("warm", 512 * NPRE + 256, 256)]
    blocks += [("main", 512 * (NPRE + k), 512) for k in range(1, NMAIN)]
    npre_seen = 0
    nout = 0
    for bi, (kind, row0, ntok) in enumerate(blocks):
        cfg["nt"] = ntok
        P.label = f"b{bi}.in"
        if kind == "pre":
            par = npre_seen % 2
            npre_seen += 1
            stage_in(row0, hTB if par == 1 else hT)
            P.label = f"b{bi}.pre"
            mixer0(True, SET_A if par == 0 else SET_B)
            continue
        stage_in(row0, hT)
        P.label = f"b{bi}.m0"
        mixer0(False)
        if kind == "warm" and "h0" in dbg_d:
            dump("h0", hT)
        P.label = f"b{bi}.f0"
        ffn(0, False)
        P.label = f"b{bi}.cf"
        conformer()
        P.label = f"b{bi}.f1"
        ffn(1, True, skip_down=(kind == "warm"))
        P.label = f"b{bi}.out"
        if kind == "warm":
            f1 = flg.t[:, 0:1]
            A("dve", lambda e: e.tensor_scalar(out=st_f.t[:], in0=st_f.t[:], scalar1=f1, scalar2=None, op0=ALU.mult), R=[st_f.r(), flg.r()], W=[st_f.r()])
            A("dve", lambda e: e.tensor_scalar(out=st_b.t[:], in0=st_b.t[:], scalar1=f1, scalar2=None, op0=ALU.mult), R=[st_b.r(), flg.r()], W=[st_b.r()])
            A("dve", lambda e: e.tensor_scalar(out=TL.t[:], in0=TL.t[:], scalar1=f1, scalar2=None, op0=ALU.mult), R=[TL.r(), flg.r()], W=[TL.r()])
            A("dve", lambda e: e.tensor_scalar(out=Vx.t[:, 0, :, :].rearrange("p a b -> p (a b)"), in0=Vx.t[:, 0, :, :].rearrange("p a b -> p (a b)"),
                                               scalar1=f1, scalar2=None, op0=ALU.mult), R=[Vx.r(), flg.r()], W=[Vx.r()])
        else:
            stage_out(nout)
            nout += 512
    fin = [f"dmachain_{xo.name}" for xo in xout] + [f"dmachain_dbg_{n}" for n in dbg_d]
    P.add("sp", None, fin, ())
    if not do_emit:
        P.sim_ns = P.schedule()
        return nc, P, allocs
    nsem, nops = P.emit()
    print(f"[build] sbuf_peak={sbuf_peak} sems={nsem} ops={nops} loads={len(stream)} sim_us={P.sim_ns / 1e3:.0f}", flush=True)
    return nc, P, allocs


def plan_psum(allocs):
    last_r = [-1] * 8
    last_t = [0.0] * 8
    plan = []
    for n, _b, ops in allocs:
        if not ops:
            plan.append(0)
            continue
        r0 = min(o.ridx for o in ops)
        r1 = max(o.ridx for o in ops)
        t0 = min(o.t0 for o in ops)
        t1 = max(o.t1 for o in ops)
        best = None
        for b in (range(8) if n == 1 else range(0, 8, 2)):
            bs = range(b, b + n)
            if any(last_r[i] >= r0 for i in bs):
                continue
            te = max(last_t[i] for i in bs)
            key = (max(te - t0, 0.0), -te)
            if best is None or key < best[0]:
                best = (key, b)
        assert best is not None, "PSUM over-subscribed in program order"
        b = best[1]
        for i in range(b, b + n):
            last_r[i] = r1
            last_t[i] = t1
        plan.append(b)
    return plan


def build_program(NPRE, NMAIN, dbg=(), iters=PLAN_ITERS):
    plan = None
    best = None
    for it in range(iters):
        _nc, P1, allocs = record_program(NPRE, NMAIN, dbg, plan=plan, do_emit=False)
        if best is None or P1.sim_ns < best[0]:
            best = (P1.sim_ns, plan)
        print(f"[build] plan iter {it}: sim_us={P1.sim_ns / 1e3:.0f}", flush=True)
        plan = plan_psum(allocs)
    nc, _P, _a = record_program(NPRE, NMAIN, dbg, plan=best[1], do_emit=True)
    return nc


def _fm(v, nt):
    return np.ascontiguousarray(np.asarray(v, np.float32).reshape(nt, 128).T)


def _t5_bucket(dist):
    max_exact = 16
    d_f = np.maximum(dist, 1).astype(np.float32)
    large = max_exact + (np.log(d_f / max_exact) / math.log(128 / max_exact) * (32 - max_exact)).astype(np.int32)
    large = np.minimum(large, 31)
    return np.where(dist < max_exact, dist, large)


def host_pack(inp):
    f32 = np.float32
    prm = np.zeros((128, NPRM), f32)

    def put(name, arr):
        arr = np.asarray(arr, f32)
        prm[:, _p[name]:_p[name] + arr.shape[1]] = arr
    put("mixw", np.concatenate([_fm(inp["mix_norm_w"][l], 8) for l in range(2)], 1))
    put("ffnw", np.concatenate([_fm(inp["ffn_norm_w"][l], 8) for l in range(2)], 1))
    cw = np.asarray(inp["ssm_conv_w"][0], f32)
    put("cw", cw.T.reshape(12, 128, 4).transpose(1, 0, 2).reshape(128, 48))
    put("cb", _fm(inp["ssm_conv_b"][0], 12))
    put("dch", _fm(np.repeat(np.asarray(inp["ssm_d"][0], f32), 64), 8))
    put("snw", _fm(inp["ssm_norm_w"][0], 8))
    put("qw", np.tile(np.asarray(inp["attn_q_norm_w"][0], f32), 2)[:, None])
    put("kw", np.tile(np.asarray(inp["attn_k_norm_w"][0], f32), 2)[:, None])
    put("pw1b", _fm(inp["conv_pw1_b"][0], 16))
    dw = np.asarray(inp["conv_dw_w"][0], f32)
    put("dww", dw.T.reshape(8, 128, 31).transpose(1, 0, 2).reshape(128, 248))
    put("dwb", _fm(inp["conv_dw_b"][0], 8))
    put("lnw", _fm(inp["conv_ln_w"][0], 8))
    put("lnb", _fm(inp["conv_ln_b"][0], 8))
    put("pw2b", _fm(inp["conv_pw2_b"][0], 8))
    fcw = np.asarray(inp["ffn_conv_w"], f32)
    put("fcw", fcw.transpose(0, 2, 1).reshape(2, 44, 128, 3).transpose(2, 0, 1, 3).reshape(128, 264))
    put("fcb", np.asarray(inp["ffn_conv_b"], f32).reshape(2, 44, 128).transpose(2, 0, 1).reshape(128, 88))
    put("dtb", np.broadcast_to(np.asarray(inp["ssm_dt_bias"][0], f32)[None, :], (128, 16)))
    put("alog", np.broadcast_to(np.asarray(inp["ssm_a_log"][0], f32)[None, :], (128, 16)))
    put("sink", np.broadcast_to(np.asarray(inp["attn_sinks"][0], f32)[None, :], (128, 16)))
    rb = np.asarray(inp["rel_bias"], f32)
    qi = np.arange(128)[:, None]
    sj = np.arange(256)[None, :]
    dist = qi + 128 - sj
    valid = (dist >= 0) & (dist < 128)
    bias = rb[_t5_bucket(np.maximum(dist, 0))]
    bias = np.where(valid[:, :, None], bias, f32(NEG)).astype(f32)
    biasT = bias.reshape(128, 2, 128, 16).transpose(2, 3, 1, 0)
    biasT = np.ascontiguousarray(biasT).reshape(128, 16 * 2 * 128)
    cst = np.zeros((128, 512), f32)
    cst[:, 0:128] = np.eye(128, dtype=f32)
    cst[:, 128:256] = np.triu(np.ones((128, 128), f32))
    cst[:, 256:384] = np.kron(np.eye(2, dtype=f32), np.ones((64, 64), f32))
    cst[:, 384:512] = np.where(np.arange(128)[None, :] >= np.arange(128)[:, None], 0.0, NEG)
    eall = np.zeros((48, 16, 128), f32)
    for h in range(16):
        eall[h, h, :] = 1.0
        eall[32 + h, h, :] = 1.0
    return prm, biasT, cst, eall.reshape(48, 2048)


_NC_CACHE = {}


def kernel(**inp):
    x = np.asarray(inp["x"], np.float32)
    B, L, _ = x.shape
    NPRE, NMAIN = 7, 9
    prm, biasT, cst, eall = host_pack(inp)
    common = {
        "prm": prm, "biasT": biasT, "cst": cst, "eall": eall,
        "w_in": np.ascontiguousarray(inp["hyb_w_in"][0], np.float32),
        "w_out": np.ascontiguousarray(inp["hyb_w_out"][0], np.float32),
        "pw1": np.ascontiguousarray(inp["conv_pw1_w"][0], np.float32),
        "pw2": np.ascontiguousarray(inp["conv_pw2_w"][0], np.float32),
        "wup0": np.ascontiguousarray(inp["ffn_w_up"][0], np.float32),
        "wup1": np.ascontiguousarray(inp["ffn_w_up"][1], np.float32),
        "wdn0": np.ascontiguousarray(inp["ffn_w_down"][0], np.float32),
        "wdn1": np.ascontiguousarray(inp["ffn_w_down"][1], np.float32),
    }
    in_maps = []
    for core in range(8):
        b, half = core // 2, core % 2
        if half == 1:
            xs = x[b]
            flag = np.ones((128, 1), np.float32)
        else:
            xs = np.concatenate([np.zeros((4096, D), np.float32), x[b, :4096]], 0)
            flag = np.zeros((128, 1), np.float32)
        m = dict(common)
        m["x"] = np.ascontiguousarray(xs)
        m["flag"] = flag
        in_maps.append(m)
    if "nc" not in _NC_CACHE:
        _NC_CACHE["nc"] = build_program(NPRE, NMAIN)
    res = run_bass_kernel_spmd(_NC_CACHE["nc"], in_maps, core_ids=list(range(8)))
    out = np.empty((B, L, D), np.float32)
    for core in range(8):
        b, half = core // 2, core % 2
        out[b, half * 4096:(half + 1) * 4096] = res.results[core]["y"]
    return out
```

```python
import contextlib
import math
import numpy as np
import concourse.bass as bass
import concourse.mybir as mybir
from concourse.bass_utils import run_bass_kernel_spmd

F32 = mybir.dt.float32
BF16 = mybir.dt.bfloat16
AF = mybir.ActivationFunctionType
ALU = mybir.AluOpType
AX = mybir.AxisListType

ENGS = ("pe", "act", "dve", "pool", "sp")
EPOCH = 8000
DMA_BPNS = 340.0
SCHED_W = 128
SCHED_EPS = 0.0
SSD_POOL = (0, 4)
ATT_POOL = (4, 4)
PLAN_ITERS = 4
NDV = 10
GR = 64
SB_LO = 16640
SB_HI = 229376


class Res:
    __slots__ = ("name", "last_writer", "readers", "dma_cnt")

    def __init__(self, name):
        self.name = name
        self.last_writer = None
        self.readers = []
        self.dma_cnt = 0


class Op:
    __slots__ = ("idx", "eng", "fn", "deps", "is_dma", "sync", "eidx", "signal", "waits", "snap", "dval", "cost", "occ",
                 "t0", "t1", "label", "ridx")


class Prog:
    def __init__(self, nc):
        self.nc = nc
        self.ops = []
        self.res = {}

    def R(self, name):
        r = self.res.get(name)
        if r is None:
            r = self.res[name] = Res(name)
        return r

    def add(self, eng, fn, reads=(), writes=(), dma=None, cost=500.0, occ=None):
        op = Op()
        op.ridx = len(self.ops)
        op.label = getattr(self, "label", "")
        op.cost = cost
        op.occ = cost if occ is None else occ
        op.idx = len(self.ops)
        op.eng = eng
        op.fn = fn
        op.is_dma = dma is not None
        op.sync = self.R("dmasem_" + dma) if dma is not None else None
        deps = set()
        rs = [self.R(r) for r in set(reads)]
        ws = [self.R(w) for w in set(writes)]
        if op.is_dma:
            ws.append(self.R("dmachain_" + dma))
        for r in rs:
            if r.last_writer is not None:
                deps.add(r.last_writer)
        for w in ws:
            if w.last_writer is not None:
                deps.add(w.last_writer)
            deps.update(w.readers)
        for w in ws:
            w.last_writer = op.idx
            w.readers = []
        for r in rs:
            r.readers.append(op.idx)
        deps.discard(op.idx)
        op.deps = deps
        self.ops.append(op)
        return op

    def schedule(self, W=None):
        ops = self.ops
        if W is None:
            W = SCHED_W
        from collections import deque
        pend = {e: deque(op for op in ops if op.eng == e) for e in ENGS}
        tfree = {e: 0.0 for e in ENGS}
        self._dma_free = 0.0
        rank = [0.0] * len(ops)
        for op in reversed(ops):
            r = rank[op.idx] + op.cost
            rank[op.idx] = r
            for d in op.deps:
                if rank[d] < r:
                    rank[d] = r
        for op in ops:
            op.t0 = None
        left = len(ops)
        EPS = SCHED_EPS
        while left:
            best = None
            for e in ENGS:
                q = pend[e]
                n = 0
                cands = []
                smin = None
                for op in q:
                    if n >= W:
                        break
                    n += 1
                    rdy = 0.0
                    ok = True
                    for d in op.deps:
                        dop = ops[d]
                        if dop.t0 is None:
                            ok = False
                            break
                        if dop.t1 > rdy:
                            rdy = dop.t1
                    if not ok:
                        continue
                    st = rdy if rdy > tfree[e] else tfree[e]
                    cands.append((st, op))
                    if smin is None or st < smin:
                        smin = st
                if smin is None:
                    continue
                pick = None
                for st, op in cands:
                    if st <= smin + EPS:
                        k2 = (-rank[op.idx], op.idx)
                        if pick is None or k2 < pick[0]:
                            pick = (k2, st, op)
                key = (pick[1], pick[2].idx)
                if best is None or key < best[0]:
                    best = (key, pick[2])
            key, op = best
            op.t0 = key[0]
            if op.is_dma:
                xfer = max(op.cost - op.occ - 2000.0, 0.0)
                st = max(op.t0 + op.occ, self._dma_free)
                self._dma_free = st + xfer
                op.t1 = st + xfer + 2000.0
            else:
                op.t1 = op.t0 + op.cost
            tfree[op.eng] = op.t0 + op.occ
            pend[op.eng].remove(op)
            left -= 1
        new = sorted(ops, key=lambda o: (o.t0, o.idx))
        remap = {o.idx: i for i, o in enumerate(new)}
        for o in new:
            o.deps = {remap[d] for d in o.deps}
        for i, o in enumerate(new):
            o.idx = i
        self.ops = new
        return max(o.t1 for o in new)

    def emit(self, sched=True):
        nc = self.nc
        self.sim_ns = self.schedule() if sched else 0.0
        ops = self.ops
        cnt = {e: 0 for e in ENGS}
        for op in ops:
            op.eidx = cnt[op.eng]
            cnt[op.eng] += 1
            op.signal = False
            op.waits = []
        known = {e: {f: -1 for f in ENGS} for e in ENGS}
        kdma = {e: set() for e in ENGS}
        for op in ops:
            kn = known[op.eng]
            kd = kdma[op.eng]
            for d in sorted(op.deps):
                dop = ops[d]
                if dop.is_dma:
                    if d in kd:
                        continue
                    kd.add(d)
                else:
                    if kn[dop.eng] >= dop.eidx:
                        continue
                    kn[dop.eng] = dop.eidx
                dop.signal = True
                op.waits.append(d)
                sk, sd = dop.snap
                for f in ENGS:
                    if sk[f] > kn[f]:
                        kn[f] = sk[f]
                kd |= sd
            op.snap = (dict(kn), frozenset(kd))
        scount = {e: 0 for e in ENGS}
        keys = []
        for op in ops:
            if op.is_dma:
                op.sync.dma_cnt += 16
                op.dval = (op.sync.name, op.sync.dma_cnt)
            elif op.signal:
                c = scount[op.eng]
                scount[op.eng] += 1
                op.dval = (f"e_{op.eng}_{c // EPOCH}", c % EPOCH + 1)
            else:
                op.dval = None
            if op.dval is not None and op.dval[0] not in keys:
                keys.append(op.dval[0])
        with contextlib.ExitStack() as st:
            semh = {k: st.enter_context(nc.semaphore(k)) for k in keys}
            block = st.enter_context(nc.Block())
            per = {e: [op for op in ops if op.eng == e] for e in ENGS}

            def run(engobj, lst):
                for op in lst:
                    for d in op.waits:
                        k, v = ops[d].dval
                        engobj.wait_ge(semh[k], v)
                    if op.fn is None:
                        continue
                    ins = op.fn(engobj)
                    if op.is_dma:
                        ins.then_inc(semh[op.dval[0]], 16)
                    elif op.signal:
                        ins.then_inc(semh[op.dval[0]], 1)

            block.tensor(lambda e: run(e, per["pe"]))
            block.scalar(lambda e: run(e, per["act"]))
            block.vector(lambda e: run(e, per["dve"]))
            block.gpsimd(lambda e: run(e, per["pool"]))
            block.sync(lambda e: run(e, per["sp"]))
        return len(keys), {e: len(per[e]) for e in ENGS}


class T:
    def __init__(self, nc, name, shape, dtype, off):
        self.esz = 2 if dtype == BF16 else 4
        self.n = int(np.prod(shape[1:]))
        self.off = off
        self.t = nc.alloc_sbuf_tensor_at(name, list(shape), dtype, offset=off)
        self.name = name

    def r(self, lo=0, hi=None):
        if hi is None:
            hi = self.n
        b0 = (self.off + lo * self.esz) // GR
        b1 = (self.off + hi * self.esz - 1) // GR
        return [f"sb{g}" for g in range(b0, b1 + 1)]


class Arena:
    def __init__(self, nc):
        self.nc = nc
        self.cur = SB_LO
        self.peak = SB_LO
        self.k = 0

    def alloc(self, name, shape, dtype):
        esz = 2 if dtype == BF16 else 4
        nbytes = int(np.prod(shape[1:])) * esz
        off = (self.cur + 63) // 64 * 64
        self.cur = off + nbytes
        self.peak = max(self.peak, self.cur)
        assert self.cur <= SB_HI, (name, self.cur)
        self.k += 1
        return T(self.nc, f"{name}_{self.k}", shape, dtype, off)


D = 1024
IN_TOTAL = 3856
C_Z, C_X, C_B, C_C, C_DT, C_Q, C_K, C_V = 0, 1024, 2048, 2304, 2560, 2576, 3600, 3728
DFF = 2816
NEG = -30000.0

_p = {}
_o = 0
for _n, _w in [("mixw", 16), ("ffnw", 16), ("cw", 48), ("cb", 12), ("dch", 8), ("snw", 8), ("qw", 1), ("kw", 1),
               ("pw1b", 16), ("dww", 248), ("dwb", 8), ("lnw", 8), ("lnb", 8), ("pw2b", 8),
               ("fcw", 264), ("fcb", 88), ("dtb", 16), ("alog", 16), ("sink", 16)]:
    _p[_n] = _o
    _o += _w
NPRM = _o
TL_A, TL_F, TL_C = 0, 36, 36 + 176
NTL = 36 + 176 + 240


def record_program(NPRE, NMAIN, dbg=(), plan=None, do_emit=True):
    nc = bass.Bass("TRN2", target_bir_lowering=False)
    NT = 512 * (NPRE + NMAIN)
    NOUT = 512 * (NMAIN - 1)
    dt_ = nc.dram_tensor
    x_d = dt_("x", [NT, D], F32, kind="ExternalInput").ap()
    flag_d = dt_("flag", [128, 1], F32, kind="ExternalInput").ap()
    prm_d = dt_("prm", [128, NPRM], F32, kind="ExternalInput").ap()
    bias_d = dt_("biasT", [128, 16 * 2 * 128], F32, kind="ExternalInput").ap()
    cst_d = dt_("cst", [128, 512], F32, kind="ExternalInput").ap()
    eall_d = dt_("eall", [48, 2048], F32, kind="ExternalInput").ap()
    win_d = dt_("w_in", [D, IN_TOTAL], F32, kind="ExternalInput").ap()
    wout_d = dt_("w_out", [2048, D], F32, kind="ExternalInput").ap()
    pw1_d = dt_("pw1", [D, 2048], F32, kind="ExternalInput").ap()
    pw2_d = dt_("pw2", [D, D], F32, kind="ExternalInput").ap()
    wup_d = [dt_(f"wup{l}", [D, 2 * DFF], F32, kind="ExternalInput").ap() for l in range(2)]
    wdn_d = [dt_(f"wdn{l}", [DFF, D], F32, kind="ExternalInput").ap() for l in range(2)]
    y_d = dt_("y", [NOUT, D], F32, kind="ExternalOutput").ap()
    diag_d = dt_("diag_scr", [8, 128, 31 * 128], BF16).ap()
    dbg_d = {n: dt_("dbg_" + n, [128, 4096], F32, kind="ExternalOutput").ap() for n in dbg}

    P = Prog(nc)
    ar = Arena(nc)
    PSt = nc.alloc_psum_tensor("PS", [128, 8, 512], F32)
    PS = PSt
    pspool = {"lo": 0, "n": 8}
    pscnt = {}
    allocs = []
    cur_alloc = {}

    def psa(n=1):
        k = len(allocs)
        if plan is not None:
            b = plan[k]
        else:
            key = (0, 8)
            c = pscnt.get(key, 0)
            if n == 2 and c % 2 == 1:
                c += 1
            b = c % 8
            pscnt[key] = c + n
        allocs.append([n, b, []])
        for i in range(n):
            cur_alloc[b + i] = k
        return b

    def pr(b, n=1):
        return [f"pb{b + i}" for i in range(n)]

    def A(eng, fn, R=(), W=(), dma=None, c=512, nbytes=1 << 19, recip=False):
        rl = [x for l in R for x in l]
        wl = [x for l in W for x in l]
        occ = None
        if dma is not None:
            occ = 1200.0 if eng == "pool" else 150.0
            cost = occ + 2000.0 + nbytes / DMA_BPNS
        elif eng == "pe":
            cost = getattr(fn, "cost", 300.0)
        elif eng == "act":
            cost = 220.0 + 0.95 * c
        else:
            cost = 80.0 + (3.2 if recip else 1.3) * c
        if eng != "pe":
            wl = wl + ["px" + nm[2:] for nm in set(rl + wl) if nm.startswith("pb")]
        op = P.add(eng, fn, rl, wl, dma, cost, occ)
        for nm in rl + wl:
            if nm.startswith("pb"):
                lst = allocs[cur_alloc[int(nm[2:])]][2]
                if not lst or lst[-1] is not op:
                    lst.append(op)
        return op

    hT = ar.alloc("hT", [128, 8, 512], F32)
    NSLOT = 4
    WS = [ar.alloc(f"ws{i}", [128, 4096], BF16) for i in range(NSLOT)]
    WK = ar.alloc("wk", [128, 8, 4, 128], BF16)
    WDV = ar.alloc("wdv", [128, 8, 144], BF16)
    biasT = ar.alloc("biasT", [128, 16, 2, 128], BF16)
    Eall = ar.alloc("eall", [48, 16, 128], BF16)
    prm = ar.alloc("prm", [128, NPRM], F32)
    cst = ar.alloc("cst", [128, 512], F32)
    cbf = ar.alloc("cbf", [128, 512], BF16)
    ones_f = ar.alloc("ones_f", [128, 128], F32)
    ones_b = ar.alloc("ones_b", [128, 128], BF16)
    diagD = ar.alloc("diagD", [128, 8, 128], BF16)
    sm = ar.alloc("sm", [128, 128], F32)
    st_f = ar.alloc("st_f", [128, 1024], F32)
    st_b = ar.alloc("st_b", [128, 1024], BF16)
    TL = ar.alloc("TL", [128, NTL], F32)
    Kp = ar.alloc("Kp", [128, 4, 640], BF16)
    Vx = ar.alloc("Vx", [128, 5, 2, 65], BF16)
    flg = ar.alloc("flg", [128, 1], F32)
    xin = [ar.alloc(f"xin{i}", [128, 1024], F32) for i in range(2)]
    ident_f = cst.t[:, 0:128]
    U_f = cst.t[:, 128:256]
    ident_b = cbf.t[:, 0:128]
    bd_b = cbf.t[:, 256:384]
    negm_b = cbf.t[:, 384:512]
    SM_A, SM_ES, SM_E6, SM_E5, SM_QW, SM_KW = 0, 16, 32, 33, 34, 35

    def pc(name, j=0, n=1):
        return prm.t[:, _p[name] + j:_p[name] + j + n]

    base_mark = ar.cur

    xsT = ar.alloc("xsT", [128, 8, 512], BF16)
    BCT = ar.alloc("BCT", [128, 4, 512], BF16)
    qnT = ar.alloc("qnT", [128, 8, 512], BF16)
    sz = ar.alloc("sz", [128, 4, 1024], F32)
    mixT = ar.alloc("mixT", [128, 16, 512], BF16)
    dtr = ar.alloc("dtr", [128, 4, 16], F32)
    dtv = ar.alloc("dtv", [128, 4, 16], F32)
    lndt = ar.alloc("lndt", [128, 4, 16], F32)
    m0_mark = ar.cur
    ar.cur = qnT.off
    hnB = ar.alloc("hnB", [128, 8, 512], BF16)
    ar.cur = mixT.off
    xsTB = ar.alloc("xsTB", [128, 8, 512], BF16)
    BCTB = ar.alloc("BCTB", [128, 4, 512], BF16)
    dtrB = ar.alloc("dtrB", [128, 4, 16], F32)
    dtvB = ar.alloc("dtvB", [128, 4, 16], F32)
    lndtB = ar.alloc("lndtB", [128, 4, 16], F32)
    xtokB = ar.alloc("xtokB", [128, 1024], BF16)
    btokB = ar.alloc("btokB", [128, 256], BF16)
    assert ar.cur <= mixT.off + 16 * 512 * 2
    ar.cur = sz.off
    hTB = ar.alloc("hTB", [128, 8, 512], F32)
    ar.cur = m0_mark
    hn = ar.alloc("hn", [128, 8, 512], BF16)
    sq = ar.alloc("sq", [128, 8, 512], BF16)
    xp = [ar.alloc(f"xp{i}", [128, 515], F32) for i in range(2)]
    acc = [ar.alloc(f"acc{i}", [128, 512], F32) for i in range(2)]
    sqq = [ar.alloc(f"sqq{i}", [128, 512], BF16) for i in range(2)]
    rs_s = ar.alloc("rs_s", [128, 512], F32)
    rs_r = ar.alloc("rs_r", [128, 512], F32)
    rs_s2 = ar.alloc("rs_s2", [128, 512], F32)
    rs_r2 = ar.alloc("rs_r2", [128, 512], F32)
    a13_end = ar.cur
    SET_A = (hn, sq, xsT, BCT, dtr, dtv, lndt, hT)
    SET_B = (hnB, sq, xsTB, BCTB, dtrB, dtvB, lndtB, hTB)
    ar.cur = m0_mark
    xtok = ar.alloc("xtok", [128, 1024], BF16)
    btok = ar.alloc("btok", [128, 256], BF16)
    xw = ar.alloc("xw", [128, 1024], BF16)
    dec = [ar.alloc(f"dec{i}", [128, 1024], F32) for i in range(2)]
    MT = ar.alloc("MT", [128, 16, 128], BF16)
    tmpf = ar.alloc("tmpf", [128, 1024], F32)
    yv = ar.alloc("yv", [128, 1024], F32)
    yn = ar.alloc("yn", [128, 1024], BF16)
    pT = [ar.alloc(f"pT{i}", [128, 512], BF16) for i in range(4)]
    attn = ar.alloc("attn", [128, 1024], BF16)
    s16s = [ar.alloc(f"s16_{i}", [128, 16, 16], F32) for i in range(4)]
    a16s = [ar.alloc(f"a16_{i}", [128, 8, 16], F32) for i in range(4)]
    acsTs = [ar.alloc(f"acsT{i}", [48, 128], BF16) for i in range(4)]
    qTs = [ar.alloc(f"qT{i}", [48, 128], BF16) for i in range(4)]
    a48s = [ar.alloc(f"a48_{i}", [128, 48], F32) for i in range(4)]
    q48s = [ar.alloc(f"q48_{i}", [128, 48], F32) for i in range(4)]
    ar.cur = max(ar.cur, a13_end)
    m0_end = ar.cur
    ar.cur = base_mark
    f_hn = ar.alloc("f_hn", [128, 8, 512], BF16)
    f_sq = ar.alloc("f_sq", [128, 8, 512], BF16)
    f_rs = ar.alloc("f_rs", [128, 512], F32)
    f_rr = ar.alloc("f_rr", [128, 512], F32)
    actT = ar.alloc("actT", [128, 22, 512], BF16)
    f_xp = [ar.alloc(f"f_xp{i}", [128, 514], F32) for i in range(8)]
    f_acc = [ar.alloc(f"f_acc{i}", [128, 512], F32) for i in range(8)]
    f_sg = [ar.alloc(f"f_sg{i}", [128, 512], F32) for i in range(4)]
    f_end = ar.cur
    ar.cur = base_mark
    c_hn = ar.alloc("c_hn", [128, 8, 512], BF16)
    c_sq = ar.alloc("c_sq", [128, 8, 512], BF16)
    c_rs = ar.alloc("c_rs", [128, 512], F32)
    c_rr = ar.alloc("c_rr", [128, 512], F32)
    upad = ar.alloc("upad", [128, 8, 544], BF16)
    yc = ar.alloc("yc", [128, 8, 512], F32)
    ybf = ar.alloc("ybf", [128, 8, 512], BF16)
    c_sig = [ar.alloc(f"c_sig{i}", [128, 512], F32) for i in range(2)]
    c_mean = ar.alloc("c_mean", [128, 512], F32)
    c_msq = ar.alloc("c_msq", [128, 512], F32)
    c_var = ar.alloc("c_var", [128, 512], F32)
    c_end = ar.cur
    ar.cur = base_mark
    dgt = ar.alloc("dgt", [128, 31 * 128], BF16)
    ar.cur = base_mark
    xout = [ar.alloc(f"xout{i}", [128, 1024], F32) for i in range(2)]
    ar.cur = ar.peak
    xtokA = ar.alloc("xtokA", [128, 1024], BF16)
    btokA = ar.alloc("btokA", [128, 256], BF16)
    xwA = ar.alloc("xwA", [128, 1024], BF16)
    xwB = ar.alloc("xwB", [128, 1024], BF16)
    PRE_A = (xtokA, btokA, xwA)
    PRE_B = (xtokB, btokB, xwB)
    sbuf_peak = ar.peak

    A("sp", lambda e: e.dma_start(out=prm.t[:], in_=prm_d), W=[prm.r()], dma="prm")
    A("sp", lambda e: e.dma_start(out=cst.t[:], in_=cst_d), W=[cst.r()], dma="cst")
    A("pool", lambda e: e.dma_start(out=biasT.t[:].rearrange("p a b c -> p (a b c)"), in_=bias_d), W=[biasT.r()], dma="biasT")
    A("pool", lambda e: e.dma_start(out=Eall.t[:].rearrange("p a b -> p (a b)"), in_=eall_d), W=[Eall.r()], dma="eall")
    A("sp", lambda e: e.dma_start(out=flg.t[:], in_=flag_d), W=[flg.r()], dma="flg")
    A("dve", lambda e: e.memset(WK.t[:].rearrange("p a b c -> p (a b c)"), 0.0), W=[WK.r()])
    for kv in range(2):
        for pad in range(2):
            def f(e, kv=kv, pad=pad):
                return e.dma_start(out=WK.t[:, :, 2 * kv + pad, 64 * pad:64 * pad + 64],
                                   in_=win_d[:, C_K + 64 * kv:C_K + 64 * kv + 64].rearrange("(k p) c -> p k c", p=128))
            A("pool", f, W=[WK.r()], dma="wk")
    A("pool", lambda e: e.dma_start(out=WDV.t[:, :, 0:16], in_=win_d[:, C_DT:C_DT + 16].rearrange("(k p) c -> p k c", p=128)),
      W=[WDV.r()], dma="wdv")
    A("pool", lambda e: e.dma_start(out=WDV.t[:, :, 16:144], in_=win_d[:, C_V:C_V + 128].rearrange("(k p) c -> p k c", p=128)),
      W=[WDV.r()], dma="wdv")
    A("dve", lambda e: e.tensor_copy(out=cbf.t[:], in_=cst.t[:]), R=[cst.r()], W=[cbf.r()])
    A("dve", lambda e: e.memset(ones_f.t[:], 1.0), W=[ones_f.r()])
    A("dve", lambda e: e.memset(ones_b.t[:], 1.0), W=[ones_b.r()])
    A("dve", lambda e: e.memset(sm.t[:], 0.0), W=[sm.r()])
    A("dve", lambda e: e.memset(sm.t[:, SM_E6:SM_E6 + 1], 1e-6), W=[sm.r()])
    A("dve", lambda e: e.memset(sm.t[:, SM_E5:SM_E5 + 1], 1e-5), W=[sm.r()])
    A("act", lambda e: e.activation(out=sm.t[:, SM_A:SM_A + 16], in_=pc("alog", 0, 16), func=AF.Exp), R=[prm.r(), sm.r()], W=[sm.r()])
    A("dve", lambda e: e.tensor_scalar(out=sm.t[:, SM_A:SM_A + 16], in0=sm.t[:, SM_A:SM_A + 16], scalar1=-1.0, scalar2=None, op0=ALU.mult),
      R=[sm.r()], W=[sm.r()])
    A("act", lambda e: e.activation(out=sm.t[:, SM_ES:SM_ES + 16], in_=pc("sink", 0, 16), func=AF.Exp), R=[prm.r(), sm.r()], W=[sm.r()])
    A("dve", lambda e: e.tensor_scalar(out=sm.t[:, SM_QW:SM_QW + 1], in0=pc("qw"), scalar1=0.125, scalar2=None, op0=ALU.mult),
      R=[prm.r(), sm.r()], W=[sm.r()])
    A("dve", lambda e: e.tensor_copy(out=sm.t[:, SM_KW:SM_KW + 1], in_=pc("kw")), R=[prm.r(), sm.r()], W=[sm.r()])
    for j in range(8):
        A("dve", lambda e, j=j: e.tensor_scalar(out=diagD.t[:, j, :], in0=ident_f, scalar1=pc("dch", j), scalar2=None, op0=ALU.mult),
          R=[cst.r(), prm.r()], W=[diagD.r()])
    for i in range(4):
        A("dve", lambda e, i=i: e.memset(a48s[i].t[:], 0.0), W=[a48s[i].r()], c=48)
        A("dve", lambda e, i=i: e.memset(q48s[i].t[:], 0.0), W=[q48s[i].r()], c=48)
    A("dve", lambda e: e.memset(st_f.t[:], 0.0), W=[st_f.r()])
    A("dve", lambda e: e.memset(st_b.t[:], 0.0), W=[st_b.r()])
    A("dve", lambda e: e.memset(TL.t[:], 0.0), W=[TL.r()])
    A("dve", lambda e: e.memset(Kp.t[:].rearrange("p a b -> p (a b)"), 0.0), W=[Kp.r()])
    A("dve", lambda e: e.memset(Vx.t[:].rearrange("p a b c -> p (a b c)"), 0.0), W=[Vx.r()])
    A("dve", lambda e: e.memset(Vx.t[:, :, :, 64:65], 1.0), W=[Vx.r()])

    for j in range(8):
        A("dve", lambda e, j=j: e.tensor_tensor(out=dgt.t[:].rearrange("p (k c) -> p k c", c=128),
                                                in0=ident_f.unsqueeze(1).to_broadcast([128, 31, 128]),
                                                in1=pc("dww", j * 31, 31).unsqueeze(2).to_broadcast([128, 31, 128]), op=ALU.mult),
          R=[cst.r(), prm.r()], W=[dgt.r()], c=3968)
        A("sp", lambda e, j=j: e.dma_start(out=diag_d[j], in_=dgt.t[:]), R=[dgt.r()], W=[["dram_diag"]], dma="dgt")

    def wsrc(w, k0, nk, c0, ncol):
        return w[k0 * 128:(k0 + nk) * 128, c0:c0 + ncol].rearrange("(k p) c -> p k c", p=128)

    def loads_mixer0(pre):
        L = []
        if not pre:
            L += [("z0", win_d, 0, 8, C_Z, 512), ("z1", win_d, 0, 8, C_Z + 512, 512)]
        L += [("x0", win_d, 0, 8, C_X, 512), ("x1", win_d, 0, 8, C_X + 512, 512)]
        if pre:
            L += [("bc", win_d, 0, 8, C_B, 256)]
        else:
            L += [("bc", win_d, 0, 8, C_B, 512), ("q0", win_d, 0, 8, C_Q, 512), ("q1", win_d, 0, 8, C_Q + 512, 512)]
            L += [(f"o{i}", wout_d, 0, 16, 256 * i, 256) for i in range(4)]
        return L

    def loads_ffn(l):
        L = []
        for g in range(6):
            nc_ = 512 if g < 5 else 256
            L += [(f"g{g}", wup_d[l], 0, 8, 512 * g, nc_), (f"u{g}", wup_d[l], 0, 8, DFF + 512 * g, nc_)]
        L += [(f"d{f}", wdn_d[l], 0, 22, 128 * f, 128) for f in range(8)]
        return L

    def loads_conf():
        L = []
        for h in range(2):
            L += [(f"a{h}", pw1_d, 0, 8, 512 * h, 512), (f"s{h}", pw1_d, 0, 8, 1024 + 512 * h, 512)]
            L += [(f"dg{j}", None, j, 31, 0, 128) for j in range(4 * h, 4 * h + 4)]
        L += [(f"p{h}", pw2_d, 0, 8, 512 * h, 512) for h in range(2)]
        return L

    stream = []
    for _ in range(NPRE + 1):
        stream += loads_mixer0(True)
    for _ in range(NMAIN):
        stream += loads_mixer0(False) + loads_ffn(0) + loads_conf() + loads_ffn(1)
    wstate = {"issued": 0, "next": 0}
    released = set()
    auto_pending = []

    def issue_loads(upto):
        while wstate["issued"] < min(upto, len(stream)):
            i = wstate["issued"]
            if i >= NSLOT and (i - NSLOT) not in released:
                break
            _, w, k0, nk, c0, ncol = stream[i]
            slot = WS[i % NSLOT]

            def f(e, slot=slot, w=w, k0=k0, nk=nk, c0=c0, ncol=ncol):
                if w is None:
                    return e.dma_start(out=slot.t[:, 0:nk * ncol], in_=diag_d[k0])
                return e.dma_start(out=slot.t[:, 0:nk * ncol].rearrange("p (k c) -> p k c", c=ncol), in_=wsrc(w, k0, nk, c0, ncol))
            A("pool", f, R=[["dram_diag"]] if w is None else [], W=[slot.r()], dma=f"ws{i % NSLOT}", nbytes=nk * ncol * 128 * (2 if w is None else 4))
            wstate["issued"] += 1

    def wrel(*idxs):
        released.update(idxs)
        issue_loads(wstate["next"] + NSLOT)

    def wnext(tag, hold=False):
        i = wstate["next"]
        assert stream[i][0] == tag, (stream[i][0], tag)
        released.update(auto_pending)
        del auto_pending[:]
        issue_loads(i + NSLOT)
        assert wstate["issued"] > i, ("weight ring deadlock", tag)
        wstate["next"] += 1
        if not hold:
            auto_pending.append(i)
        _, w, k0, nk, c0, ncol = stream[i]
        slot = WS[i % NSLOT]
        return slot.t[:, 0:nk * ncol].rearrange("p (k c) -> p k c", c=ncol), slot.r(), i

    def mm_group(specs):
        def f(e):
            ins = None
            for (o, l, r, s0, s1) in specs:
                ins = e.matmul(o, lhsT=l, rhs=r, start=s0, stop=s1)
            return ins
        cost = 0.0
        for (o, l, r, s0, s1) in specs:
            n = int(np.prod(o.shape[1:]))
            cost += (max(n, 96) / 1.9) * (4.0 if l.dtype == F32 else 1.0) + 12.0
        f.cost = cost
        return f

    cfg = {"nt": 512}

    def tr_group(specs):
        def f(e):
            ins = None
            for (o, i_) in specs:
                ins = e.transpose(o, i_, ident_f)
            return ins
        f.cost = len(specs) * 110.0
        return f

    def rmsnorm(wname, wl, hn_, sq_, rs_, rr_, hs=None, bank=None):
        hs = hT if hs is None else hs
        nt = cfg["nt"]
        if bank is None:
            A("act", lambda e: e.activation(out=sq_.t[:, :, 0:nt], in_=hs.t[:, :, 0:nt], func=AF.Square), R=[hs.r()], W=[sq_.r()], c=8 * nt)
            b = psa()
            A("pe", mm_group([(PS[:, b, 0:nt], ones_b.t[:], sq_.t[:, kt, 0:nt], kt == 0, kt == 7) for kt in range(8)]),
              R=[sq_.r(), ones_b.r()], W=[pr(b)])
        else:
            b = bank
        A("act", lambda e: e.activation(out=rs_.t[:, 0:nt], in_=PS[:, b, 0:nt], func=AF.Ln, bias=sm.t[:, SM_E6:SM_E6 + 1], scale=1.0 / 1024),
          R=[pr(b), sm.r()], W=[rs_.r()], c=nt)
        A("act", lambda e: e.activation(out=rr_.t[:, 0:nt], in_=rs_.t[:, 0:nt], func=AF.Exp, scale=-0.5), R=[rs_.r()], W=[rr_.r()], c=nt)
        for kt in range(8):
            A("dve", lambda e, kt=kt: e.scalar_tensor_tensor(out=hn_.t[:, kt, 0:nt], in0=hs.t[:, kt, 0:nt], scalar=pc(wname, wl * 8 + kt),
                                                            in1=rr_.t[:, 0:nt], op0=ALU.mult, op1=ALU.mult),
              R=[hs.r(kt * 512, kt * 512 + 512), rr_.r(), prm.r()], W=[hn_.r(kt * 512, kt * 512 + 512)], c=nt)

    nrm = {"bank": None}

    def ssq_hook(f, sq_next):
        nt = cfg["nt"]
        A("act", lambda e: e.activation(out=sq_next.t[:, f, 0:nt], in_=hT.t[:, f, 0:nt], func=AF.Square),
          R=[hT.r(f * 512, f * 512 + 512)], W=[sq_next.r(f * 512, f * 512 + 512)], c=nt)
        if f == 0:
            nrm["bank"] = psa()
        b = nrm["bank"]
        A("pe", mm_group([(PS[:, b, 0:nt], ones_b.t[:], sq_next.t[:, f, 0:nt], f == 0, f == 7)]),
          R=[sq_next.r(f * 512, f * 512 + 512), ones_b.r()], W=[pr(b)])

    def proj_fm(wv, wres, jloc, hn_):
        nt = cfg["nt"]
        b = psa()
        A("pe", mm_group([(PS[:, b, 0:nt], wv[:, kt, jloc * 128:(jloc + 1) * 128], hn_.t[:, kt, 0:nt], kt == 0, kt == 7) for kt in range(8)]),
          R=[wres, hn_.r()], W=[pr(b)])
        return b

    def conv_fm(b, xp_, acc_, K, wcol, bcol, tl_off, tail_engs=("dve", "act")):
        nt = cfg["nt"]
        def cp(eng, o, i_):
            return (lambda e: e.activation(out=o, in_=i_, func=AF.Copy)) if eng == "act" else (lambda e: e.tensor_copy(out=o, in_=i_))
        A(tail_engs[0], cp(tail_engs[0], xp_.t[:, 0:K - 1], TL.t[:, tl_off:tl_off + K - 1]),
          R=[TL.r(tl_off, tl_off + K - 1)], W=[xp_.r(0, K - 1)], c=4)
        A("act", lambda e: e.activation(out=xp_.t[:, K - 1:K - 1 + nt], in_=PS[:, b, 0:nt], func=AF.Copy), R=[pr(b)], W=[xp_.r(K - 1, K + 511)], c=nt)
        A("act", lambda e: e.activation(out=acc_.t[:, 0:nt], in_=PS[:, b, 0:nt], func=AF.Identity, scale=wcol(K - 1), bias=bcol),
          R=[pr(b), prm.r()], W=[acc_.r()], c=nt)
        for k in range(K - 2, -1, -1):
            A("dve", lambda e, k=k: e.scalar_tensor_tensor(out=acc_.t[:, 0:nt], in0=xp_.t[:, k:k + nt], scalar=wcol(k), in1=acc_.t[:, 0:nt],
                                                          op0=ALU.mult, op1=ALU.add), R=[xp_.r(), acc_.r(), prm.r()], W=[acc_.r()], c=nt)
        A(tail_engs[1], cp(tail_engs[1], TL.t[:, tl_off:tl_off + K - 1], xp_.t[:, nt:nt + K - 1]),
          R=[xp_.r()], W=[TL.r(tl_off, tl_off + K - 1)], c=4)

    def dump(name, tt):
        if name in dbg_d:
            A("sp", lambda e: e.dma_start(out=dbg_d[name][:, 0:tt.n], in_=tt.t[:].rearrange("p a b -> p (a b)")),
              R=[tt.r()], dma="dbg_" + name)

    def hchunk(hd, c):
        return [x for j in range(8) for x in hd.r(j * 512 + c * 128, j * 512 + c * 128 + 128)]

    def stage_in(row0, hd=None):
        hd = hT if hd is None else hd
        for c in range(cfg["nt"] // 128):
            xi = xin[c % 2]
            r0 = row0 + c * 128
            A("sp", lambda e, xi=xi, r0=r0: e.dma_start(out=xi.t[:], in_=x_d[r0:r0 + 128, :]), W=[xi.r()], dma=xi.name)
            b = psa(2)
            A("pe", tr_group([(PS[:, b + j // 4, (j % 4) * 128:(j % 4) * 128 + 128], xi.t[:, j * 128:(j + 1) * 128]) for j in range(8)]),
              R=[xi.r(), cst.r()], W=[pr(b, 2)])
            A("act", lambda e, b=b, c=c: e.activation(out=hd.t[:, :, c * 128:(c + 1) * 128],
                                                      in_=PS[:, b:b + 2, :].rearrange("p a (j t) -> p (a j) t", t=128), func=AF.Copy),
              R=[pr(b, 2)], W=[hchunk(hd, c)], c=1024)

    def stage_out(orow0):
        for c in range(cfg["nt"] // 128):
            xo = xout[c % 2]
            r0 = orow0 + c * 128
            b = psa(2)
            A("pe", tr_group([(PS[:, b + j // 4, (j % 4) * 128:(j % 4) * 128 + 128], hT.t[:, j, c * 128:(c + 1) * 128]) for j in range(8)]),
              R=[hchunk(hT, c), cst.r()], W=[pr(b, 2)])
            A("act", lambda e, b=b, xo=xo: e.activation(out=xo.t[:], in_=PS[:, b:b + 2, :].rearrange("p a t -> p (a t)"), func=AF.Copy),
              R=[pr(b, 2)], W=[xo.r()], c=1024)
            A("sp", lambda e, xo=xo, r0=r0: e.dma_start(out=y_d[r0:r0 + 128, :], in_=xo.t[:]), R=[xo.r()], dma=xo.name)

    def ssd_chunk(c, pre, bs):
        hn, sq, xsT, BCT, dtr, dtv, lndt, _h = bs
        xtok_, btok_, xw_ = (xtok, btok, xw) if not pre else (PRE_A if bs is SET_A else PRE_B)
        s16, acsT, qT, a48, q48 = s16s[c], acsTs[c], qTs[c], a48s[c], q48s[c]
        S = lambda i: s16.t[:, i, :]
        sr = s16.r()
        cs = slice(c * 128, (c + 1) * 128)
        dup = lambda ap: ap.unsqueeze(1).to_broadcast([128, 2, 16])
        v48 = lambda t_: t_.t[:].rearrange("p (r c) -> p r c", c=16)[:, 0:3:2, :]
        A("dve", lambda e: e.tensor_tensor(out=v48(a48), in0=dup(dtv.t[:, c, :]), in1=dup(sm.t[:, SM_A:SM_A + 16]), op=ALU.mult),
          R=[dtv.r(), sm.r()], W=[a48.r()], c=32)
        a_ = a48.t[:, 0:16]
        b = psa()
        A("pe", mm_group([(PS[:, b, 0:16], U_f, a_, True, True),
                          (PS[0:48, b, 16:144], a48.t[:], U_f, True, True),
                          (PS[:, b, 144:160], ones_f.t[:], a_, True, True)]), R=[a48.r(), cst.r(), ones_f.r()], W=[pr(b)])
        A("dve", lambda e: e.tensor_tensor(out=v48(q48), in0=dup(lndt.t[:, c, :]), in1=dup(PS[:, b, 0:16]), op=ALU.subtract),
          R=[lndt.r(), pr(b)], W=[q48.r()], c=32)
        A("dve", lambda e: e.tensor_tensor(out=S(3), in0=q48.t[:, 0:16], in1=PS[:, b, 144:160], op=ALU.add), R=[q48.r(), pr(b)], W=[sr], c=16)
        A("act", lambda e: e.activation(out=S(4), in_=S(3), func=AF.Exp), R=[sr], W=[sr], c=16)
        A("act", lambda e: e.activation(out=S(6), in_=PS[:, b, 144:160], func=AF.Exp), R=[pr(b)], W=[sr], c=16)
        if not pre:
            A("act", lambda e: e.activation(out=S(5), in_=PS[:, b, 0:16], func=AF.Exp), R=[pr(b)], W=[sr], c=16)
            A("act", lambda e: e.activation(out=acsT.t[:], in_=PS[0:48, b, 16:144], func=AF.Copy), R=[pr(b)], W=[acsT.r()], c=128)
            A("dve", lambda e: e.tensor_tensor(out=acsT.t[32:48, :], in0=PS[32:48, b, 16:144], in1=acsT.t[32:48, :], op=ALU.subtract),
              R=[pr(b), acsT.r()], W=[acsT.r()], c=128)
            b2 = psa()
            A("pe", mm_group([(PS[0:48, b2, 0:128], q48.t[:], ident_f, True, True)]), R=[q48.r(), cst.r()], W=[pr(b2)])
            A("act", lambda e: e.activation(out=qT.t[:], in_=PS[0:48, b2, 0:128], func=AF.Copy), R=[pr(b2)], W=[qT.r()], c=128)
            A("dve", lambda e: e.tensor_tensor(out=qT.t[32:48, :], in0=PS[32:48, b2, 0:128], in1=qT.t[32:48, :], op=ALU.subtract),
              R=[pr(b2), qT.r()], W=[qT.r()], c=128)
            bcb = psa()
            A("pe", mm_group([(PS[:, bcb, g * 128:(g + 1) * 128], BCT.t[:, g, cs], BCT.t[:, 2 + g, cs], True, True) for g in range(2)]),
              R=[BCT.r()], W=[pr(bcb)])
            for half in range(2):
                bb = psa(2)
                specs = []
                for hh in range(8):
                    h = half * 8 + hh
                    o = PS[:, bb + hh // 4, (hh % 4) * 128:(hh % 4) * 128 + 128]
                    specs += [(o, Eall.t[:, h, :], acsT.t[:], True, False), (o, qT.t[:], Eall.t[:, h, :], False, False),
                              (o, ident_b, negm_b, False, True)]
                A("pe", mm_group(specs), R=[Eall.r(), acsT.r(), qT.r(), cbf.r()], W=[pr(bb, 2)])
                dc = dec[half]
                A("act", lambda e, bb=bb, dc=dc: e.activation(out=dc.t[:], in_=PS[:, bb:bb + 2, :].rearrange("p a t -> p (a t)"), func=AF.Exp),
                  R=[pr(bb, 2)], W=[dc.r()], c=1024)
                A("dve", lambda e, half=half, dc=dc: e.tensor_tensor(
                    out=MT.t[:, half * 8:half * 8 + 8, :], in0=dc.t[:].rearrange("p (h l) -> p h l", l=128),
                    in1=PS[:, bcb, half * 128:half * 128 + 128].unsqueeze(1).to_broadcast([128, 8, 128]), op=ALU.mult),
                  R=[dc.r(), pr(bcb)], W=[MT.r(half * 1024, half * 1024 + 1024)], c=1024)
        bx = psa(2)
        A("pe", mm_group([(PS[:, bx + j // 4, (j % 4) * 128:(j % 4) * 128 + 128], xsT.t[:, j, cs], ident_b, True, True) for j in range(8)]),
          R=[xsT.r(), cbf.r()], W=[pr(bx, 2)])
        A("act", lambda e: e.activation(out=xtok_.t[:], in_=PS[:, bx:bx + 2, :].rearrange("p a t -> p (a t)"), func=AF.Copy),
          R=[pr(bx, 2)], W=[xtok_.r()], c=1024)
        bB = psa()
        A("pe", mm_group([(PS[:, bB, g * 128:(g + 1) * 128], BCT.t[:, g, cs], ident_b, True, True) for g in range(2)]),
          R=[BCT.r(), cbf.r()], W=[pr(bB)])
        A("act", lambda e: e.activation(out=btok_.t[:], in_=PS[:, bB, 0:256], func=AF.Copy), R=[pr(bB)], W=[btok_.r()], c=256)
        A("dve", lambda e: e.tensor_tensor(out=xw_.t[:].rearrange("p (h d) -> p h d", d=64), in0=xtok_.t[:].rearrange("p (h d) -> p h d", d=64),
                                           in1=S(4).unsqueeze(2).to_broadcast([128, 16, 64]), op=ALU.mult), R=[xtok_.r(), sr], W=[xw_.r()], c=1024)
        if not pre:
            by = psa(2)
            specs = []
            for h in range(16):
                o = PS[:, by + h // 8, (h % 8) * 64:(h % 8) * 64 + 64]
                specs += [(o, MT.t[:, h, :], xtok_.t[:, h * 64:(h + 1) * 64], True, False),
                          (o, xsT.t[:, h // 2, cs], diagD.t[:, h // 2, (h % 2) * 64:(h % 2) * 64 + 64], False, True)]
            A("pe", mm_group(specs), R=[MT.r(), xtok_.r(), xsT.r(), diagD.r()], W=[pr(by, 2)])
            bo = psa(2)
            A("pe", mm_group([(PS[:, bo + g, :], BCT.t[:, 2 + g, cs], st_b.t[:, g * 512:(g + 1) * 512], True, True) for g in range(2)]),
              R=[BCT.r(), st_b.r()], W=[pr(bo, 2)])
            A("dve", lambda e: e.tensor_tensor(out=tmpf.t[:].rearrange("p (h d) -> p h d", d=64),
                                               in0=PS[:, bo:bo + 2, :].rearrange("p a (h d) -> p (a h) d", d=64),
                                               in1=S(5).unsqueeze(2).to_broadcast([128, 16, 64]), op=ALU.mult), R=[pr(bo, 2), sr], W=[tmpf.r()], c=1024)
            A("dve", lambda e: e.tensor_tensor(out=yv.t[:], in0=PS[:, by:by + 2, :].rearrange("p a t -> p (a t)"), in1=tmpf.t[:], op=ALU.add),
              R=[pr(by, 2), tmpf.r()], W=[yv.r()], c=1024)
        bs = psa(2)
        A("pe", mm_group([(PS[:, bs + g, :], btok_.t[:, g * 128:(g + 1) * 128], xw_.t[:, g * 512:(g + 1) * 512], True, True) for g in range(2)]),
          R=[btok_.r(), xw_.r()], W=[pr(bs, 2)])
        A("dve", lambda e: e.tensor_tensor(out=st_f.t[:].rearrange("p (h d) -> p h d", d=64), in0=st_f.t[:].rearrange("p (h d) -> p h d", d=64),
                                           in1=S(6).unsqueeze(2).to_broadcast([128, 16, 64]), op=ALU.mult), R=[st_f.r(), sr], W=[st_f.r()], c=1024)
        A("dve", lambda e: e.tensor_tensor(out=st_f.t[:], in0=st_f.t[:], in1=PS[:, bs:bs + 2, :].rearrange("p a t -> p (a t)"), op=ALU.add),
          R=[st_f.r(), pr(bs, 2)], W=[st_f.r()], c=1024)
        A("act", lambda e: e.activation(out=st_b.t[:], in_=st_f.t[:], func=AF.Copy), R=[st_f.r()], W=[st_b.r()], c=1024)
        if pre:
            return
        A("dve", lambda e: e.tensor_tensor(out=yv.t[:], in0=yv.t[:], in1=sz.t[:, c, :], op=ALU.mult), R=[yv.r(), sz.r(c * 1024, c * 1024 + 1024)], W=[yv.r()], c=1024)
        A("dve", lambda e: e.memset(S(7)[:, 0:2], 0.0), W=[sr], c=2)
        for g in range(2):
            A("act", lambda e, g=g: e.activation(out=tmpf.t[:, g * 512:(g + 1) * 512], in_=yv.t[:, g * 512:(g + 1) * 512], func=AF.Square,
                                                 accum_out=S(7)[:, g:g + 1]), R=[yv.r(), sr], W=[tmpf.r(), sr])
        A("act", lambda e: e.activation(out=S(8)[:, 0:2], in_=S(7)[:, 0:2], func=AF.Ln, bias=sm.t[:, SM_E6:SM_E6 + 1], scale=1.0 / 512),
          R=[sr, sm.r()], W=[sr], c=2)
        A("act", lambda e: e.activation(out=S(9)[:, 0:2], in_=S(8)[:, 0:2], func=AF.Exp, scale=-0.5), R=[sr], W=[sr], c=2)
        for g in range(2):
            A("act", lambda e, g=g: e.activation(out=yn.t[:, g * 512:(g + 1) * 512], in_=yv.t[:, g * 512:(g + 1) * 512], func=AF.Copy,
                                                 scale=S(9)[:, g:g + 1]), R=[yv.r(), sr], W=[yn.r(g * 512, g * 512 + 512)])
        bt = psa(2)
        A("pe", mm_group([(PS[:, bt + j // 4, (j % 4) * 128:(j % 4) * 128 + 128], yn.t[:, j * 128:(j + 1) * 128], ident_b, True, True)
                          for j in range(8)]), R=[yn.r(), cbf.r()], W=[pr(bt, 2)])
        for j in range(8):
            A("act", lambda e, j=j: e.activation(out=mixT.t[:, j, cs], in_=PS[:, bt + j // 4, (j % 4) * 128:(j % 4) * 128 + 128], func=AF.Copy,
                                                 scale=pc("snw", j)), R=[pr(bt + j // 4), prm.r()], W=[mixT.r(j * 512 + c * 128, j * 512 + c * 128 + 128)], c=128)

    def attn_chunk(c):
        cs = slice(c * 128, (c + 1) * 128)
        S = lambda i: a16s[c].t[:, i - 10, :]
        sr = a16s[c].r()
        gi = 0
        for kv in range(2):
            for pad in range(2):
                h0 = 8 * kv + pad
                pts = []
                for kb in range(2):
                    b = psa()
                    ov_ = PS[:, b, :].rearrange("p (j q) -> p j q", q=128)
                    A("pe", mm_group([(ov_, Kp.t[:, 2 * kv + pad, (c + kb) * 128:(c + kb + 1) * 128], qnT.t[:, 4 * kv:4 * kv + 4, cs], True, False),
                                      (ov_, ident_b, biasT.t[:, h0:h0 + 7:2, kb, :], False, True)]),
                      R=[Kp.r(), qnT.r(), biasT.r(), cbf.r()], W=[pr(b)])
                    p_ = pT[(gi % 2) * 2 + kb]
                    A("act", lambda e, b=b, p_=p_: e.activation(out=p_.t[:], in_=PS[:, b, :], func=AF.Exp), R=[pr(b)], W=[p_.r()])
                    pts.append(p_)
                o = psa()
                specs = []
                for j in range(4):
                    for kb in range(2):
                        specs.append((PS[:, o, j * 65:(j + 1) * 65], pts[kb].t[:, j * 128:(j + 1) * 128], Vx.t[:, c + kb, kv, :], kb == 0, kb == 1))
                A("pe", mm_group(specs), R=[pts[0].r(), pts[1].r(), Vx.r()], W=[pr(o)])
                ov = PS[:, o, 0:260].rearrange("p (j d) -> p j d", d=65)
                A("dve", lambda e, ov=ov, h0=h0: e.tensor_tensor(out=S(10)[:, 0:4], in0=ov[:, :, 64], in1=sm.t[:, SM_ES + h0:SM_ES + h0 + 7:2], op=ALU.add),
                  R=[pr(o), sm.r()], W=[sr])
                A("dve", lambda e: e.reciprocal(out=S(11)[:, 0:4], in_=S(10)[:, 0:4]), R=[sr], W=[sr])
                A("dve", lambda e, ov=ov, h0=h0: e.tensor_tensor(out=attn.t[:].rearrange("p (h d) -> p h d", d=64)[:, h0:h0 + 7:2, :], in0=ov[:, :, 0:64],
                                                                in1=S(11)[:, 0:4].unsqueeze(2).to_broadcast([128, 4, 64]), op=ALU.mult),
                  R=[pr(o), sr], W=[attn.r()])
                gi += 1
        bt = psa(2)
        A("pe", mm_group([(PS[:, bt + j // 4, (j % 4) * 128:(j % 4) * 128 + 128], attn.t[:, j * 128:(j + 1) * 128], ident_b, True, True)
                          for j in range(8)]), R=[attn.r(), cbf.r()], W=[pr(bt, 2)])
        A("act", lambda e: e.activation(out=mixT.t[:, 8:16, cs], in_=PS[:, bt:bt + 2, :].rearrange("p a (j t) -> p (a j) t", t=128), func=AF.Copy),
          R=[pr(bt, 2)], W=[mixT.r(8 * 512, 16 * 512)], c=1024)

    def mixer0(pre, bs=None):
        bs = SET_A if bs is None else bs
        hn, sq, xsT, BCT, dtr, dtv, lndt, hsrc = bs
        nt = cfg["nt"]
        nch = nt // 128
        rmsnorm("mixw", 0, hn, sq, rs_s, rs_r, hsrc)
        if not pre:
            wz0, rz0, iz0 = wnext("z0", hold=True)
            wz1, rz1, iz1 = wnext("z1", hold=True)
            for c in range(nch):
                b = psa(2)
                specs = []
                for hf, wv in ((0, wz0), (1, wz1)):
                    specs += [(PS[:, b + hf, :], hn.t[:, kt, c * 128:(c + 1) * 128], wv[:, kt, :], kt == 0, kt == 7) for kt in range(8)]
                A("pe", mm_group(specs), R=[hn.r(), rz0, rz1], W=[pr(b, 2)])
                A("act", lambda e, b=b, c=c: e.activation(out=sz.t[:, c, :], in_=PS[:, b:b + 2, :].rearrange("p a t -> p (a t)"), func=AF.Silu),
                  R=[pr(b, 2)], W=[sz.r(c * 1024, c * 1024 + 1024)], c=1024)
            wrel(iz0, iz1)
        for c in range(nch):
            b = psa()
            A("pe", mm_group([(PS[:, b, 0:144], hn.t[:, kt, c * 128:(c + 1) * 128], WDV.t[:, kt, :], kt == 0, kt == 7) for kt in range(8)]),
              R=[hn.r(), WDV.r()], W=[pr(b)])
            A("dve", lambda e, b=b, c=c: e.tensor_tensor(out=dtr.t[:, c, :], in0=PS[:, b, 0:16], in1=pc("dtb", 0, 16), op=ALU.add),
              R=[pr(b), prm.r()], W=[dtr.r()])
            if not pre:
                A("act", lambda e, b=b, c=c: e.activation(out=Vx.t[:, 1 + c, :, 0:64], in_=PS[:, b, 16:144].rearrange("p (k d) -> p k d", d=64),
                                                          func=AF.Copy), R=[pr(b)], W=[Vx.r()])
        fl = lambda t_: t_.t[:, 0:nch, :].rearrange("p a b -> p (a b)")
        A("act", lambda e: e.activation(out=fl(dtr), in_=fl(dtr), func=AF.Exp), R=[dtr.r()], W=[dtr.r()])
        A("act", lambda e: e.activation(out=fl(dtv), in_=fl(dtr), func=AF.Ln, bias=1.0), R=[dtr.r()], W=[dtv.r()])
        A("act", lambda e: e.activation(out=fl(lndt), in_=fl(dtv), func=AF.Ln), R=[dtv.r()], W=[lndt.r()])
        tiles = list(range(10)) if pre else list(range(12))
        wv = wr = None
        for j in tiles:
            if j == 0:
                wv, wr, _ = wnext("x0")
            elif j == 4:
                wv, wr, _ = wnext("x1")
            elif j == 8:
                wv, wr, _ = wnext("bc")
            b = proj_fm(wv, wr, j % 4, hn)
            xp_, acc_ = xp[j % 2], acc[j % 2]
            conv_fm(b, xp_, acc_, 4, lambda k, j=j: pc("cw", j * 4 + k), pc("cb", j), TL_A + 3 * j, tail_engs=("dve", "dve"))
            dst = xsT.t[:, j, 0:nt] if j < 8 else BCT.t[:, j - 8, 0:nt]
            dres = xsT.r(j * 512, j * 512 + 512) if j < 8 else BCT.r((j - 8) * 512, (j - 8) * 512 + 512)
            A("act", lambda e, acc_=acc_, dst=dst: e.activation(out=dst, in_=acc_.t[:, 0:nt], func=AF.Silu), R=[acc_.r()], W=[dres], c=nt)
        if not pre:
            for j in range(12):
                if j == 0:
                    wv, wr, _ = wnext("q0")
                elif j == 4:
                    wv, wr, _ = wnext("q1")
                if j < 8:
                    b = proj_fm(wv, wr, j % 4, hn)
                else:
                    b = psa()
                    t = j - 8
                    A("pe", mm_group([(PS[:, b, 0:nt], WK.t[:, kt, t, :], hn.t[:, kt, 0:nt], kt == 0, kt == 7) for kt in range(8)]),
                      R=[WK.r(), hn.r()], W=[pr(b)])
                sq_ = sqq[j % 2]
                A("act", lambda e, b=b, sq_=sq_: e.activation(out=sq_.t[:, 0:nt], in_=PS[:, b, 0:nt], func=AF.Square), R=[pr(b)], W=[sq_.r()], c=nt)
                b2 = psa()
                A("pe", mm_group([(PS[:, b2, 0:nt], bd_b, sq_.t[:, 0:nt], True, True)]), R=[sq_.r(), cbf.r()], W=[pr(b2)])
                rs1, rs2 = (rs_s, rs_r) if j % 2 == 0 else (rs_s2, rs_r2)
                A("act", lambda e, b2=b2, rs1=rs1: e.activation(out=rs1.t[:, 0:nt], in_=PS[:, b2, 0:nt], func=AF.Ln, bias=sm.t[:, SM_E6:SM_E6 + 1], scale=1.0 / 64),
                  R=[pr(b2), sm.r()], W=[rs1.r()], c=nt)
                A("act", lambda e, rs1=rs1, rs2=rs2: e.activation(out=rs2.t[:, 0:nt], in_=rs1.t[:, 0:nt], func=AF.Exp, scale=-0.5), R=[rs1.r()], W=[rs2.r()], c=nt)
                if j < 8:
                    dst, dres, wc = qnT.t[:, j, 0:nt], qnT.r(j * 512, j * 512 + 512), sm.t[:, SM_QW:SM_QW + 1]
                else:
                    dst, dres, wc = Kp.t[:, j - 8, 128:128 + nt], Kp.r(), sm.t[:, SM_KW:SM_KW + 1]
                A("dve", lambda e, b=b, dst=dst, wc=wc, rs2=rs2: e.scalar_tensor_tensor(out=dst, in0=PS[:, b, 0:nt], scalar=wc, in1=rs2.t[:, 0:nt], op0=ALU.mult, op1=ALU.mult),
                  R=[pr(b), rs2.r(), sm.r()], W=[dres], c=nt)
        lab0 = P.label
        pool0 = dict(pspool)
        for c in range(nch):
            P.label = lab0 + f".ssd{c}"
            if not pre:
                pspool.update(lo=SSD_POOL[0], n=SSD_POOL[1])
            ssd_chunk(c, pre, bs)
            if not pre:
                P.label = lab0 + f".att{c}"
                pspool.update(lo=ATT_POOL[0], n=ATT_POOL[1])
                attn_chunk(c)
        pspool.update(pool0)
        P.label = lab0 + ".oproj"
        if pre:
            return
        A("act", lambda e: e.activation(out=Kp.t[:, :, 0:128], in_=Kp.t[:, :, nt:nt + 128], func=AF.Copy), R=[Kp.r()], W=[Kp.r()])
        A("act", lambda e: e.activation(out=Vx.t[:, 0, :, :], in_=Vx.t[:, nch, :, :], func=AF.Copy), R=[Vx.r()], W=[Vx.r()])
        for i in range(4):
            wv, wr, _ = wnext(f"o{i}")
            for f2 in range(2):
                f = 2 * i + f2
                b = psa()
                A("pe", mm_group([(PS[:, b, 0:nt], wv[:, kt, f2 * 128:(f2 + 1) * 128], mixT.t[:, kt, 0:nt], kt == 0, kt == 15) for kt in range(16)]),
                  R=[wr, mixT.r()], W=[pr(b)])
                A("dve", lambda e, b=b, f=f: e.tensor_tensor(out=hT.t[:, f, 0:nt], in0=hT.t[:, f, 0:nt], in1=PS[:, b, 0:nt], op=ALU.add),
                  R=[pr(b), hT.r(f * 512, f * 512 + 512)], W=[hT.r(f * 512, f * 512 + 512)], c=nt)
                ssq_hook(f, f_sq)

    def ffn(l, last, skip_down=False):
        nt = cfg["nt"]
        rmsnorm("ffnw", l, f_hn, f_sq, f_rs, f_rr, bank=nrm["bank"])
        for g in range(6):
            ntl = 4 if g < 5 else 2
            wg, rg, ig = wnext(f"g{g}", hold=True)
            wu, ru, iu = wnext(f"u{g}", hold=True)
            for i in range(ntl):
                t = 4 * g + i
                bg = proj_fm(wg, rg, i, f_hn)
                bu = proj_fm(wu, ru, i, f_hn)
                k2 = (t % 4) * 2
                for (b, tt, xi) in ((bg, t, k2), (bu, 22 + t, k2 + 1)):
                    conv_fm(b, f_xp[xi], f_acc[xi], 3, lambda k, tt=tt: pc("fcw", (l * 44 + tt) * 3 + k), pc("fcb", l * 44 + tt),
                            TL_F + l * 88 + 2 * tt)
                sg = f_sg[t % 4]
                ag, au = f_acc[k2], f_acc[k2 + 1]
                A("act", lambda e, sg=sg, ag=ag: e.activation(out=sg.t[:, 0:nt], in_=ag.t[:, 0:nt], func=AF.Silu), R=[ag.r()], W=[sg.r()])
                A("dve", lambda e, sg=sg, au=au, t=t: e.tensor_tensor(out=actT.t[:, t, 0:nt], in0=sg.t[:, 0:nt], in1=au.t[:, 0:nt], op=ALU.mult),
                  R=[sg.r(), au.r()], W=[actT.r(t * 512, t * 512 + 512)])
            wrel(ig, iu)
        for f in range(8):
            wv, wr, _ = wnext(f"d{f}")
            if skip_down:
                continue
            b = psa()
            A("pe", mm_group([(PS[:, b, 0:nt], wv[:, kt, :], actT.t[:, kt, 0:nt], kt == 0, kt == 21) for kt in range(22)]),
              R=[wr, actT.r()], W=[pr(b)])
            A("dve", lambda e, b=b, f=f: e.tensor_tensor(out=hT.t[:, f, 0:nt], in0=hT.t[:, f, 0:nt], in1=PS[:, b, 0:nt], op=ALU.add),
              R=[pr(b), hT.r(f * 512, f * 512 + 512)], W=[hT.r(f * 512, f * 512 + 512)])
            if not last:
                ssq_hook(f, c_sq)

    def conformer():
        nt = cfg["nt"]
        rmsnorm("mixw", 1, c_hn, c_sq, c_rs, c_rr, bank=nrm["bank"])
        for half in range(2):
            wa, ra, ia = wnext(f"a{half}", hold=True)
            wg, rg, ig = wnext(f"s{half}", hold=True)
            for i in range(4):
                j = half * 4 + i
                ba = proj_fm(wa, ra, i, c_hn)
                bg = proj_fm(wg, rg, i, c_hn)
                sg = c_sig[j % 2]
                A("act", lambda e, bg=bg, sg=sg, j=j: e.activation(out=sg.t[:, 0:nt], in_=PS[:, bg, 0:nt], func=AF.Sigmoid, bias=pc("pw1b", 8 + j)),
                  R=[pr(bg), prm.r()], W=[sg.r()])
                ur = upad.r(j * 544, j * 544 + 544)
                A("dve", lambda e, ba=ba, sg=sg, j=j: e.scalar_tensor_tensor(out=upad.t[:, j, 30:30 + nt], in0=PS[:, ba, 0:nt], scalar=pc("pw1b", j),
                                                                            in1=sg.t[:, 0:nt], op0=ALU.add, op1=ALU.mult),
                  R=[pr(ba), sg.r(), prm.r()], W=[ur])
                A("act", lambda e, j=j: e.activation(out=upad.t[:, j, 0:30], in_=TL.t[:, TL_C + 30 * j:TL_C + 30 * j + 30], func=AF.Copy),
                  R=[TL.r(TL_C + 30 * j, TL_C + 30 * j + 30)], W=[ur], c=30)
                A("act", lambda e, j=j: e.activation(out=TL.t[:, TL_C + 30 * j:TL_C + 30 * j + 30], in_=upad.t[:, j, nt:nt + 30], func=AF.Copy),
                  R=[ur], W=[TL.r(TL_C + 30 * j, TL_C + 30 * j + 30)], c=30)
            wrel(ia, ig)
            for i in range(4):
                j = half * 4 + i
                ur = upad.r(j * 544, j * 544 + 544)
                yr = yc.r(j * 512, j * 512 + 512)
                wd, rd, _ = wnext(f"dg{j}")
                bc_ = psa()
                A("pe", mm_group([(PS[:, bc_, 0:nt], wd[:, k, :], upad.t[:, j, k:k + nt], k == NDV, k == 30) for k in range(NDV, 31)]),
                  R=[rd, ur], W=[pr(bc_)])
                A("dve", lambda e, j=j: e.tensor_scalar(out=yc.t[:, j, 0:nt], in0=upad.t[:, j, 0:nt], scalar1=pc("dww", j * 31), scalar2=None, op0=ALU.mult),
                  R=[ur, prm.r()], W=[yr])
                for k in range(1, NDV):
                    A("dve", lambda e, j=j, k=k: e.scalar_tensor_tensor(out=yc.t[:, j, 0:nt], in0=upad.t[:, j, k:k + nt], scalar=pc("dww", j * 31 + k),
                                                                       in1=yc.t[:, j, 0:nt], op0=ALU.mult, op1=ALU.add), R=[ur, yr, prm.r()], W=[yr])
                A("dve", lambda e, j=j, bc_=bc_: e.scalar_tensor_tensor(out=yc.t[:, j, 0:nt], in0=PS[:, bc_, 0:nt], scalar=pc("dwb", j), in1=yc.t[:, j, 0:nt],
                                                                       op0=ALU.add, op1=ALU.add), R=[pr(bc_), yr, prm.r()], W=[yr])
                A("act", lambda e, j=j: e.activation(out=ybf.t[:, j, 0:nt], in_=yc.t[:, j, 0:nt], func=AF.Copy), R=[yr], W=[ybf.r(j * 512, j * 512 + 512)])
                A("act", lambda e, j=j: e.activation(out=c_sq.t[:, j, 0:nt], in_=yc.t[:, j, 0:nt], func=AF.Square), R=[yr], W=[c_sq.r(j * 512, j * 512 + 512)])
        b1 = psa()
        A("pe", mm_group([(PS[:, b1, 0:nt], ones_b.t[:], ybf.t[:, j, 0:nt], j == 0, j == 7) for j in range(8)]), R=[ybf.r(), ones_b.r()], W=[pr(b1)])
        b2 = psa()
        A("pe", mm_group([(PS[:, b2, 0:nt], ones_b.t[:], c_sq.t[:, j, 0:nt], j == 0, j == 7) for j in range(8)]), R=[c_sq.r(), ones_b.r()], W=[pr(b2)])
        A("dve", lambda e: e.tensor_scalar(out=c_mean.t[:, 0:nt], in0=PS[:, b1, 0:nt], scalar1=1.0 / 1024, scalar2=None, op0=ALU.mult), R=[pr(b1)], W=[c_mean.r()])
        A("dve", lambda e: e.tensor_tensor(out=c_msq.t[:, 0:nt], in0=c_mean.t[:, 0:nt], in1=c_mean.t[:, 0:nt], op=ALU.mult), R=[c_mean.r()], W=[c_msq.r()])
        A("dve", lambda e: e.scalar_tensor_tensor(out=c_var.t[:, 0:nt], in0=PS[:, b2, 0:nt], scalar=1.0 / 1024, in1=c_msq.t[:, 0:nt], op0=ALU.mult, op1=ALU.subtract),
          R=[pr(b2), c_msq.r()], W=[c_var.r()])
        A("act", lambda e: e.activation(out=c_rs.t[:, 0:nt], in_=c_var.t[:, 0:nt], func=AF.Ln, bias=sm.t[:, SM_E5:SM_E5 + 1], scale=1.0), R=[c_var.r(), sm.r()], W=[c_rs.r()])
        A("act", lambda e: e.activation(out=c_rr.t[:, 0:nt], in_=c_rs.t[:, 0:nt], func=AF.Exp, scale=-0.5), R=[c_rs.r()], W=[c_rr.r()])
        for j in range(8):
            yr = yc.r(j * 512, j * 512 + 512)
            A("dve", lambda e, j=j: e.tensor_tensor(out=yc.t[:, j, 0:nt], in0=yc.t[:, j, 0:nt], in1=c_mean.t[:, 0:nt], op=ALU.subtract), R=[yr, c_mean.r()], W=[yr])
            A("dve", lambda e, j=j: e.scalar_tensor_tensor(out=yc.t[:, j, 0:nt], in0=yc.t[:, j, 0:nt], scalar=pc("lnw", j), in1=c_rr.t[:, 0:nt], op0=ALU.mult, op1=ALU.mult),
              R=[yr, c_rr.r(), prm.r()], W=[yr])
            A("act", lambda e, j=j: e.activation(out=ybf.t[:, j, 0:nt], in_=yc.t[:, j, 0:nt], func=AF.Silu, bias=pc("lnb", j)), R=[yr, prm.r()],
              W=[ybf.r(j * 512, j * 512 + 512)])
        for half in range(2):
            wv, wr, _ = wnext(f"p{half}")
            for i in range(4):
                f = half * 4 + i
                b = proj_fm(wv, wr, i, ybf)
                A("dve", lambda e, b=b, f=f: e.scalar_tensor_tensor(out=hT.t[:, f, 0:nt], in0=PS[:, b, 0:nt], scalar=pc("pw2b", f), in1=hT.t[:, f, 0:nt],
                                                                   op0=ALU.add, op1=ALU.add),
                  R=[pr(b), hT.r(f * 512, f * 512 + 512), prm.r()], W=[hT.r(f * 512, f * 512 + 512)])
                ssq_hook(f, f_sq)

    blocks = [("pre", 512 * i, 512) for i in range(NPRE)] + [("pre", 512 * NPRE, 256), ("warm", 512 * NPRE + 256, 256)]
    blocks += [("main", 512 * (NPRE + k), 512) for k in range(1, NMAIN)]
    npre_seen = 0
    nout = 0
    for bi, (kind, row0, ntok) in enumerate(blocks):
        cfg["nt"] = ntok
        P.label = f"b{bi}.in"
        if kind == "pre":
            par = npre_seen % 2
            npre_seen += 1
            stage_in(row0, hTB if par == 1 else hT)
            P.label = f"b{bi}.pre"
            mixer0(True, SET_A if par == 0 else SET_B)
            continue
        stage_in(row0, hT)
        P.label = f"b{bi}.m0"
        mixer0(False)
        if kind == "warm" and "h0" in dbg_d:
            dump("h0", hT)
        P.label = f"b{bi}.f0"
        ffn(0, False)
        P.label = f"b{bi}.cf"
        conformer()
        P.label = f"b{bi}.f1"
        ffn(1, True, skip_down=(kind == "warm"))
        P.label = f"b{bi}.out"
        if kind == "warm":
            f1 = flg.t[:, 0:1]
            A("dve", lambda e: e.tensor_scalar(out=st_f.t[:], in0=st_f.t[:], scalar1=f1, scalar2=None, op0=ALU.mult), R=[st_f.r(), flg.r()], W=[st_f.r()])
            A("dve", lambda e: e.tensor_scalar(out=st_b.t[:], in0=st_b.t[:], scalar1=f1, scalar2=None, op0=ALU.mult), R=[st_b.r(), flg.r()], W=[st_b.r()])
            A("dve", lambda e: e.tensor_scalar(out=TL.t[:], in0=TL.t[:], scalar1=f1, scalar2=None, op0=ALU.mult), R=[TL.r(), flg.r()], W=[TL.r()])
            A("dve", lambda e: e.tensor_scalar(out=Vx.t[:, 0, :, :].rearrange("p a b -> p (a b)"), in0=Vx.t[:, 0, :, :].rearrange("p a b -> p (a b)"),
                                               scalar1=f1, scalar2=None, op0=ALU.mult), R=[Vx.r(), flg.r()], W=[Vx.r()])
        else:
            stage_out(nout)
            nout += 512
    fin = [f"dmachain_{xo.name}" for xo in xout] + [f"dmachain_dbg_{n}" for n in dbg_d]
    P.add("sp", None, fin, ())
    if not do_emit:
        P.sim_ns = P.schedule()
        return nc, P, allocs
    nsem, nops = P.emit()
    print(f"[build] sbuf_peak={sbuf_peak} sems={nsem} ops={nops} loads={len(stream)} sim_us={P.sim_ns / 1e3:.0f}", flush=True)
    return nc, P, allocs


def plan_psum(allocs):
    last_r = [-1] * 8
    last_t = [0.0] * 8
    plan = []
    for n, _b, ops in allocs:
        if not ops:
            plan.append(0)
            continue
        r0 = min(o.ridx for o in ops)
        r1 = max(o.ridx for o in ops)
        t0 = min(o.t0 for o in ops)
        t1 = max(o.t1 for o in ops)
        best = None
        for b in (range(8) if n == 1 else range(0, 8, 2)):
            bs = range(b, b + n)
            if any(last_r[i] >= r0 for i in bs):
                continue
            te = max(last_t[i] for i in bs)
            key = (max(te - t0, 0.0), -te)
            if best is None or key < best[0]:
                best = (key, b)
        assert best is not None, "PSUM over-subscribed in program order"
        b = best[1]
        for i in range(b, b + n):
            last_r[i] = r1
            last_t[i] = t1
        plan.append(b)
    return plan


def build_program(NPRE, NMAIN, dbg=(), iters=PLAN_ITERS):
    plan = None
    best = None
    for it in range(iters):
        _nc, P1, allocs = record_program(NPRE, NMAIN, dbg, plan=plan, do_emit=False)
        if best is None or P1.sim_ns < best[0]:
            best = (P1.sim_ns, plan)
        print(f"[build] plan iter {it}: sim_us={P1.sim_ns / 1e3:.0f}", flush=True)
        plan = plan_psum(allocs)
    nc, _P, _a = record_program(NPRE, NMAIN, dbg, plan=best[1], do_emit=True)
    return nc


def _fm(v, nt):
    return np.ascontiguousarray(np.asarray(v, np.float32).reshape(nt, 128).T)


def _t5_bucket(dist):
    max_exact = 16
    d_f = np.maximum(dist, 1).astype(np.float32)
    large = max_exact + (np.log(d_f / max_exact) / math.log(128 / max_exact) * (32 - max_exact)).astype(np.int32)
    large = np.minimum(large, 31)
    return np.where(dist < max_exact, dist, large)


def host_pack(inp):
    f32 = np.float32
    prm = np.zeros((128, NPRM), f32)

    def put(name, arr):
        arr = np.asarray(arr, f32)
        prm[:, _p[name]:_p[name] + arr.shape[1]] = arr
    put("mixw", np.concatenate([_fm(inp["mix_norm_w"][l], 8) for l in range(2)], 1))
    put("ffnw", np.concatenate([_fm(inp["ffn_norm_w"][l], 8) for l in range(2)], 1))
    cw = np.asarray(inp["ssm_conv_w"][0], f32)
    put("cw", cw.T.reshape(12, 128, 4).transpose(1, 0, 2).reshape(128, 48))
    put("cb", _fm(inp["ssm_conv_b"][0], 12))
    put("dch", _fm(np.repeat(np.asarray(inp["ssm_d"][0], f32), 64), 8))
    put("snw", _fm(inp["ssm_norm_w"][0], 8))
    put("qw", np.tile(np.asarray(inp["attn_q_norm_w"][0], f32), 2)[:, None])
    put("kw", np.tile(np.asarray(inp["attn_k_norm_w"][0], f32), 2)[:, None])
    put("pw1b", _fm(inp["conv_pw1_b"][0], 16))
    dw = np.asarray(inp["conv_dw_w"][0], f32)
    put("dww", dw.T.reshape(8, 128, 31).transpose(1, 0, 2).reshape(128, 248))
    put("dwb", _fm(inp["conv_dw_b"][0], 8))
    put("lnw", _fm(inp["conv_ln_w"][0], 8))
    put("lnb", _fm(inp["conv_ln_b"][0], 8))
    put("pw2b", _fm(inp["conv_pw2_b"][0], 8))
    fcw = np.asarray(inp["ffn_conv_w"], f32)
    put("fcw", fcw.transpose(0, 2, 1).reshape(2, 44, 128, 3).transpose(2, 0, 1, 3).reshape(128, 264))
    put("fcb", np.asarray(inp["ffn_conv_b"], f32).reshape(2, 44, 128).transpose(2, 0, 1).reshape(128, 88))
    put("dtb", np.broadcast_to(np.asarray(inp["ssm_dt_bias"][0], f32)[None, :], (128, 16)))
    put("alog", np.broadcast_to(np.asarray(inp["ssm_a_log"][0], f32)[None, :], (128, 16)))
    put("sink", np.broadcast_to(np.asarray(inp["attn_sinks"][0], f32)[None, :], (128, 16)))
    rb = np.asarray(inp["rel_bias"], f32)
    qi = np.arange(128)[:, None]
    sj = np.arange(256)[None, :]
    dist = qi + 128 - sj
    valid = (dist >= 0) & (dist < 128)
    bias = rb[_t5_bucket(np.maximum(dist, 0))]
    bias = np.where(valid[:, :, None], bias, f32(NEG)).astype(f32)
    biasT = bias.reshape(128, 2, 128, 16).transpose(2, 3, 1, 0)
    biasT = np.ascontiguousarray(biasT).reshape(128, 16 * 2 * 128)
    cst = np.zeros((128, 512), f32)
    cst[:, 0:128] = np.eye(128, dtype=f32)
    cst[:, 128:256] = np.triu(np.ones((128, 128), f32))
    cst[:, 256:384] = np.kron(np.eye(2, dtype=f32), np.ones((64, 64), f32))
    cst[:, 384:512] = np.where(np.arange(128)[None, :] >= np.arange(128)[:, None], 0.0, NEG)
    eall = np.zeros((48, 16, 128), f32)
    for h in range(16):
        eall[h, h, :] = 1.0
        eall[32 + h, h, :] = 1.0
    return prm, biasT, cst, eall.reshape(48, 2048)


_NC_CACHE = {}


def kernel(**inp):
    x = np.asarray(inp["x"], np.float32)
    B, L, _ = x.shape
    NPRE, NMAIN = 7, 9
    prm, biasT, cst, eall = host_pack(inp)
    common = {
        "prm": prm, "biasT": biasT, "cst": cst, "eall": eall,
        "w_in": np.ascontiguousarray(inp["hyb_w_in"][0], np.float32),
        "w_out": np.ascontiguousarray(inp["hyb_w_out"][0], np.float32),
        "pw1": np.ascontiguousarray(inp["conv_pw1_w"][0], np.float32),
        "pw2": np.ascontiguousarray(inp["conv_pw2_w"][0], np.float32),
        "wup0": np.ascontiguousarray(inp["ffn_w_up"][0], np.float32),
        "wup1": np.ascontiguousarray(inp["ffn_w_up"][1], np.float32),
        "wdn0": np.ascontiguousarray(inp["ffn_w_down"][0], np.float32),
        "wdn1": np.ascontiguousarray(inp["ffn_w_down"][1], np.float32),
    }
    in_maps = []
    for core in range(8):
        b, half = core // 2, core % 2
        if half == 1:
            xs = x[b]
            flag = np.ones((128, 1), np.float32)
        else:
            xs = np.concatenate([np.zeros((4096, D), np.float32), x[b, :4096]], 0)
            flag = np.zeros((128, 1), np.float32)
        m = dict(common)
        m["x"] = np.ascontiguousarray(xs)
        m["flag"] = flag
        in_maps.append(m)
    if "nc" not in _NC_CACHE:
        _NC_CACHE["nc"] = build_program(NPRE, NMAIN)
    res = run_bass_kernel_spmd(_NC_CACHE["nc"], in_maps, core_ids=list(range(8)))
    out = np.empty((B, L, D), np.float32)
    for core in range(8):
        b, half = core // 2, core % 2
        out[b, half * 4096:(half + 1) * 4096] = res.results[core]["y"]
    return out
```

```python
import contextlib
import math
import numpy as np
import concourse.bass as bass
import concourse.mybir as mybir
from concourse.bass_utils import run_bass_kernel_spmd

F32 = mybir.dt.float32
BF16 = mybir.dt.bfloat16
AF = mybir.ActivationFunctionType
ALU = mybir.AluOpType
AX = mybir.AxisListType

ENGS = ("pe", "act", "dve", "pool", "sp")
EPOCH = 8000
DMA_BPNS = 340.0
SCHED_W = 128
SCHED_EPS = 0.0
XLAT = 0.0
SSD_POOL = (0, 4)
ATT_POOL = (4, 4)
PLAN_ITERS = 4
NDV = 10
GR = 64
SB_LO = 16640
SB_HI = 229376


class Res:
    __slots__ = ("name", "last_writer", "readers", "dma_cnt")

    def __init__(self, name):
        self.name = name
        self.last_writer = None
        self.readers = []
        self.dma_cnt = 0


class Op:
    __slots__ = ("idx", "eng", "fn", "deps", "is_dma", "sync", "eidx", "signal", "waits", "snap", "dval", "cost", "occ",
                 "t0", "t1", "label", "ridx")


class Prog:
    def __init__(self, nc):
        self.nc = nc
        self.ops = []
        self.res = {}

    def R(self, name):
        r = self.res.get(name)
        if r is None:
            r = self.res[name] = Res(name)
        return r

    def add(self, eng, fn, reads=(), writes=(), dma=None, cost=500.0, occ=None):
        op = Op()
        op.ridx = len(self.ops)
        op.label = getattr(self, "label", "")
        op.cost = cost
        op.occ = cost if occ is None else occ
        op.idx = len(self.ops)
        op.eng = eng
        op.fn = fn
        op.is_dma = dma is not None
        op.sync = self.R("dmasem_" + dma) if dma is not None else None
        deps = set()
        rs = [self.R(r) for r in set(reads)]
        ws = [self.R(w) for w in set(writes)]
        if op.is_dma:
            ws.append(self.R("dmachain_" + dma))
        for r in rs:
            if r.last_writer is not None:
                deps.add(r.last_writer)
        for w in ws:
            if w.last_writer is not None:
                deps.add(w.last_writer)
            deps.update(w.readers)
        for w in ws:
            w.last_writer = op.idx
            w.readers = []
        for r in rs:
            r.readers.append(op.idx)
        deps.discard(op.idx)
        op.deps = deps
        self.ops.append(op)
        return op

    def schedule(self, W=None):
        ops = self.ops
        if W is None:
            W = SCHED_W
        from collections import deque
        pend = {e: deque(op for op in ops if op.eng == e) for e in ENGS}
        tfree = {e: 0.0 for e in ENGS}
        self._dma_free = 0.0
        rank = [0.0] * len(ops)
        for op in reversed(ops):
            r = rank[op.idx] + op.cost
            rank[op.idx] = r
            for d in op.deps:
                if rank[d] < r:
                    rank[d] = r
        for op in ops:
            op.t0 = None
        left = len(ops)
        EPS = SCHED_EPS
        while left:
            best = None
            for e in ENGS:
                q = pend[e]
                n = 0
                cands = []
                smin = None
                for op in q:
                    if n >= W:
                        break
                    n += 1
                    rdy = 0.0
                    ok = True
                    for d in op.deps:
                        dop = ops[d]
                        if dop.t0 is None:
                            ok = False
                            break
                        t1d = dop.t1 if dop.eng == e else dop.t1 + XLAT
                        if t1d > rdy:
                            rdy = t1d
                    if not ok:
                        continue
                    st = rdy if rdy > tfree[e] else tfree[e]
                    cands.append((st, op))
                    if smin is None or st < smin:
                        smin = st
                if smin is None:
                    continue
                pick = None
                for st, op in cands:
                    if st <= smin + EPS:
                        k2 = (-rank[op.idx], op.idx)
                        if pick is None or k2 < pick[0]:
                            pick = (k2, st, op)
                key = (pick[1], pick[2].idx)
                if best is None or key < best[0]:
                    best = (key, pick[2])
            key, op = best
            op.t0 = key[0]
            if op.is_dma:
                xfer = max(op.cost - op.occ - 2000.0, 0.0)
                st = max(op.t0 + op.occ, self._dma_free)
                self._dma_free = st + xfer
                op.t1 = st + xfer + 2000.0
            else:
                op.t1 = op.t0 + op.cost
            tfree[op.eng] = op.t0 + op.occ
            pend[op.eng].remove(op)
            left -= 1
        new = sorted(ops, key=lambda o: (o.t0, o.idx))
        remap = {o.idx: i for i, o in enumerate(new)}
        for o in new:
            o.deps = {remap[d] for d in o.deps}
        for i, o in enumerate(new):
            o.idx = i
        self.ops = new
        return max(o.t1 for o in new)

    def emit(self, sched=True):
        nc = self.nc
        self.sim_ns = self.schedule() if sched else 0.0
        ops = self.ops
        cnt = {e: 0 for e in ENGS}
        for op in ops:
            op.eidx = cnt[op.eng]
            cnt[op.eng] += 1
            op.signal = False
            op.waits = []
        known = {e: {f: -1 for f in ENGS} for e in ENGS}
        kdma = {e: set() for e in ENGS}
        for op in ops:
            kn = known[op.eng]
            kd = kdma[op.eng]
            for d in sorted(op.deps):
                dop = ops[d]
                if dop.is_dma:
                    if d in kd:
                        continue
                    kd.add(d)
                else:
                    if kn[dop.eng] >= dop.eidx:
                        continue
                    kn[dop.eng] = dop.eidx
                dop.signal = True
                op.waits.append(d)
                sk, sd = dop.snap
                for f in ENGS:
                    if sk[f] > kn[f]:
                        kn[f] = sk[f]
                kd |= sd
            op.snap = (dict(kn), frozenset(kd))
        scount = {e: 0 for e in ENGS}
        keys = []
        for op in ops:
            if op.is_dma:
                op.sync.dma_cnt += 16
                op.dval = (op.sync.name, op.sync.dma_cnt)
            elif op.signal:
                c = scount[op.eng]
                scount[op.eng] += 1
                op.dval = (f"e_{op.eng}_{c // EPOCH}", c % EPOCH + 1)
            else:
                op.dval = None
            if op.dval is not None and op.dval[0] not in keys:
                keys.append(op.dval[0])
        with contextlib.ExitStack() as st:
            semh = {k: st.enter_context(nc.semaphore(k)) for k in keys}
            block = st.enter_context(nc.Block())
            per = {e: [op for op in ops if op.eng == e] for e in ENGS}

            def run(engobj, lst):
                for op in lst:
                    for d in op.waits:
                        k, v = ops[d].dval
                        engobj.wait_ge(semh[k], v)
                    if op.fn is None:
                        continue
                    ins = op.fn(engobj)
                    if op.is_dma:
                        ins.then_inc(semh[op.dval[0]], 16)
                    elif op.signal:
                        ins.then_inc(semh[op.dval[0]], 1)

            block.tensor(lambda e: run(e, per["pe"]))
            block.scalar(lambda e: run(e, per["act"]))
            block.vector(lambda e: run(e, per["dve"]))
            block.gpsimd(lambda e: run(e, per["pool"]))
            block.sync(lambda e: run(e, per["sp"]))
        return len(keys), {e: len(per[e]) for e in ENGS}


class T:
    def __init__(self, nc, name, shape, dtype, off):
        self.esz = 2 if dtype == BF16 else 4
        self.n = int(np.prod(shape[1:]))
        self.off = off
        self.t = nc.alloc_sbuf_tensor_at(name, list(shape), dtype, offset=off)
        self.name = name

    def r(self, lo=0, hi=None):
        if hi is None:
            hi = self.n
        b0 = (self.off + lo * self.esz) // GR
        b1 = (self.off + hi * self.esz - 1) // GR
        return [f"sb{g}" for g in range(b0, b1 + 1)]


class Arena:
    def __init__(self, nc):
        self.nc = nc
        self.cur = SB_LO
        self.peak = SB_LO
        self.k = 0

    def alloc(self, name, shape, dtype):
        esz = 2 if dtype == BF16 else 4
        nbytes = int(np.prod(shape[1:])) * esz
        off = (self.cur + 63) // 64 * 64
        self.cur = off + nbytes
        self.peak = max(self.peak, self.cur)
        assert self.cur <= SB_HI, (name, self.cur)
        self.k += 1
        return T(self.nc, f"{name}_{self.k}", shape, dtype, off)


D = 1024
IN_TOTAL = 3856
C_Z, C_X, C_B, C_C, C_DT, C_Q, C_K, C_V = 0, 1024, 2048, 2304, 2560, 2576, 3600, 3728
DFF = 2816
NEG = -30000.0

_p = {}
_o = 0
for _n, _w in [("mixw", 16), ("ffnw", 16), ("cw", 48), ("cb", 12), ("dch", 8), ("snw", 8), ("qw", 1), ("kw", 1),
               ("pw1b", 16), ("dww", 248), ("dwb", 8), ("lnw", 8), ("lnb", 8), ("pw2b", 8),
               ("fcw", 264), ("fcb", 88), ("dtb", 16), ("alog", 16), ("sink", 16)]:
    _p[_n] = _o
    _o += _w
NPRM = _o
TL_A, TL_F, TL_C = 0, 36, 36 + 176
NTL = 36 + 176 + 240


def record_program(NPRE, NMAIN, dbg=(), plan=None, do_emit=True):
    nc = bass.Bass("TRN2", target_bir_lowering=False)
    NT = 512 * (NPRE + NMAIN)
    NOUT = 512 * (NMAIN - 1)
    dt_ = nc.dram_tensor
    x_d = dt_("x", [NT, D], F32, kind="ExternalInput").ap()
    flag_d = dt_("flag", [128, 1], F32, kind="ExternalInput").ap()
    prm_d = dt_("prm", [128, NPRM], F32, kind="ExternalInput").ap()
    bias_d = dt_("biasT", [128, 16 * 2 * 128], F32, kind="ExternalInput").ap()
    cst_d = dt_("cst", [128, 512], F32, kind="ExternalInput").ap()
    eall_d = dt_("eall", [48, 2048], F32, kind="ExternalInput").ap()
    win_d = dt_("w_in", [D, IN_TOTAL], F32, kind="ExternalInput").ap()
    wout_d = dt_("w_out", [2048, D], F32, kind="ExternalInput").ap()
    pw1_d = dt_("pw1", [D, 2048], F32, kind="ExternalInput").ap()
    pw2_d = dt_("pw2", [D, D], F32, kind="ExternalInput").ap()
    wup_d = [dt_(f"wup{l}", [D, 2 * DFF], F32, kind="ExternalInput").ap() for l in range(2)]
    wdn_d = [dt_(f"wdn{l}", [DFF, D], F32, kind="ExternalInput").ap() for l in range(2)]
    y_d = dt_("y", [NOUT, D], F32, kind="ExternalOutput").ap()
    diag_d = dt_("diag_scr", [8, 128, 31 * 128], BF16).ap()
    dbg_d = {n: dt_("dbg_" + n, [128, 4096], F32, kind="ExternalOutput").ap() for n in dbg}

    P = Prog(nc)
    ar = Arena(nc)
    PSt = nc.alloc_psum_tensor("PS", [128, 8, 512], F32)
    PS = PSt
    pspool = {"lo": 0, "n": 8}
    pscnt = {}
    allocs = []
    cur_alloc = {}

    def psa(n=1):
        k = len(allocs)
        if plan is not None:
            b = plan[k]
        else:
            key = (0, 8)
            c = pscnt.get(key, 0)
            if n == 2 and c % 2 == 1:
                c += 1
            b = c % 8
            pscnt[key] = c + n
        allocs.append([n, b, []])
        for i in range(n):
            cur_alloc[b + i] = k
        return b

    def pr(b, n=1):
        return [f"pb{b + i}" for i in range(n)]

    def A(eng, fn, R=(), W=(), dma=None, c=512, nbytes=1 << 19, recip=False):
        rl = [x for l in R for x in l]
        wl = [x for l in W for x in l]
        occ = None
        if dma is not None:
            occ = 1200.0 if eng == "pool" else 150.0
            cost = occ + 2000.0 + nbytes / DMA_BPNS
        elif eng == "pe":
            cost = getattr(fn, "cost", 300.0)
        elif eng == "act":
            cost = 220.0 + 0.95 * c
        else:
            cost = 80.0 + (3.2 if recip else 1.3) * c
        if eng != "pe":
            toks = ["px" + nm[2:] for nm in set(rl + wl) if nm.startswith("pb")]
            if eng == "act":
                rl = rl + toks
            else:
                wl = wl + toks
        op = P.add(eng, fn, rl, wl, dma, cost, occ)
        for nm in rl + wl:
            if nm.startswith("pb"):
                lst = allocs[cur_alloc[int(nm[2:])]][2]
                if not lst or lst[-1] is not op:
                    lst.append(op)
        return op

    hT = ar.alloc("hT", [128, 8, 512], F32)
    NSLOT = 4
    WS = [ar.alloc(f"ws{i}", [128, 4096], BF16) for i in range(NSLOT)]
    WK = ar.alloc("wk", [128, 8, 4, 128], BF16)
    WDV = ar.alloc("wdv", [128, 8, 144], BF16)
    biasT = ar.alloc("biasT", [128, 16, 2, 128], BF16)
    Eall = ar.alloc("eall", [48, 16, 128], BF16)
    prm = ar.alloc("prm", [128, NPRM], F32)
    cst = ar.alloc("cst", [128, 512], F32)
    cbf = ar.alloc("cbf", [128, 512], BF16)
    ones_f = ar.alloc("ones_f", [128, 128], F32)
    ones_b = ar.alloc("ones_b", [128, 128], BF16)
    diagD = ar.alloc("diagD", [128, 8, 128], BF16)
    sm = ar.alloc("sm", [128, 128], F32)
    st_f = ar.alloc("st_f", [128, 1024], F32)
    st_b = ar.alloc("st_b", [128, 1024], BF16)
    TL = ar.alloc("TL", [128, NTL], F32)
    Kp = ar.alloc("Kp", [128, 4, 640], BF16)
    Vx = ar.alloc("Vx", [128, 5, 2, 65], BF16)
    flg = ar.alloc("flg", [128, 1], F32)
    xin = [ar.alloc(f"xin{i}", [128, 1024], F32) for i in range(2)]
    ident_f = cst.t[:, 0:128]
    U_f = cst.t[:, 128:256]
    ident_b = cbf.t[:, 0:128]
    bd_b = cbf.t[:, 256:384]
    negm_b = cbf.t[:, 384:512]
    SM_A, SM_ES, SM_E6, SM_E5, SM_QW, SM_KW = 0, 16, 32, 33, 34, 35

    def pc(name, j=0, n=1):
        return prm.t[:, _p[name] + j:_p[name] + j + n]

    base_mark = ar.cur

    xsT = ar.alloc("xsT", [128, 8, 512], BF16)
    BCT = ar.alloc("BCT", [128, 4, 512], BF16)
    qnT = ar.alloc("qnT", [128, 8, 512], BF16)
    sz = ar.alloc("sz", [128, 4, 1024], F32)
    mixT = ar.alloc("mixT", [128, 16, 512], BF16)
    dtr = ar.alloc("dtr", [128, 4, 16], F32)
    dtv = ar.alloc("dtv", [128, 4, 16], F32)
    lndt = ar.alloc("lndt", [128, 4, 16], F32)
    m0_mark = ar.cur
    ar.cur = qnT.off
    hnB = ar.alloc("hnB", [128, 8, 512], BF16)
    ar.cur = mixT.off
    xsTB = ar.alloc("xsTB", [128, 8, 512], BF16)
    BCTB = ar.alloc("BCTB", [128, 4, 512], BF16)
    dtrB = ar.alloc("dtrB", [128, 4, 16], F32)
    dtvB = ar.alloc("dtvB", [128, 4, 16], F32)
    lndtB = ar.alloc("lndtB", [128, 4, 16], F32)
    xtokB = ar.alloc("xtokB", [128, 1024], BF16)
    btokB = ar.alloc("btokB", [128, 256], BF16)
    assert ar.cur <= mixT.off + 16 * 512 * 2
    ar.cur = sz.off
    hTB = ar.alloc("hTB", [128, 8, 512], F32)
    ar.cur = m0_mark
    hn = ar.alloc("hn", [128, 8, 512], BF16)
    sq = ar.alloc("sq", [128, 8, 512], BF16)
    xp = [ar.alloc(f"xp{i}", [128, 515], F32) for i in range(2)]
    acc = [ar.alloc(f"acc{i}", [128, 512], F32) for i in range(2)]
    sqq = [ar.alloc(f"sqq{i}", [128, 512], BF16) for i in range(2)]
    rs_s = ar.alloc("rs_s", [128, 512], F32)
    rs_r = ar.alloc("rs_r", [128, 512], F32)
    rs_s2 = ar.alloc("rs_s2", [128, 512], F32)
    rs_r2 = ar.alloc("rs_r2", [128, 512], F32)
    a13_end = ar.cur
    SET_A = (hn, sq, xsT, BCT, dtr, dtv, lndt, hT)
    SET_B = (hnB, sq, xsTB, BCTB, dtrB, dtvB, lndtB, hTB)
    ar.cur = m0_mark
    xtok = ar.alloc("xtok", [128, 1024], BF16)
    btok = ar.alloc("btok", [128, 256], BF16)
    xw = ar.alloc("xw", [128, 1024], BF16)
    dec = [ar.alloc(f"dec{i}", [128, 1024], F32) for i in range(2)]
    MT = ar.alloc("MT", [128, 16, 128], BF16)
    tmpf = ar.alloc("tmpf", [128, 1024], F32)
    yv = ar.alloc("yv", [128, 1024], F32)
    yn = ar.alloc("yn", [128, 1024], BF16)
    pT = [ar.alloc(f"pT{i}", [128, 512], BF16) for i in range(4)]
    attn = ar.alloc("attn", [128, 1024], BF16)
    s16s = [ar.alloc(f"s16_{i}", [128, 16, 16], F32) for i in range(4)]
    a16s = [ar.alloc(f"a16_{i}", [128, 8, 16], F32) for i in range(4)]
    acsTs = [ar.alloc(f"acsT{i}", [48, 128], BF16) for i in range(4)]
    qTs = [ar.alloc(f"qT{i}", [48, 128], BF16) for i in range(4)]
    a48s = [ar.alloc(f"a48_{i}", [128, 48], F32) for i in range(4)]
    q48s = [ar.alloc(f"q48_{i}", [128, 48], F32) for i in range(4)]
    ar.cur = max(ar.cur, a13_end)
    m0_end = ar.cur
    ar.cur = base_mark
    f_hn = ar.alloc("f_hn", [128, 8, 512], BF16)
    f_sq = ar.alloc("f_sq", [128, 8, 512], BF16)
    f_rs = ar.alloc("f_rs", [128, 512], F32)
    f_rr = ar.alloc("f_rr", [128, 512], F32)
    actT = ar.alloc("actT", [128, 22, 512], BF16)
    f_xp = [ar.alloc(f"f_xp{i}", [128, 514], F32) for i in range(8)]
    f_acc = [ar.alloc(f"f_acc{i}", [128, 512], F32) for i in range(8)]
    f_sg = [ar.alloc(f"f_sg{i}", [128, 512], F32) for i in range(4)]
    f_end = ar.cur
    ar.cur = base_mark
    c_hn = ar.alloc("c_hn", [128, 8, 512], BF16)
    c_sq = ar.alloc("c_sq", [128, 8, 512], BF16)
    c_rs = ar.alloc("c_rs", [128, 512], F32)
    c_rr = ar.alloc("c_rr", [128, 512], F32)
    upad = ar.alloc("upad", [128, 8, 544], BF16)
    yc = ar.alloc("yc", [128, 8, 512], F32)
    ybf = ar.alloc("ybf", [128, 8, 512], BF16)
    c_sig = [ar.alloc(f"c_sig{i}", [128, 512], F32) for i in range(2)]
    c_mean = ar.alloc("c_mean", [128, 512], F32)
    c_msq = ar.alloc("c_msq", [128, 512], F32)
    c_var = ar.alloc("c_var", [128, 512], F32)
    c_end = ar.cur
    ar.cur = base_mark
    dgt = ar.alloc("dgt", [128, 31 * 128], BF16)
    ar.cur = base_mark
    xout = [ar.alloc(f"xout{i}", [128, 1024], F32) for i in range(2)]
    ar.cur = ar.peak
    xtokA = ar.alloc("xtokA", [128, 1024], BF16)
    btokA = ar.alloc("btokA", [128, 256], BF16)
    xwA = ar.alloc("xwA", [128, 1024], BF16)
    xwB = ar.alloc("xwB", [128, 1024], BF16)
    PRE_A = (xtokA, btokA, xwA)
    PRE_B = (xtokB, btokB, xwB)
    sbuf_peak = ar.peak

    A("sp", lambda e: e.dma_start(out=prm.t[:], in_=prm_d), W=[prm.r()], dma="prm")
    A("sp", lambda e: e.dma_start(out=cst.t[:], in_=cst_d), W=[cst.r()], dma="cst")
    A("pool", lambda e: e.dma_start(out=biasT.t[:].rearrange("p a b c -> p (a b c)"), in_=bias_d), W=[biasT.r()], dma="biasT")
    A("pool", lambda e: e.dma_start(out=Eall.t[:].rearrange("p a b -> p (a b)"), in_=eall_d), W=[Eall.r()], dma="eall")
    A("sp", lambda e: e.dma_start(out=flg.t[:], in_=flag_d), W=[flg.r()], dma="flg")
    A("dve", lambda e: e.memset(WK.t[:].rearrange("p a b c -> p (a b c)"), 0.0), W=[WK.r()])
    for kv in range(2):
        for pad in range(2):
            def f(e, kv=kv, pad=pad):
                return e.dma_start(out=WK.t[:, :, 2 * kv + pad, 64 * pad:64 * pad + 64],
                                   in_=win_d[:, C_K + 64 * kv:C_K + 64 * kv + 64].rearrange("(k p) c -> p k c", p=128))
            A("pool", f, W=[WK.r()], dma="wk")
    A("pool", lambda e: e.dma_start(out=WDV.t[:, :, 0:16], in_=win_d[:, C_DT:C_DT + 16].rearrange("(k p) c -> p k c", p=128)),
      W=[WDV.r()], dma="wdv")
    A("pool", lambda e: e.dma_start(out=WDV.t[:, :, 16:144], in_=win_d[:, C_V:C_V + 128].rearrange("(k p) c -> p k c", p=128)),
      W=[WDV.r()], dma="wdv")
    A("dve", lambda e: e.tensor_copy(out=cbf.t[:], in_=cst.t[:]), R=[cst.r()], W=[cbf.r()])
    A("dve", lambda e: e.memset(ones_f.t[:], 1.0), W=[ones_f.r()])
    A("dve", lambda e: e.memset(ones_b.t[:], 1.0), W=[ones_b.r()])
    A("dve", lambda e: e.memset(sm.t[:], 0.0), W=[sm.r()])
    A("dve", lambda e: e.memset(sm.t[:, SM_E6:SM_E6 + 1], 1e-6), W=[sm.r()])
    A("dve", lambda e: e.memset(sm.t[:, SM_E5:SM_E5 + 1], 1e-5), W=[sm.r()])
    A("act", lambda e: e.activation(out=sm.t[:, SM_A:SM_A + 16], in_=pc("alog", 0, 16), func=AF.Exp), R=[prm.r(), sm.r()], W=[sm.r()])
    A("dve", lambda e: e.tensor_scalar(out=sm.t[:, SM_A:SM_A + 16], in0=sm.t[:, SM_A:SM_A + 16], scalar1=-1.0, scalar2=None, op0=ALU.mult),
      R=[sm.r()], W=[sm.r()])
    A("act", lambda e: e.activation(out=sm.t[:, SM_ES:SM_ES + 16], in_=pc("sink", 0, 16), func=AF.Exp), R=[prm.r(), sm.r()], W=[sm.r()])
    A("dve", lambda e: e.tensor_scalar(out=sm.t[:, SM_QW:SM_QW + 1], in0=pc("qw"), scalar1=0.125, scalar2=None, op0=ALU.mult),
      R=[prm.r(), sm.r()], W=[sm.r()])
    A("dve", lambda e: e.tensor_copy(out=sm.t[:, SM_KW:SM_KW + 1], in_=pc("kw")), R=[prm.r(), sm.r()], W=[sm.r()])
    for j in range(8):
        A("dve", lambda e, j=j: e.tensor_scalar(out=diagD.t[:, j, :], in0=ident_f, scalar1=pc("dch", j), scalar2=None, op0=ALU.mult),
          R=[cst.r(), prm.r()], W=[diagD.r()])
    for i in range(4):
        A("dve", lambda e, i=i: e.memset(a48s[i].t[:], 0.0), W=[a48s[i].r()], c=48)
        A("dve", lambda e, i=i: e.memset(q48s[i].t[:], 0.0), W=[q48s[i].r()], c=48)
    A("dve", lambda e: e.memset(st_f.t[:], 0.0), W=[st_f.r()])
    A("dve", lambda e: e.memset(st_b.t[:], 0.0), W=[st_b.r()])
    A("dve", lambda e: e.memset(TL.t[:], 0.0), W=[TL.r()])
    A("dve", lambda e: e.memset(Kp.t[:].rearrange("p a b -> p (a b)"), 0.0), W=[Kp.r()])
    A("dve", lambda e: e.memset(Vx.t[:].rearrange("p a b c -> p (a b c)"), 0.0), W=[Vx.r()])
    A("dve", lambda e: e.memset(Vx.t[:, :, :, 64:65], 1.0), W=[Vx.r()])

    for j in range(8):
        A("dve", lambda e, j=j: e.tensor_tensor(out=dgt.t[:].rearrange("p (k c) -> p k c", c=128),
                                                in0=ident_f.unsqueeze(1).to_broadcast([128, 31, 128]),
                                                in1=pc("dww", j * 31, 31).unsqueeze(2).to_broadcast([128, 31, 128]), op=ALU.mult),
          R=[cst.r(), prm.r()], W=[dgt.r()], c=3968)
        A("sp", lambda e, j=j: e.dma_start(out=diag_d[j], in_=dgt.t[:]), R=[dgt.r()], W=[["dram_diag"]], dma="dgt")

    def wsrc(w, k0, nk, c0, ncol):
        return w[k0 * 128:(k0 + nk) * 128, c0:c0 + ncol].rearrange("(k p) c -> p k c", p=128)

    def loads_mixer0(pre):
        L = []
        if not pre:
            L += [("z0", win_d, 0, 8, C_Z, 512), ("z1", win_d, 0, 8, C_Z + 512, 512)]
        L += [("x0", win_d, 0, 8, C_X, 512), ("x1", win_d, 0, 8, C_X + 512, 512)]
        if pre:
            L += [("bc", win_d, 0, 8, C_B, 256)]
        else:
            L += [("bc", win_d, 0, 8, C_B, 512), ("q0", win_d, 0, 8, C_Q, 512), ("q1", win_d, 0, 8, C_Q + 512, 512)]
            L += [(f"o{i}", wout_d, 0, 16, 256 * i, 256) for i in range(4)]
        return L

    def loads_ffn(l):
        L = []
        for g in range(6):
            nc_ = 512 if g < 5 else 256
            L += [(f"g{g}", wup_d[l], 0, 8, 512 * g, nc_), (f"u{g}", wup_d[l], 0, 8, DFF + 512 * g, nc_)]
        L += [(f"d{f}", wdn_d[l], 0, 22, 128 * f, 128) for f in range(8)]
        return L

    def loads_conf():
        L = []
        for h in range(2):
            L += [(f"a{h}", pw1_d, 0, 8, 512 * h, 512), (f"s{h}", pw1_d, 0, 8, 1024 + 512 * h, 512)]
            L += [(f"dg{j}", None, j, 31, 0, 128) for j in range(4 * h, 4 * h + 4)]
        L += [(f"p{h}", pw2_d, 0, 8, 512 * h, 512) for h in range(2)]
        return L

    stream = []
    for _ in range(NPRE + 1):
        stream += loads_mixer0(True)
    for _ in range(NMAIN):
        stream += loads_mixer0(False) + loads_ffn(0) + loads_conf() + loads_ffn(1)
    wstate = {"issued": 0, "next": 0}
    released = set()
    auto_pending = []

    def issue_loads(upto):
        while wstate["issued"] < min(upto, len(stream)):
            i = wstate["issued"]
            if i >= NSLOT and (i - NSLOT) not in released:
                break
            _, w, k0, nk, c0, ncol = stream[i]
            slot = WS[i % NSLOT]

            def f(e, slot=slot, w=w, k0=k0, nk=nk, c0=c0, ncol=ncol):
                if w is None:
                    return e.dma_start(out=slot.t[:, 0:nk * ncol], in_=diag_d[k0])
                return e.dma_start(out=slot.t[:, 0:nk * ncol].rearrange("p (k c) -> p k c", c=ncol), in_=wsrc(w, k0, nk, c0, ncol))
            A("pool", f, R=[["dram_diag"]] if w is None else [], W=[slot.r()], dma=f"ws{i % NSLOT}", nbytes=nk * ncol * 128 * (2 if w is None else 4))
            wstate["issued"] += 1

    def wrel(*idxs):
        released.update(idxs)
        issue_loads(wstate["next"] + NSLOT)

    def wnext(tag, hold=False):
        i = wstate["next"]
        assert stream[i][0] == tag, (stream[i][0], tag)
        released.update(auto_pending)
        del auto_pending[:]
        issue_loads(i + NSLOT)
        assert wstate["issued"] > i, ("weight ring deadlock", tag)
        wstate["next"] += 1
        if not hold:
            auto_pending.append(i)
        _, w, k0, nk, c0, ncol = stream[i]
        slot = WS[i % NSLOT]
        return slot.t[:, 0:nk * ncol].rearrange("p (k c) -> p k c", c=ncol), slot.r(), i

    def mm_group(specs):
        def f(e):
            ins = None
            for (o, l, r, s0, s1) in specs:
                ins = e.matmul(o, lhsT=l, rhs=r, start=s0, stop=s1)
            return ins
        cost = 0.0
        for (o, l, r, s0, s1) in specs:
            n = int(np.prod(o.shape[1:]))
            cost += (max(n, 96) / 1.9) * (4.0 if l.dtype == F32 else 1.0) + 12.0
        f.cost = cost
        return f

    cfg = {"nt": 512}

    def tr_group(specs):
        def f(e):
            ins = None
            for (o, i_) in specs:
                ins = e.transpose(o, i_, ident_f)
            return ins
        f.cost = len(specs) * 110.0
        return f

    def rmsnorm(wname, wl, hn_, sq_, rs_, rr_, hs=None, bank=None):
        hs = hT if hs is None else hs
        nt = cfg["nt"]
        if bank is None:
            A("act", lambda e: e.activation(out=sq_.t[:, :, 0:nt], in_=hs.t[:, :, 0:nt], func=AF.Square), R=[hs.r()], W=[sq_.r()], c=8 * nt)
            b = psa()
            A("pe", mm_group([(PS[:, b, 0:nt], ones_b.t[:], sq_.t[:, kt, 0:nt], kt == 0, kt == 7) for kt in range(8)]),
              R=[sq_.r(), ones_b.r()], W=[pr(b)])
        else:
            b = bank
        A("act", lambda e: e.activation(out=rs_.t[:, 0:nt], in_=PS[:, b, 0:nt], func=AF.Ln, bias=sm.t[:, SM_E6:SM_E6 + 1], scale=1.0 / 1024),
          R=[pr(b), sm.r()], W=[rs_.r()], c=nt)
        A("act", lambda e: e.activation(out=rr_.t[:, 0:nt], in_=rs_.t[:, 0:nt], func=AF.Exp, scale=-0.5), R=[rs_.r()], W=[rr_.r()], c=nt)
        for kt in range(8):
            A("dve", lambda e, kt=kt: e.scalar_tensor_tensor(out=hn_.t[:, kt, 0:nt], in0=hs.t[:, kt, 0:nt], scalar=pc(wname, wl * 8 + kt),
                                                            in1=rr_.t[:, 0:nt], op0=ALU.mult, op1=ALU.mult),
              R=[hs.r(kt * 512, kt * 512 + 512), rr_.r(), prm.r()], W=[hn_.r(kt * 512, kt * 512 + 512)], c=nt)

    nrm = {"bank": None}

    def ssq_hook(f, sq_next):
        nt = cfg["nt"]
        A("act", lambda e: e.activation(out=sq_next.t[:, f, 0:nt], in_=hT.t[:, f, 0:nt], func=AF.Square),
          R=[hT.r(f * 512, f * 512 + 512)], W=[sq_next.r(f * 512, f * 512 + 512)], c=nt)
        if f == 0:
            nrm["bank"] = psa()
        b = nrm["bank"]
        A("pe", mm_group([(PS[:, b, 0:nt], ones_b.t[:], sq_next.t[:, f, 0:nt], f == 0, f == 7)]),
          R=[sq_next.r(f * 512, f * 512 + 512), ones_b.r()], W=[pr(b)])

    def proj_fm(wv, wres, jloc, hn_):
        nt = cfg["nt"]
        b = psa()
        A("pe", mm_group([(PS[:, b, 0:nt], wv[:, kt, jloc * 128:(jloc + 1) * 128], hn_.t[:, kt, 0:nt], kt == 0, kt == 7) for kt in range(8)]),
          R=[wres, hn_.r()], W=[pr(b)])
        return b

    def conv_fm(b, xp_, acc_, K, wcol, bcol, tl_off):
        nt = cfg["nt"]
        A("act", lambda e: e.activation(out=xp_.t[:, 0:K - 1], in_=TL.t[:, tl_off:tl_off + K - 1], func=AF.Copy),
          R=[TL.r(tl_off, tl_off + K - 1)], W=[xp_.r(0, K - 1)], c=4)
        A("act", lambda e: e.activation(out=xp_.t[:, K - 1:K - 1 + nt], in_=PS[:, b, 0:nt], func=AF.Copy), R=[pr(b)], W=[xp_.r(K - 1, K + 511)], c=nt)
        A("act", lambda e: e.activation(out=acc_.t[:, 0:nt], in_=PS[:, b, 0:nt], func=AF.Identity, scale=wcol(K - 1), bias=bcol),
          R=[pr(b), prm.r()], W=[acc_.r()], c=nt)
        for k in range(K - 2, -1, -1):
            A("dve", lambda e, k=k: e.scalar_tensor_tensor(out=acc_.t[:, 0:nt], in0=xp_.t[:, k:k + nt], scalar=wcol(k), in1=acc_.t[:, 0:nt],
                                                          op0=ALU.mult, op1=ALU.add), R=[xp_.r(), acc_.r(), prm.r()], W=[acc_.r()], c=nt)
        A("act", lambda e: e.activation(out=TL.t[:, tl_off:tl_off + K - 1], in_=xp_.t[:, nt:nt + K - 1], func=AF.Copy),
          R=[xp_.r()], W=[TL.r(tl_off, tl_off + K - 1)], c=4)

    def dump(name, tt):
        if name in dbg_d:
            A("sp", lambda e: e.dma_start(out=dbg_d[name][:, 0:tt.n], in_=tt.t[:].rearrange("p a b -> p (a b)")),
              R=[tt.r()], dma="dbg_" + name)

    def hchunk(hd, c):
        return [x for j in range(8) for x in hd.r(j * 512 + c * 128, j * 512 + c * 128 + 128)]

    def stage_in(row0, hd=None):
        hd = hT if hd is None else hd
        for c in range(cfg["nt"] // 128):
            xi = xin[c % 2]
            r0 = row0 + c * 128
            A("sp", lambda e, xi=xi, r0=r0: e.dma_start(out=xi.t[:], in_=x_d[r0:r0 + 128, :]), W=[xi.r()], dma=xi.name)
            b = psa(2)
            A("pe", tr_group([(PS[:, b + j // 4, (j % 4) * 128:(j % 4) * 128 + 128], xi.t[:, j * 128:(j + 1) * 128]) for j in range(8)]),
              R=[xi.r(), cst.r()], W=[pr(b, 2)])
            A("act", lambda e, b=b, c=c: e.activation(out=hd.t[:, :, c * 128:(c + 1) * 128],
                                                      in_=PS[:, b:b + 2, :].rearrange("p a (j t) -> p (a j) t", t=128), func=AF.Copy),
              R=[pr(b, 2)], W=[hchunk(hd, c)], c=1024)

    def stage_out(orow0):
        for c in range(cfg["nt"] // 128):
            xo = xout[c % 2]
            r0 = orow0 + c * 128
            b = psa(2)
            A("pe", tr_group([(PS[:, b + j // 4, (j % 4) * 128:(j % 4) * 128 + 128], hT.t[:, j, c * 128:(c + 1) * 128]) for j in range(8)]),
              R=[hchunk(hT, c), cst.r()], W=[pr(b, 2)])
            A("act", lambda e, b=b, xo=xo: e.activation(out=xo.t[:], in_=PS[:, b:b + 2, :].rearrange("p a t -> p (a t)"), func=AF.Copy),
              R=[pr(b, 2)], W=[xo.r()], c=1024)
            A("sp", lambda e, xo=xo, r0=r0: e.dma_start(out=y_d[r0:r0 + 128, :], in_=xo.t[:]), R=[xo.r()], dma=xo.name)

    def ssd_chunk(c, pre, bs):
        hn, sq, xsT, BCT, dtr, dtv, lndt, _h = bs
        xtok_, btok_, xw_ = (xtok, btok, xw) if not pre else (PRE_A if bs is SET_A else PRE_B)
        s16, acsT, qT, a48, q48 = s16s[c], acsTs[c], qTs[c], a48s[c], q48s[c]
        S = lambda i: s16.t[:, i, :]
        sr = s16.r()
        cs = slice(c * 128, (c + 1) * 128)
        dup = lambda ap: ap.unsqueeze(1).to_broadcast([128, 2, 16])
        v48 = lambda t_: t_.t[:].rearrange("p (r c) -> p r c", c=16)[:, 0:3:2, :]
        A("dve", lambda e: e.tensor_tensor(out=v48(a48), in0=dup(dtv.t[:, c, :]), in1=dup(sm.t[:, SM_A:SM_A + 16]), op=ALU.mult),
          R=[dtv.r(), sm.r()], W=[a48.r()], c=32)
        a_ = a48.t[:, 0:16]
        b = psa()
        A("pe", mm_group([(PS[:, b, 0:16], U_f, a_, True, True),
                          (PS[0:48, b, 16:144], a48.t[:], U_f, True, True),
                          (PS[:, b, 144:160], ones_f.t[:], a_, True, True)]), R=[a48.r(), cst.r(), ones_f.r()], W=[pr(b)])
        A("dve", lambda e: e.tensor_tensor(out=v48(q48), in0=dup(lndt.t[:, c, :]), in1=dup(PS[:, b, 0:16]), op=ALU.subtract),
          R=[lndt.r(), pr(b)], W=[q48.r()], c=32)
        A("dve", lambda e: e.tensor_tensor(out=S(3), in0=q48.t[:, 0:16], in1=PS[:, b, 144:160], op=ALU.add), R=[q48.r(), pr(b)], W=[sr], c=16)
        A("act", lambda e: e.activation(out=S(4), in_=S(3), func=AF.Exp), R=[sr], W=[sr], c=16)
        A("act", lambda e: e.activation(out=S(6), in_=PS[:, b, 144:160], func=AF.Exp), R=[pr(b)], W=[sr], c=16)
        if not pre:
            A("act", lambda e: e.activation(out=S(5), in_=PS[:, b, 0:16], func=AF.Exp), R=[pr(b)], W=[sr], c=16)
            A("act", lambda e: e.activation(out=acsT.t[:], in_=PS[0:48, b, 16:144], func=AF.Copy), R=[pr(b)], W=[acsT.r()], c=128)
            A("dve", lambda e: e.tensor_tensor(out=acsT.t[32:48, :], in0=PS[32:48, b, 16:144], in1=acsT.t[32:48, :], op=ALU.subtract),
              R=[pr(b), acsT.r()], W=[acsT.r()], c=128)
            b2 = psa()
            A("pe", mm_group([(PS[0:48, b2, 0:128], q48.t[:], ident_f, True, True)]), R=[q48.r(), cst.r()], W=[pr(b2)])
            A("act", lambda e: e.activation(out=qT.t[:], in_=PS[0:48, b2, 0:128], func=AF.Copy), R=[pr(b2)], W=[qT.r()], c=128)
            A("dve", lambda e: e.tensor_tensor(out=qT.t[32:48, :], in0=PS[32:48, b2, 0:128], in1=qT.t[32:48, :], op=ALU.subtract),
              R=[pr(b2), qT.r()], W=[qT.r()], c=128)
            bcb = psa()
            A("pe", mm_group([(PS[:, bcb, g * 128:(g + 1) * 128], BCT.t[:, g, cs], BCT.t[:, 2 + g, cs], True, True) for g in range(2)]),
              R=[BCT.r()], W=[pr(bcb)])
            for half in range(2):
                bb = psa(2)
                specs = []
                for hh in range(8):
                    h = half * 8 + hh
                    o = PS[:, bb + hh // 4, (hh % 4) * 128:(hh % 4) * 128 + 128]
                    specs += [(o, Eall.t[:, h, :], acsT.t[:], True, False), (o, qT.t[:], Eall.t[:, h, :], False, False),
                              (o, ident_b, negm_b, False, True)]
                A("pe", mm_group(specs), R=[Eall.r(), acsT.r(), qT.r(), cbf.r()], W=[pr(bb, 2)])
                dc = dec[half]
                A("act", lambda e, bb=bb, dc=dc: e.activation(out=dc.t[:], in_=PS[:, bb:bb + 2, :].rearrange("p a t -> p (a t)"), func=AF.Exp),
                  R=[pr(bb, 2)], W=[dc.r()], c=1024)
                A("dve", lambda e, half=half, dc=dc: e.tensor_tensor(
                    out=MT.t[:, half * 8:half * 8 + 8, :], in0=dc.t[:].rearrange("p (h l) -> p h l", l=128),
                    in1=PS[:, bcb, half * 128:half * 128 + 128].unsqueeze(1).to_broadcast([128, 8, 128]), op=ALU.mult),
                  R=[dc.r(), pr(bcb)], W=[MT.r(half * 1024, half * 1024 + 1024)], c=1024)
        bx = psa(2)
        A("pe", mm_group([(PS[:, bx + j // 4, (j % 4) * 128:(j % 4) * 128 + 128], xsT.t[:, j, cs], ident_b, True, True) for j in range(8)]),
          R=[xsT.r(), cbf.r()], W=[pr(bx, 2)])
        A("act", lambda e: e.activation(out=xtok_.t[:], in_=PS[:, bx:bx + 2, :].rearrange("p a t -> p (a t)"), func=AF.Copy),
          R=[pr(bx, 2)], W=[xtok_.r()], c=1024)
        bB = psa()
        A("pe", mm_group([(PS[:, bB, g * 128:(g + 1) * 128], BCT.t[:, g, cs], ident_b, True, True) for g in range(2)]),
          R=[BCT.r(), cbf.r()], W=[pr(bB)])
        A("act", lambda e: e.activation(out=btok_.t[:], in_=PS[:, bB, 0:256], func=AF.Copy), R=[pr(bB)], W=[btok_.r()], c=256)
        A("dve", lambda e: e.tensor_tensor(out=xw_.t[:].rearrange("p (h d) -> p h d", d=64), in0=xtok_.t[:].rearrange("p (h d) -> p h d", d=64),
                                           in1=S(4).unsqueeze(2).to_broadcast([128, 16, 64]), op=ALU.mult), R=[xtok_.r(), sr], W=[xw_.r()], c=1024)
        if not pre:
            by = psa(2)
            specs = []
            for h in range(16):
                o = PS[:, by + h // 8, (h % 8) * 64:(h % 8) * 64 + 64]
                specs += [(o, MT.t[:, h, :], xtok_.t[:, h * 64:(h + 1) * 64], True, False),
                          (o, xsT.t[:, h // 2, cs], diagD.t[:, h // 2, (h % 2) * 64:(h % 2) * 64 + 64], False, True)]
            A("pe", mm_group(specs), R=[MT.r(), xtok_.r(), xsT.r(), diagD.r()], W=[pr(by, 2)])
            bo = psa(2)
            A("pe", mm_group([(PS[:, bo + g, :], BCT.t[:, 2 + g, cs], st_b.t[:, g * 512:(g + 1) * 512], True, True) for g in range(2)]),
              R=[BCT.r(), st_b.r()], W=[pr(bo, 2)])
            A("dve", lambda e: e.tensor_tensor(out=tmpf.t[:].rearrange("p (h d) -> p h d", d=64),
                                               in0=PS[:, bo:bo + 2, :].rearrange("p a (h d) -> p (a h) d", d=64),
                                               in1=S(5).unsqueeze(2).to_broadcast([128, 16, 64]), op=ALU.mult), R=[pr(bo, 2), sr], W=[tmpf.r()], c=1024)
            A("dve", lambda e: e.tensor_tensor(out=yv.t[:], in0=PS[:, by:by + 2, :].rearrange("p a t -> p (a t)"), in1=tmpf.t[:], op=ALU.add),
              R=[pr(by, 2), tmpf.r()], W=[yv.r()], c=1024)
        bs = psa(2)
        A("pe", mm_group([(PS[:, bs + g, :], btok_.t[:, g * 128:(g + 1) * 128], xw_.t[:, g * 512:(g + 1) * 512], True, True) for g in range(2)]),
          R=[btok_.r(), xw_.r()], W=[pr(bs, 2)])
        A("dve", lambda e: e.tensor_tensor(out=st_f.t[:].rearrange("p (h d) -> p h d", d=64), in0=st_f.t[:].rearrange("p (h d) -> p h d", d=64),
                                           in1=S(6).unsqueeze(2).to_broadcast([128, 16, 64]), op=ALU.mult), R=[st_f.r(), sr], W=[st_f.r()], c=1024)
        A("dve", lambda e: e.tensor_tensor(out=st_f.t[:], in0=st_f.t[:], in1=PS[:, bs:bs + 2, :].rearrange("p a t -> p (a t)"), op=ALU.add),
          R=[st_f.r(), pr(bs, 2)], W=[st_f.r()], c=1024)
        A("act", lambda e: e.activation(out=st_b.t[:], in_=st_f.t[:], func=AF.Copy), R=[st_f.r()], W=[st_b.r()], c=1024)
        if pre:
            return
        A("dve", lambda e: e.tensor_tensor(out=yv.t[:], in0=yv.t[:], in1=sz.t[:, c, :], op=ALU.mult), R=[yv.r(), sz.r(c * 1024, c * 1024 + 1024)], W=[yv.r()], c=1024)
        A("dve", lambda e: e.memset(S(7)[:, 0:2], 0.0), W=[sr], c=2)
        for g in range(2):
            A("act", lambda e, g=g: e.activation(out=tmpf.t[:, g * 512:(g + 1) * 512], in_=yv.t[:, g * 512:(g + 1) * 512], func=AF.Square,
                                                 accum_out=S(7)[:, g:g + 1]), R=[yv.r(), sr], W=[tmpf.r(), sr])
        A("act", lambda e: e.activation(out=S(8)[:, 0:2], in_=S(7)[:, 0:2], func=AF.Ln, bias=sm.t[:, SM_E6:SM_E6 + 1], scale=1.0 / 512),
          R=[sr, sm.r()], W=[sr], c=2)
        A("act", lambda e: e.activation(out=S(9)[:, 0:2], in_=S(8)[:, 0:2], func=AF.Exp, scale=-0.5), R=[sr], W=[sr], c=2)
        for g in range(2):
            A("act", lambda e, g=g: e.activation(out=yn.t[:, g * 512:(g + 1) * 512], in_=yv.t[:, g * 512:(g + 1) * 512], func=AF.Copy,
                                                 scale=S(9)[:, g:g + 1]), R=[yv.r(), sr], W=[yn.r(g * 512, g * 512 + 512)])
        bt = psa(2)
        A("pe", mm_group([(PS[:, bt + j // 4, (j % 4) * 128:(j % 4) * 128 + 128], yn.t[:, j * 128:(j + 1) * 128], ident_b, True, True)
                          for j in range(8)]), R=[yn.r(), cbf.r()], W=[pr(bt, 2)])
        for j in range(8):
            A("act", lambda e, j=j: e.activation(out=mixT.t[:, j, cs], in_=PS[:, bt + j // 4, (j % 4) * 128:(j % 4) * 128 + 128], func=AF.Copy,
                                                 scale=pc("snw", j)), R=[pr(bt + j // 4), prm.r()], W=[mixT.r(j * 512 + c * 128, j * 512 + c * 128 + 128)], c=128)

    def attn_chunk(c):
        cs = slice(c * 128, (c + 1) * 128)
        S = lambda i: a16s[c].t[:, i - 10, :]
        sr = a16s[c].r()
        gi = 0
        for kv in range(2):
            for pad in range(2):
                h0 = 8 * kv + pad
                pts = []
                for kb in range(2):
                    b = psa()
                    ov_ = PS[:, b, :].rearrange("p (j q) -> p j q", q=128)
                    A("pe", mm_group([(ov_, Kp.t[:, 2 * kv + pad, (c + kb) * 128:(c + kb + 1) * 128], qnT.t[:, 4 * kv:4 * kv + 4, cs], True, False),
                                      (ov_, ident_b, biasT.t[:, h0:h0 + 7:2, kb, :], False, True)]),
                      R=[Kp.r(), qnT.r(), biasT.r(), cbf.r()], W=[pr(b)])
                    p_ = pT[(gi % 2) * 2 + kb]
                    A("act", lambda e, b=b, p_=p_: e.activation(out=p_.t[:], in_=PS[:, b, :], func=AF.Exp), R=[pr(b)], W=[p_.r()])
                    pts.append(p_)
                o = psa()
                specs = []
                for j in range(4):
                    for kb in range(2):
                        specs.append((PS[:, o, j * 65:(j + 1) * 65], pts[kb].t[:, j * 128:(j + 1) * 128], Vx.t[:, c + kb, kv, :], kb == 0, kb == 1))
                A("pe", mm_group(specs), R=[pts[0].r(), pts[1].r(), Vx.r()], W=[pr(o)])
                ov = PS[:, o, 0:260].rearrange("p (j d) -> p j d", d=65)
                A("dve", lambda e, ov=ov, h0=h0: e.tensor_tensor(out=S(10)[:, 0:4], in0=ov[:, :, 64], in1=sm.t[:, SM_ES + h0:SM_ES + h0 + 7:2], op=ALU.add),
                  R=[pr(o), sm.r()], W=[sr])
                A("dve", lambda e: e.reciprocal(out=S(11)[:, 0:4], in_=S(10)[:, 0:4]), R=[sr], W=[sr])
                A("dve", lambda e, ov=ov, h0=h0: e.tensor_tensor(out=attn.t[:].rearrange("p (h d) -> p h d", d=64)[:, h0:h0 + 7:2, :], in0=ov[:, :, 0:64],
                                                                in1=S(11)[:, 0:4].unsqueeze(2).to_broadcast([128, 4, 64]), op=ALU.mult),
                  R=[pr(o), sr], W=[attn.r()])
                gi += 1
        bt = psa(2)
        A("pe", mm_group([(PS[:, bt + j // 4, (j % 4) * 128:(j % 4) * 128 + 128], attn.t[:, j * 128:(j + 1) * 128], ident_b, True, True)
                          for j in range(8)]), R=[attn.r(), cbf.r()], W=[pr(bt, 2)])
        A("act", lambda e: e.activation(out=mixT.t[:, 8:16, cs], in_=PS[:, bt:bt + 2, :].rearrange("p a (j t) -> p (a j) t", t=128), func=AF.Copy),
          R=[pr(bt, 2)], W=[mixT.r(8 * 512, 16 * 512)], c=1024)

    def mixer0(pre, bs=None):
        bs = SET_A if bs is None else bs
        hn, sq, xsT, BCT, dtr, dtv, lndt, hsrc = bs
        nt = cfg["nt"]
        nch = nt // 128
        rmsnorm("mixw", 0, hn, sq, rs_s, rs_r, hsrc)
        if not pre:
            wz0, rz0, iz0 = wnext("z0", hold=True)
            wz1, rz1, iz1 = wnext("z1", hold=True)
            for c in range(nch):
                b = psa(2)
                specs = []
                for hf, wv in ((0, wz0), (1, wz1)):
                    specs += [(PS[:, b + hf, :], hn.t[:, kt, c * 128:(c + 1) * 128], wv[:, kt, :], kt == 0, kt == 7) for kt in range(8)]
                A("pe", mm_group(specs), R=[hn.r(), rz0, rz1], W=[pr(b, 2)])
                A("act", lambda e, b=b, c=c: e.activation(out=sz.t[:, c, :], in_=PS[:, b:b + 2, :].rearrange("p a t -> p (a t)"), func=AF.Silu),
                  R=[pr(b, 2)], W=[sz.r(c * 1024, c * 1024 + 1024)], c=1024)
            wrel(iz0, iz1)
        for c in range(nch):
            b = psa()
            A("pe", mm_group([(PS[:, b, 0:144], hn.t[:, kt, c * 128:(c + 1) * 128], WDV.t[:, kt, :], kt == 0, kt == 7) for kt in range(8)]),
              R=[hn.r(), WDV.r()], W=[pr(b)])
            A("dve", lambda e, b=b, c=c: e.tensor_tensor(out=dtr.t[:, c, :], in0=PS[:, b, 0:16], in1=pc("dtb", 0, 16), op=ALU.add),
              R=[pr(b), prm.r()], W=[dtr.r()])
            if not pre:
                A("act", lambda e, b=b, c=c: e.activation(out=Vx.t[:, 1 + c, :, 0:64], in_=PS[:, b, 16:144].rearrange("p (k d) -> p k d", d=64),
                                                          func=AF.Copy), R=[pr(b)], W=[Vx.r()])
        fl = lambda t_: t_.t[:, 0:nch, :].rearrange("p a b -> p (a b)")
        A("act", lambda e: e.activation(out=fl(dtr), in_=fl(dtr), func=AF.Exp), R=[dtr.r()], W=[dtr.r()])
        A("act", lambda e: e.activation(out=fl(dtv), in_=fl(dtr), func=AF.Ln, bias=1.0), R=[dtr.r()], W=[dtv.r()])
        A("act", lambda e: e.activation(out=fl(lndt), in_=fl(dtv), func=AF.Ln), R=[dtv.r()], W=[lndt.r()])
        tiles = list(range(10)) if pre else list(range(12))
        wv = wr = None
        for j in tiles:
            if j == 0:
                wv, wr, _ = wnext("x0")
            elif j == 4:
                wv, wr, _ = wnext("x1")
            elif j == 8:
                wv, wr, _ = wnext("bc")
            b = proj_fm(wv, wr, j % 4, hn)
            xp_, acc_ = xp[j % 2], acc[j % 2]
            conv_fm(b, xp_, acc_, 4, lambda k, j=j: pc("cw", j * 4 + k), pc("cb", j), TL_A + 3 * j)
            dst = xsT.t[:, j, 0:nt] if j < 8 else BCT.t[:, j - 8, 0:nt]
            dres = xsT.r(j * 512, j * 512 + 512) if j < 8 else BCT.r((j - 8) * 512, (j - 8) * 512 + 512)
            A("act", lambda e, acc_=acc_, dst=dst: e.activation(out=dst, in_=acc_.t[:, 0:nt], func=AF.Silu), R=[acc_.r()], W=[dres], c=nt)
        if not pre:
            for j in range(12):
                if j == 0:
                    wv, wr, _ = wnext("q0")
                elif j == 4:
                    wv, wr, _ = wnext("q1")
                if j < 8:
                    b = proj_fm(wv, wr, j % 4, hn)
                else:
                    b = psa()
                    t = j - 8
                    A("pe", mm_group([(PS[:, b, 0:nt], WK.t[:, kt, t, :], hn.t[:, kt, 0:nt], kt == 0, kt == 7) for kt in range(8)]),
                      R=[WK.r(), hn.r()], W=[pr(b)])
                sq_ = sqq[j % 2]
                A("act", lambda e, b=b, sq_=sq_: e.activation(out=sq_.t[:, 0:nt], in_=PS[:, b, 0:nt], func=AF.Square), R=[pr(b)], W=[sq_.r()], c=nt)
                b2 = psa()
                A("pe", mm_group([(PS[:, b2, 0:nt], bd_b, sq_.t[:, 0:nt], True, True)]), R=[sq_.r(), cbf.r()], W=[pr(b2)])
                rs1, rs2 = (rs_s, rs_r) if j % 2 == 0 else (rs_s2, rs_r2)
                A("act", lambda e, b2=b2, rs1=rs1: e.activation(out=rs1.t[:, 0:nt], in_=PS[:, b2, 0:nt], func=AF.Ln, bias=sm.t[:, SM_E6:SM_E6 + 1], scale=1.0 / 64),
                  R=[pr(b2), sm.r()], W=[rs1.r()], c=nt)
                A("act", lambda e, rs1=rs1, rs2=rs2: e.activation(out=rs2.t[:, 0:nt], in_=rs1.t[:, 0:nt], func=AF.Exp, scale=-0.5), R=[rs1.r()], W=[rs2.r()], c=nt)
                if j < 8:
                    dst, dres, wc = qnT.t[:, j, 0:nt], qnT.r(j * 512, j * 512 + 512), sm.t[:, SM_QW:SM_QW + 1]
                else:
                    dst, dres, wc = Kp.t[:, j - 8, 128:128 + nt], Kp.r(), sm.t[:, SM_KW:SM_KW + 1]
                A("dve", lambda e, b=b, dst=dst, wc=wc, rs2=rs2: e.scalar_tensor_tensor(out=dst, in0=PS[:, b, 0:nt], scalar=wc, in1=rs2.t[:, 0:nt], op0=ALU.mult, op1=ALU.mult),
                  R=[pr(b), rs2.r(), sm.r()], W=[dres], c=nt)
        lab0 = P.label
        pool0 = dict(pspool)
        for c in range(nch):
            P.label = lab0 + f".ssd{c}"
            if not pre:
                pspool.update(lo=SSD_POOL[0], n=SSD_POOL[1])
            ssd_chunk(c, pre, bs)
            if not pre:
                P.label = lab0 + f".att{c}"
                pspool.update(lo=ATT_POOL[0], n=ATT_POOL[1])
                attn_chunk(c)
        pspool.update(pool0)
        P.label = lab0 + ".oproj"
        if pre:
            return
        A("act", lambda e: e.activation(out=Kp.t[:, :, 0:128], in_=Kp.t[:, :, nt:nt + 128], func=AF.Copy), R=[Kp.r()], W=[Kp.r()])
        A("act", lambda e: e.activation(out=Vx.t[:, 0, :, :], in_=Vx.t[:, nch, :, :], func=AF.Copy), R=[Vx.r()], W=[Vx.r()])
        for i in range(4):
            wv, wr, _ = wnext(f"o{i}")
            for f2 in range(2):
                f = 2 * i + f2
                b = psa()
                A("pe", mm_group([(PS[:, b, 0:nt], wv[:, kt, f2 * 128:(f2 + 1) * 128], mixT.t[:, kt, 0:nt], kt == 0, kt == 15) for kt in range(16)]),
                  R=[wr, mixT.r()], W=[pr(b)])
                A("dve", lambda e, b=b, f=f: e.tensor_tensor(out=hT.t[:, f, 0:nt], in0=hT.t[:, f, 0:nt], in1=PS[:, b, 0:nt], op=ALU.add),
                  R=[pr(b), hT.r(f * 512, f * 512 + 512)], W=[hT.r(f * 512, f * 512 + 512)], c=nt)
                ssq_hook(f, f_sq)

    def ffn(l, last, skip_down=False):
        nt = cfg["nt"]
        rmsnorm("ffnw", l, f_hn, f_sq, f_rs, f_rr, bank=nrm["bank"])
        for g in range(6):
            ntl = 4 if g < 5 else 2
            wg, rg, ig = wnext(f"g{g}", hold=True)
            wu, ru, iu = wnext(f"u{g}", hold=True)
            for i in range(ntl):
                t = 4 * g + i
                bg = proj_fm(wg, rg, i, f_hn)
                bu = proj_fm(wu, ru, i, f_hn)
                k2 = (t % 4) * 2
                for (b, tt, xi) in ((bg, t, k2), (bu, 22 + t, k2 + 1)):
                    conv_fm(b, f_xp[xi], f_acc[xi], 3, lambda k, tt=tt: pc("fcw", (l * 44 + tt) * 3 + k), pc("fcb", l * 44 + tt),
                            TL_F + l * 88 + 2 * tt)
                sg = f_sg[t % 4]
                ag, au = f_acc[k2], f_acc[k2 + 1]
                A("act", lambda e, sg=sg, ag=ag: e.activation(out=sg.t[:, 0:nt], in_=ag.t[:, 0:nt], func=AF.Silu), R=[ag.r()], W=[sg.r()])
                A("dve", lambda e, sg=sg, au=au, t=t: e.tensor_tensor(out=actT.t[:, t, 0:nt], in0=sg.t[:, 0:nt], in1=au.t[:, 0:nt], op=ALU.mult),
                  R=[sg.r(), au.r()], W=[actT.r(t * 512, t * 512 + 512)])
            wrel(ig, iu)
        for f in range(8):
            wv, wr, _ = wnext(f"d{f}")
            if skip_down:
                continue
            b = psa()
            A("pe", mm_group([(PS[:, b, 0:nt], wv[:, kt, :], actT.t[:, kt, 0:nt], kt == 0, kt == 21) for kt in range(22)]),
              R=[wr, actT.r()], W=[pr(b)])
            A("dve", lambda e, b=b, f=f: e.tensor_tensor(out=hT.t[:, f, 0:nt], in0=hT.t[:, f, 0:nt], in1=PS[:, b, 0:nt], op=ALU.add),
              R=[pr(b), hT.r(f * 512, f * 512 + 512)], W=[hT.r(f * 512, f * 512 + 512)])
            if not last:
                ssq_hook(f, c_sq)

    def conformer():
        nt = cfg["nt"]
        rmsnorm("mixw", 1, c_hn, c_sq, c_rs, c_rr, bank=nrm["bank"])
        for half in range(2):
            wa, ra, ia = wnext(f"a{half}", hold=True)
            wg, rg, ig = wnext(f"s{half}", hold=True)
            for i in range(4):
                j = half * 4 + i
                ba = proj_fm(wa, ra, i, c_hn)
                bg = proj_fm(wg, rg, i, c_hn)
                sg = c_sig[j % 2]
                A("act", lambda e, bg=bg, sg=sg, j=j: e.activation(out=sg.t[:, 0:nt], in_=PS[:, bg, 0:nt], func=AF.Sigmoid, bias=pc("pw1b", 8 + j)),
                  R=[pr(bg), prm.r()], W=[sg.r()])
                ur = upad.r(j * 544, j * 544 + 544)
                A("dve", lambda e, ba=ba, sg=sg, j=j: e.scalar_tensor_tensor(out=upad.t[:, j, 30:30 + nt], in0=PS[:, ba, 0:nt], scalar=pc("pw1b", j),
                                                                            in1=sg.t[:, 0:nt], op0=ALU.add, op1=ALU.mult),
                  R=[pr(ba), sg.r(), prm.r()], W=[ur])
                A("act", lambda e, j=j: e.activation(out=upad.t[:, j, 0:30], in_=TL.t[:, TL_C + 30 * j:TL_C + 30 * j + 30], func=AF.Copy),
                  R=[TL.r(TL_C + 30 * j, TL_C + 30 * j + 30)], W=[ur], c=30)
                A("act", lambda e, j=j: e.activation(out=TL.t[:, TL_C + 30 * j:TL_C + 30 * j + 30], in_=upad.t[:, j, nt:nt + 30], func=AF.Copy),
                  R=[ur], W=[TL.r(TL_C + 30 * j, TL_C + 30 * j + 30)], c=30)
            wrel(ia, ig)
            for i in range(4):
                j = half * 4 + i
                ur = upad.r(j * 544, j * 544 + 544)
                yr = yc.r(j * 512, j * 512 + 512)
                wd, rd, _ = wnext(f"dg{j}")
                bc_ = psa()
                A("pe", mm_group([(PS[:, bc_, 0:nt], wd[:, k, :], upad.t[:, j, k:k + nt], k == NDV, k == 30) for k in range(NDV, 31)]),
                  R=[rd, ur], W=[pr(bc_)])
                A("dve", lambda e, j=j: e.tensor_scalar(out=yc.t[:, j, 0:nt], in0=upad.t[:, j, 0:nt], scalar1=pc("dww", j * 31), scalar2=None, op0=ALU.mult),
                  R=[ur, prm.r()], W=[yr])
                for k in range(1, NDV):
                    A("dve", lambda e, j=j, k=k: e.scalar_tensor_tensor(out=yc.t[:, j, 0:nt], in0=upad.t[:, j, k:k + nt], scalar=pc("dww", j * 31 + k),
                                                                       in1=yc.t[:, j, 0:nt], op0=ALU.mult, op1=ALU.add), R=[ur, yr, prm.r()], W=[yr])
                A("dve", lambda e, j=j, bc_=bc_: e.scalar_tensor_tensor(out=yc.t[:, j, 0:nt], in0=PS[:, bc_, 0:nt], scalar=pc("dwb", j), in1=yc.t[:, j, 0:nt],
                                                                       op0=ALU.add, op1=ALU.add), R=[pr(bc_), yr, prm.r()], W=[yr])
                A("act", lambda e, j=j: e.activation(out=ybf.t[:, j, 0:nt], in_=yc.t[:, j, 0:nt], func=AF.Copy), R=[yr], W=[ybf.r(j * 512, j * 512 + 512)])
                A("act", lambda e, j=j: e.activation(out=c_sq.t[:, j, 0:nt], in_=yc.t[:, j, 0:nt], func=AF.Square), R=[yr], W=[c_sq.r(j * 512, j * 512 + 512)])
        b1 = psa()
        A("pe", mm_group([(PS[:, b1, 0:nt], ones_b.t[:], ybf.t[:, j, 0:nt], j == 0, j == 7) for j in range(8)]), R=[ybf.r(), ones_b.r()], W=[pr(b1)])
        b2 = psa()
        A("pe", mm_group([(PS[:, b2, 0:nt], ones_b.t[:], c_sq.t[:, j, 0:nt], j == 0, j == 7) for j in range(8)]), R=[c_sq.r(), ones_b.r()], W=[pr(b2)])
        A("dve", lambda e: e.tensor_scalar(out=c_mean.t[:, 0:nt], in0=PS[:, b1, 0:nt], scalar1=1.0 / 1024, scalar2=None, op0=ALU.mult), R=[pr(b1)], W=[c_mean.r()])
        A("dve", lambda e: e.tensor_tensor(out=c_msq.t[:, 0:nt], in0=c_mean.t[:, 0:nt], in1=c_mean.t[:, 0:nt], op=ALU.mult), R=[c_mean.r()], W=[c_msq.r()])
        A("dve", lambda e: e.scalar_tensor_tensor(out=c_var.t[:, 0:nt], in0=PS[:, b2, 0:nt], scalar=1.0 / 1024, in1=c_msq.t[:, 0:nt], op0=ALU.mult, op1=ALU.subtract),
          R=[pr(b2), c_msq.r()], W=[c_var.r()])
        A("act", lambda e: e.activation(out=c_rs.t[:, 0:nt], in_=c_var.t[:, 0:nt], func=AF.Ln, bias=sm.t[:, SM_E5:SM_E5 + 1], scale=1.0), R=[c_var.r(), sm.r()], W=[c_rs.r()])
        A("act", lambda e: e.activation(out=c_rr.t[:, 0:nt], in_=c_rs.t[:, 0:nt], func=AF.Exp, scale=-0.5), R=[c_rs.r()], W=[c_rr.r()])
        for j in range(8):
            yr = yc.r(j * 512, j * 512 + 512)
            A("dve", lambda e, j=j: e.tensor_tensor(out=yc.t[:, j, 0:nt], in0=yc.t[:, j, 0:nt], in1=c_mean.t[:, 0:nt], op=ALU.subtract), R=[yr, c_mean.r()], W=[yr])
            A("dve", lambda e, j=j: e.scalar_tensor_tensor(out=yc.t[:, j, 0:nt], in0=yc.t[:, j, 0:nt], scalar=pc("lnw", j), in1=c_rr.t[:, 0:nt], op0=ALU.mult, op1=ALU.mult),
              R=[yr, c_rr.r(), prm.r()], W=[yr])
            A("act", lambda e, j=j: e.activation(out=ybf.t[:, j, 0:nt], in_=yc.t[:, j, 0:nt], func=AF.Silu, bias=pc("lnb", j)), R=[yr, prm.r()],
              W=[ybf.r(j * 512, j * 512 + 512)])
        for half in range(2):
            wv, wr, _ = wnext(f"p{half}")
            for i in range(4):
                f = half * 4 + i
                b = proj_fm(wv, wr, i, ybf)
                A("dve", lambda e, b=b, f=f: e.scalar_tensor_tensor(out=hT.t[:, f, 0:nt], in0=PS[:, b, 0:nt], scalar=pc("pw2b", f), in1=hT.t[:, f, 0:nt],
                                                                   op0=ALU.add, op1=ALU.add),
                  R=[pr(b), hT.r(f * 512, f * 512 + 512), prm.r()], W=[hT.r(f * 512, f * 512 + 512)])
                ssq_hook(f, f_sq)

    blocks = [("pre", 512 * i, 512) for i in range(NPRE)] + [("pre", 512 * NPRE, 256), ("warm", 512 * NPRE + 256, 256)]
    blocks += [("main", 512 * (NPRE + k), 512) for k in range(1, NMAIN)]
    npre_seen = 0
    nout = 0
    for bi, (kind, row0, ntok) in enumerate(blocks):
        cfg["nt"] = ntok
        P.label = f"b{bi}.in"
        if kind == "pre":
            par = npre_seen % 2
            npre_seen += 1
            stage_in(row0, hTB if par == 1 else hT)
            P.label = f"b{bi}.pre"
            mixer0(True, SET_A if par == 0 else SET_B)
            continue
        stage_in(row0, hT)
        P.label = f"b{bi}.m0"
        mixer0(False)
        if kind == "warm" and "h0" in dbg_d:
            dump("h0", hT)
        P.label = f"b{bi}.f0"
        ffn(0, False)
        P.label = f"b{bi}.cf"
        conformer()
        P.label = f"b{bi}.f1"
        ffn(1, True, skip_down=(kind == "warm"))
        P.label = f"b{bi}.out"
        if kind == "warm":
            f1 = flg.t[:, 0:1]
            A("dve", lambda e: e.tensor_scalar(out=st_f.t[:], in0=st_f.t[:], scalar1=f1, scalar2=None, op0=ALU.mult), R=[st_f.r(), flg.r()], W=[st_f.r()])
            A("dve", lambda e: e.tensor_scalar(out=st_b.t[:], in0=st_b.t[:], scalar1=f1, scalar2=None, op0=ALU.mult), R=[st_b.r(), flg.r()], W=[st_b.r()])
            A("dve", lambda e: e.tensor_scalar(out=TL.t[:], in0=TL.t[:], scalar1=f1, scalar2=None, op0=ALU.mult), R=[TL.r(), flg.r()], W=[TL.r()])
            A("dve", lambda e: e.tensor_scalar(out=Vx.t[:, 0, :, :].rearrange("p a b -> p (a b)"), in0=Vx.t[:, 0, :, :].rearrange("p a b -> p (a b)"),
                                               scalar1=f1, scalar2=None, op0=ALU.mult), R=[Vx.r(), flg.r()], W=[Vx.r()])
        else:
            stage_out(nout)
            nout += 512
    fin = [f"dmachain_{xo.name}" for xo in xout] + [f"dmachain_dbg_{n}" for n in dbg_d]
    P.add("sp", None, fin, ())
    if not do_emit:
        P.sim_ns = P.schedule()
        return nc, P, allocs
    nsem, nops = P.emit()
    print(f"[build] sbuf_peak={sbuf_peak} sems={nsem} ops={nops} loads={len(stream)} sim_us={P.sim_ns / 1e3:.0f}", flush=True)
    return nc, P, allocs


def plan_psum(allocs):
    last_r = [-1] * 8
    last_t = [0.0] * 8
    plan = []
    for n, _b, ops in allocs:
        if not ops:
            plan.append(0)
            continue
        r0 = min(o.ridx for o in ops)
        r1 = max(o.ridx for o in ops)
        t0 = min(o.t0 for o in ops)
        t1 = max(o.t1 for o in ops)
        best = None
        for b in (range(8) if n == 1 else range(0, 8, 2)):
            bs = range(b, b + n)
            if any(last_r[i] >= r0 for i in bs):
                continue
            te = max(last_t[i] for i in bs)
            key = (max(te - t0, 0.0), -te)
            if best is None or key < best[0]:
                best = (key, b)
        assert best is not None, "PSUM over-subscribed in program order"
        b = best[1]
        for i in range(b, b + n):
            last_r[i] = r1
            last_t[i] = t1
        plan.append(b)
    return plan


def build_program(NPRE, NMAIN, dbg=(), iters=PLAN_ITERS):
    plan = None
    best = None
    for it in range(iters):
        _nc, P1, allocs = record_program(NPRE, NMAIN, dbg, plan=plan, do_emit=False)
        if best is None or P1.sim_ns < best[0]:
            best = (P1.sim_ns, plan)
        print(f"[build] plan iter {it}: sim_us={P1.sim_ns / 1e3:.0f}", flush=True)
        plan = plan_psum(allocs)
    nc, _P, _a = record_program(NPRE, NMAIN, dbg, plan=best[1], do_emit=True)
    return nc


def _fm(v, nt):
    return np.ascontiguousarray(np.asarray(v, np.float32).reshape(nt, 128).T)


def _t5_bucket(dist):
    max_exact = 16
    d_f = np.maximum(dist, 1).astype(np.float32)
    large = max_exact + (np.log(d_f / max_exact) / math.log(128 / max_exact) * (32 - max_exact)).astype(np.int32)
    large = np.minimum(large, 31)
    return np.where(dist < max_exact, dist, large)


def host_pack(inp):
    f32 = np.float32
    prm = np.zeros((128, NPRM), f32)

    def put(name, arr):
        arr = np.asarray(arr, f32)
        prm[:, _p[name]:_p[name] + arr.shape[1]] = arr
    put("mixw", np.concatenate([_fm(inp["mix_norm_w"][l], 8) for l in range(2)], 1))
    put("ffnw", np.concatenate([_fm(inp["ffn_norm_w"][l], 8) for l in range(2)], 1))
    cw = np.asarray(inp["ssm_conv_w"][0], f32)
    put("cw", cw.T.reshape(12, 128, 4).transpose(1, 0, 2).reshape(128, 48))
    put("cb", _fm(inp["ssm_conv_b"][0], 12))
    put("dch", _fm(np.repeat(np.asarray(inp["ssm_d"][0], f32), 64), 8))
    put("snw", _fm(inp["ssm_norm_w"][0], 8))
    put("qw", np.tile(np.asarray(inp["attn_q_norm_w"][0], f32), 2)[:, None])
    put("kw", np.tile(np.asarray(inp["attn_k_norm_w"][0], f32), 2)[:, None])
    put("pw1b", _fm(inp["conv_pw1_b"][0], 16))
    dw = np.asarray(inp["conv_dw_w"][0], f32)
    put("dww", dw.T.reshape(8, 128, 31).transpose(1, 0, 2).reshape(128, 248))
    put("dwb", _fm(inp["conv_dw_b"][0], 8))
    put("lnw", _fm(inp["conv_ln_w"][0], 8))
    put("lnb", _fm(inp["conv_ln_b"][0], 8))
    put("pw2b", _fm(inp["conv_pw2_b"][0], 8))
    fcw = np.asarray(inp["ffn_conv_w"], f32)
    put("fcw", fcw.transpose(0, 2, 1).reshape(2, 44, 128, 3).transpose(2, 0, 1, 3).reshape(128, 264))
    put("fcb", np.asarray(inp["ffn_conv_b"], f32).reshape(2, 44, 128).transpose(2, 0, 1).reshape(128, 88))
    put("dtb", np.broadcast_to(np.asarray(inp["ssm_dt_bias"][0], f32)[None, :], (128, 16)))
    put("alog", np.broadcast_to(np.asarray(inp["ssm_a_log"][0], f32)[None, :], (128, 16)))
    put("sink", np.broadcast_to(np.asarray(inp["attn_sinks"][0], f32)[None, :], (128, 16)))
    rb = np.asarray(inp["rel_bias"], f32)
    qi = np.arange(128)[:, None]
    sj = np.arange(256)[None, :]
    dist = qi + 128 - sj
    valid = (dist >= 0) & (dist < 128)
    bias = rb[_t5_bucket(np.maximum(dist, 0))]
    bias = np.where(valid[:, :, None], bias, f32(NEG)).astype(f32)
    biasT = bias.reshape(128, 2, 128, 16).transpose(2, 3, 1, 0)
    biasT = np.ascontiguousarray(biasT).reshape(128, 16 * 2 * 128)
    cst = np.zeros((128, 512), f32)
    cst[:, 0:128] = np.eye(128, dtype=f32)
    cst[:, 128:256] = np.triu(np.ones((128, 128), f32))
    cst[:, 256:384] = np.kron(np.eye(2, dtype=f32), np.ones((64, 64), f32))
    cst[:, 384:512] = np.where(np.arange(128)[None, :] >= np.arange(128)[:, None], 0.0, NEG)
    eall = np.zeros((48, 16, 128), f32)
    for h in range(16):
        eall[h, h, :] = 1.0
        eall[32 + h, h, :] = 1.0
    return prm, biasT, cst, eall.reshape(48, 2048)


_NC_CACHE = {}


def kernel(**inp):
    x = np.asarray(inp["x"], np.float32)
    B, L, _ = x.shape
    NPRE, NMAIN = 7, 9
    prm, biasT, cst, eall = host_pack(inp)
    common = {
        "prm": prm, "biasT": biasT, "cst": cst, "eall": eall,
        "w_in": np.ascontiguousarray(inp["hyb_w_in"][0], np.float32),
        "w_out": np.ascontiguousarray(inp["hyb_w_out"][0], np.float32),
        "pw1": np.ascontiguousarray(inp["conv_pw1_w"][0], np.float32),
        "pw2": np.ascontiguousarray(inp["conv_pw2_w"][0], np.float32),
        "wup0": np.ascontiguousarray(inp["ffn_w_up"][0], np.float32),
        "wup1": np.ascontiguousarray(inp["ffn_w_up"][1], np.float32),
        "wdn0": np.ascontiguousarray(inp["ffn_w_down"][0], np.float32),
        "wdn1": np.ascontiguousarray(inp["ffn_w_down"][1], np.float32),
    }
    in_maps = []
    for core in range(8):
        b, half = core // 2, core % 2
        if half == 1:
            xs = x[b]
            flag = np.ones((128, 1), np.float32)
        else:
            xs = np.concatenate([np.zeros((4096, D), np.float32), x[b, :4096]], 0)
            flag = np.zeros((128, 1), np.float32)
        m = dict(common)
        m["x"] = np.ascontiguousarray(xs)
        m["flag"] = flag
        in_maps.append(m)
    if "nc" not in _NC_CACHE:
        _NC_CACHE["nc"] = build_program(NPRE, NMAIN)
    res = run_bass_kernel_spmd(_NC_CACHE["nc"], in_maps, core_ids=list(range(8)))
    out = np.empty((B, L, D), np.float32)
    for core in range(8):
        b, half = core // 2, core % 2
        out[b, half * 4096:(half + 1) * 4096] = res.results[core]["y"]
    return out
```

```python
import contextlib
import math
import numpy as np
import concourse.bass as bass
import concourse.mybir as mybir
from concourse.bass_utils import run_bass_kernel_spmd

F32 = mybir.dt.float32
BF16 = mybir.dt.bfloat16
AF = mybir.ActivationFunctionType
ALU = mybir.AluOpType
AX = mybir.AxisListType

ENGS = ("pe", "act", "dve", "pool", "sp")
EPOCH = 8000
DMA_BPNS = 340.0
SCHED_W = 128
SCHED_EPS = 0.0
XLAT = 150.0
SSD_POOL = (0, 4)
ATT_POOL = (4, 4)
PLAN_ITERS = 4
NDV = 10
GR = 64
SB_LO = 16640
SB_HI = 229376


class Res:
    __slots__ = ("name", "last_writer", "readers", "dma_cnt")

    def __init__(self, name):
        self.name = name
        self.last_writer = None
        self.readers = []
        self.dma_cnt = 0


class Op:
    __slots__ = ("idx", "eng", "fn", "deps", "is_dma", "sync", "eidx", "signal", "waits", "snap", "dval", "cost", "occ",
                 "t0", "t1", "label", "ridx")


class Prog:
    def __init__(self, nc):
        self.nc = nc
        self.ops = []
        self.res = {}

    def R(self, name):
        r = self.res.get(name)
        if r is None:
            r = self.res[name] = Res(name)
        return r

    def add(self, eng, fn, reads=(), writes=(), dma=None, cost=500.0, occ=None):
        op = Op()
        op.ridx = len(self.ops)
        op.label = getattr(self, "label", "")
        op.cost = cost
        op.occ = cost if occ is None else occ
        op.idx = len(self.ops)
        op.eng = eng
        op.fn = fn
        op.is_dma = dma is not None
        op.sync = self.R("dmasem_" + dma) if dma is not None else None
        deps = set()
        rs = [self.R(r) for r in set(reads)]
        ws = [self.R(w) for w in set(writes)]
        if op.is_dma:
            ws.append(self.R("dmachain_" + dma))
        for r in rs:
            if r.last_writer is not None:
                deps.add(r.last_writer)
        for w in ws:
            if w.last_writer is not None:
                deps.add(w.last_writer)
            deps.update(w.readers)
        for w in ws:
            w.last_writer = op.idx
            w.readers = []
        for r in rs:
            r.readers.append(op.idx)
        deps.discard(op.idx)
        op.deps = deps
        self.ops.append(op)
        return op

    def schedule(self, W=None):
        ops = self.ops
        if W is None:
            W = SCHED_W
        from collections import deque
        pend = {e: deque(op for op in ops if op.eng == e) for e in ENGS}
        tfree = {e: 0.0 for e in ENGS}
        self._dma_free = 0.0
        rank = [0.0] * len(ops)
        for op in reversed(ops):
            r = rank[op.idx] + op.cost
            rank[op.idx] = r
            for d in op.deps:
                if rank[d] < r:
                    rank[d] = r
        for op in ops:
            op.t0 = None
        left = len(ops)
        EPS = SCHED_EPS
        while left:
            best = None
            for e in ENGS:
                q = pend[e]
                n = 0
                cands = []
                smin = None
                for op in q:
                    if n >= W:
                        break
                    n += 1
                    rdy = 0.0
                    ok = True
                    for d in op.deps:
                        dop = ops[d]
                        if dop.t0 is None:
                            ok = False
                            break
                        t1d = dop.t1 if dop.eng == e else dop.t1 + XLAT
                        if t1d > rdy:
                            rdy = t1d
                    if not ok:
                        continue
                    st = rdy if rdy > tfree[e] else tfree[e]
                    cands.append((st, op))
                    if smin is None or st < smin:
                        smin = st
                if smin is None:
                    continue
                pick = None
                for st, op in cands:
                    if st <= smin + EPS:
                        k2 = (-rank[op.idx], op.idx)
                        if pick is None or k2 < pick[0]:
                            pick = (k2, st, op)
                key = (pick[1], pick[2].idx)
                if best is None or key < best[0]:
                    best = (key, pick[2])
            key, op = best
            op.t0 = key[0]
            if op.is_dma:
                xfer = max(op.cost - op.occ - 2000.0, 0.0)
                st = max(op.t0 + op.occ, self._dma_free)
                self._dma_free = st + xfer
                op.t1 = st + xfer + 2000.0
            else:
                op.t1 = op.t0 + op.cost
            tfree[op.eng] = op.t0 + op.occ
            pend[op.eng].remove(op)
            left -= 1
        new = sorted(ops, key=lambda o: (o.t0, o.idx))
        remap = {o.idx: i for i, o in enumerate(new)}
        for o in new:
            o.deps = {remap[d] for d in o.deps}
        for i, o in enumerate(new):
            o.idx = i
        self.ops = new
        return max(o.t1 for o in new)

    def emit(self, sched=True):
        nc = self.nc
        self.sim_ns = self.schedule() if sched else 0.0
        ops = self.ops
        cnt = {e: 0 for e in ENGS}
        for op in ops:
            op.eidx = cnt[op.eng]
            cnt[op.eng] += 1
            op.signal = False
            op.waits = []
        known = {e: {f: -1 for f in ENGS} for e in ENGS}
        kdma = {e: set() for e in ENGS}
        for op in ops:
            kn = known[op.eng]
            kd = kdma[op.eng]
            for d in sorted(op.deps):
                dop = ops[d]
                if dop.is_dma:
                    if d in kd:
                        continue
                    kd.add(d)
                else:
                    if kn[dop.eng] >= dop.eidx:
                        continue
                    kn[dop.eng] = dop.eidx
                dop.signal = True
                op.waits.append(d)
                sk, sd = dop.snap
                for f in ENGS:
                    if sk[f] > kn[f]:
                        kn[f] = sk[f]
                kd |= sd
            op.snap = (dict(kn), frozenset(kd))
        scount = {e: 0 for e in ENGS}
        keys = []
        for op in ops:
            if op.is_dma:
                op.sync.dma_cnt += 16
                op.dval = (op.sync.name, op.sync.dma_cnt)
            elif op.signal:
                c = scount[op.eng]
                scount[op.eng] += 1
                op.dval = (f"e_{op.eng}_{c // EPOCH}", c % EPOCH + 1)
            else:
                op.dval = None
            if op.dval is not None and op.dval[0] not in keys:
                keys.append(op.dval[0])
        with contextlib.ExitStack() as st:
            semh = {k: st.enter_context(nc.semaphore(k)) for k in keys}
            block = st.enter_context(nc.Block())
            per = {e: [op for op in ops if op.eng == e] for e in ENGS}

            def run(engobj, lst):
                for op in lst:
                    for d in op.waits:
                        k, v = ops[d].dval
                        engobj.wait_ge(semh[k], v)
                    if op.fn is None:
                        continue
                    ins = op.fn(engobj)
                    if op.is_dma:
                        ins.then_inc(semh[op.dval[0]], 16)
                    elif op.signal:
                        ins.then_inc(semh[op.dval[0]], 1)

            block.tensor(lambda e: run(e, per["pe"]))
            block.scalar(lambda e: run(e, per["act"]))
            block.vector(lambda e: run(e, per["dve"]))
            block.gpsimd(lambda e: run(e, per["pool"]))
            block.sync(lambda e: run(e, per["sp"]))
        return len(keys), {e: len(per[e]) for e in ENGS}


class T:
    def __init__(self, nc, name, shape, dtype, off):
        self.esz = 2 if dtype == BF16 else 4
        self.n = int(np.prod(shape[1:]))
        self.off = off
        self.t = nc.alloc_sbuf_tensor_at(name, list(shape), dtype, offset=off)
        self.name = name

    def r(self, lo=0, hi=None):
        if hi is None:
            hi = self.n
        b0 = (self.off + lo * self.esz) // GR
        b1 = (self.off + hi * self.esz - 1) // GR
        return [f"sb{g}" for g in range(b0, b1 + 1)]


class Arena:
    def __init__(self, nc):
        self.nc = nc
        self.cur = SB_LO
        self.peak = SB_LO
        self.k = 0

    def alloc(self, name, shape, dtype):
        esz = 2 if dtype == BF16 else 4
        nbytes = int(np.prod(shape[1:])) * esz
        off = (self.cur + 63) // 64 * 64
        self.cur = off + nbytes
        self.peak = max(self.peak, self.cur)
        assert self.cur <= SB_HI, (name, self.cur)
        self.k += 1
        return T(self.nc, f"{name}_{self.k}", shape, dtype, off)


D = 1024
IN_TOTAL = 3856
C_Z, C_X, C_B, C_C, C_DT, C_Q, C_K, C_V = 0, 1024, 2048, 2304, 2560, 2576, 3600, 3728
DFF = 2816
NEG = -30000.0

_p = {}
_o = 0
for _n, _w in [("mixw", 16), ("ffnw", 16), ("cw", 48), ("cb", 12), ("dch", 8), ("snw", 8), ("qw", 1), ("kw", 1),
               ("pw1b", 16), ("dww", 248), ("dwb", 8), ("lnw", 8), ("lnb", 8), ("pw2b", 8),
               ("fcw", 264), ("fcb", 88), ("dtb", 16), ("alog", 16), ("sink", 16)]:
    _p[_n] = _o
    _o += _w
NPRM = _o
TL_A, TL_F, TL_C = 0, 36, 36 + 176
NTL = 36 + 176 + 240


def record_program(NPRE, NMAIN, dbg=(), plan=None, do_emit=True):
    nc = bass.Bass("TRN2", target_bir_lowering=False)
    NT = 512 * (NPRE + NMAIN)
    NOUT = 512 * (NMAIN - 1)
    dt_ = nc.dram_tensor
    x_d = dt_("x", [NT, D], F32, kind="ExternalInput").ap()
    flag_d = dt_("flag", [128, 1], F32, kind="ExternalInput").ap()
    prm_d = dt_("prm", [128, NPRM], F32, kind="ExternalInput").ap()
    bias_d = dt_("biasT", [128, 16 * 2 * 128], F32, kind="ExternalInput").ap()
    cst_d = dt_("cst", [128, 512], F32, kind="ExternalInput").ap()
    eall_d = dt_("eall", [48, 2048], F32, kind="ExternalInput").ap()
    win_d = dt_("w_in", [D, IN_TOTAL], F32, kind="ExternalInput").ap()
    wout_d = dt_("w_out", [2048, D], F32, kind="ExternalInput").ap()
    pw1_d = dt_("pw1", [D, 2048], F32, kind="ExternalInput").ap()
    pw2_d = dt_("pw2", [D, D], F32, kind="ExternalInput").ap()
    wup_d = [dt_(f"wup{l}", [D, 2 * DFF], F32, kind="ExternalInput").ap() for l in range(2)]
    wdn_d = [dt_(f"wdn{l}", [DFF, D], F32, kind="ExternalInput").ap() for l in range(2)]
    y_d = dt_("y", [NOUT, D], F32, kind="ExternalOutput").ap()
    diag_d = dt_("diag_scr", [8, 128, 31 * 128], BF16).ap()
    dbg_d = {n: dt_("dbg_" + n, [128, 4096], F32, kind="ExternalOutput").ap() for n in dbg}

    P = Prog(nc)
    ar = Arena(nc)
    PSt = nc.alloc_psum_tensor("PS", [128, 8, 512], F32)
    PS = PSt
    pspool = {"lo": 0, "n": 8}
    pscnt = {}
    allocs = []
    cur_alloc = {}

    def psa(n=1):
        k = len(allocs)
        if plan is not None:
            b = plan[k]
        else:
            key = (0, 8)
            c = pscnt.get(key, 0)
            if n == 2 and c % 2 == 1:
                c += 1
            b = c % 8
            pscnt[key] = c + n
        allocs.append([n, b, []])
        for i in range(n):
            cur_alloc[b + i] = k
        return b

    def pr(b, n=1):
        return [f"pb{b + i}" for i in range(n)]

    def A(eng, fn, R=(), W=(), dma=None, c=512, nbytes=1 << 19, recip=False):
        rl = [x for l in R for x in l]
        wl = [x for l in W for x in l]
        occ = None
        if dma is not None:
            occ = 1200.0 if eng == "pool" else 150.0
            cost = occ + 2000.0 + nbytes / DMA_BPNS
        elif eng == "pe":
            cost = getattr(fn, "cost", 300.0)
        elif eng == "act":
            cost = 220.0 + 0.95 * c
        else:
            cost = 80.0 + (3.2 if recip else 1.3) * c
        if eng != "pe":
            toks = ["px" + nm[2:] for nm in set(rl + wl) if nm.startswith("pb")]
            if eng == "act":
                rl = rl + toks
            else:
                wl = wl + toks
        op = P.add(eng, fn, rl, wl, dma, cost, occ)
        for nm in rl + wl:
            if nm.startswith("pb"):
                lst = allocs[cur_alloc[int(nm[2:])]][2]
                if not lst or lst[-1] is not op:
                    lst.append(op)
        return op

    hT = ar.alloc("hT", [128, 8, 512], F32)
    NSLOT = 4
    WS = [ar.alloc(f"ws{i}", [128, 4096], BF16) for i in range(NSLOT)]
    WK = ar.alloc("wk", [128, 8, 4, 128], BF16)
    WDV = ar.alloc("wdv", [128, 8, 144], BF16)
    biasT = ar.alloc("biasT", [128, 16, 2, 128], BF16)
    Eall = ar.alloc("eall", [48, 16, 128], BF16)
    prm = ar.alloc("prm", [128, NPRM], F32)
    cst = ar.alloc("cst", [128, 512], F32)
    cbf = ar.alloc("cbf", [128, 512], BF16)
    ones_f = ar.alloc("ones_f", [128, 128], F32)
    ones_b = ar.alloc("ones_b", [128, 128], BF16)
    diagD = ar.alloc("diagD", [128, 8, 128], BF16)
    sm = ar.alloc("sm", [128, 128], F32)
    st_f = ar.alloc("st_f", [128, 1024], F32)
    st_b = ar.alloc("st_b", [128, 1024], BF16)
    TL = ar.alloc("TL", [128, NTL], F32)
    Kp = ar.alloc("Kp", [128, 4, 640], BF16)
    Vx = ar.alloc("Vx", [128, 5, 2, 65], BF16)
    flg = ar.alloc("flg", [128, 1], F32)
    xin = [ar.alloc(f"xin{i}", [128, 1024], F32) for i in range(2)]
    ident_f = cst.t[:, 0:128]
    U_f = cst.t[:, 128:256]
    ident_b = cbf.t[:, 0:128]
    bd_b = cbf.t[:, 256:384]
    negm_b = cbf.t[:, 384:512]
    SM_A, SM_ES, SM_E6, SM_E5, SM_QW, SM_KW = 0, 16, 32, 33, 34, 35

    def pc(name, j=0, n=1):
        return prm.t[:, _p[name] + j:_p[name] + j + n]

    base_mark = ar.cur

    xsT = ar.alloc("xsT", [128, 8, 512], BF16)
    BCT = ar.alloc("BCT", [128, 4, 512], BF16)
    qnT = ar.alloc("qnT", [128, 8, 512], BF16)
    sz = ar.alloc("sz", [128, 4, 1024], F32)
    mixT = ar.alloc("mixT", [128, 16, 512], BF16)
    dtr = ar.alloc("dtr", [128, 4, 16], F32)
    dtv = ar.alloc("dtv", [128, 4, 16], F32)
    lndt = ar.alloc("lndt", [128, 4, 16], F32)
    m0_mark = ar.cur
    ar.cur = qnT.off
    hnB = ar.alloc("hnB", [128, 8, 512], BF16)
    ar.cur = mixT.off
    xsTB = ar.alloc("xsTB", [128, 8, 512], BF16)
    BCTB = ar.alloc("BCTB", [128, 4, 512], BF16)
    dtrB = ar.alloc("dtrB", [128, 4, 16], F32)
    dtvB = ar.alloc("dtvB", [128, 4, 16], F32)
    lndtB = ar.alloc("lndtB", [128, 4, 16], F32)
    xtokB = ar.alloc("xtokB", [128, 1024], BF16)
    btokB = ar.alloc("btokB", [128, 256], BF16)
    assert ar.cur <= mixT.off + 16 * 512 * 2
    ar.cur = sz.off
    hTB = ar.alloc("hTB", [128, 8, 512], F32)
    ar.cur = m0_mark
    hn = ar.alloc("hn", [128, 8, 512], BF16)
    sq = ar.alloc("sq", [128, 8, 512], BF16)
    xp = [ar.alloc(f"xp{i}", [128, 515], F32) for i in range(2)]
    acc = [ar.alloc(f"acc{i}", [128, 512], F32) for i in range(2)]
    sqq = [ar.alloc(f"sqq{i}", [128, 512], BF16) for i in range(2)]
    rs_s = ar.alloc("rs_s", [128, 512], F32)
    rs_r = ar.alloc("rs_r", [128, 512], F32)
    rs_s2 = ar.alloc("rs_s2", [128, 512], F32)
    rs_r2 = ar.alloc("rs_r2", [128, 512], F32)
    a13_end = ar.cur
    SET_A = (hn, sq, xsT, BCT, dtr, dtv, lndt, hT)
    SET_B = (hnB, sq, xsTB, BCTB, dtrB, dtvB, lndtB, hTB)
    ar.cur = m0_mark
    xtok = ar.alloc("xtok", [128, 1024], BF16)
    btok = ar.alloc("btok", [128, 256], BF16)
    xw = ar.alloc("xw", [128, 1024], BF16)
    dec = [ar.alloc(f"dec{i}", [128, 1024], F32) for i in range(2)]
    MT = ar.alloc("MT", [128, 16, 128], BF16)
    tmpf = ar.alloc("tmpf", [128, 1024], F32)
    yv = ar.alloc("yv", [128, 1024], F32)
    yn = ar.alloc("yn", [128, 1024], BF16)
    pT = [ar.alloc(f"pT{i}", [128, 512], BF16) for i in range(4)]
    attn = ar.alloc("attn", [128, 1024], BF16)
    s16s = [ar.alloc(f"s16_{i}", [128, 16, 16], F32) for i in range(4)]
    a16s = [ar.alloc(f"a16_{i}", [128, 8, 16], F32) for i in range(4)]
    acsTs = [ar.alloc(f"acsT{i}", [48, 128], BF16) for i in range(4)]
    qTs = [ar.alloc(f"qT{i}", [48, 128], BF16) for i in range(4)]
    a48s = [ar.alloc(f"a48_{i}", [128, 48], F32) for i in range(4)]
    q48s = [ar.alloc(f"q48_{i}", [128, 48], F32) for i in range(4)]
    ar.cur = max(ar.cur, a13_end)
    m0_end = ar.cur
    ar.cur = base_mark
    f_hn = ar.alloc("f_hn", [128, 8, 512], BF16)
    f_sq = ar.alloc("f_sq", [128, 8, 512], BF16)
    f_rs = ar.alloc("f_rs", [128, 512], F32)
    f_rr = ar.alloc("f_rr", [128, 512], F32)
    actT = ar.alloc("actT", [128, 22, 512], BF16)
    f_xp = [ar.alloc(f"f_xp{i}", [128, 514], F32) for i in range(8)]
    f_acc = [ar.alloc(f"f_acc{i}", [128, 512], F32) for i in range(8)]
    f_sg = [ar.alloc(f"f_sg{i}", [128, 512], F32) for i in range(4)]
    f_end = ar.cur
    ar.cur = base_mark
    c_hn = ar.alloc("c_hn", [128, 8, 512], BF16)
    c_sq = ar.alloc("c_sq", [128, 8, 512], BF16)
    c_rs = ar.alloc("c_rs", [128, 512], F32)
    c_rr = ar.alloc("c_rr", [128, 512], F32)
    upad = ar.alloc("upad", [128, 8, 544], BF16)
    yc = ar.alloc("yc", [128, 8, 512], F32)
    ybf = ar.alloc("ybf", [128, 8, 512], BF16)
    c_sig = [ar.alloc(f"c_sig{i}", [128, 512], F32) for i in range(2)]
    c_mean = ar.alloc("c_mean", [128, 512], F32)
    c_msq = ar.alloc("c_msq", [128, 512], F32)
    c_var = ar.alloc("c_var", [128, 512], F32)
    c_end = ar.cur
    ar.cur = base_mark
    dgt = ar.alloc("dgt", [128, 31 * 128], BF16)
    ar.cur = base_mark
    xout = [ar.alloc(f"xout{i}", [128, 1024], F32) for i in range(2)]
    ar.cur = ar.peak
    xtokA = ar.alloc("xtokA", [128, 1024], BF16)
    btokA = ar.alloc("btokA", [128, 256], BF16)
    xwA = ar.alloc("xwA", [128, 1024], BF16)
    xwB = ar.alloc("xwB", [128, 1024], BF16)
    PRE_A = (xtokA, btokA, xwA)
    PRE_B = (xtokB, btokB, xwB)
    sbuf_peak = ar.peak

    A("sp", lambda e: e.dma_start(out=prm.t[:], in_=prm_d), W=[prm.r()], dma="prm")
    A("sp", lambda e: e.dma_start(out=cst.t[:], in_=cst_d), W=[cst.r()], dma="cst")
    A("pool", lambda e: e.dma_start(out=biasT.t[:].rearrange("p a b c -> p (a b c)"), in_=bias_d), W=[biasT.r()], dma="biasT")
    A("pool", lambda e: e.dma_start(out=Eall.t[:].rearrange("p a b -> p (a b)"), in_=eall_d), W=[Eall.r()], dma="eall")
    A("sp", lambda e: e.dma_start(out=flg.t[:], in_=flag_d), W=[flg.r()], dma="flg")
    A("dve", lambda e: e.memset(WK.t[:].rearrange("p a b c -> p (a b c)"), 0.0), W=[WK.r()])
    for kv in range(2):
        for pad in range(2):
            def f(e, kv=kv, pad=pad):
                return e.dma_start(out=WK.t[:, :, 2 * kv + pad, 64 * pad:64 * pad + 64],
                                   in_=win_d[:, C_K + 64 * kv:C_K + 64 * kv + 64].rearrange("(k p) c -> p k c", p=128))
            A("pool", f, W=[WK.r()], dma="wk")
    A("pool", lambda e: e.dma_start(out=WDV.t[:, :, 0:16], in_=win_d[:, C_DT:C_DT + 16].rearrange("(k p) c -> p k c", p=128)),
      W=[WDV.r()], dma="wdv")
    A("pool", lambda e: e.dma_start(out=WDV.t[:, :, 16:144], in_=win_d[:, C_V:C_V + 128].rearrange("(k p) c -> p k c", p=128)),
      W=[WDV.r()], dma="wdv")
    A("dve", lambda e: e.tensor_copy(out=cbf.t[:], in_=cst.t[:]), R=[cst.r()], W=[cbf.r()])
    A("dve", lambda e: e.memset(ones_f.t[:], 1.0), W=[ones_f.r()])
    A("dve", lambda e: e.memset(ones_b.t[:], 1.0), W=[ones_b.r()])
    A("dve", lambda e: e.memset(sm.t[:], 0.0), W=[sm.r()])
    A("dve", lambda e: e.memset(sm.t[:, SM_E6:SM_E6 + 1], 1e-6), W=[sm.r()])
    A("dve", lambda e: e.memset(sm.t[:, SM_E5:SM_E5 + 1], 1e-5), W=[sm.r()])
    A("act", lambda e: e.activation(out=sm.t[:, SM_A:SM_A + 16], in_=pc("alog", 0, 16), func=AF.Exp), R=[prm.r(), sm.r()], W=[sm.r()])
    A("dve", lambda e: e.tensor_scalar(out=sm.t[:, SM_A:SM_A + 16], in0=sm.t[:, SM_A:SM_A + 16], scalar1=-1.0, scalar2=None, op0=ALU.mult),
      R=[sm.r()], W=[sm.r()])
    A("act", lambda e: e.activation(out=sm.t[:, SM_ES:SM_ES + 16], in_=pc("sink", 0, 16), func=AF.Exp), R=[prm.r(), sm.r()], W=[sm.r()])
    A("dve", lambda e: e.tensor_scalar(out=sm.t[:, SM_QW:SM_QW + 1], in0=pc("qw"), scalar1=0.125, scalar2=None, op0=ALU.mult),
      R=[prm.r(), sm.r()], W=[sm.r()])
    A("dve", lambda e: e.tensor_copy(out=sm.t[:, SM_KW:SM_KW + 1], in_=pc("kw")), R=[prm.r(), sm.r()], W=[sm.r()])
    for j in range(8):
        A("dve", lambda e, j=j: e.tensor_scalar(out=diagD.t[:, j, :], in0=ident_f, scalar1=pc("dch", j), scalar2=None, op0=ALU.mult),
          R=[cst.r(), prm.r()], W=[diagD.r()])
    for i in range(4):
        A("dve", lambda e, i=i: e.memset(a48s[i].t[:], 0.0), W=[a48s[i].r()], c=48)
        A("dve", lambda e, i=i: e.memset(q48s[i].t[:], 0.0), W=[q48s[i].r()], c=48)
    A("dve", lambda e: e.memset(st_f.t[:], 0.0), W=[st_f.r()])
    A("dve", lambda e: e.memset(st_b.t[:], 0.0), W=[st_b.r()])
    A("dve", lambda e: e.memset(TL.t[:], 0.0), W=[TL.r()])
    A("dve", lambda e: e.memset(Kp.t[:].rearrange("p a b -> p (a b)"), 0.0), W=[Kp.r()])
    A("dve", lambda e: e.memset(Vx.t[:].rearrange("p a b c -> p (a b c)"), 0.0), W=[Vx.r()])
    A("dve", lambda e: e.memset(Vx.t[:, :, :, 64:65], 1.0), W=[Vx.r()])

    for j in range(8):
        A("dve", lambda e, j=j: e.tensor_tensor(out=dgt.t[:].rearrange("p (k c) -> p k c", c=128),
                                                in0=ident_f.unsqueeze(1).to_broadcast([128, 31, 128]),
                                                in1=pc("dww", j * 31, 31).unsqueeze(2).to_broadcast([128, 31, 128]), op=ALU.mult),
          R=[cst.r(), prm.r()], W=[dgt.r()], c=3968)
        A("sp", lambda e, j=j: e.dma_start(out=diag_d[j], in_=dgt.t[:]), R=[dgt.r()], W=[["dram_diag"]], dma="dgt")

    def wsrc(w, k0, nk, c0, ncol):
        return w[k0 * 128:(k0 + nk) * 128, c0:c0 + ncol].rearrange("(k p) c -> p k c", p=128)

    def loads_mixer0(pre):
        L = []
        if not pre:
            L += [("z0", win_d, 0, 8, C_Z, 512), ("z1", win_d, 0, 8, C_Z + 512, 512)]
        L += [("x0", win_d, 0, 8, C_X, 512), ("x1", win_d, 0, 8, C_X + 512, 512)]
        if pre:
            L += [("bc", win_d, 0, 8, C_B, 256)]
        else:
            L += [("bc", win_d, 0, 8, C_B, 512), ("q0", win_d, 0, 8, C_Q, 512), ("q1", win_d, 0, 8, C_Q + 512, 512)]
            L += [(f"o{i}", wout_d, 0, 16, 256 * i, 256) for i in range(4)]
        return L

    def loads_ffn(l):
        L = []
        for g in range(6):
            nc_ = 512 if g < 5 else 256
            L += [(f"g{g}", wup_d[l], 0, 8, 512 * g, nc_), (f"u{g}", wup_d[l], 0, 8, DFF + 512 * g, nc_)]
        L += [(f"d{f}", wdn_d[l], 0, 22, 128 * f, 128) for f in range(8)]
        return L

    def loads_conf():
        L = []
        for h in range(2):
            L += [(f"a{h}", pw1_d, 0, 8, 512 * h, 512), (f"s{h}", pw1_d, 0, 8, 1024 + 512 * h, 512)]
            L += [(f"dg{j}", None, j, 31, 0, 128) for j in range(4 * h, 4 * h + 4)]
        L += [(f"p{h}", pw2_d, 0, 8, 512 * h, 512) for h in range(2)]
        return L

    stream = []
    for _ in range(NPRE + 1):
        stream += loads_mixer0(True)
    for _ in range(NMAIN):
        stream += loads_mixer0(False) + loads_ffn(0) + loads_conf() + loads_ffn(1)
    wstate = {"issued": 0, "next": 0}
    released = set()
    auto_pending = []

    def issue_loads(upto):
        while wstate["issued"] < min(upto, len(stream)):
            i = wstate["issued"]
            if i >= NSLOT and (i - NSLOT) not in released:
                break
            _, w, k0, nk, c0, ncol = stream[i]
            slot = WS[i % NSLOT]

            def f(e, slot=slot, w=w, k0=k0, nk=nk, c0=c0, ncol=ncol):
                if w is None:
                    return e.dma_start(out=slot.t[:, 0:nk * ncol], in_=diag_d[k0])
                return e.dma_start(out=slot.t[:, 0:nk * ncol].rearrange("p (k c) -> p k c", c=ncol), in_=wsrc(w, k0, nk, c0, ncol))
            A("pool", f, R=[["dram_diag"]] if w is None else [], W=[slot.r()], dma=f"ws{i % NSLOT}", nbytes=nk * ncol * 128 * (2 if w is None else 4))
            wstate["issued"] += 1

    def wrel(*idxs):
        released.update(idxs)
        issue_loads(wstate["next"] + NSLOT)

    def wnext(tag, hold=False):
        i = wstate["next"]
        assert stream[i][0] == tag, (stream[i][0], tag)
        released.update(auto_pending)
        del auto_pending[:]
        issue_loads(i + NSLOT)
        assert wstate["issued"] > i, ("weight ring deadlock", tag)
        wstate["next"] += 1
        if not hold:
            auto_pending.append(i)
        _, w, k0, nk, c0, ncol = stream[i]
        slot = WS[i % NSLOT]
        return slot.t[:, 0:nk * ncol].rearrange("p (k c) -> p k c", c=ncol), slot.r(), i

    def mm_group(specs):
        def f(e):
            ins = None
            for (o, l, r, s0, s1) in specs:
                ins = e.matmul(o, lhsT=l, rhs=r, start=s0, stop=s1)
            return ins
        cost = 0.0
        for (o, l, r, s0, s1) in specs:
            n = int(np.prod(o.shape[1:]))
            cost += (max(n, 96) / 1.9) * (4.0 if l.dtype == F32 else 1.0) + 12.0
        f.cost = cost
        return f

    cfg = {"nt": 512}

    def tr_group(specs):
        def f(e):
            ins = None
            for (o, i_) in specs:
                ins = e.transpose(o, i_, ident_f)
            return ins
        f.cost = len(specs) * 110.0
        return f

    def rmsnorm(wname, wl, hn_, sq_, rs_, rr_, hs=None, bank=None):
        hs = hT if hs is None else hs
        nt = cfg["nt"]
        if bank is None:
            A("act", lambda e: e.activation(out=sq_.t[:, :, 0:nt], in_=hs.t[:, :, 0:nt], func=AF.Square), R=[hs.r()], W=[sq_.r()], c=8 * nt)
            b = psa()
            A("pe", mm_group([(PS[:, b, 0:nt], ones_b.t[:], sq_.t[:, kt, 0:nt], kt == 0, kt == 7) for kt in range(8)]),
              R=[sq_.r(), ones_b.r()], W=[pr(b)])
        else:
            b = bank
        A("act", lambda e: e.activation(out=rs_.t[:, 0:nt], in_=PS[:, b, 0:nt], func=AF.Ln, bias=sm.t[:, SM_E6:SM_E6 + 1], scale=1.0 / 1024),
          R=[pr(b), sm.r()], W=[rs_.r()], c=nt)
        A("act", lambda e: e.activation(out=rr_.t[:, 0:nt], in_=rs_.t[:, 0:nt], func=AF.Exp, scale=-0.5), R=[rs_.r()], W=[rr_.r()], c=nt)
        for kt in range(8):
            A("dve", lambda e, kt=kt: e.scalar_tensor_tensor(out=hn_.t[:, kt, 0:nt], in0=hs.t[:, kt, 0:nt], scalar=pc(wname, wl * 8 + kt),
                                                            in1=rr_.t[:, 0:nt], op0=ALU.mult, op1=ALU.mult),
              R=[hs.r(kt * 512, kt * 512 + 512), rr_.r(), prm.r()], W=[hn_.r(kt * 512, kt * 512 + 512)], c=nt)

    nrm = {"bank": None}

    def ssq_hook(f, sq_next):
        nt = cfg["nt"]
        A("act", lambda e: e.activation(out=sq_next.t[:, f, 0:nt], in_=hT.t[:, f, 0:nt], func=AF.Square),
          R=[hT.r(f * 512, f * 512 + 512)], W=[sq_next.r(f * 512, f * 512 + 512)], c=nt)
        if f == 0:
            nrm["bank"] = psa()
        b = nrm["bank"]
        A("pe", mm_group([(PS[:, b, 0:nt], ones_b.t[:], sq_next.t[:, f, 0:nt], f == 0, f == 7)]),
          R=[sq_next.r(f * 512, f * 512 + 512), ones_b.r()], W=[pr(b)])

    def proj_fm(wv, wres, jloc, hn_):
        nt = cfg["nt"]
        b = psa()
        A("pe", mm_group([(PS[:, b, 0:nt], wv[:, kt, jloc * 128:(jloc + 1) * 128], hn_.t[:, kt, 0:nt], kt == 0, kt == 7) for kt in range(8)]),
          R=[wres, hn_.r()], W=[pr(b)])
        return b

    def conv_fm(b, xp_, acc_, K, wcol, bcol, tl_off):
        nt = cfg["nt"]
        A("act", lambda e: e.activation(out=xp_.t[:, 0:K - 1], in_=TL.t[:, tl_off:tl_off + K - 1], func=AF.Copy),
          R=[TL.r(tl_off, tl_off + K - 1)], W=[xp_.r(0, K - 1)], c=4)
        A("act", lambda e: e.activation(out=xp_.t[:, K - 1:K - 1 + nt], in_=PS[:, b, 0:nt], func=AF.Copy), R=[pr(b)], W=[xp_.r(K - 1, K + 511)], c=nt)
        A("act", lambda e: e.activation(out=acc_.t[:, 0:nt], in_=PS[:, b, 0:nt], func=AF.Identity, scale=wcol(K - 1), bias=bcol),
          R=[pr(b), prm.r()], W=[acc_.r()], c=nt)
        for k in range(K - 2, -1, -1):
            A("dve", lambda e, k=k: e.scalar_tensor_tensor(out=acc_.t[:, 0:nt], in0=xp_.t[:, k:k + nt], scalar=wcol(k), in1=acc_.t[:, 0:nt],
                                                          op0=ALU.mult, op1=ALU.add), R=[xp_.r(), acc_.r(), prm.r()], W=[acc_.r()], c=nt)
        A("act", lambda e: e.activation(out=TL.t[:, tl_off:tl_off + K - 1], in_=xp_.t[:, nt:nt + K - 1], func=AF.Copy),
          R=[xp_.r()], W=[TL.r(tl_off, tl_off + K - 1)], c=4)

    def dump(name, tt):
        if name in dbg_d:
            A("sp", lambda e: e.dma_start(out=dbg_d[name][:, 0:tt.n], in_=tt.t[:].rearrange("p a b -> p (a b)")),
              R=[tt.r()], dma="dbg_" + name)

    def hchunk(hd, c):
        return [x for j in range(8) for x in hd.r(j * 512 + c * 128, j * 512 + c * 128 + 128)]

    def stage_in(row0, hd=None):
        hd = hT if hd is None else hd
        for c in range(cfg["nt"] // 128):
            xi = xin[c % 2]
            r0 = row0 + c * 128
            A("sp", lambda e, xi=xi, r0=r0: e.dma_start(out=xi.t[:], in_=x_d[r0:r0 + 128, :]), W=[xi.r()], dma=xi.name)
            b = psa(2)
            A("pe", tr_group([(PS[:, b + j // 4, (j % 4) * 128:(j % 4) * 128 + 128], xi.t[:, j * 128:(j + 1) * 128]) for j in range(8)]),
              R=[xi.r(), cst.r()], W=[pr(b, 2)])
            A("act", lambda e, b=b, c=c: e.activation(out=hd.t[:, :, c * 128:(c + 1) * 128],
                                                      in_=PS[:, b:b + 2, :].rearrange("p a (j t) -> p (a j) t", t=128), func=AF.Copy),
              R=[pr(b, 2)], W=[hchunk(hd, c)], c=1024)

    def stage_out(orow0):
        for c in range(cfg["nt"] // 128):
            xo = xout[c % 2]
            r0 = orow0 + c * 128
            b = psa(2)
            A("pe", tr_group([(PS[:, b + j // 4, (j % 4) * 128:(j % 4) * 128 + 128], hT.t[:, j, c * 128:(c + 1) * 128]) for j in range(8)]),
              R=[hchunk(hT, c), cst.r()], W=[pr(b, 2)])
            A("act", lambda e, b=b, xo=xo: e.activation(out=xo.t[:], in_=PS[:, b:b + 2, :].rearrange("p a t -> p (a t)"), func=AF.Copy),
              R=[pr(b, 2)], W=[xo.r()], c=1024)
            A("sp", lambda e, xo=xo, r0=r0: e.dma_start(out=y_d[r0:r0 + 128, :], in_=xo.t[:]), R=[xo.r()], dma=xo.name)

    def ssd_chunk(c, pre, bs):
        hn, sq, xsT, BCT, dtr, dtv, lndt, _h = bs
        xtok_, btok_, xw_ = (xtok, btok, xw) if not pre else (PRE_A if bs is SET_A else PRE_B)
        s16, acsT, qT, a48, q48 = s16s[c], acsTs[c], qTs[c], a48s[c], q48s[c]
        S = lambda i: s16.t[:, i, :]
        sr = s16.r()
        cs = slice(c * 128, (c + 1) * 128)
        dup = lambda ap: ap.unsqueeze(1).to_broadcast([128, 2, 16])
        v48 = lambda t_: t_.t[:].rearrange("p (r c) -> p r c", c=16)[:, 0:3:2, :]
        A("dve", lambda e: e.tensor_tensor(out=v48(a48), in0=dup(dtv.t[:, c, :]), in1=dup(sm.t[:, SM_A:SM_A + 16]), op=ALU.mult),
          R=[dtv.r(), sm.r()], W=[a48.r()], c=32)
        a_ = a48.t[:, 0:16]
        b = psa()
        A("pe", mm_group([(PS[:, b, 0:16], U_f, a_, True, True),
                          (PS[0:48, b, 16:144], a48.t[:], U_f, True, True),
                          (PS[:, b, 144:160], ones_f.t[:], a_, True, True)]), R=[a48.r(), cst.r(), ones_f.r()], W=[pr(b)])
        A("dve", lambda e: e.tensor_tensor(out=v48(q48), in0=dup(lndt.t[:, c, :]), in1=dup(PS[:, b, 0:16]), op=ALU.subtract),
          R=[lndt.r(), pr(b)], W=[q48.r()], c=32)
        A("dve", lambda e: e.tensor_tensor(out=S(3), in0=q48.t[:, 0:16], in1=PS[:, b, 144:160], op=ALU.add), R=[q48.r(), pr(b)], W=[sr], c=16)
        A("act", lambda e: e.activation(out=S(4), in_=S(3), func=AF.Exp), R=[sr], W=[sr], c=16)
        A("act", lambda e: e.activation(out=S(6), in_=PS[:, b, 144:160], func=AF.Exp), R=[pr(b)], W=[sr], c=16)
        if not pre:
            A("act", lambda e: e.activation(out=S(5), in_=PS[:, b, 0:16], func=AF.Exp), R=[pr(b)], W=[sr], c=16)
            A("act", lambda e: e.activation(out=acsT.t[:], in_=PS[0:48, b, 16:144], func=AF.Copy), R=[pr(b)], W=[acsT.r()], c=128)
            A("dve", lambda e: e.tensor_tensor(out=acsT.t[32:48, :], in0=PS[32:48, b, 16:144], in1=acsT.t[32:48, :], op=ALU.subtract),
              R=[pr(b), acsT.r()], W=[acsT.r()], c=128)
            b2 = psa()
            A("pe", mm_group([(PS[0:48, b2, 0:128], q48.t[:], ident_f, True, True)]), R=[q48.r(), cst.r()], W=[pr(b2)])
            A("act", lambda e: e.activation(out=qT.t[:], in_=PS[0:48, b2, 0:128], func=AF.Copy), R=[pr(b2)], W=[qT.r()], c=128)
            A("dve", lambda e: e.tensor_tensor(out=qT.t[32:48, :], in0=PS[32:48, b2, 0:128], in1=qT.t[32:48, :], op=ALU.subtract),
              R=[pr(b2), qT.r()], W=[qT.r()], c=128)
            bcb = psa()
            A("pe", mm_group([(PS[:, bcb, g * 128:(g + 1) * 128], BCT.t[:, g, cs], BCT.t[:, 2 + g, cs], True, True) for g in range(2)]),
              R=[BCT.r()], W=[pr(bcb)])
            for half in range(2):
                bb = psa(2)
                specs = []
                for hh in range(8):
                    h = half * 8 + hh
                    o = PS[:, bb + hh // 4, (hh % 4) * 128:(hh % 4) * 128 + 128]
                    specs += [(o, Eall.t[:, h, :], acsT.t[:], True, False), (o, qT.t[:], Eall.t[:, h, :], False, False),
                              (o, ident_b, negm_b, False, True)]
                A("pe", mm_group(specs), R=[Eall.r(), acsT.r(), qT.r(), cbf.r()], W=[pr(bb, 2)])
                dc = dec[half]
                A("act", lambda e, bb=bb, dc=dc: e.activation(out=dc.t[:], in_=PS[:, bb:bb + 2, :].rearrange("p a t -> p (a t)"), func=AF.Exp),
                  R=[pr(bb, 2)], W=[dc.r()], c=1024)
                A("dve", lambda e, half=half, dc=dc: e.tensor_tensor(
                    out=MT.t[:, half * 8:half * 8 + 8, :], in0=dc.t[:].rearrange("p (h l) -> p h l", l=128),
                    in1=PS[:, bcb, half * 128:half * 128 + 128].unsqueeze(1).to_broadcast([128, 8, 128]), op=ALU.mult),
                  R=[dc.r(), pr(bcb)], W=[MT.r(half * 1024, half * 1024 + 1024)], c=1024)
        bx = psa(2)
        A("pe", mm_group([(PS[:, bx + j // 4, (j % 4) * 128:(j % 4) * 128 + 128], xsT.t[:, j, cs], ident_b, True, True) for j in range(8)]),
          R=[xsT.r(), cbf.r()], W=[pr(bx, 2)])
        A("act", lambda e: e.activation(out=xtok_.t[:], in_=PS[:, bx:bx + 2, :].rearrange("p a t -> p (a t)"), func=AF.Copy),
          R=[pr(bx, 2)], W=[xtok_.r()], c=1024)
        bB = psa()
        A("pe", mm_group([(PS[:, bB, g * 128:(g + 1) * 128], BCT.t[:, g, cs], ident_b, True, True) for g in range(2)]),
          R=[BCT.r(), cbf.r()], W=[pr(bB)])
        A("act", lambda e: e.activation(out=btok_.t[:], in_=PS[:, bB, 0:256], func=AF.Copy), R=[pr(bB)], W=[btok_.r()], c=256)
        A("dve", lambda e: e.tensor_tensor(out=xw_.t[:].rearrange("p (h d) -> p h d", d=64), in0=xtok_.t[:].rearrange("p (h d) -> p h d", d=64),
                                           in1=S(4).unsqueeze(2).to_broadcast([128, 16, 64]), op=ALU.mult), R=[xtok_.r(), sr], W=[xw_.r()], c=1024)
        if not pre:
            by = psa(2)
            specs = []
            for h in range(16):
                o = PS[:, by + h // 8, (h % 8) * 64:(h % 8) * 64 + 64]
                specs += [(o, MT.t[:, h, :], xtok_.t[:, h * 64:(h + 1) * 64], True, False),
                          (o, xsT.t[:, h // 2, cs], diagD.t[:, h // 2, (h % 2) * 64:(h % 2) * 64 + 64], False, True)]
            A("pe", mm_group(specs), R=[MT.r(), xtok_.r(), xsT.r(), diagD.r()], W=[pr(by, 2)])
            bo = psa(2)
            A("pe", mm_group([(PS[:, bo + g, :], BCT.t[:, 2 + g, cs], st_b.t[:, g * 512:(g + 1) * 512], True, True) for g in range(2)]),
              R=[BCT.r(), st_b.r()], W=[pr(bo, 2)])
            A("dve", lambda e: e.tensor_tensor(out=tmpf.t[:].rearrange("p (h d) -> p h d", d=64),
                                               in0=PS[:, bo:bo + 2, :].rearrange("p a (h d) -> p (a h) d", d=64),
                                               in1=S(5).unsqueeze(2).to_broadcast([128, 16, 64]), op=ALU.mult), R=[pr(bo, 2), sr], W=[tmpf.r()], c=1024)
            A("dve", lambda e: e.tensor_tensor(out=yv.t[:], in0=PS[:, by:by + 2, :].rearrange("p a t -> p (a t)"), in1=tmpf.t[:], op=ALU.add),
              R=[pr(by, 2), tmpf.r()], W=[yv.r()], c=1024)
        bs = psa(2)
        A("pe", mm_group([(PS[:, bs + g, :], btok_.t[:, g * 128:(g + 1) * 128], xw_.t[:, g * 512:(g + 1) * 512], True, True) for g in range(2)]),
          R=[btok_.r(), xw_.r()], W=[pr(bs, 2)])
        A("dve", lambda e: e.tensor_tensor(out=st_f.t[:].rearrange("p (h d) -> p h d", d=64), in0=st_f.t[:].rearrange("p (h d) -> p h d", d=64),
                                           in1=S(6).unsqueeze(2).to_broadcast([128, 16, 64]), op=ALU.mult), R=[st_f.r(), sr], W=[st_f.r()], c=1024)
        A("dve", lambda e: e.tensor_tensor(out=st_f.t[:], in0=st_f.t[:], in1=PS[:, bs:bs + 2, :].rearrange("p a t -> p (a t)"), op=ALU.add),
          R=[st_f.r(), pr(bs, 2)], W=[st_f.r()], c=1024)
        A("act", lambda e: e.activation(out=st_b.t[:], in_=st_f.t[:], func=AF.Copy), R=[st_f.r()], W=[st_b.r()], c=1024)
        if pre:
            return
        A("dve", lambda e: e.tensor_tensor(out=yv.t[:], in0=yv.t[:], in1=sz.t[:, c, :], op=ALU.mult), R=[yv.r(), sz.r(c * 1024, c * 1024 + 1024)], W=[yv.r()], c=1024)
        A("dve", lambda e: e.memset(S(7)[:, 0:2], 0.0), W=[sr], c=2)
        for g in range(2):
            A("act", lambda e, g=g: e.activation(out=tmpf.t[:, g * 512:(g + 1) * 512], in_=yv.t[:, g * 512:(g + 1) * 512], func=AF.Square,
                                                 accum_out=S(7)[:, g:g + 1]), R=[yv.r(), sr], W=[tmpf.r(), sr])
        A("act", lambda e: e.activation(out=S(8)[:, 0:2], in_=S(7)[:, 0:2], func=AF.Ln, bias=sm.t[:, SM_E6:SM_E6 + 1], scale=1.0 / 512),
          R=[sr, sm.r()], W=[sr], c=2)
        A("act", lambda e: e.activation(out=S(9)[:, 0:2], in_=S(8)[:, 0:2], func=AF.Exp, scale=-0.5), R=[sr], W=[sr], c=2)
        for g in range(2):
            A("act", lambda e, g=g: e.activation(out=yn.t[:, g * 512:(g + 1) * 512], in_=yv.t[:, g * 512:(g + 1) * 512], func=AF.Copy,
                                                 scale=S(9)[:, g:g + 1]), R=[yv.r(), sr], W=[yn.r(g * 512, g * 512 + 512)])
        bt = psa(2)
        A("pe", mm_group([(PS[:, bt + j // 4, (j % 4) * 128:(j % 4) * 128 + 128], yn.t[:, j * 128:(j + 1) * 128], ident_b, True, True)
                          for j in range(8)]), R=[yn.r(), cbf.r()], W=[pr(bt, 2)])
        for j in range(8):
            A("act", lambda e, j=j: e.activation(out=mixT.t[:, j, cs], in_=PS[:, bt + j // 4, (j % 4) * 128:(j % 4) * 128 + 128], func=AF.Copy,
                                                 scale=pc("snw", j)), R=[pr(bt + j // 4), prm.r()], W=[mixT.r(j * 512 + c * 128, j * 512 + c * 128 + 128)], c=128)

    def attn_chunk(c):
        cs = slice(c * 128, (c + 1) * 128)
        S = lambda i: a16s[c].t[:, i - 10, :]
        sr = a16s[c].r()
        gi = 0
        for kv in range(2):
            for pad in range(2):
                h0 = 8 * kv + pad
                pts = []
                for kb in range(2):
                    b = psa()
                    ov_ = PS[:, b, :].rearrange("p (j q) -> p j q", q=128)
                    A("pe", mm_group([(ov_, Kp.t[:, 2 * kv + pad, (c + kb) * 128:(c + kb + 1) * 128], qnT.t[:, 4 * kv:4 * kv + 4, cs], True, False),
                                      (ov_, ident_b, biasT.t[:, h0:h0 + 7:2, kb, :], False, True)]),
                      R=[Kp.r(), qnT.r(), biasT.r(), cbf.r()], W=[pr(b)])
                    p_ = pT[(gi % 2) * 2 + kb]
                    A("act", lambda e, b=b, p_=p_: e.activation(out=p_.t[:], in_=PS[:, b, :], func=AF.Exp), R=[pr(b)], W=[p_.r()])
                    pts.append(p_)
                o = psa()
                specs = []
                for j in range(4):
                    for kb in range(2):
                        specs.append((PS[:, o, j * 65:(j + 1) * 65], pts[kb].t[:, j * 128:(j + 1) * 128], Vx.t[:, c + kb, kv, :], kb == 0, kb == 1))
                A("pe", mm_group(specs), R=[pts[0].r(), pts[1].r(), Vx.r()], W=[pr(o)])
                ov = PS[:, o, 0:260].rearrange("p (j d) -> p j d", d=65)
                A("dve", lambda e, ov=ov, h0=h0: e.tensor_tensor(out=S(10)[:, 0:4], in0=ov[:, :, 64], in1=sm.t[:, SM_ES + h0:SM_ES + h0 + 7:2], op=ALU.add),
                  R=[pr(o), sm.r()], W=[sr])
                A("dve", lambda e: e.reciprocal(out=S(11)[:, 0:4], in_=S(10)[:, 0:4]), R=[sr], W=[sr])
                A("dve", lambda e, ov=ov, h0=h0: e.tensor_tensor(out=attn.t[:].rearrange("p (h d) -> p h d", d=64)[:, h0:h0 + 7:2, :], in0=ov[:, :, 0:64],
                                                                in1=S(11)[:, 0:4].unsqueeze(2).to_broadcast([128, 4, 64]), op=ALU.mult),
                  R=[pr(o), sr], W=[attn.r()])
                gi += 1
        bt = psa(2)
        A("pe", mm_group([(PS[:, bt + j // 4, (j % 4) * 128:(j % 4) * 128 + 128], attn.t[:, j * 128:(j + 1) * 128], ident_b, True, True)
                          for j in range(8)]), R=[attn.r(), cbf.r()], W=[pr(bt, 2)])
        A("act", lambda e: e.activation(out=mixT.t[:, 8:16, cs], in_=PS[:, bt:bt + 2, :].rearrange("p a (j t) -> p (a j) t", t=128), func=AF.Copy),
          R=[pr(bt, 2)], W=[mixT.r(8 * 512, 16 * 512)], c=1024)

    def mixer0(pre, bs=None):
        bs = SET_A if bs is None else bs
        hn, sq, xsT, BCT, dtr, dtv, lndt, hsrc = bs
        nt = cfg["nt"]
        nch = nt // 128
        rmsnorm("mixw", 0, hn, sq, rs_s, rs_r, hsrc)
        if not pre:
            wz0, rz0, iz0 = wnext("z0", hold=True)
            wz1, rz1, iz1 = wnext("z1", hold=True)
            for c in range(nch):
                b = psa(2)
                specs = []
                for hf, wv in ((0, wz0), (1, wz1)):
                    specs += [(PS[:, b + hf, :], hn.t[:, kt, c * 128:(c + 1) * 128], wv[:, kt, :], kt == 0, kt == 7) for kt in range(8)]
                A("pe", mm_group(specs), R=[hn.r(), rz0, rz1], W=[pr(b, 2)])
                A("act", lambda e, b=b, c=c: e.activation(out=sz.t[:, c, :], in_=PS[:, b:b + 2, :].rearrange("p a t -> p (a t)"), func=AF.Silu),
                  R=[pr(b, 2)], W=[sz.r(c * 1024, c * 1024 + 1024)], c=1024)
            wrel(iz0, iz1)
        for c in range(nch):
            b = psa()
            A("pe", mm_group([(PS[:, b, 0:144], hn.t[:, kt, c * 128:(c + 1) * 128], WDV.t[:, kt, :], kt == 0, kt == 7) for kt in range(8)]),
              R=[hn.r(), WDV.r()], W=[pr(b)])
            A("dve", lambda e, b=b, c=c: e.tensor_tensor(out=dtr.t[:, c, :], in0=PS[:, b, 0:16], in1=pc("dtb", 0, 16), op=ALU.add),
              R=[pr(b), prm.r()], W=[dtr.r()])
            if not pre:
                A("act", lambda e, b=b, c=c: e.activation(out=Vx.t[:, 1 + c, :, 0:64], in_=PS[:, b, 16:144].rearrange("p (k d) -> p k d", d=64),
                                                          func=AF.Copy), R=[pr(b)], W=[Vx.r()])
        fl = lambda t_: t_.t[:, 0:nch, :].rearrange("p a b -> p (a b)")
        A("act", lambda e: e.activation(out=fl(dtr), in_=fl(dtr), func=AF.Exp), R=[dtr.r()], W=[dtr.r()])
        A("act", lambda e: e.activation(out=fl(dtv), in_=fl(dtr), func=AF.Ln, bias=1.0), R=[dtr.r()], W=[dtv.r()])
        A("act", lambda e: e.activation(out=fl(lndt), in_=fl(dtv), func=AF.Ln), R=[dtv.r()], W=[lndt.r()])
        tiles = list(range(10)) if pre else list(range(12))
        wv = wr = None
        for j in tiles:
            if j == 0:
                wv, wr, _ = wnext("x0")
            elif j == 4:
                wv, wr, _ = wnext("x1")
            elif j == 8:
                wv, wr, _ = wnext("bc")
            b = proj_fm(wv, wr, j % 4, hn)
            xp_, acc_ = xp[j % 2], acc[j % 2]
            conv_fm(b, xp_, acc_, 4, lambda k, j=j: pc("cw", j * 4 + k), pc("cb", j), TL_A + 3 * j)
            dst = xsT.t[:, j, 0:nt] if j < 8 else BCT.t[:, j - 8, 0:nt]
            dres = xsT.r(j * 512, j * 512 + 512) if j < 8 else BCT.r((j - 8) * 512, (j - 8) * 512 + 512)
            A("act", lambda e, acc_=acc_, dst=dst: e.activation(out=dst, in_=acc_.t[:, 0:nt], func=AF.Silu), R=[acc_.r()], W=[dres], c=nt)
        if not pre:
            for j in range(12):
                if j == 0:
                    wv, wr, _ = wnext("q0")
                elif j == 4:
                    wv, wr, _ = wnext("q1")
                if j < 8:
                    b = proj_fm(wv, wr, j % 4, hn)
                else:
                    b = psa()
                    t = j - 8
                    A("pe", mm_group([(PS[:, b, 0:nt], WK.t[:, kt, t, :], hn.t[:, kt, 0:nt], kt == 0, kt == 7) for kt in range(8)]),
                      R=[WK.r(), hn.r()], W=[pr(b)])
                sq_ = sqq[j % 2]
                A("act", lambda e, b=b, sq_=sq_: e.activation(out=sq_.t[:, 0:nt], in_=PS[:, b, 0:nt], func=AF.Square), R=[pr(b)], W=[sq_.r()], c=nt)
                b2 = psa()
                A("pe", mm_group([(PS[:, b2, 0:nt], bd_b, sq_.t[:, 0:nt], True, True)]), R=[sq_.r(), cbf.r()], W=[pr(b2)])
                rs1, rs2 = (rs_s, rs_r) if j % 2 == 0 else (rs_s2, rs_r2)
                A("act", lambda e, b2=b2, rs1=rs1: e.activation(out=rs1.t[:, 0:nt], in_=PS[:, b2, 0:nt], func=AF.Ln, bias=sm.t[:, SM_E6:SM_E6 + 1], scale=1.0 / 64),
                  R=[pr(b2), sm.r()], W=[rs1.r()], c=nt)
                A("act", lambda e, rs1=rs1, rs2=rs2: e.activation(out=rs2.t[:, 0:nt], in_=rs1.t[:, 0:nt], func=AF.Exp, scale=-0.5), R=[rs1.r()], W=[rs2.r()], c=nt)
                if j < 8:
                    dst, dres, wc = qnT.t[:, j, 0:nt], qnT.r(j * 512, j * 512 + 512), sm.t[:, SM_QW:SM_QW + 1]
                else:
                    dst, dres, wc = Kp.t[:, j - 8, 128:128 + nt], Kp.r(), sm.t[:, SM_KW:SM_KW + 1]
                A("dve", lambda e, b=b, dst=dst, wc=wc, rs2=rs2: e.scalar_tensor_tensor(out=dst, in0=PS[:, b, 0:nt], scalar=wc, in1=rs2.t[:, 0:nt], op0=ALU.mult, op1=ALU.mult),
                  R=[pr(b), rs2.r(), sm.r()], W=[dres], c=nt)
        lab0 = P.label
        pool0 = dict(pspool)
        for c in range(nch):
            P.label = lab0 + f".ssd{c}"
            if not pre:
                pspool.update(lo=SSD_POOL[0], n=SSD_POOL[1])
            ssd_chunk(c, pre, bs)
            if not pre:
                P.label = lab0 + f".att{c}"
                pspool.update(lo=ATT_POOL[0], n=ATT_POOL[1])
                attn_chunk(c)
        pspool.update(pool0)
        P.label = lab0 + ".oproj"
        if pre:
            return
        A("act", lambda e: e.activation(out=Kp.t[:, :, 0:128], in_=Kp.t[:, :, nt:nt + 128], func=AF.Copy), R=[Kp.r()], W=[Kp.r()])
        A("act", lambda e: e.activation(out=Vx.t[:, 0, :, :], in_=Vx.t[:, nch, :, :], func=AF.Copy), R=[Vx.r()], W=[Vx.r()])
        for i in range(4):
            wv, wr, _ = wnext(f"o{i}")
            for f2 in range(2):
                f = 2 * i + f2
                b = psa()
                A("pe", mm_group([(PS[:, b, 0:nt], wv[:, kt, f2 * 128:(f2 + 1) * 128], mixT.t[:, kt, 0:nt], kt == 0, kt == 15) for kt in range(16)]),
                  R=[wr, mixT.r()], W=[pr(b)])
                A("dve", lambda e, b=b, f=f: e.tensor_tensor(out=hT.t[:, f, 0:nt], in0=hT.t[:, f, 0:nt], in1=PS[:, b, 0:nt], op=ALU.add),
                  R=[pr(b), hT.r(f * 512, f * 512 + 512)], W=[hT.r(f * 512, f * 512 + 512)], c=nt)
                ssq_hook(f, f_sq)

    def ffn(l, last, skip_down=False):
        nt = cfg["nt"]
        rmsnorm("ffnw", l, f_hn, f_sq, f_rs, f_rr, bank=nrm["bank"])
        for g in range(6):
            ntl = 4 if g < 5 else 2
            wg, rg, ig = wnext(f"g{g}", hold=True)
            wu, ru, iu = wnext(f"u{g}", hold=True)
            for i in range(ntl):
                t = 4 * g + i
                bg = proj_fm(wg, rg, i, f_hn)
                bu = proj_fm(wu, ru, i, f_hn)
                k2 = (t % 4) * 2
                for (b, tt, xi) in ((bg, t, k2), (bu, 22 + t, k2 + 1)):
                    conv_fm(b, f_xp[xi], f_acc[xi], 3, lambda k, tt=tt: pc("fcw", (l * 44 + tt) * 3 + k), pc("fcb", l * 44 + tt),
                            TL_F + l * 88 + 2 * tt)
                sg = f_sg[t % 4]
                ag, au = f_acc[k2], f_acc[k2 + 1]
                A("act", lambda e, sg=sg, ag=ag: e.activation(out=sg.t[:, 0:nt], in_=ag.t[:, 0:nt], func=AF.Silu), R=[ag.r()], W=[sg.r()])
                A("dve", lambda e, sg=sg, au=au, t=t: e.tensor_tensor(out=actT.t[:, t, 0:nt], in0=sg.t[:, 0:nt], in1=au.t[:, 0:nt], op=ALU.mult),
                  R=[sg.r(), au.r()], W=[actT.r(t * 512, t * 512 + 512)])
            wrel(ig, iu)
        for f in range(8):
            wv, wr, _ = wnext(f"d{f}")
            if skip_down:
                continue
            b = psa()
            A("pe", mm_group([(PS[:, b, 0:nt], wv[:, kt, :], actT.t[:, kt, 0:nt], kt == 0, kt == 21) for kt in range(22)]),
              R=[wr, actT.r()], W=[pr(b)])
            A("dve", lambda e, b=b, f=f: e.tensor_tensor(out=hT.t[:, f, 0:nt], in0=hT.t[:, f, 0:nt], in1=PS[:, b, 0:nt], op=ALU.add),
              R=[pr(b), hT.r(f * 512, f * 512 + 512)], W=[hT.r(f * 512, f * 512 + 512)])
            if not last:
                ssq_hook(f, c_sq)

    def conformer():
        nt = cfg["nt"]
        rmsnorm("mixw", 1, c_hn, c_sq, c_rs, c_rr, bank=nrm["bank"])
        for half in range(2):
            wa, ra, ia = wnext(f"a{half}", hold=True)
            wg, rg, ig = wnext(f"s{half}", hold=True)
            for i in range(4):
                j = half * 4 + i
                ba = proj_fm(wa, ra, i, c_hn)
                bg = proj_fm(wg, rg, i, c_hn)
                sg = c_sig[j % 2]
                A("act", lambda e, bg=bg, sg=sg, j=j: e.activation(out=sg.t[:, 0:nt], in_=PS[:, bg, 0:nt], func=AF.Sigmoid, bias=pc("pw1b", 8 + j)),
                  R=[pr(bg), prm.r()], W=[sg.r()])
                ur = upad.r(j * 544, j * 544 + 544)
                A("dve", lambda e, ba=ba, sg=sg, j=j: e.scalar_tensor_tensor(out=upad.t[:, j, 30:30 + nt], in0=PS[:, ba, 0:nt], scalar=pc("pw1b", j),
                                                                            in1=sg.t[:, 0:nt], op0=ALU.add, op1=ALU.mult),
                  R=[pr(ba), sg.r(), prm.r()], W=[ur])
                A("act", lambda e, j=j: e.activation(out=upad.t[:, j, 0:30], in_=TL.t[:, TL_C + 30 * j:TL_C + 30 * j + 30], func=AF.Copy),
                  R=[TL.r(TL_C + 30 * j, TL_C + 30 * j + 30)], W=[ur], c=30)
                A("act", lambda e, j=j: e.activation(out=TL.t[:, TL_C + 30 * j:TL_C + 30 * j + 30], in_=upad.t[:, j, nt:nt + 30], func=AF.Copy),
                  R=[ur], W=[TL.r(TL_C + 30 * j, TL_C + 30 * j + 30)], c=30)
            wrel(ia, ig)
            for i in range(4):
                j = half * 4 + i
                ur = upad.r(j * 544, j * 544 + 544)
                yr = yc.r(j * 512, j * 512 + 512)
                wd, rd, _ = wnext(f"dg{j}")
                bc_ = psa()
                A("pe", mm_group([(PS[:, bc_, 0:nt], wd[:, k, :], upad.t[:, j, k:k + nt], k == NDV, k == 30) for k in range(NDV, 31)]),
                  R=[rd, ur], W=[pr(bc_)])
                A("dve", lambda e, j=j: e.tensor_scalar(out=yc.t[:, j, 0:nt], in0=upad.t[:, j, 0:nt], scalar1=pc("dww", j * 31), scalar2=None, op0=ALU.mult),
                  R=[ur, prm.r()], W=[yr])
                for k in range(1, NDV):
                    A("dve", lambda e, j=j, k=k: e.scalar_tensor_tensor(out=yc.t[:, j, 0:nt], in0=upad.t[:, j, k:k + nt], scalar=pc("dww", j * 31 + k),
                                                                       in1=yc.t[:, j, 0:nt], op0=ALU.mult, op1=ALU.add), R=[ur, yr, prm.r()], W=[yr])
                A("dve", lambda e, j=j, bc_=bc_: e.scalar_tensor_tensor(out=yc.t[:, j, 0:nt], in0=PS[:, bc_, 0:nt], scalar=pc("dwb", j), in1=yc.t[:, j, 0:nt],
                                                                       op0=ALU.add, op1=ALU.add), R=[pr(bc_), yr, prm.r()], W=[yr])
                A("act", lambda e, j=j: e.activation(out=ybf.t[:, j, 0:nt], in_=yc.t[:, j, 0:nt], func=AF.Copy), R=[yr], W=[ybf.r(j * 512, j * 512 + 512)])
                A("act", lambda e, j=j: e.activation(out=c_sq.t[:, j, 0:nt], in_=yc.t[:, j, 0:nt], func=AF.Square), R=[yr], W=[c_sq.r(j * 512, j * 512 + 512)])
        b1 = psa()
        A("pe", mm_group([(PS[:, b1, 0:nt], ones_b.t[:], ybf.t[:, j, 0:nt], j == 0, j == 7) for j in range(8)]), R=[ybf.r(), ones_b.r()], W=[pr(b1)])
        b2 = psa()
        A("pe", mm_group([(PS[:, b2, 0:nt], ones_b.t[:], c_sq.t[:, j, 0:nt], j == 0, j == 7) for j in range(8)]), R=[c_sq.r(), ones_b.r()], W=[pr(b2)])
        A("dve", lambda e: e.tensor_scalar(out=c_mean.t[:, 0:nt], in0=PS[:, b1, 0:nt], scalar1=1.0 / 1024, scalar2=None, op0=ALU.mult), R=[pr(b1)], W=[c_mean.r()])
        A("dve", lambda e: e.tensor_tensor(out=c_msq.t[:, 0:nt], in0=c_mean.t[:, 0:nt], in1=c_mean.t[:, 0:nt], op=ALU.mult), R=[c_mean.r()], W=[c_msq.r()])
        A("dve", lambda e: e.scalar_tensor_tensor(out=c_var.t[:, 0:nt], in0=PS[:, b2, 0:nt], scalar=1.0 / 1024, in1=c_msq.t[:, 0:nt], op0=ALU.mult, op1=ALU.subtract),
          R=[pr(b2), c_msq.r()], W=[c_var.r()])
        A("act", lambda e: e.activation(out=c_rs.t[:, 0:nt], in_=c_var.t[:, 0:nt], func=AF.Ln, bias=sm.t[:, SM_E5:SM_E5 + 1], scale=1.0), R=[c_var.r(), sm.r()], W=[c_rs.r()])
        A("act", lambda e: e.activation(out=c_rr.t[:, 0:nt], in_=c_rs.t[:, 0:nt], func=AF.Exp, scale=-0.5), R=[c_rs.r()], W=[c_rr.r()])
        for j in range(8):
            yr = yc.r(j * 512, j * 512 + 512)
            A("dve", lambda e, j=j: e.tensor_tensor(out=yc.t[:, j, 0:nt], in0=yc.t[:, j, 0:nt], in1=c_mean.t[:, 0:nt], op=ALU.subtract), R=[yr, c_mean.r()], W=[yr])
            A("dve", lambda e, j=j: e.scalar_tensor_tensor(out=yc.t[:, j, 0:nt], in0=yc.t[:, j, 0:nt], scalar=pc("lnw", j), in1=c_rr.t[:, 0:nt], op0=ALU.mult, op1=ALU.mult),
              R=[yr, c_rr.r(), prm.r()], W=[yr])
            A("act", lambda e, j=j: e.activation(out=ybf.t[:, j, 0:nt], in_=yc.t[:, j, 0:nt], func=AF.Silu, bias=pc("lnb", j)), R=[yr, prm.r()],
              W=[ybf.r(j * 512, j * 512 + 512)])
        for half in range(2):
            wv, wr, _ = wnext(f"p{half}")
            for i in range(4):
                f = half * 4 + i
                b = proj_fm(wv, wr, i, ybf)
                A("dve", lambda e, b=b, f=f: e.scalar_tensor_tensor(out=hT.t[:, f, 0:nt], in0=PS[:, b, 0:nt], scalar=pc("pw2b", f), in1=hT.t[:, f, 0:nt],
                                                                   op0=ALU.add, op1=ALU.add),
                  R=[pr(b), hT.r(f * 512, f * 512 + 512), prm.r()], W=[hT.r(f * 512, f * 512 + 512)])
                ssq_hook(f, f_sq)

    blocks = [("pre", 512 * i, 512) for i in range(NPRE)] + [("pre", 512 * NPRE, 256), ("warm", 512 * NPRE + 256, 256)]
    blocks += [("main", 512 * (NPRE + k), 512) for k in range(1, NMAIN)]
    npre_seen = 0
    nout = 0
    for bi, (kind, row0, ntok) in enumerate(blocks):
        cfg["nt"] = ntok
        P.label = f"b{bi}.in"
        if kind == "pre":
            par = npre_seen % 2
            npre_seen += 1
            stage_in(row0, hTB if par == 1 else hT)
            P.label = f"b{bi}.pre"
            mixer0(True, SET_A if par == 0 else SET_B)
            continue
        stage_in(row0, hT)
        P.label = f"b{bi}.m0"
        mixer0(False)
        if kind == "warm" and "h0" in dbg_d:
            dump("h0", hT)
        P.label = f"b{bi}.f0"
        ffn(0, False)
        P.label = f"b{bi}.cf"
        conformer()
        P.label = f"b{bi}.f1"
        ffn(1, True, skip_down=(kind == "warm"))
        P.label = f"b{bi}.out"
        if kind == "warm":
            f1 = flg.t[:, 0:1]
            A("dve", lambda e: e.tensor_scalar(out=st_f.t[:], in0=st_f.t[:], scalar1=f1, scalar2=None, op0=ALU.mult), R=[st_f.r(), flg.r()], W=[st_f.r()])
            A("dve", lambda e: e.tensor_scalar(out=st_b.t[:], in0=st_b.t[:], scalar1=f1, scalar2=None, op0=ALU.mult), R=[st_b.r(), flg.r()], W=[st_b.r()])
            A("dve", lambda e: e.tensor_scalar(out=TL.t[:], in0=TL.t[:], scalar1=f1, scalar2=None, op0=ALU.mult), R=[TL.r(), flg.r()], W=[TL.r()])
            A("dve", lambda e: e.tensor_scalar(out=Vx.t[:, 0, :, :].rearrange("p a b -> p (a b)"), in0=Vx.t[:, 0, :, :].rearrange("p a b -> p (a b)"),
                                               scalar1=f1, scalar2=None, op0=ALU.mult), R=[Vx.r(), flg.r()], W=[Vx.r()])
        else:
            stage_out(nout)
            nout += 512
    fin = [f"dmachain_{xo.name}" for xo in xout] + [f"dmachain_dbg_{n}" for n in dbg_d]
    P.add("sp", None, fin, ())
    if not do_emit:
        P.sim_ns = P.schedule()
        return nc, P, allocs
    nsem, nops = P.emit()
    print(f"[build] sbuf_peak={sbuf_peak} sems={nsem} ops={nops} loads={len(stream)} sim_us={P.sim_ns / 1e3:.0f}", flush=True)
    return nc, P, allocs


def plan_psum(allocs):
    last_r = [-1] * 8
    last_t = [0.0] * 8
    plan = []
    for n, _b, ops in allocs:
        if not ops:
            plan.append(0)
            continue
        r0 = min(o.ridx for o in ops)
        r1 = max(o.ridx for o in ops)
        t0 = min(o.t0 for o in ops)
        t1 = max(o.t1 for o in ops)
        best = None
        for b in (range(8) if n == 1 else range(0, 8, 2)):
            bs = range(b, b + n)
            if any(last_r[i] >= r0 for i in bs):
                continue
            te = max(last_t[i] for i in bs)
            key = (max(te - t0, 0.0), -te)
            if best is None or key < best[0]:
                best = (key, b)
        assert best is not None, "PSUM over-subscribed in program order"
        b = best[1]
        for i in range(b, b + n):
            last_r[i] = r1
            last_t[i] = t1
        plan.append(b)
    return plan


def build_program(NPRE, NMAIN, dbg=(), iters=PLAN_ITERS):
    plan = None
    best = None
    for it in range(iters):
        _nc, P1, allocs = record_program(NPRE, NMAIN, dbg, plan=plan, do_emit=False)
        if best is None or P1.sim_ns < best[0]:
            best = (P1.sim_ns, plan)
        print(f"[build] plan iter {it}: sim_us={P1.sim_ns / 1e3:.0f}", flush=True)
        plan = plan_psum(allocs)
    nc, _P, _a = record_program(NPRE, NMAIN, dbg, plan=best[1], do_emit=True)
    return nc


def _fm(v, nt):
    return np.ascontiguousarray(np.asarray(v, np.float32).reshape(nt, 128).T)


def _t5_bucket(dist):
    max_exact = 16
    d_f = np.maximum(dist, 1).astype(np.float32)
    large = max_exact + (np.log(d_f / max_exact) / math.log(128 / max_exact) * (32 - max_exact)).astype(np.int32)
    large = np.minimum(large, 31)
    return np.where(dist < max_exact, dist, large)


def host_pack(inp):
    f32 = np.float32
    prm = np.zeros((128, NPRM), f32)

    def put(name, arr):
        arr = np.asarray(arr, f32)
        prm[:, _p[name]:_p[name] + arr.shape[1]] = arr
    put("mixw", np.concatenate([_fm(inp["mix_norm_w"][l], 8) for l in range(2)], 1))
    put("ffnw", np.concatenate([_fm(inp["ffn_norm_w"][l], 8) for l in range(2)], 1))
    cw = np.asarray(inp["ssm_conv_w"][0], f32)
    put("cw", cw.T.reshape(12, 128, 4).transpose(1, 0, 2).reshape(128, 48))
    put("cb", _fm(inp["ssm_conv_b"][0], 12))
    put("dch", _fm(np.repeat(np.asarray(inp["ssm_d"][0], f32), 64), 8))
    put("snw", _fm(inp["ssm_norm_w"][0], 8))
    put("qw", np.tile(np.asarray(inp["attn_q_norm_w"][0], f32), 2)[:, None])
    put("kw", np.tile(np.asarray(inp["attn_k_norm_w"][0], f32), 2)[:, None])
    put("pw1b", _fm(inp["conv_pw1_b"][0], 16))
    dw = np.asarray(inp["conv_dw_w"][0], f32)
    put("dww", dw.T.reshape(8, 128, 31).transpose(1, 0, 2).reshape(128, 248))
    put("dwb", _fm(inp["conv_dw_b"][0], 8))
    put("lnw", _fm(inp["conv_ln_w"][0], 8))
    put("lnb", _fm(inp["conv_ln_b"][0], 8))
    put("pw2b", _fm(inp["conv_pw2_b"][0], 8))
    fcw = np.asarray(inp["ffn_conv_w"], f32)
    put("fcw", fcw.transpose(0, 2, 1).reshape(2, 44, 128, 3).transpose(2, 0, 1, 3).reshape(128, 264))
    put("fcb", np.asarray(inp["ffn_conv_b"], f32).reshape(2, 44, 128).transpose(2, 0, 1).reshape(128, 88))
    put("dtb", np.broadcast_to(np.asarray(inp["ssm_dt_bias"][0], f32)[None, :], (128, 16)))
    put("alog", np.broadcast_to(np.asarray(inp["ssm_a_log"][0], f32)[None, :], (128, 16)))
    put("sink", np.broadcast_to(np.asarray(inp["attn_sinks"][0], f32)[None, :], (128, 16)))
    rb = np.asarray(inp["rel_bias"], f32)
    qi = np.arange(128)[:, None]
    sj = np.arange(256)[None, :]
    dist = qi + 128 - sj
    valid = (dist >= 0) & (dist < 128)
    bias = rb[_t5_bucket(np.maximum(dist, 0))]
    bias = np.where(valid[:, :, None], bias, f32(NEG)).astype(f32)
    biasT = bias.reshape(128, 2, 128, 16).transpose(2, 3, 1, 0)
    biasT = np.ascontiguousarray(biasT).reshape(128, 16 * 2 * 128)
    cst = np.zeros((128, 512), f32)
    cst[:, 0:128] = np.eye(128, dtype=f32)
    cst[:, 128:256] = np.triu(np.ones((128, 128), f32))
    cst[:, 256:384] = np.kron(np.eye(2, dtype=f32), np.ones((64, 64), f32))
    cst[:, 384:512] = np.where(np.arange(128)[None, :] >= np.arange(128)[:, None], 0.0, NEG)
    eall = np.zeros((48, 16, 128), f32)
    for h in range(16):
        eall[h, h, :] = 1.0
        eall[32 + h, h, :] = 1.0
    return prm, biasT, cst, eall.reshape(48, 2048)


_NC_CACHE = {}


def kernel(**inp):
    x = np.asarray(inp["x"], np.float32)
    B, L, _ = x.shape
    NPRE, NMAIN = 7, 9
    prm, biasT, cst, eall = host_pack(inp)
    common = {
        "prm": prm, "biasT": biasT, "cst": cst, "eall": eall,
        "w_in": np.ascontiguousarray(inp["hyb_w_in"][0], np.float32),
        "w_out": np.ascontiguousarray(inp["hyb_w_out"][0], np.float32),
        "pw1": np.ascontiguousarray(inp["conv_pw1_w"][0], np.float32),
        "pw2": np.ascontiguousarray(inp["conv_pw2_w"][0], np.float32),
        "wup0": np.ascontiguousarray(inp["ffn_w_up"][0], np.float32),
        "wup1": np.ascontiguousarray(inp["ffn_w_up"][1], np.float32),
        "wdn0": np.ascontiguousarray(inp["ffn_w_down"][0], np.float32),
        "wdn1": np.ascontiguousarray(inp["ffn_w_down"][1], np.float32),
    }
    in_maps = []
    for core in range(8):
        b, half = core // 2, core % 2
        if half == 1:
            xs = x[b]
            flag = np.ones((128, 1), np.float32)
        else:
            xs = np.concatenate([np.zeros((4096, D), np.float32), x[b, :4096]], 0)
            flag = np.zeros((128, 1), np.float32)
        m = dict(common)
        m["x"] = np.ascontiguousarray(xs)
        m["flag"] = flag
        in_maps.append(m)
    if "nc" not in _NC_CACHE:
        _NC_CACHE["nc"] = build_program(NPRE, NMAIN)
    res = run_bass_kernel_spmd(_NC_CACHE["nc"], in_maps, core_ids=list(range(8)))
    out = np.empty((B, L, D), np.float32)
    for core in range(8):
        b, half = core // 2, core % 2
        out[b, half * 4096:(half + 1) * 4096] = res.results[core]["y"]
    return out
```

```python
import contextlib
import math
import numpy as np
import concourse.bass as bass
import concourse.mybir as mybir
from concourse.bass_utils import run_bass_kernel_spmd

F32 = mybir.dt.float32
BF16 = mybir.dt.bfloat16
AF = mybir.ActivationFunctionType
ALU = mybir.AluOpType
AX = mybir.AxisListType

ENGS = ("pe", "act", "dve", "pool", "sp")
EPOCH = 8000
DMA_BPNS = 340.0
SCHED_W = 128
SCHED_EPS = 0.0
XLAT = 220.0
SSD_POOL = (0, 4)
ATT_POOL = (4, 4)
PLAN_ITERS = 4
NDV = 10
GR = 64
SB_LO = 16640
SB_HI = 229376


class Res:
    __slots__ = ("name", "last_writer", "readers", "dma_cnt")

    def __init__(self, name):
        self.name = name
        self.last_writer = None
        self.readers = []
        self.dma_cnt = 0


class Op:
    __slots__ = ("idx", "eng", "fn", "deps", "is_dma", "sync", "eidx", "signal", "waits", "snap", "dval", "cost", "occ",
                 "t0", "t1", "label", "ridx")


class Prog:
    def __init__(self, nc):
        self.nc = nc
        self.ops = []
        self.res = {}

    def R(self, name):
        r = self.res.get(name)
        if r is None:
            r = self.res[name] = Res(name)
        return r

    def add(self, eng, fn, reads=(), writes=(), dma=None, cost=500.0, occ=None):
        op = Op()
        op.ridx = len(self.ops)
        op.label = getattr(self, "label", "")
        op.cost = cost
        op.occ = cost if occ is None else occ
        op.idx = len(self.ops)
        op.eng = eng
        op.fn = fn
        op.is_dma = dma is not None
        op.sync = self.R("dmasem_" + dma) if dma is not None else None
        deps = set()
        rs = [self.R(r) for r in set(reads)]
        ws = [self.R(w) for w in set(writes)]
        if op.is_dma:
            ws.append(self.R("dmachain_" + dma))
        for r in rs:
            if r.last_writer is not None:
                deps.add(r.last_writer)
        for w in ws:
            if w.last_writer is not None:
                deps.add(w.last_writer)
            deps.update(w.readers)
        for w in ws:
            w.last_writer = op.idx
            w.readers = []
        for r in rs:
            r.readers.append(op.idx)
        deps.discard(op.idx)
        op.deps = deps
        self.ops.append(op)
        return op

    def schedule(self, W=None):
        ops = self.ops
        if W is None:
            W = SCHED_W
        from collections import deque
        pend = {e: deque(op for op in ops if op.eng == e) for e in ENGS}
        tfree = {e: 0.0 for e in ENGS}
        self._dma_free = 0.0
        rank = [0.0] * len(ops)
        for op in reversed(ops):
            r = rank[op.idx] + op.cost
            rank[op.idx] = r
            for d in op.deps:
                if rank[d] < r:
                    rank[d] = r
        for op in ops:
            op.t0 = None
        left = len(ops)
        EPS = SCHED_EPS
        while left:
            best = None
            for e in ENGS:
                q = pend[e]
                n = 0
                cands = []
                smin = None
                for op in q:
                    if n >= W:
                        break
                    n += 1
                    rdy = 0.0
                    ok = True
                    for d in op.deps:
                        dop = ops[d]
                        if dop.t0 is None:
                            ok = False
                            break
                        t1d = dop.t1 if dop.eng == e else dop.t1 + XLAT
                        if t1d > rdy:
                            rdy = t1d
                    if not ok:
                        continue
                    st = rdy if rdy > tfree[e] else tfree[e]
                    cands.append((st, op))
                    if smin is None or st < smin:
                        smin = st
                if smin is None:
                    continue
                pick = None
                for st, op in cands:
                    if st <= smin + EPS:
                        k2 = (-rank[op.idx], op.idx)
                        if pick is None or k2 < pick[0]:
                            pick = (k2, st, op)
                key = (pick[1], pick[2].idx)
                if best is None or key < best[0]:
                    best = (key, pick[2])
            key, op = best
            op.t0 = key[0]
            if op.is_dma:
                xfer = max(op.cost - op.occ - 2000.0, 0.0)
                st = max(op.t0 + op.occ, self._dma_free)
                self._dma_free = st + xfer
                op.t1 = st + xfer + 2000.0
            else:
                op.t1 = op.t0 + op.cost
            tfree[op.eng] = op.t0 + op.occ
            pend[op.eng].remove(op)
            left -= 1
        new = sorted(ops, key=lambda o: (o.t0, o.idx))
        remap = {o.idx: i for i, o in enumerate(new)}
        for o in new:
            o.deps = {remap[d] for d in o.deps}
        for i, o in enumerate(new):
            o.idx = i
        self.ops = new
        return max(o.t1 for o in new)

    def emit(self, sched=True):
        nc = self.nc
        self.sim_ns = self.schedule() if sched else 0.0
        ops = self.ops
        cnt = {e: 0 for e in ENGS}
        for op in ops:
            op.eidx = cnt[op.eng]
            cnt[op.eng] += 1
            op.signal = False
            op.waits = []
        known = {e: {f: -1 for f in ENGS} for e in ENGS}
        kdma = {e: set() for e in ENGS}
        for op in ops:
            kn = known[op.eng]
            kd = kdma[op.eng]
            for d in sorted(op.deps):
                dop = ops[d]
                if dop.is_dma:
                    if d in kd:
                        continue
                    kd.add(d)
                else:
                    if kn[dop.eng] >= dop.eidx:
                        continue
                    kn[dop.eng] = dop.eidx
                dop.signal = True
                op.waits.append(d)
                sk, sd = dop.snap
                for f in ENGS:
                    if sk[f] > kn[f]:
                        kn[f] = sk[f]
                kd |= sd
            op.snap = (dict(kn), frozenset(kd))
        scount = {e: 0 for e in ENGS}
        keys = []
        for op in ops:
            if op.is_dma:
                op.sync.dma_cnt += 16
                op.dval = (op.sync.name, op.sync.dma_cnt)
            elif op.signal:
                c = scount[op.eng]
                scount[op.eng] += 1
                op.dval = (f"e_{op.eng}_{c // EPOCH}", c % EPOCH + 1)
            else:
                op.dval = None
            if op.dval is not None and op.dval[0] not in keys:
                keys.append(op.dval[0])
        with contextlib.ExitStack() as st:
            semh = {k: st.enter_context(nc.semaphore(k)) for k in keys}
            block = st.enter_context(nc.Block())
            per = {e: [op for op in ops if op.eng == e] for e in ENGS}

            def run(engobj, lst):
                for op in lst:
                    for d in op.waits:
                        k, v = ops[d].dval
                        engobj.wait_ge(semh[k], v)
                    if op.fn is None:
                        continue
                    ins = op.fn(engobj)
                    if op.is_dma:
                        ins.then_inc(semh[op.dval[0]], 16)
                    elif op.signal:
                        ins.then_inc(semh[op.dval[0]], 1)

            block.tensor(lambda e: run(e, per["pe"]))
            block.scalar(lambda e: run(e, per["act"]))
            block.vector(lambda e: run(e, per["dve"]))
            block.gpsimd(lambda e: run(e, per["pool"]))
            block.sync(lambda e: run(e, per["sp"]))
        return len(keys), {e: len(per[e]) for e in ENGS}


class T:
    def __init__(self, nc, name, shape, dtype, off):
        self.esz = 2 if dtype == BF16 else 4
        self.n = int(np.prod(shape[1:]))
        self.off = off
        self.t = nc.alloc_sbuf_tensor_at(name, list(shape), dtype, offset=off)
        self.name = name

    def r(self, lo=0, hi=None):
        if hi is None:
            hi = self.n
        b0 = (self.off + lo * self.esz) // GR
        b1 = (self.off + hi * self.esz - 1) // GR
        return [f"sb{g}" for g in range(b0, b1 + 1)]


class Arena:
    def __init__(self, nc):
        self.nc = nc
        self.cur = SB_LO
        self.peak = SB_LO
        self.k = 0

    def alloc(self, name, shape, dtype):
        esz = 2 if dtype == BF16 else 4
        nbytes = int(np.prod(shape[1:])) * esz
        off = (self.cur + 63) // 64 * 64
        self.cur = off + nbytes
        self.peak = max(self.peak, self.cur)
        assert self.cur <= SB_HI, (name, self.cur)
        self.k += 1
        return T(self.nc, f"{name}_{self.k}", shape, dtype, off)


D = 1024
IN_TOTAL = 3856
C_Z, C_X, C_B, C_C, C_DT, C_Q, C_K, C_V = 0, 1024, 2048, 2304, 2560, 2576, 3600, 3728
DFF = 2816
NEG = -30000.0

_p = {}
_o = 0
for _n, _w in [("mixw", 16), ("ffnw", 16), ("cw", 48), ("cb", 12), ("dch", 8), ("snw", 8), ("qw", 1), ("kw", 1),
               ("pw1b", 16), ("dww", 248), ("dwb", 8), ("lnw", 8), ("lnb", 8), ("pw2b", 8),
               ("fcw", 264), ("fcb", 88), ("dtb", 16), ("alog", 16), ("sink", 16)]:
    _p[_n] = _o
    _o += _w
NPRM = _o
TL_A, TL_F, TL_C = 0, 36, 36 + 176
NTL = 36 + 176 + 240


def record_program(NPRE, NMAIN, dbg=(), plan=None, do_emit=True):
    nc = bass.Bass("TRN2", target_bir_lowering=False)
    NT = 512 * (NPRE + NMAIN)
    NOUT = 512 * (NMAIN - 1)
    dt_ = nc.dram_tensor
    x_d = dt_("x", [NT, D], F32, kind="ExternalInput").ap()
    flag_d = dt_("flag", [128, 1], F32, kind="ExternalInput").ap()
    prm_d = dt_("prm", [128, NPRM], F32, kind="ExternalInput").ap()
    bias_d = dt_("biasT", [128, 16 * 2 * 128], F32, kind="ExternalInput").ap()
    cst_d = dt_("cst", [128, 512], F32, kind="ExternalInput").ap()
    eall_d = dt_("eall", [48, 2048], F32, kind="ExternalInput").ap()
    win_d = dt_("w_in", [D, IN_TOTAL], F32, kind="ExternalInput").ap()
    wout_d = dt_("w_out", [2048, D], F32, kind="ExternalInput").ap()
    pw1_d = dt_("pw1", [D, 2048], F32, kind="ExternalInput").ap()
    pw2_d = dt_("pw2", [D, D], F32, kind="ExternalInput").ap()
    wup_d = [dt_(f"wup{l}", [D, 2 * DFF], F32, kind="ExternalInput").ap() for l in range(2)]
    wdn_d = [dt_(f"wdn{l}", [DFF, D], F32, kind="ExternalInput").ap() for l in range(2)]
    y_d = dt_("y", [NOUT, D], F32, kind="ExternalOutput").ap()
    diag_d = dt_("diag_scr", [8, 128, 31 * 128], BF16).ap()
    dbg_d = {n: dt_("dbg_" + n, [128, 4096], F32, kind="ExternalOutput").ap() for n in dbg}

    P = Prog(nc)
    ar = Arena(nc)
    PSt = nc.alloc_psum_tensor("PS", [128, 8, 512], F32)
    PS = PSt
    pspool = {"lo": 0, "n": 8}
    pscnt = {}
    allocs = []
    cur_alloc = {}

    def psa(n=1):
        k = len(allocs)
        if plan is not None:
            b = plan[k]
        else:
            key = (0, 8)
            c = pscnt.get(key, 0)
            if n == 2 and c % 2 == 1:
                c += 1
            b = c % 8
            pscnt[key] = c + n
        allocs.append([n, b, []])
        for i in range(n):
            cur_alloc[b + i] = k
        return b

    def pr(b, n=1):
        return [f"pb{b + i}" for i in range(n)]

    def A(eng, fn, R=(), W=(), dma=None, c=512, nbytes=1 << 19, recip=False):
        rl = [x for l in R for x in l]
        wl = [x for l in W for x in l]
        occ = None
        if dma is not None:
            occ = 1200.0 if eng == "pool" else 150.0
            cost = occ + 2000.0 + nbytes / DMA_BPNS
        elif eng == "pe":
            cost = getattr(fn, "cost", 300.0)
        elif eng == "act":
            cost = 220.0 + 0.95 * c
        else:
            cost = 80.0 + (3.2 if recip else 1.3) * c
        if eng != "pe":
            toks = ["px" + nm[2:] for nm in set(rl + wl) if nm.startswith("pb")]
            if eng == "act":
                rl = rl + toks
            else:
                wl = wl + toks
        op = P.add(eng, fn, rl, wl, dma, cost, occ)
        for nm in rl + wl:
            if nm.startswith("pb"):
                lst = allocs[cur_alloc[int(nm[2:])]][2]
                if not lst or lst[-1] is not op:
                    lst.append(op)
        return op

    hT = ar.alloc("hT", [128, 8, 512], F32)
    NSLOT = 4
    WS = [ar.alloc(f"ws{i}", [128, 4096], BF16) for i in range(NSLOT)]
    WK = ar.alloc("wk", [128, 8, 4, 128], BF16)
    WDV = ar.alloc("wdv", [128, 8, 144], BF16)
    biasT = ar.alloc("biasT", [128, 16, 2, 128], BF16)
    Eall = ar.alloc("eall", [48, 16, 128], BF16)
    prm = ar.alloc("prm", [128, NPRM], F32)
    cst = ar.alloc("cst", [128, 512], F32)
    cbf = ar.alloc("cbf", [128, 512], BF16)
    ones_f = ar.alloc("ones_f", [128, 128], F32)
    ones_b = ar.alloc("ones_b", [128, 128], BF16)
    diagD = ar.alloc("diagD", [128, 8, 128], BF16)
    sm = ar.alloc("sm", [128, 128], F32)
    st_f = ar.alloc("st_f", [128, 1024], F32)
    st_b = ar.alloc("st_b", [128, 1024], BF16)
    TL = ar.alloc("TL", [128, NTL], F32)
    Kp = ar.alloc("Kp", [128, 4, 640], BF16)
    Vx = ar.alloc("Vx", [128, 5, 2, 65], BF16)
    flg = ar.alloc("flg", [128, 1], F32)
    xin = [ar.alloc(f"xin{i}", [128, 1024], F32) for i in range(2)]
    ident_f = cst.t[:, 0:128]
    U_f = cst.t[:, 128:256]
    ident_b = cbf.t[:, 0:128]
    bd_b = cbf.t[:, 256:384]
    negm_b = cbf.t[:, 384:512]
    SM_A, SM_ES, SM_E6, SM_E5, SM_QW, SM_KW = 0, 16, 32, 33, 34, 35

    def pc(name, j=0, n=1):
        return prm.t[:, _p[name] + j:_p[name] + j + n]

    base_mark = ar.cur

    xsT = ar.alloc("xsT", [128, 8, 512], BF16)
    BCT = ar.alloc("BCT", [128, 4, 512], BF16)
    qnT = ar.alloc("qnT", [128, 8, 512], BF16)
    sz = ar.alloc("sz", [128, 4, 1024], F32)
    mixT = ar.alloc("mixT", [128, 16, 512], BF16)
    dtr = ar.alloc("dtr", [128, 4, 16], F32)
    dtv = ar.alloc("dtv", [128, 4, 16], F32)
    lndt = ar.alloc("lndt", [128, 4, 16], F32)
    m0_mark = ar.cur
    ar.cur = qnT.off
    hnB = ar.alloc("hnB", [128, 8, 512], BF16)
    ar.cur = mixT.off
    xsTB = ar.alloc("xsTB", [128, 8, 512], BF16)
    BCTB = ar.alloc("BCTB", [128, 4, 512], BF16)
    dtrB = ar.alloc("dtrB", [128, 4, 16], F32)
    dtvB = ar.alloc("dtvB", [128, 4, 16], F32)
    lndtB = ar.alloc("lndtB", [128, 4, 16], F32)
    xtokB = ar.alloc("xtokB", [128, 1024], BF16)
    btokB = ar.alloc("btokB", [128, 256], BF16)
    assert ar.cur <= mixT.off + 16 * 512 * 2
    ar.cur = sz.off
    hTB = ar.alloc("hTB", [128, 8, 512], F32)
    ar.cur = m0_mark
    hn = ar.alloc("hn", [128, 8, 512], BF16)
    sq = ar.alloc("sq", [128, 8, 512], BF16)
    xp = [ar.alloc(f"xp{i}", [128, 515], F32) for i in range(2)]
    acc = [ar.alloc(f"acc{i}", [128, 512], F32) for i in range(2)]
    sqq = [ar.alloc(f"sqq{i}", [128, 512], BF16) for i in range(2)]
    rs_s = ar.alloc("rs_s", [128, 512], F32)
    rs_r = ar.alloc("rs_r", [128, 512], F32)
    rs_s2 = ar.alloc("rs_s2", [128, 512], F32)
    rs_r2 = ar.alloc("rs_r2", [128, 512], F32)
    a13_end = ar.cur
    SET_A = (hn, sq, xsT, BCT, dtr, dtv, lndt, hT)
    SET_B = (hnB, sq, xsTB, BCTB, dtrB, dtvB, lndtB, hTB)
    ar.cur = m0_mark
    xtok = ar.alloc("xtok", [128, 1024], BF16)
    btok = ar.alloc("btok", [128, 256], BF16)
    xw = ar.alloc("xw", [128, 1024], BF16)
    dec = [ar.alloc(f"dec{i}", [128, 1024], F32) for i in range(2)]
    MT = ar.alloc("MT", [128, 16, 128], BF16)
    tmpf = ar.alloc("tmpf", [128, 1024], F32)
    yv = ar.alloc("yv", [128, 1024], F32)
    yn = ar.alloc("yn", [128, 1024], BF16)
    pT = [ar.alloc(f"pT{i}", [128, 512], BF16) for i in range(4)]
    attn = ar.alloc("attn", [128, 1024], BF16)
    s16s = [ar.alloc(f"s16_{i}", [128, 16, 16], F32) for i in range(4)]
    a16s = [ar.alloc(f"a16_{i}", [128, 8, 16], F32) for i in range(4)]
    acsTs = [ar.alloc(f"acsT{i}", [48, 128], BF16) for i in range(4)]
    qTs = [ar.alloc(f"qT{i}", [48, 128], BF16) for i in range(4)]
    a48s = [ar.alloc(f"a48_{i}", [128, 48], F32) for i in range(4)]
    q48s = [ar.alloc(f"q48_{i}", [128, 48], F32) for i in range(4)]
    ar.cur = max(ar.cur, a13_end)
    m0_end = ar.cur
    ar.cur = base_mark
    f_hn = ar.alloc("f_hn", [128, 8, 512], BF16)
    f_sq = ar.alloc("f_sq", [128, 8, 512], BF16)
    f_rs = ar.alloc("f_rs", [128, 512], F32)
    f_rr = ar.alloc("f_rr", [128, 512], F32)
    actT = ar.alloc("actT", [128, 22, 512], BF16)
    f_xp = [ar.alloc(f"f_xp{i}", [128, 514], F32) for i in range(8)]
    f_acc = [ar.alloc(f"f_acc{i}", [128, 512], F32) for i in range(8)]
    f_sg = [ar.alloc(f"f_sg{i}", [128, 512], F32) for i in range(4)]
    f_end = ar.cur
    ar.cur = base_mark
    c_hn = ar.alloc("c_hn", [128, 8, 512], BF16)
    c_sq = ar.alloc("c_sq", [128, 8, 512], BF16)
    c_rs = ar.alloc("c_rs", [128, 512], F32)
    c_rr = ar.alloc("c_rr", [128, 512], F32)
    upad = ar.alloc("upad", [128, 8, 544], BF16)
    yc = ar.alloc("yc", [128, 8, 512], F32)
    ybf = ar.alloc("ybf", [128, 8, 512], BF16)
    c_sig = [ar.alloc(f"c_sig{i}", [128, 512], F32) for i in range(2)]
    c_mean = ar.alloc("c_mean", [128, 512], F32)
    c_msq = ar.alloc("c_msq", [128, 512], F32)
    c_var = ar.alloc("c_var", [128, 512], F32)
    c_end = ar.cur
    ar.cur = base_mark
    dgt = ar.alloc("dgt", [128, 31 * 128], BF16)
    ar.cur = base_mark
    xout = [ar.alloc(f"xout{i}", [128, 1024], F32) for i in range(2)]
    ar.cur = ar.peak
    xtokA = ar.alloc("xtokA", [128, 1024], BF16)
    btokA = ar.alloc("btokA", [128, 256], BF16)
    xwA = ar.alloc("xwA", [128, 1024], BF16)
    xwB = ar.alloc("xwB", [128, 1024], BF16)
    PRE_A = (xtokA, btokA, xwA)
    PRE_B = (xtokB, btokB, xwB)
    sbuf_peak = ar.peak

    A("sp", lambda e: e.dma_start(out=prm.t[:], in_=prm_d), W=[prm.r()], dma="prm")
    A("sp", lambda e: e.dma_start(out=cst.t[:], in_=cst_d), W=[cst.r()], dma="cst")
    A("pool", lambda e: e.dma_start(out=biasT.t[:].rearrange("p a b c -> p (a b c)"), in_=bias_d), W=[biasT.r()], dma="biasT")
    A("pool", lambda e: e.dma_start(out=Eall.t[:].rearrange("p a b -> p (a b)"), in_=eall_d), W=[Eall.r()], dma="eall")
    A("sp", lambda e: e.dma_start(out=flg.t[:], in_=flag_d), W=[flg.r()], dma="flg")
    A("dve", lambda e: e.memset(WK.t[:].rearrange("p a b c -> p (a b c)"), 0.0), W=[WK.r()])
    for kv in range(2):
        for pad in range(2):
            def f(e, kv=kv, pad=pad):
                return e.dma_start(out=WK.t[:, :, 2 * kv + pad, 64 * pad:64 * pad + 64],
                                   in_=win_d[:, C_K + 64 * kv:C_K + 64 * kv + 64].rearrange("(k p) c -> p k c", p=128))
            A("pool", f, W=[WK.r()], dma="wk")
    A("pool", lambda e: e.dma_start(out=WDV.t[:, :, 0:16], in_=win_d[:, C_DT:C_DT + 16].rearrange("(k p) c -> p k c", p=128)),
      W=[WDV.r()], dma="wdv")
    A("pool", lambda e: e.dma_start(out=WDV.t[:, :, 16:144], in_=win_d[:, C_V:C_V + 128].rearrange("(k p) c -> p k c", p=128)),
      W=[WDV.r()], dma="wdv")
    A("dve", lambda e: e.tensor_copy(out=cbf.t[:], in_=cst.t[:]), R=[cst.r()], W=[cbf.r()])
    A("dve", lambda e: e.memset(ones_f.t[:], 1.0), W=[ones_f.r()])
    A("dve", lambda e: e.memset(ones_b.t[:], 1.0), W=[ones_b.r()])
    A("dve", lambda e: e.memset(sm.t[:], 0.0), W=[sm.r()])
    A("dve", lambda e: e.memset(sm.t[:, SM_E6:SM_E6 + 1], 1e-6), W=[sm.r()])
    A("dve", lambda e: e.memset(sm.t[:, SM_E5:SM_E5 + 1], 1e-5), W=[sm.r()])
    A("act", lambda e: e.activation(out=sm.t[:, SM_A:SM_A + 16], in_=pc("alog", 0, 16), func=AF.Exp), R=[prm.r(), sm.r()], W=[sm.r()])
    A("dve", lambda e: e.tensor_scalar(out=sm.t[:, SM_A:SM_A + 16], in0=sm.t[:, SM_A:SM_A + 16], scalar1=-1.0, scalar2=None, op0=ALU.mult),
      R=[sm.r()], W=[sm.r()])
    A("act", lambda e: e.activation(out=sm.t[:, SM_ES:SM_ES + 16], in_=pc("sink", 0, 16), func=AF.Exp), R=[prm.r(), sm.r()], W=[sm.r()])
    A("dve", lambda e: e.tensor_scalar(out=sm.t[:, SM_QW:SM_QW + 1], in0=pc("qw"), scalar1=0.125, scalar2=None, op0=ALU.mult),
      R=[prm.r(), sm.r()], W=[sm.r()])
    A("dve", lambda e: e.tensor_copy(out=sm.t[:, SM_KW:SM_KW + 1], in_=pc("kw")), R=[prm.r(), sm.r()], W=[sm.r()])
    for j in range(8):
        A("dve", lambda e, j=j: e.tensor_scalar(out=diagD.t[:, j, :], in0=ident_f, scalar1=pc("dch", j), scalar2=None, op0=ALU.mult),
          R=[cst.r(), prm.r()], W=[diagD.r()])
    for i in range(4):
        A("dve", lambda e, i=i: e.memset(a48s[i].t[:], 0.0), W=[a48s[i].r()], c=48)
        A("dve", lambda e, i=i: e.memset(q48s[i].t[:], 0.0), W=[q48s[i].r()], c=48)
    A("dve", lambda e: e.memset(st_f.t[:], 0.0), W=[st_f.r()])
    A("dve", lambda e: e.memset(st_b.t[:], 0.0), W=[st_b.r()])
    A("dve", lambda e: e.memset(TL.t[:], 0.0), W=[TL.r()])
    A("dve", lambda e: e.memset(Kp.t[:].rearrange("p a b -> p (a b)"), 0.0), W=[Kp.r()])
    A("dve", lambda e: e.memset(Vx.t[:].rearrange("p a b c -> p (a b c)"), 0.0), W=[Vx.r()])
    A("dve", lambda e: e.memset(Vx.t[:, :, :, 64:65], 1.0), W=[Vx.r()])

    for j in range(8):
        A("dve", lambda e, j=j: e.tensor_tensor(out=dgt.t[:].rearrange("p (k c) -> p k c", c=128),
                                                in0=ident_f.unsqueeze(1).to_broadcast([128, 31, 128]),
                                                in1=pc("dww", j * 31, 31).unsqueeze(2).to_broadcast([128, 31, 128]), op=ALU.mult),
          R=[cst.r(), prm.r()], W=[dgt.r()], c=3968)
        A("sp", lambda e, j=j: e.dma_start(out=diag_d[j], in_=dgt.t[:]), R=[dgt.r()], W=[["dram_diag"]], dma="dgt")

    def wsrc(w, k0, nk, c0, ncol):
        return w[k0 * 128:(k0 + nk) * 128, c0:c0 + ncol].rearrange("(k p) c -> p k c", p=128)

    def loads_mixer0(pre):
        L = []
        if not pre:
            L += [("z0", win_d, 0, 8, C_Z, 512), ("z1", win_d, 0, 8, C_Z + 512, 512)]
        L += [("x0", win_d, 0, 8, C_X, 512), ("x1", win_d, 0, 8, C_X + 512, 512)]
        if pre:
            L += [("bc", win_d, 0, 8, C_B, 256)]
        else:
            L += [("bc", win_d, 0, 8, C_B, 512), ("q0", win_d, 0, 8, C_Q, 512), ("q1", win_d, 0, 8, C_Q + 512, 512)]
            L += [(f"o{i}", wout_d, 0, 16, 256 * i, 256) for i in range(4)]
        return L

    def loads_ffn(l):
        L = []
        for g in range(6):
            nc_ = 512 if g < 5 else 256
            L += [(f"g{g}", wup_d[l], 0, 8, 512 * g, nc_), (f"u{g}", wup_d[l], 0, 8, DFF + 512 * g, nc_)]
        L += [(f"d{f}", wdn_d[l], 0, 22, 128 * f, 128) for f in range(8)]
        return L

    def loads_conf():
        L = []
        for h in range(2):
            L += [(f"a{h}", pw1_d, 0, 8, 512 * h, 512), (f"s{h}", pw1_d, 0, 8, 1024 + 512 * h, 512)]
            L += [(f"dg{j}", None, j, 31, 0, 128) for j in range(4 * h, 4 * h + 4)]
        L += [(f"p{h}", pw2_d, 0, 8, 512 * h, 512) for h in range(2)]
        return L

    stream = []
    for _ in range(NPRE + 1):
        stream += loads_mixer0(True)
    for _ in range(NMAIN):
        stream += loads_mixer0(False) + loads_ffn(0) + loads_conf() + loads_ffn(1)
    wstate = {"issued": 0, "next": 0}
    released = set()
    auto_pending = []

    def issue_loads(upto):
        while wstate["issued"] < min(upto, len(stream)):
            i = wstate["issued"]
            if i >= NSLOT and (i - NSLOT) not in released:
                break
            _, w, k0, nk, c0, ncol = stream[i]
            slot = WS[i % NSLOT]

            def f(e, slot=slot, w=w, k0=k0, nk=nk, c0=c0, ncol=ncol):
                if w is None:
                    return e.dma_start(out=slot.t[:, 0:nk * ncol], in_=diag_d[k0])
                return e.dma_start(out=slot.t[:, 0:nk * ncol].rearrange("p (k c) -> p k c", c=ncol), in_=wsrc(w, k0, nk, c0, ncol))
            A("pool", f, R=[["dram_diag"]] if w is None else [], W=[slot.r()], dma=f"ws{i % NSLOT}", nbytes=nk * ncol * 128 * (2 if w is None else 4))
            wstate["issued"] += 1

    def wrel(*idxs):
        released.update(idxs)
        issue_loads(wstate["next"] + NSLOT)

    def wnext(tag, hold=False):
        i = wstate["next"]
        assert stream[i][0] == tag, (stream[i][0], tag)
        released.update(auto_pending)
        del auto_pending[:]
        issue_loads(i + NSLOT)
        assert wstate["issued"] > i, ("weight ring deadlock", tag)
        wstate["next"] += 1
        if not hold:
            auto_pending.append(i)
        _, w, k0, nk, c0, ncol = stream[i]
        slot = WS[i % NSLOT]
        return slot.t[:, 0:nk * ncol].rearrange("p (k c) -> p k c", c=ncol), slot.r(), i

    def mm_group(specs):
        def f(e):
            ins = None
            for (o, l, r, s0, s1) in specs:
                ins = e.matmul(o, lhsT=l, rhs=r, start=s0, stop=s1)
            return ins
        cost = 0.0
        for (o, l, r, s0, s1) in specs:
            n = int(np.prod(o.shape[1:]))
            cost += (max(n, 96) / 1.9) * (4.0 if l.dtype == F32 else 1.0) + 12.0
        f.cost = cost
        return f

    cfg = {"nt": 512}

    def tr_group(specs):
        def f(e):
            ins = None
            for (o, i_) in specs:
                ins = e.transpose(o, i_, ident_f)
            return ins
        f.cost = len(specs) * 110.0
        return f

    def rmsnorm(wname, wl, hn_, sq_, rs_, rr_, hs=None, bank=None):
        hs = hT if hs is None else hs
        nt = cfg["nt"]
        if bank is None:
            A("act", lambda e: e.activation(out=sq_.t[:, :, 0:nt], in_=hs.t[:, :, 0:nt], func=AF.Square), R=[hs.r()], W=[sq_.r()], c=8 * nt)
            b = psa()
            A("pe", mm_group([(PS[:, b, 0:nt], ones_b.t[:], sq_.t[:, kt, 0:nt], kt == 0, kt == 7) for kt in range(8)]),
              R=[sq_.r(), ones_b.r()], W=[pr(b)])
        else:
            b = bank
        A("act", lambda e: e.activation(out=rs_.t[:, 0:nt], in_=PS[:, b, 0:nt], func=AF.Ln, bias=sm.t[:, SM_E6:SM_E6 + 1], scale=1.0 / 1024),
          R=[pr(b), sm.r()], W=[rs_.r()], c=nt)
        A("act", lambda e: e.activation(out=rr_.t[:, 0:nt], in_=rs_.t[:, 0:nt], func=AF.Exp, scale=-0.5), R=[rs_.r()], W=[rr_.r()], c=nt)
        for kt in range(8):
            A("dve", lambda e, kt=kt: e.scalar_tensor_tensor(out=hn_.t[:, kt, 0:nt], in0=hs.t[:, kt, 0:nt], scalar=pc(wname, wl * 8 + kt),
                                                            in1=rr_.t[:, 0:nt], op0=ALU.mult, op1=ALU.mult),
              R=[hs.r(kt * 512, kt * 512 + 512), rr_.r(), prm.r()], W=[hn_.r(kt * 512, kt * 512 + 512)], c=nt)

    nrm = {"bank": None}

    def ssq_hook(f, sq_next):
        nt = cfg["nt"]
        A("act", lambda e: e.activation(out=sq_next.t[:, f, 0:nt], in_=hT.t[:, f, 0:nt], func=AF.Square),
          R=[hT.r(f * 512, f * 512 + 512)], W=[sq_next.r(f * 512, f * 512 + 512)], c=nt)
        if f == 0:
            nrm["bank"] = psa()
        b = nrm["bank"]
        A("pe", mm_group([(PS[:, b, 0:nt], ones_b.t[:], sq_next.t[:, f, 0:nt], f == 0, f == 7)]),
          R=[sq_next.r(f * 512, f * 512 + 512), ones_b.r()], W=[pr(b)])

    def proj_fm(wv, wres, jloc, hn_):
        nt = cfg["nt"]
        b = psa()
        A("pe", mm_group([(PS[:, b, 0:nt], wv[:, kt, jloc * 128:(jloc + 1) * 128], hn_.t[:, kt, 0:nt], kt == 0, kt == 7) for kt in range(8)]),
          R=[wres, hn_.r()], W=[pr(b)])
        return b

    def conv_fm(b, xp_, acc_, K, wcol, bcol, tl_off):
        nt = cfg["nt"]
        A("act", lambda e: e.activation(out=xp_.t[:, 0:K - 1], in_=TL.t[:, tl_off:tl_off + K - 1], func=AF.Copy),
          R=[TL.r(tl_off, tl_off + K - 1)], W=[xp_.r(0, K - 1)], c=4)
        A("act", lambda e: e.activation(out=xp_.t[:, K - 1:K - 1 + nt], in_=PS[:, b, 0:nt], func=AF.Copy), R=[pr(b)], W=[xp_.r(K - 1, K + 511)], c=nt)
        A("act", lambda e: e.activation(out=acc_.t[:, 0:nt], in_=PS[:, b, 0:nt], func=AF.Identity, scale=wcol(K - 1), bias=bcol),
          R=[pr(b), prm.r()], W=[acc_.r()], c=nt)
        for k in range(K - 2, -1, -1):
            A("dve", lambda e, k=k: e.scalar_tensor_tensor(out=acc_.t[:, 0:nt], in0=xp_.t[:, k:k + nt], scalar=wcol(k), in1=acc_.t[:, 0:nt],
                                                          op0=ALU.mult, op1=ALU.add), R=[xp_.r(), acc_.r(), prm.r()], W=[acc_.r()], c=nt)
        A("act", lambda e: e.activation(out=TL.t[:, tl_off:tl_off + K - 1], in_=xp_.t[:, nt:nt + K - 1], func=AF.Copy),
          R=[xp_.r()], W=[TL.r(tl_off, tl_off + K - 1)], c=4)

    def dump(name, tt):
        if name in dbg_d:
            A("sp", lambda e: e.dma_start(out=dbg_d[name][:, 0:tt.n], in_=tt.t[:].rearrange("p a b -> p (a b)")),
              R=[tt.r()], dma="dbg_" + name)

    def hchunk(hd, c):
        return [x for j in range(8) for x in hd.r(j * 512 + c * 128, j * 512 + c * 128 + 128)]

    def stage_in(row0, hd=None):
        hd = hT if hd is None else hd
        for c in range(cfg["nt"] // 128):
            xi = xin[c % 2]
            r0 = row0 + c * 128
            A("sp", lambda e, xi=xi, r0=r0: e.dma_start(out=xi.t[:], in_=x_d[r0:r0 + 128, :]), W=[xi.r()], dma=xi.name)
            b = psa(2)
            A("pe", tr_group([(PS[:, b + j // 4, (j % 4) * 128:(j % 4) * 128 + 128], xi.t[:, j * 128:(j + 1) * 128]) for j in range(8)]),
              R=[xi.r(), cst.r()], W=[pr(b, 2)])
            A("act", lambda e, b=b, c=c: e.activation(out=hd.t[:, :, c * 128:(c + 1) * 128],
                                                      in_=PS[:, b:b + 2, :].rearrange("p a (j t) -> p (a j) t", t=128), func=AF.Copy),
              R=[pr(b, 2)], W=[hchunk(hd, c)], c=1024)

    def stage_out(orow0):
        for c in range(cfg["nt"] // 128):
            xo = xout[c % 2]
            r0 = orow0 + c * 128
            b = psa(2)
            A("pe", tr_group([(PS[:, b + j // 4, (j % 4) * 128:(j % 4) * 128 + 128], hT.t[:, j, c * 128:(c + 1) * 128]) for j in range(8)]),
              R=[hchunk(hT, c), cst.r()], W=[pr(b, 2)])
            A("act", lambda e, b=b, xo=xo: e.activation(out=xo.t[:], in_=PS[:, b:b + 2, :].rearrange("p a t -> p (a t)"), func=AF.Copy),
              R=[pr(b, 2)], W=[xo.r()], c=1024)
            A("sp", lambda e, xo=xo, r0=r0: e.dma_start(out=y_d[r0:r0 + 128, :], in_=xo.t[:]), R=[xo.r()], dma=xo.name)

    def ssd_chunk(c, pre, bs):
        hn, sq, xsT, BCT, dtr, dtv, lndt, _h = bs
        xtok_, btok_, xw_ = (xtok, btok, xw) if not pre else (PRE_A if bs is SET_A else PRE_B)
        s16, acsT, qT, a48, q48 = s16s[c], acsTs[c], qTs[c], a48s[c], q48s[c]
        S = lambda i: s16.t[:, i, :]
        sr = s16.r()
        cs = slice(c * 128, (c + 1) * 128)
        dup = lambda ap: ap.unsqueeze(1).to_broadcast([128, 2, 16])
        v48 = lambda t_: t_.t[:].rearrange("p (r c) -> p r c", c=16)[:, 0:3:2, :]
        A("dve", lambda e: e.tensor_tensor(out=v48(a48), in0=dup(dtv.t[:, c, :]), in1=dup(sm.t[:, SM_A:SM_A + 16]), op=ALU.mult),
          R=[dtv.r(), sm.r()], W=[a48.r()], c=32)
        a_ = a48.t[:, 0:16]
        b = psa()
        A("pe", mm_group([(PS[:, b, 0:16], U_f, a_, True, True),
                          (PS[0:48, b, 16:144], a48.t[:], U_f, True, True),
                          (PS[:, b, 144:160], ones_f.t[:], a_, True, True)]), R=[a48.r(), cst.r(), ones_f.r()], W=[pr(b)])
        A("dve", lambda e: e.tensor_tensor(out=v48(q48), in0=dup(lndt.t[:, c, :]), in1=dup(PS[:, b, 0:16]), op=ALU.subtract),
          R=[lndt.r(), pr(b)], W=[q48.r()], c=32)
        A("dve", lambda e: e.tensor_tensor(out=S(3), in0=q48.t[:, 0:16], in1=PS[:, b, 144:160], op=ALU.add), R=[q48.r(), pr(b)], W=[sr], c=16)
        A("act", lambda e: e.activation(out=S(4), in_=S(3), func=AF.Exp), R=[sr], W=[sr], c=16)
        A("act", lambda e: e.activation(out=S(6), in_=PS[:, b, 144:160], func=AF.Exp), R=[pr(b)], W=[sr], c=16)
        if not pre:
            A("act", lambda e: e.activation(out=S(5), in_=PS[:, b, 0:16], func=AF.Exp), R=[pr(b)], W=[sr], c=16)
            A("act", lambda e: e.activation(out=acsT.t[:], in_=PS[0:48, b, 16:144], func=AF.Copy), R=[pr(b)], W=[acsT.r()], c=128)
            A("dve", lambda e: e.tensor_tensor(out=acsT.t[32:48, :], in0=PS[32:48, b, 16:144], in1=acsT.t[32:48, :], op=ALU.subtract),
              R=[pr(b), acsT.r()], W=[acsT.r()], c=128)
            b2 = psa()
            A("pe", mm_group([(PS[0:48, b2, 0:128], q48.t[:], ident_f, True, True)]), R=[q48.r(), cst.r()], W=[pr(b2)])
            A("act", lambda e: e.activation(out=qT.t[:], in_=PS[0:48, b2, 0:128], func=AF.Copy), R=[pr(b2)], W=[qT.r()], c=128)
            A("dve", lambda e: e.tensor_tensor(out=qT.t[32:48, :], in0=PS[32:48, b2, 0:128], in1=qT.t[32:48, :], op=ALU.subtract),
              R=[pr(b2), qT.r()], W=[qT.r()], c=128)
            bcb = psa()
            A("pe", mm_group([(PS[:, bcb, g * 128:(g + 1) * 128], BCT.t[:, g, cs], BCT.t[:, 2 + g, cs], True, True) for g in range(2)]),
              R=[BCT.r()], W=[pr(bcb)])
            for half in range(2):
                bb = psa(2)
                specs = []
                for hh in range(8):
                    h = half * 8 + hh
                    o = PS[:, bb + hh // 4, (hh % 4) * 128:(hh % 4) * 128 + 128]
                    specs += [(o, Eall.t[:, h, :], acsT.t[:], True, False), (o, qT.t[:], Eall.t[:, h, :], False, False),
                              (o, ident_b, negm_b, False, True)]
                A("pe", mm_group(specs), R=[Eall.r(), acsT.r(), qT.r(), cbf.r()], W=[pr(bb, 2)])
                dc = dec[half]
                A("act", lambda e, bb=bb, dc=dc: e.activation(out=dc.t[:], in_=PS[:, bb:bb + 2, :].rearrange("p a t -> p (a t)"), func=AF.Exp),
                  R=[pr(bb, 2)], W=[dc.r()], c=1024)
                A("dve", lambda e, half=half, dc=dc: e.tensor_tensor(
                    out=MT.t[:, half * 8:half * 8 + 8, :], in0=dc.t[:].rearrange("p (h l) -> p h l", l=128),
                    in1=PS[:, bcb, half * 128:half * 128 + 128].unsqueeze(1).to_broadcast([128, 8, 128]), op=ALU.mult),
                  R=[dc.r(), pr(bcb)], W=[MT.r(half * 1024, half * 1024 + 1024)], c=1024)
        bx = psa(2)
        A("pe", mm_group([(PS[:, bx + j // 4, (j % 4) * 128:(j % 4) * 128 + 128], xsT.t[:, j, cs], ident_b, True, True) for j in range(8)]),
          R=[xsT.r(), cbf.r()], W=[pr(bx, 2)])
        A("act", lambda e: e.activation(out=xtok_.t[:], in_=PS[:, bx:bx + 2, :].rearrange("p a t -> p (a t)"), func=AF.Copy),
          R=[pr(bx, 2)], W=[xtok_.r()], c=1024)
        bB = psa()
        A("pe", mm_group([(PS[:, bB, g * 128:(g + 1) * 128], BCT.t[:, g, cs], ident_b, True, True) for g in range(2)]),
          R=[BCT.r(), cbf.r()], W=[pr(bB)])
        A("act", lambda e: e.activation(out=btok_.t[:], in_=PS[:, bB, 0:256], func=AF.Copy), R=[pr(bB)], W=[btok_.r()], c=256)
        A("dve", lambda e: e.tensor_tensor(out=xw_.t[:].rearrange("p (h d) -> p h d", d=64), in0=xtok_.t[:].rearrange("p (h d) -> p h d", d=64),
                                           in1=S(4).unsqueeze(2).to_broadcast([128, 16, 64]), op=ALU.mult), R=[xtok_.r(), sr], W=[xw_.r()], c=1024)
        if not pre:
            by = psa(2)
            specs = []
            for h in range(16):
                o = PS[:, by + h // 8, (h % 8) * 64:(h % 8) * 64 + 64]
                specs += [(o, MT.t[:, h, :], xtok_.t[:, h * 64:(h + 1) * 64], True, False),
                          (o, xsT.t[:, h // 2, cs], diagD.t[:, h // 2, (h % 2) * 64:(h % 2) * 64 + 64], False, True)]
            A("pe", mm_group(specs), R=[MT.r(), xtok_.r(), xsT.r(), diagD.r()], W=[pr(by, 2)])
            bo = psa(2)
            A("pe", mm_group([(PS[:, bo + g, :], BCT.t[:, 2 + g, cs], st_b.t[:, g * 512:(g + 1) * 512], True, True) for g in range(2)]),
              R=[BCT.r(), st_b.r()], W=[pr(bo, 2)])
            A("dve", lambda e: e.tensor_tensor(out=tmpf.t[:].rearrange("p (h d) -> p h d", d=64),
                                               in0=PS[:, bo:bo + 2, :].rearrange("p a (h d) -> p (a h) d", d=64),
                                               in1=S(5).unsqueeze(2).to_broadcast([128, 16, 64]), op=ALU.mult), R=[pr(bo, 2), sr], W=[tmpf.r()], c=1024)
            A("dve", lambda e: e.tensor_tensor(out=yv.t[:], in0=PS[:, by:by + 2, :].rearrange("p a t -> p (a t)"), in1=tmpf.t[:], op=ALU.add),
              R=[pr(by, 2), tmpf.r()], W=[yv.r()], c=1024)
        bs = psa(2)
        A("pe", mm_group([(PS[:, bs + g, :], btok_.t[:, g * 128:(g + 1) * 128], xw_.t[:, g * 512:(g + 1) * 512], True, True) for g in range(2)]),
          R=[btok_.r(), xw_.r()], W=[pr(bs, 2)])
        A("dve", lambda e: e.tensor_tensor(out=st_f.t[:].rearrange("p (h d) -> p h d", d=64), in0=st_f.t[:].rearrange("p (h d) -> p h d", d=64),
                                           in1=S(6).unsqueeze(2).to_broadcast([128, 16, 64]), op=ALU.mult), R=[st_f.r(), sr], W=[st_f.r()], c=1024)
        A("dve", lambda e: e.tensor_tensor(out=st_f.t[:], in0=st_f.t[:], in1=PS[:, bs:bs + 2, :].rearrange("p a t -> p (a t)"), op=ALU.add),
          R=[st_f.r(), pr(bs, 2)], W=[st_f.r()], c=1024)
        A("act", lambda e: e.activation(out=st_b.t[:], in_=st_f.t[:], func=AF.Copy), R=[st_f.r()], W=[st_b.r()], c=1024)
        if pre:
            return
        A("dve", lambda e: e.tensor_tensor(out=yv.t[:], in0=yv.t[:], in1=sz.t[:, c, :], op=ALU.mult), R=[yv.r(), sz.r(c * 1024, c * 1024 + 1024)], W=[yv.r()], c=1024)
        A("dve", lambda e: e.memset(S(7)[:, 0:2], 0.0), W=[sr], c=2)
        for g in range(2):
            A("act", lambda e, g=g: e.activation(out=tmpf.t[:, g * 512:(g + 1) * 512], in_=yv.t[:, g * 512:(g + 1) * 512], func=AF.Square,
                                                 accum_out=S(7)[:, g:g + 1]), R=[yv.r(), sr], W=[tmpf.r(), sr])
        A("act", lambda e: e.activation(out=S(8)[:, 0:2], in_=S(7)[:, 0:2], func=AF.Ln, bias=sm.t[:, SM_E6:SM_E6 + 1], scale=1.0 / 512),
          R=[sr, sm.r()], W=[sr], c=2)
        A("act", lambda e: e.activation(out=S(9)[:, 0:2], in_=S(8)[:, 0:2], func=AF.Exp, scale=-0.5), R=[sr], W=[sr], c=2)
        for g in range(2):
            A("act", lambda e, g=g: e.activation(out=yn.t[:, g * 512:(g + 1) * 512], in_=yv.t[:, g * 512:(g + 1) * 512], func=AF.Copy,
                                                 scale=S(9)[:, g:g + 1]), R=[yv.r(), sr], W=[yn.r(g * 512, g * 512 + 512)])
        bt = psa(2)
        A("pe", mm_group([(PS[:, bt + j // 4, (j % 4) * 128:(j % 4) * 128 + 128], yn.t[:, j * 128:(j + 1) * 128], ident_b, True, True)
                          for j in range(8)]), R=[yn.r(), cbf.r()], W=[pr(bt, 2)])
        for j in range(8):
            A("act", lambda e, j=j: e.activation(out=mixT.t[:, j, cs], in_=PS[:, bt + j // 4, (j % 4) * 128:(j % 4) * 128 + 128], func=AF.Copy,
                                                 scale=pc("snw", j)), R=[pr(bt + j // 4), prm.r()], W=[mixT.r(j * 512 + c * 128, j * 512 + c * 128 + 128)], c=128)

    def attn_chunk(c):
        cs = slice(c * 128, (c + 1) * 128)
        S = lambda i: a16s[c].t[:, i - 10, :]
        sr = a16s[c].r()
        gi = 0
        for kv in range(2):
            for pad in range(2):
                h0 = 8 * kv + pad
                pts = []
                for kb in range(2):
                    b = psa()
                    ov_ = PS[:, b, :].rearrange("p (j q) -> p j q", q=128)
                    A("pe", mm_group([(ov_, Kp.t[:, 2 * kv + pad, (c + kb) * 128:(c + kb + 1) * 128], qnT.t[:, 4 * kv:4 * kv + 4, cs], True, False),
                                      (ov_, ident_b, biasT.t[:, h0:h0 + 7:2, kb, :], False, True)]),
                      R=[Kp.r(), qnT.r(), biasT.r(), cbf.r()], W=[pr(b)])
                    p_ = pT[(gi % 2) * 2 + kb]
                    A("act", lambda e, b=b, p_=p_: e.activation(out=p_.t[:], in_=PS[:, b, :], func=AF.Exp), R=[pr(b)], W=[p_.r()])
                    pts.append(p_)
                o = psa()
                specs = []
                for j in range(4):
                    for kb in range(2):
                        specs.append((PS[:, o, j * 65:(j + 1) * 65], pts[kb].t[:, j * 128:(j + 1) * 128], Vx.t[:, c + kb, kv, :], kb == 0, kb == 1))
                A("pe", mm_group(specs), R=[pts[0].r(), pts[1].r(), Vx.r()], W=[pr(o)])
                ov = PS[:, o, 0:260].rearrange("p (j d) -> p j d", d=65)
                A("dve", lambda e, ov=ov, h0=h0: e.tensor_tensor(out=S(10)[:, 0:4], in0=ov[:, :, 64], in1=sm.t[:, SM_ES + h0:SM_ES + h0 + 7:2], op=ALU.add),
                  R=[pr(o), sm.r()], W=[sr])
                A("dve", lambda e: e.reciprocal(out=S(11)[:, 0:4], in_=S(10)[:, 0:4]), R=[sr], W=[sr])
                A("dve", lambda e, ov=ov, h0=h0: e.tensor_tensor(out=attn.t[:].rearrange("p (h d) -> p h d", d=64)[:, h0:h0 + 7:2, :], in0=ov[:, :, 0:64],
                                                                in1=S(11)[:, 0:4].unsqueeze(2).to_broadcast([128, 4, 64]), op=ALU.mult),
                  R=[pr(o), sr], W=[attn.r()])
                gi += 1
        bt = psa(2)
        A("pe", mm_group([(PS[:, bt + j // 4, (j % 4) * 128:(j % 4) * 128 + 128], attn.t[:, j * 128:(j + 1) * 128], ident_b, True, True)
                          for j in range(8)]), R=[attn.r(), cbf.r()], W=[pr(bt, 2)])
        A("act", lambda e: e.activation(out=mixT.t[:, 8:16, cs], in_=PS[:, bt:bt + 2, :].rearrange("p a (j t) -> p (a j) t", t=128), func=AF.Copy),
          R=[pr(bt, 2)], W=[mixT.r(8 * 512, 16 * 512)], c=1024)

    def mixer0(pre, bs=None):
        bs = SET_A if bs is None else bs
        hn, sq, xsT, BCT, dtr, dtv, lndt, hsrc = bs
        nt = cfg["nt"]
        nch = nt // 128
        rmsnorm("mixw", 0, hn, sq, rs_s, rs_r, hsrc)
        if not pre:
            wz0, rz0, iz0 = wnext("z0", hold=True)
            wz1, rz1, iz1 = wnext("z1", hold=True)
            for c in range(nch):
                b = psa(2)
                specs = []
                for hf, wv in ((0, wz0), (1, wz1)):
                    specs += [(PS[:, b + hf, :], hn.t[:, kt, c * 128:(c + 1) * 128], wv[:, kt, :], kt == 0, kt == 7) for kt in range(8)]
                A("pe", mm_group(specs), R=[hn.r(), rz0, rz1], W=[pr(b, 2)])
                A("act", lambda e, b=b, c=c: e.activation(out=sz.t[:, c, :], in_=PS[:, b:b + 2, :].rearrange("p a t -> p (a t)"), func=AF.Silu),
                  R=[pr(b, 2)], W=[sz.r(c * 1024, c * 1024 + 1024)], c=1024)
            wrel(iz0, iz1)
        for c in range(nch):
            b = psa()
            A("pe", mm_group([(PS[:, b, 0:144], hn.t[:, kt, c * 128:(c + 1) * 128], WDV.t[:, kt, :], kt == 0, kt == 7) for kt in range(8)]),
              R=[hn.r(), WDV.r()], W=[pr(b)])
            A("dve", lambda e, b=b, c=c: e.tensor_tensor(out=dtr.t[:, c, :], in0=PS[:, b, 0:16], in1=pc("dtb", 0, 16), op=ALU.add),
              R=[pr(b), prm.r()], W=[dtr.r()])
            if not pre:
                A("act", lambda e, b=b, c=c: e.activation(out=Vx.t[:, 1 + c, :, 0:64], in_=PS[:, b, 16:144].rearrange("p (k d) -> p k d", d=64),
                                                          func=AF.Copy), R=[pr(b)], W=[Vx.r()])
        fl = lambda t_: t_.t[:, 0:nch, :].rearrange("p a b -> p (a b)")
        A("act", lambda e: e.activation(out=fl(dtr), in_=fl(dtr), func=AF.Exp), R=[dtr.r()], W=[dtr.r()])
        A("act", lambda e: e.activation(out=fl(dtv), in_=fl(dtr), func=AF.Ln, bias=1.0), R=[dtr.r()], W=[dtv.r()])
        A("act", lambda e: e.activation(out=fl(lndt), in_=fl(dtv), func=AF.Ln), R=[dtv.r()], W=[lndt.r()])
        tiles = list(range(10)) if pre else list(range(12))
        wv = wr = None
        for j in tiles:
            if j == 0:
                wv, wr, _ = wnext("x0")
            elif j == 4:
                wv, wr, _ = wnext("x1")
            elif j == 8:
                wv, wr, _ = wnext("bc")
            b = proj_fm(wv, wr, j % 4, hn)
            xp_, acc_ = xp[j % 2], acc[j % 2]
            conv_fm(b, xp_, acc_, 4, lambda k, j=j: pc("cw", j * 4 + k), pc("cb", j), TL_A + 3 * j)
            dst = xsT.t[:, j, 0:nt] if j < 8 else BCT.t[:, j - 8, 0:nt]
            dres = xsT.r(j * 512, j * 512 + 512) if j < 8 else BCT.r((j - 8) * 512, (j - 8) * 512 + 512)
            A("act", lambda e, acc_=acc_, dst=dst: e.activation(out=dst, in_=acc_.t[:, 0:nt], func=AF.Silu), R=[acc_.r()], W=[dres], c=nt)
        if not pre:
            for j in range(12):
                if j == 0:
                    wv, wr, _ = wnext("q0")
                elif j == 4:
                    wv, wr, _ = wnext("q1")
                if j < 8:
                    b = proj_fm(wv, wr, j % 4, hn)
                else:
                    b = psa()
                    t = j - 8
                    A("pe", mm_group([(PS[:, b, 0:nt], WK.t[:, kt, t, :], hn.t[:, kt, 0:nt], kt == 0, kt == 7) for kt in range(8)]),
                      R=[WK.r(), hn.r()], W=[pr(b)])
                sq_ = sqq[j % 2]
                A("act", lambda e, b=b, sq_=sq_: e.activation(out=sq_.t[:, 0:nt], in_=PS[:, b, 0:nt], func=AF.Square), R=[pr(b)], W=[sq_.r()], c=nt)
                b2 = psa()
                A("pe", mm_group([(PS[:, b2, 0:nt], bd_b, sq_.t[:, 0:nt], True, True)]), R=[sq_.r(), cbf.r()], W=[pr(b2)])
                rs1, rs2 = (rs_s, rs_r) if j % 2 == 0 else (rs_s2, rs_r2)
                A("act", lambda e, b2=b2, rs1=rs1: e.activation(out=rs1.t[:, 0:nt], in_=PS[:, b2, 0:nt], func=AF.Ln, bias=sm.t[:, SM_E6:SM_E6 + 1], scale=1.0 / 64),
                  R=[pr(b2), sm.r()], W=[rs1.r()], c=nt)
                A("act", lambda e, rs1=rs1, rs2=rs2: e.activation(out=rs2.t[:, 0:nt], in_=rs1.t[:, 0:nt], func=AF.Exp, scale=-0.5), R=[rs1.r()], W=[rs2.r()], c=nt)
                if j < 8:
                    dst, dres, wc = qnT.t[:, j, 0:nt], qnT.r(j * 512, j * 512 + 512), sm.t[:, SM_QW:SM_QW + 1]
                else:
                    dst, dres, wc = Kp.t[:, j - 8, 128:128 + nt], Kp.r(), sm.t[:, SM_KW:SM_KW + 1]
                A("dve", lambda e, b=b, dst=dst, wc=wc, rs2=rs2: e.scalar_tensor_tensor(out=dst, in0=PS[:, b, 0:nt], scalar=wc, in1=rs2.t[:, 0:nt], op0=ALU.mult, op1=ALU.mult),
                  R=[pr(b), rs2.r(), sm.r()], W=[dres], c=nt)
        lab0 = P.label
        pool0 = dict(pspool)
        for c in range(nch):
            P.label = lab0 + f".ssd{c}"
            if not pre:
                pspool.update(lo=SSD_POOL[0], n=SSD_POOL[1])
            ssd_chunk(c, pre, bs)
            if not pre:
                P.label = lab0 + f".att{c}"
                pspool.update(lo=ATT_POOL[0], n=ATT_POOL[1])
                attn_chunk(c)
        pspool.update(pool0)
        P.label = lab0 + ".oproj"
        if pre:
            return
        A("act", lambda e: e.activation(out=Kp.t[:, :, 0:128], in_=Kp.t[:, :, nt:nt + 128], func=AF.Copy), R=[Kp.r()], W=[Kp.r()])
        A("act", lambda e: e.activation(out=Vx.t[:, 0, :, :], in_=Vx.t[:, nch, :, :], func=AF.Copy), R=[Vx.r()], W=[Vx.r()])
        for i in range(4):
            wv, wr, _ = wnext(f"o{i}")
            for f2 in range(2):
                f = 2 * i + f2
                b = psa()
                A("pe", mm_group([(PS[:, b, 0:nt], wv[:, kt, f2 * 128:(f2 + 1) * 128], mixT.t[:, kt, 0:nt], kt == 0, kt == 15) for kt in range(16)]),
                  R=[wr, mixT.r()], W=[pr(b)])
                A("dve", lambda e, b=b, f=f: e.tensor_tensor(out=hT.t[:, f, 0:nt], in0=hT.t[:, f, 0:nt], in1=PS[:, b, 0:nt], op=ALU.add),
                  R=[pr(b), hT.r(f * 512, f * 512 + 512)], W=[hT.r(f * 512, f * 512 + 512)], c=nt)
                ssq_hook(f, f_sq)

    def ffn(l, last, skip_down=False):
        nt = cfg["nt"]
        rmsnorm("ffnw", l, f_hn, f_sq, f_rs, f_rr, bank=nrm["bank"])
        for g in range(6):
            ntl = 4 if g < 5 else 2
            wg, rg, ig = wnext(f"g{g}", hold=True)
            wu, ru, iu = wnext(f"u{g}", hold=True)
            for i in range(ntl):
                t = 4 * g + i
                bg = proj_fm(wg, rg, i, f_hn)
                bu = proj_fm(wu, ru, i, f_hn)
                k2 = (t % 4) * 2
                for (b, tt, xi) in ((bg, t, k2), (bu, 22 + t, k2 + 1)):
                    conv_fm(b, f_xp[xi], f_acc[xi], 3, lambda k, tt=tt: pc("fcw", (l * 44 + tt) * 3 + k), pc("fcb", l * 44 + tt),
                            TL_F + l * 88 + 2 * tt)
                sg = f_sg[t % 4]
                ag, au = f_acc[k2], f_acc[k2 + 1]
                A("act", lambda e, sg=sg, ag=ag: e.activation(out=sg.t[:, 0:nt], in_=ag.t[:, 0:nt], func=AF.Silu), R=[ag.r()], W=[sg.r()])
                A("dve", lambda e, sg=sg, au=au, t=t: e.tensor_tensor(out=actT.t[:, t, 0:nt], in0=sg.t[:, 0:nt], in1=au.t[:, 0:nt], op=ALU.mult),
                  R=[sg.r(), au.r()], W=[actT.r(t * 512, t * 512 + 512)])
            wrel(ig, iu)
        for f in range(8):
            wv, wr, _ = wnext(f"d{f}")
            if skip_down:
                continue
            b = psa()
            A("pe", mm_group([(PS[:, b, 0:nt], wv[:, kt, :], actT.t[:, kt, 0:nt], kt == 0, kt == 21) for kt in range(22)]),
              R=[wr, actT.r()], W=[pr(b)])
            A("dve", lambda e, b=b, f=f: e.tensor_tensor(out=hT.t[:, f, 0:nt], in0=hT.t[:, f, 0:nt], in1=PS[:, b, 0:nt], op=ALU.add),
              R=[pr(b), hT.r(f * 512, f * 512 + 512)], W=[hT.r(f * 512, f * 512 + 512)])
            if not last:
                ssq_hook(f, c_sq)

    def conformer():
        nt = cfg["nt"]
        rmsnorm("mixw", 1, c_hn, c_sq, c_rs, c_rr, bank=nrm["bank"])
        for half in range(2):
            wa, ra, ia = wnext(f"a{half}", hold=True)
            wg, rg, ig = wnext(f"s{half}", hold=True)
            for i in range(4):
                j = half * 4 + i
                ba = proj_fm(wa, ra, i, c_hn)
                bg = proj_fm(wg, rg, i, c_hn)
                sg = c_sig[j % 2]
                A("act", lambda e, bg=bg, sg=sg, j=j: e.activation(out=sg.t[:, 0:nt], in_=PS[:, bg, 0:nt], func=AF.Sigmoid, bias=pc("pw1b", 8 + j)),
                  R=[pr(bg), prm.r()], W=[sg.r()])
                ur = upad.r(j * 544, j * 544 + 544)
                A("dve", lambda e, ba=ba, sg=sg, j=j: e.scalar_tensor_tensor(out=upad.t[:, j, 30:30 + nt], in0=PS[:, ba, 0:nt], scalar=pc("pw1b", j),
                                                                            in1=sg.t[:, 0:nt], op0=ALU.add, op1=ALU.mult),
                  R=[pr(ba), sg.r(), prm.r()], W=[ur])
                A("act", lambda e, j=j: e.activation(out=upad.t[:, j, 0:30], in_=TL.t[:, TL_C + 30 * j:TL_C + 30 * j + 30], func=AF.Copy),
                  R=[TL.r(TL_C + 30 * j, TL_C + 30 * j + 30)], W=[ur], c=30)
                A("act", lambda e, j=j: e.activation(out=TL.t[:, TL_C + 30 * j:TL_C + 30 * j + 30], in_=upad.t[:, j, nt:nt + 30], func=AF.Copy),
                  R=[ur], W=[TL.r(TL_C + 30 * j, TL_C + 30 * j + 30)], c=30)
            wrel(ia, ig)
            for i in range(4):
                j = half * 4 + i
                ur = upad.r(j * 544, j * 544 + 544)
                yr = yc.r(j * 512, j * 512 + 512)
                wd, rd, _ = wnext(f"dg{j}")
                bc_ = psa()
                A("pe", mm_group([(PS[:, bc_, 0:nt], wd[:, k, :], upad.t[:, j, k:k + nt], k == NDV, k == 30) for k in range(NDV, 31)]),
                  R=[rd, ur], W=[pr(bc_)])
                A("dve", lambda e, j=j: e.tensor_scalar(out=yc.t[:, j, 0:nt], in0=upad.t[:, j, 0:nt], scalar1=pc("dww", j * 31), scalar2=None, op0=ALU.mult),
                  R=[ur, prm.r()], W=[yr])
                for k in range(1, NDV):
                    A("dve", lambda e, j=j, k=k: e.scalar_tensor_tensor(out=yc.t[:, j, 0:nt], in0=upad.t[:, j, k:k + nt], scalar=pc("dww", j * 31 + k),
                                                                       in1=yc.t[:, j, 0:nt], op0=ALU.mult, op1=ALU.add), R=[ur, yr, prm.r()], W=[yr])
                A("dve", lambda e, j=j, bc_=bc_: e.scalar_tensor_tensor(out=yc.t[:, j, 0:nt], in0=PS[:, bc_, 0:nt], scalar=pc("dwb", j), in1=yc.t[:, j, 0:nt],
                                                                       op0=ALU.add, op1=ALU.add), R=[pr(bc_), yr, prm.r()], W=[yr])
                A("act", lambda e, j=j: e.activation(out=ybf.t[:, j, 0:nt], in_=yc.t[:, j, 0:nt], func=AF.Copy), R=[yr], W=[ybf.r(j * 512, j * 512 + 512)])
                A("act", lambda e, j=j: e.activation(out=c_sq.t[:, j, 0:nt], in_=yc.t[:, j, 0:nt], func=AF.Square), R=[yr], W=[c_sq.r(j * 512, j * 512 + 512)])
        b1 = psa()
        A("pe", mm_group([(PS[:, b1, 0:nt], ones_b.t[:], ybf.t[:, j, 0:nt], j == 0, j == 7) for j in range(8)]), R=[ybf.r(), ones_b.r()], W=[pr(b1)])
        b2 = psa()
        A("pe", mm_group([(PS[:, b2, 0:nt], ones_b.t[:], c_sq.t[:, j, 0:nt], j == 0, j == 7) for j in range(8)]), R=[c_sq.r(), ones_b.r()], W=[pr(b2)])
        A("dve", lambda e: e.tensor_scalar(out=c_mean.t[:, 0:nt], in0=PS[:, b1, 0:nt], scalar1=1.0 / 1024, scalar2=None, op0=ALU.mult), R=[pr(b1)], W=[c_mean.r()])
        A("dve", lambda e: e.tensor_tensor(out=c_msq.t[:, 0:nt], in0=c_mean.t[:, 0:nt], in1=c_mean.t[:, 0:nt], op=ALU.mult), R=[c_mean.r()], W=[c_msq.r()])
        A("dve", lambda e: e.scalar_tensor_tensor(out=c_var.t[:, 0:nt], in0=PS[:, b2, 0:nt], scalar=1.0 / 1024, in1=c_msq.t[:, 0:nt], op0=ALU.mult, op1=ALU.subtract),
          R=[pr(b2), c_msq.r()], W=[c_var.r()])
        A("act", lambda e: e.activation(out=c_rs.t[:, 0:nt], in_=c_var.t[:, 0:nt], func=AF.Ln, bias=sm.t[:, SM_E5:SM_E5 + 1], scale=1.0), R=[c_var.r(), sm.r()], W=[c_rs.r()])
        A("act", lambda e: e.activation(out=c_rr.t[:, 0:nt], in_=c_rs.t[:, 0:nt], func=AF.Exp, scale=-0.5), R=[c_rs.r()], W=[c_rr.r()])
        for j in range(8):
            yr = yc.r(j * 512, j * 512 + 512)
            A("dve", lambda e, j=j: e.tensor_tensor(out=yc.t[:, j, 0:nt], in0=yc.t[:, j, 0:nt], in1=c_mean.t[:, 0:nt], op=ALU.subtract), R=[yr, c_mean.r()], W=[yr])
            A("dve", lambda e, j=j: e.scalar_tensor_tensor(out=yc.t[:, j, 0:nt], in0=yc.t[:, j, 0:nt], scalar=pc("lnw", j), in1=c_rr.t[:, 0:nt], op0=ALU.mult, op1=ALU.mult),
              R=[yr, c_rr.r(), prm.r()], W=[yr])
            A("act", lambda e, j=j: e.activation(out=ybf.t[:, j, 0:nt], in_=yc.t[:, j, 0:nt], func=AF.Silu, bias=pc("lnb", j)), R=[yr, prm.r()],
              W=[ybf.r(j * 512, j * 512 + 512)])
        for half in range(2):
            wv, wr, _ = wnext(f"p{half}")
            for i in range(4):
                f = half * 4 + i
                b = proj_fm(wv, wr, i, ybf)
                A("dve", lambda e, b=b, f=f: e.scalar_tensor_tensor(out=hT.t[:, f, 0:nt], in0=PS[:, b, 0:nt], scalar=pc("pw2b", f), in1=hT.t[:, f, 0:nt],
                                                                   op0=ALU.add, op1=ALU.add),
                  R=[pr(b), hT.r(f * 512, f * 512 + 512), prm.r()], W=[hT.r(f * 512, f * 512 + 512)])
                ssq_hook(f, f_sq)

    blocks = [("pre", 512 * i, 512) for i in range(NPRE)] + [("pre", 512 * NPRE, 256), ("warm", 512 * NPRE + 256, 256)]
    blocks += [("main", 512 * (NPRE + k), 512) for k in range(1, NMAIN)]
    npre_seen = 0
    nout = 0
    for bi, (kind, row0, ntok) in enumerate(blocks):
        cfg["nt"] = ntok
        P.label = f"b{bi}.in"
        if kind == "pre":
            par = npre_seen % 2
            npre_seen += 1
            stage_in(row0, hTB if par == 1 else hT)
            P.label = f"b{bi}.pre"
            mixer0(True, SET_A if par == 0 else SET_B)
            continue
        stage_in(row0, hT)
        P.label = f"b{bi}.m0"
        mixer0(False)
        if kind == "warm" and "h0" in dbg_d:
            dump("h0", hT)
        P.label = f"b{bi}.f0"
        ffn(0, False)
        P.label = f"b{bi}.cf"
        conformer()
        P.label = f"b{bi}.f1"
        ffn(1, True, skip_down=(kind == "warm"))
        P.label = f"b{bi}.out"
        if kind == "warm":
            f1 = flg.t[:, 0:1]
            A("dve", lambda e: e.tensor_scalar(out=st_f.t[:], in0=st_f.t[:], scalar1=f1, scalar2=None, op0=ALU.mult), R=[st_f.r(), flg.r()], W=[st_f.r()])
            A("dve", lambda e: e.tensor_scalar(out=st_b.t[:], in0=st_b.t[:], scalar1=f1, scalar2=None, op0=ALU.mult), R=[st_b.r(), flg.r()], W=[st_b.r()])
            A("dve", lambda e: e.tensor_scalar(out=TL.t[:], in0=TL.t[:], scalar1=f1, scalar2=None, op0=ALU.mult), R=[TL.r(), flg.r()], W=[TL.r()])
            A("dve", lambda e: e.tensor_scalar(out=Vx.t[:, 0, :, :].rearrange("p a b -> p (a b)"), in0=Vx.t[:, 0, :, :].rearrange("p a b -> p (a b)"),
                                               scalar1=f1, scalar2=None, op0=ALU.mult), R=[Vx.r(), flg.r()], W=[Vx.r()])
        else:
            stage_out(nout)
            nout += 512
    fin = [f"dmachain_{xo.name}" for xo in xout] + [f"dmachain_dbg_{n}" for n in dbg_d]
    P.add("sp", None, fin, ())
    if not do_emit:
        P.sim_ns = P.schedule()
        return nc, P, allocs
    nsem, nops = P.emit()
    print(f"[build] sbuf_peak={sbuf_peak} sems={nsem} ops={nops} loads={len(stream)} sim_us={P.sim_ns / 1e3:.0f}", flush=True)
    return nc, P, allocs


def plan_psum(allocs):
    last_r = [-1] * 8
    last_t = [0.0] * 8
    plan = []
    for n, _b, ops in allocs:
        if not ops:
            plan.append(0)
            continue
        r0 = min(o.ridx for o in ops)
        r1 = max(o.ridx for o in ops)
        t0 = min(o.t0 for o in ops)
        t1 = max(o.t1 for o in ops)
        best = None
        for b in (range(8) if n == 1 else range(0, 8, 2)):
            bs = range(b, b + n)
            if any(last_r[i] >= r0 for i in bs):
                continue
            te = max(last_t[i] for i in bs)
            key = (max(te - t0, 0.0), -te)
            if best is None or key < best[0]:
                best = (key, b)
        assert best is not None, "PSUM over-subscribed in program order"
        b = best[1]
        for i in range(b, b + n):
            last_r[i] = r1
            last_t[i] = t1
        plan.append(b)
    return plan


def build_program(NPRE, NMAIN, dbg=(), iters=PLAN_ITERS):
    plan = None
    best = None
    for it in range(iters):
        _nc, P1, allocs = record_program(NPRE, NMAIN, dbg, plan=plan, do_emit=False)
        if best is None or P1.sim_ns < best[0]:
            best = (P1.sim_ns, plan)
        print(f"[build] plan iter {it}: sim_us={P1.sim_ns / 1e3:.0f}", flush=True)
        plan = plan_psum(allocs)
    nc, _P, _a = record_program(NPRE, NMAIN, dbg, plan=best[1], do_emit=True)
    return nc


def _fm(v, nt):
    return np.ascontiguousarray(np.asarray(v, np.float32).reshape(nt, 128).T)


def _t5_bucket(dist):
    max_exact = 16
    d_f = np.maximum(dist, 1).astype(np.float32)
    large = max_exact + (np.log(d_f / max_exact) / math.log(128 / max_exact) * (32 - max_exact)).astype(np.int32)
    large = np.minimum(large, 31)
    return np.where(dist < max_exact, dist, large)


def host_pack(inp):
    f32 = np.float32
    prm = np.zeros((128, NPRM), f32)

    def put(name, arr):
        arr = np.asarray(arr, f32)
        prm[:, _p[name]:_p[name] + arr.shape[1]] = arr
    put("mixw", np.concatenate([_fm(inp["mix_norm_w"][l], 8) for l in range(2)], 1))
    put("ffnw", np.concatenate([_fm(inp["ffn_norm_w"][l], 8) for l in range(2)], 1))
    cw = np.asarray(inp["ssm_conv_w"][0], f32)
    put("cw", cw.T.reshape(12, 128, 4).transpose(1, 0, 2).reshape(128, 48))
    put("cb", _fm(inp["ssm_conv_b"][0], 12))
    put("dch", _fm(np.repeat(np.asarray(inp["ssm_d"][0], f32), 64), 8))
    put("snw", _fm(inp["ssm_norm_w"][0], 8))
    put("qw", np.tile(np.asarray(inp["attn_q_norm_w"][0], f32), 2)[:, None])
    put("kw", np.tile(np.asarray(inp["attn_k_norm_w"][0], f32), 2)[:, None])
    put("pw1b", _fm(inp["conv_pw1_b"][0], 16))
    dw = np.asarray(inp["conv_dw_w"][0], f32)
    put("dww", dw.T.reshape(8, 128, 31).transpose(1, 0, 2).reshape(128, 248))
    put("dwb", _fm(inp["conv_dw_b"][0], 8))
    put("lnw", _fm(inp["conv_ln_w"][0], 8))
    put("lnb", _fm(inp["conv_ln_b"][0], 8))
    put("pw2b", _fm(inp["conv_pw2_b"][0], 8))
    fcw = np.asarray(inp["ffn_conv_w"], f32)
    put("fcw", fcw.transpose(0, 2, 1).reshape(2, 44, 128, 3).transpose(2, 0, 1, 3).reshape(128, 264))
    put("fcb", np.asarray(inp["ffn_conv_b"], f32).reshape(2, 44, 128).transpose(2, 0, 1).reshape(128, 88))
    put("dtb", np.broadcast_to(np.asarray(inp["ssm_dt_bias"][0], f32)[None, :], (128, 16)))
    put("alog", np.broadcast_to(np.asarray(inp["ssm_a_log"][0], f32)[None, :], (128, 16)))
    put("sink", np.broadcast_to(np.asarray(inp["attn_sinks"][0], f32)[None, :], (128, 16)))
    rb = np.asarray(inp["rel_bias"], f32)
    qi = np.arange(128)[:, None]
    sj = np.arange(256)[None, :]
    dist = qi + 128 - sj
    valid = (dist >= 0) & (dist < 128)
    bias = rb[_t5_bucket(np.maximum(dist, 0))]
    bias = np.where(valid[:, :, None], bias, f32(NEG)).astype(f32)
    biasT = bias.reshape(128, 2, 128, 16).transpose(2, 3, 1, 0)
    biasT = np.ascontiguousarray(biasT).reshape(128, 16 * 2 * 128)
    cst = np.zeros((128, 512), f32)
    cst[:, 0:128] = np.eye(128, dtype=f32)
    cst[:, 128:256] = np.triu(np.ones((128, 128), f32))
    cst[:, 256:384] = np.kron(np.eye(2, dtype=f32), np.ones((64, 64), f32))
    cst[:, 384:512] = np.where(np.arange(128)[None, :] >= np.arange(128)[:, None], 0.0, NEG)
    eall = np.zeros((48, 16, 128), f32)
    for h in range(16):
        eall[h, h, :] = 1.0
        eall[32 + h, h, :] = 1.0
    return prm, biasT, cst, eall.reshape(48, 2048)


_NC_CACHE = {}


def kernel(**inp):
    x = np.asarray(inp["x"], np.float32)
    B, L, _ = x.shape
    NPRE, NMAIN = 7, 9
    prm, biasT, cst, eall = host_pack(inp)
    common = {
        "prm": prm, "biasT": biasT, "cst": cst, "eall": eall,
        "w_in": np.ascontiguousarray(inp["hyb_w_in"][0], np.float32),
        "w_out": np.ascontiguousarray(inp["hyb_w_out"][0], np.float32),
        "pw1": np.ascontiguousarray(inp["conv_pw1_w"][0], np.float32),
        "pw2": np.ascontiguousarray(inp["conv_pw2_w"][0], np.float32),
        "wup0": np.ascontiguousarray(inp["ffn_w_up"][0], np.float32),
        "wup1": np.ascontiguousarray(inp["ffn_w_up"][1], np.float32),
        "wdn0": np.ascontiguousarray(inp["ffn_w_down"][0], np.float32),
        "wdn1": np.ascontiguousarray(inp["ffn_w_down"][1], np.float32),
    }
    in_maps = []
    for core in range(8):
        b, half = core // 2, core % 2
        if half == 1:
            xs = x[b]
            flag = np.ones((128, 1), np.float32)
        else:
            xs = np.concatenate([np.zeros((4096, D), np.float32), x[b, :4096]], 0)
            flag = np.zeros((128, 1), np.float32)
        m = dict(common)
        m["x"] = np.ascontiguousarray(xs)
        m["flag"] = flag
        in_maps.append(m)
    if "nc" not in _NC_CACHE:
        _NC_CACHE["nc"] = build_program(NPRE, NMAIN)
    res = run_bass_kernel_spmd(_NC_CACHE["nc"], in_maps, core_ids=list(range(8)))
    out = np.empty((B, L, D), np.float32)
    for core in range(8):
        b, half = core // 2, core % 2
        out[b, half * 4096:(half + 1) * 4096] = res.results[core]["y"]
    return out
```
